# Optimizing a Trainium2 kernel written in Bass

```python
import jax, jax.numpy as jnp
from jax import lax
import numpy as np

D_MODEL = 1024
BATCH = 2
SEQ = 8192
DEPTH = 1
DEC_BATCH = 128
DEC_SEQ = 1
PAST_LEN = 2048
PAGE_SIZE = 128

N_HEADS = 8
KV_HEADS = 2
HEADS_PER_GROUP = N_HEADS // KV_HEADS
HEAD_DIM = 64
ROPE_DIM = HEAD_DIM // 4
ROPE_THETA = 500000.0
CMP_BLOCK = 32
CMP_STRIDE = 16
CMP_HIDDEN = 256
SEL_BLOCK = 64
N_SEL = 16
WINDOW = 512
Q_BLOCK = 128
FORCED_SCORE = 1e6
CONV_WIDTH = D_MODEL // 2
CONV_K = 3
ATTN_WIDTH = N_HEADS * HEAD_DIM
KV_WIDTH = KV_HEADS * HEAD_DIM
N_BRANCH = 2
SPLIT_SIZES = (ATTN_WIDTH, 6 * KV_WIDTH, 3 * N_HEADS, ATTN_WIDTH, 3 * CONV_WIDTH, CONV_WIDTH, N_BRANCH * D_MODEL)
IN_WIDTH = sum(SPLIT_SIZES)
RMS_EPS = 1e-6

kernel_name = "nsa_shortconv_gated_hybrid_step"


def rmsnorm(x, g):
    x32 = x.astype(jnp.float32)
    y = x32 * lax.rsqrt(jnp.mean(x32 * x32, axis=-1, keepdims=True) + RMS_EPS)
    return (y * g.astype(jnp.float32)).astype(x.dtype)


def rope(x, pos):
    half = ROPE_DIM // 2
    inv = ROPE_THETA ** (-jnp.arange(half, dtype=jnp.float32) / half)
    ang = pos.astype(jnp.float32)[:, None] * inv[None, :]
    cos = jnp.cos(ang)[None, :, None, :]
    sin = jnp.sin(ang)[None, :, None, :]
    xr = x[..., :ROPE_DIM].astype(jnp.float32)
    x1, x2 = xr[..., :half], xr[..., half:]
    rot = jnp.concatenate([x1 * cos - x2 * sin, x1 * sin + x2 * cos], axis=-1).astype(x.dtype)
    return jnp.concatenate([rot, x[..., ROPE_DIM:]], axis=-1)


def masked_softmax(s, mask):
    s = jnp.where(mask, s, -jnp.inf)
    m = jnp.max(s, axis=-1, keepdims=True)
    m = jnp.where(jnp.isfinite(m), m, 0.0)
    e = jnp.exp(s - m)
    return e / jnp.maximum(jnp.sum(e, axis=-1, keepdims=True), 1e-30)


def compress(k, pe, w1, w2):
    B, T, G, hd = k.shape
    r = CMP_BLOCK // CMP_STRIDE
    n_chunk = T // CMP_STRIDE
    n_cmp = n_chunk - r + 1
    kc = k[:, :n_chunk * CMP_STRIDE].reshape(B, n_chunk, CMP_STRIDE, G, hd)
    blocks = jnp.concatenate([kc[:, i:i + n_cmp] for i in range(r)], axis=2)
    blocks = blocks + pe[None, None, :, None, :].astype(k.dtype)
    flat = jnp.transpose(blocks, (0, 1, 3, 2, 4)).reshape(B, n_cmp, G, CMP_BLOCK * hd)
    return jax.nn.silu(flat @ w1) @ w2


def cmp_sel_overlap(n_cmp, n_selb):
    cs = jnp.arange(n_cmp) * CMP_STRIDE
    ss = jnp.arange(n_selb) * SEL_BLOCK
    ov = (cs[:, None] < ss[None, :] + SEL_BLOCK) & (cs[:, None] + CMP_BLOCK > ss[None, :])
    return ov.astype(jnp.float32)


def sel_blocks(k):
    B, T, G, hd = k.shape
    n_selb = -(-T // SEL_BLOCK)
    k = jnp.pad(k, ((0, 0), (0, n_selb * SEL_BLOCK - T), (0, 0), (0, 0)))
    return jnp.transpose(k.reshape(B, n_selb, SEL_BLOCK, G, hd), (0, 3, 1, 2, 4))


def gather_blocks(blocks, idx):
    return jax.vmap(jax.vmap(lambda blk, ix: blk[ix]))(blocks, idx)


def nsa_attend(q, gates, qpos, kc, vc, ksb, vsb, ovl, kw, vw, kwpos):
    f32 = jnp.float32
    scale = HEAD_DIM ** -0.5
    n_cmp = kc.shape[1]
    s = jnp.einsum('bqghd,bcgd->bghqc', q, kc, preferred_element_type=f32) * scale
    c_end = jnp.arange(n_cmp) * CMP_STRIDE + (CMP_BLOCK - 1)
    p_cmp = masked_softmax(s, c_end[None, :] <= qpos[:, None])
    o_cmp = jnp.einsum('bghqc,bcgd->bqghd', p_cmp.astype(vc.dtype), vc)
    n_selb = ksb.shape[2]
    imp = jnp.einsum('bghqc,cj->bgqj', p_cmp, ovl)
    j = jnp.arange(n_selb)[None, :]
    cur = (qpos // SEL_BLOCK)[:, None]
    forced = (j == 0) | (j == cur) | (j == cur - 1)
    valid = j <= cur
    score = jnp.where(forced, FORCED_SCORE, jnp.where(valid, imp, -1.0))
    _, idx = lax.top_k(score, min(N_SEL, n_selb))
    B, G, Tq, kk = idx.shape
    kg = gather_blocks(ksb, idx).reshape(B, G, Tq, kk * SEL_BLOCK, HEAD_DIM)
    vg = gather_blocks(vsb, idx).reshape(B, G, Tq, kk * SEL_BLOCK, HEAD_DIM)
    kpos = (idx[..., None] * SEL_BLOCK + jnp.arange(SEL_BLOCK)).reshape(B, G, Tq, kk * SEL_BLOCK)
    s = jnp.einsum('bqghd,bgqsd->bghqs', q, kg, preferred_element_type=f32) * scale
    p = masked_softmax(s, (kpos <= qpos[:, None])[:, :, None])
    o_sel = jnp.einsum('bghqs,bgqsd->bqghd', p.astype(vg.dtype), vg)
    s = jnp.einsum('bqghd,bsgd->bghqs', q, kw, preferred_element_type=f32) * scale
    dt = qpos[:, None] - kwpos[None, :]
    p = masked_softmax(s, (dt >= 0) & (dt < WINDOW) & (kwpos >= 0)[None, :])
    o_win = jnp.einsum('bghqs,bsgd->bqghd', p.astype(vw.dtype), vw)
    return gates[..., 0:1] * o_cmp + gates[..., 1:2] * o_sel + gates[..., 2:3] * o_win


def project(x, c, pos, p):
    B, T, _ = x.shape
    mod = jax.nn.silu(c) @ p['w_ada'] + p['b_ada']
    shift, scale, gate = jnp.split(mod[:, None, :], 3, axis=-1)
    h = rmsnorm(x, p['g_pre']) * (1 + scale) + shift
    z = h @ p['w_in']
    zq, zkv, zg, za, zconv, zcg, zm = jnp.split(z, np.cumsum(SPLIT_SIZES)[:-1].tolist(), axis=-1)
    q = rope(zq.reshape(B, T, N_HEADS, HEAD_DIM), pos).reshape(B, T, KV_HEADS, HEADS_PER_GROUP, HEAD_DIM)
    kv = zkv.reshape(B, T, 6, KV_HEADS, HEAD_DIM)
    kv = jnp.stack([rope(kv[:, :, i], pos) if i % 2 == 0 else kv[:, :, i] for i in range(6)], axis=2)
    nsa_g = jax.nn.sigmoid(zg).reshape(B, T, KV_HEADS, HEADS_PER_GROUP, 3)
    cb, cc, cx = jnp.split(zconv, 3, axis=-1)
    return dict(q=q, kv=kv, nsa_g=nsa_g, a_gate=za, cb=cb, u=cc * cx, cgate=zcg, merge=zm, gate=gate)


def causal_conv(up, w):
    T = up.shape[1] - (CONV_K - 1)
    out = w[0] * up[:, 0:T]
    for j in range(1, CONV_K):
        out = out + w[j] * up[:, j:j + T]
    return out


def finish(x, o_attn, conv_out, pr, p):
    ya = (o_attn * jax.nn.silu(pr['a_gate'])) @ p['w_br_a']
    yb = (pr['cb'] * conv_out * jax.nn.silu(pr['cgate'])) @ p['w_br_b']
    ga, gb = jnp.split(jax.nn.sigmoid(pr['merge']), 2, axis=-1)
    o = (ga * ya + gb * yb) @ p['w_out']
    return x + pr['gate'] * rmsnorm(o, p['g_post'])


def prompt_layer(x, c, p):
    B, T, _ = x.shape
    pr = project(x, c, jnp.arange(T), p)
    kv = pr['kv']
    kc = compress(kv[:, :, 0], p['pe_cmp'][0], p['w_cmp1'][0], p['w_cmp2'][0])
    vc = compress(kv[:, :, 1], p['pe_cmp'][1], p['w_cmp1'][1], p['w_cmp2'][1])
    ksb = sel_blocks(kv[:, :, 2])
    vsb = sel_blocks(kv[:, :, 3])
    ovl = cmp_sel_overlap(kc.shape[1], ksb.shape[2])
    kwp = jnp.pad(kv[:, :, 4:6], ((0, 0), (WINDOW, 0), (0, 0), (0, 0), (0, 0)))
    n_qb = T // Q_BLOCK
    qb = jnp.moveaxis(pr['q'].reshape(B, n_qb, Q_BLOCK, KV_HEADS, HEADS_PER_GROUP, HEAD_DIM), 1, 0)
    gb = jnp.moveaxis(pr['nsa_g'].reshape(B, n_qb, Q_BLOCK, KV_HEADS, HEADS_PER_GROUP, 3), 1, 0)

    def body(args):
        q_blk, g_blk, b = args
        qpos = b * Q_BLOCK + jnp.arange(Q_BLOCK)
        kw = lax.dynamic_slice_in_dim(kwp, b * Q_BLOCK, WINDOW + Q_BLOCK, axis=1)
        kwpos = b * Q_BLOCK - WINDOW + jnp.arange(WINDOW + Q_BLOCK)
        return nsa_attend(q_blk, g_blk, qpos, kc, vc, ksb, vsb, ovl, kw[:, :, 0], kw[:, :, 1], kwpos)

    o = lax.map(body, (qb, gb, jnp.arange(n_qb)))
    o = jnp.moveaxis(o, 0, 1).reshape(B, T, ATTN_WIDTH)
    up = jnp.pad(pr['u'], ((0, 0), (CONV_K - 1, 0), (0, 0)))
    y = finish(x, o, causal_conv(up, p['conv_w']), pr, p)
    return y, kv[:, :, :4], kv[:, T - min(WINDOW, T):, 4:6], up[:, -(CONV_K - 1):]


def sample_layer(x, c, cache_pages, page_table, win_buf, conv_buf, p):
    B, T, _ = x.shape
    past_len = page_table.shape[1] * cache_pages.shape[1]
    pos = past_len + jnp.arange(T)
    pr = project(x, c, pos, p)
    kv = pr['kv']
    past = cache_pages[page_table].reshape(B, past_len, 4, KV_HEADS, HEAD_DIM)
    full = jnp.concatenate([past, kv[:, :, :4]], axis=1)
    kc = compress(full[:, :, 0], p['pe_cmp'][0], p['w_cmp1'][0], p['w_cmp2'][0])
    vc = compress(full[:, :, 1], p['pe_cmp'][1], p['w_cmp1'][1], p['w_cmp2'][1])
    ksb = sel_blocks(full[:, :, 2])
    vsb = sel_blocks(full[:, :, 3])
    ovl = cmp_sel_overlap(kc.shape[1], ksb.shape[2])
    wbuf = win_buf.shape[1]
    wk = jnp.concatenate([win_buf, kv[:, :, 4:6]], axis=1)
    kwpos = past_len - wbuf + jnp.arange(wbuf + T)
    o = nsa_attend(pr['q'], pr['nsa_g'], pos, kc, vc, ksb, vsb, ovl, wk[:, :, 0], wk[:, :, 1], kwpos)
    o = o.reshape(B, T, ATTN_WIDTH)
    up = jnp.concatenate([conv_buf, pr['u']], axis=1)
    y = finish(x, o, causal_conv(up, p['conv_w']), pr, p)
    n_keep = min(WINDOW, wbuf + T)
    return y, kv[:, :, :4], wk[:, wbuf + T - n_keep:], up[:, -(CONV_K - 1):]


def setup_inputs(seed: int = 0) -> dict:
    key = jax.random.key(seed)
    ks = jax.random.split(key, 24)
    n_pages = PAST_LEN // PAGE_SIZE
    n_phys = (5 * DEC_BATCH * n_pages) // 4
    wbuf = min(WINDOW, PAST_LEN)
    f32 = jnp.float32

    def nrm(k, shape, s):
        return jax.random.normal(k, shape, f32) * s

    page_table = jax.random.permutation(ks[0], n_phys)[:DEC_BATCH * n_pages].reshape(DEC_BATCH, n_pages).astype(jnp.int32)
    return {
        "x_prompt": nrm(ks[1], (BATCH, SEQ, D_MODEL), 1.0),
        "x_sample": nrm(ks[2], (DEC_BATCH, DEC_SEQ, D_MODEL), 1.0),
        "cache_kv_pages": nrm(ks[3], (DEPTH, n_phys, PAGE_SIZE, 4, KV_HEADS, HEAD_DIM), 1.0),
        "state_win_kv": nrm(ks[4], (DEPTH, DEC_BATCH, wbuf, 2, KV_HEADS, HEAD_DIM), 1.0),
        "state_conv": nrm(ks[5], (DEPTH, DEC_BATCH, CONV_K - 1, CONV_WIDTH), 1.0),
        "page_table": page_table,
        "c_prompt": nrm(ks[6], (BATCH, D_MODEL), 1.0),
        "c_sample": nrm(ks[7], (DEC_BATCH, D_MODEL), 1.0),
        "w_ada": nrm(ks[8], (DEPTH, D_MODEL, 3 * D_MODEL), 0.5 * D_MODEL ** -0.5),
        "b_ada": nrm(ks[9], (DEPTH, 3 * D_MODEL), 0.01),
        "g_pre": 1.0 + nrm(ks[10], (DEPTH, D_MODEL), 0.02),
        "g_post": 1.0 + nrm(ks[11], (DEPTH, D_MODEL), 0.02),
        "w_in": nrm(ks[12], (DEPTH, D_MODEL, IN_WIDTH), D_MODEL ** -0.5),
        "pe_cmp": nrm(ks[13], (DEPTH, 2, CMP_BLOCK, HEAD_DIM), 0.02),
        "w_cmp1": nrm(ks[14], (DEPTH, 2, CMP_BLOCK * HEAD_DIM, CMP_HIDDEN), (CMP_BLOCK * HEAD_DIM) ** -0.5),
        "w_cmp2": nrm(ks[15], (DEPTH, 2, CMP_HIDDEN, HEAD_DIM), CMP_HIDDEN ** -0.5),
        "conv_w": nrm(ks[16], (DEPTH, CONV_K, CONV_WIDTH), CONV_K ** -0.5),
        "w_br_a": nrm(ks[17], (DEPTH, ATTN_WIDTH, D_MODEL), ATTN_WIDTH ** -0.5),
        "w_br_b": nrm(ks[18], (DEPTH, CONV_WIDTH, D_MODEL), CONV_WIDTH ** -0.5),
        "w_out": nrm(ks[19], (DEPTH, D_MODEL, D_MODEL), D_MODEL ** -0.5),
    }


def reference(x_prompt, x_sample, cache_kv_pages, state_win_kv, state_conv, page_table, c_prompt, c_sample,
              w_ada, b_ada, g_pre, g_post, w_in, pe_cmp, w_cmp1, w_cmp2, conv_w, w_br_a, w_br_b, w_out):
    hp, hs = x_prompt, x_sample
    kvp_l, wp_l, cp_l, kvs_l, ws_l, cs_l = [], [], [], [], [], []
    for l in range(DEPTH):
        p = dict(w_ada=w_ada[l], b_ada=b_ada[l], g_pre=g_pre[l], g_post=g_post[l], w_in=w_in[l],
                 pe_cmp=pe_cmp[l], w_cmp1=w_cmp1[l], w_cmp2=w_cmp2[l], conv_w=conv_w[l],
                 w_br_a=w_br_a[l], w_br_b=w_br_b[l], w_out=w_out[l])
        hp, kvp, wp, cp = prompt_layer(hp, c_prompt, p)
        hs, kvs, ws, cs = sample_layer(hs, c_sample, cache_kv_pages[l], page_table, state_win_kv[l], state_conv[l], p)
        kvp_l.append(kvp); wp_l.append(wp); cp_l.append(cp)
        kvs_l.append(kvs); ws_l.append(ws); cs_l.append(cs)
    kv_rows_prompt = jnp.stack(kvp_l)
    win_kv_prompt = jnp.stack(wp_l)
    conv_state_prompt = jnp.stack(cp_l)
    kv_rows_sample = jnp.stack(kvs_l)
    win_kv_sample = jnp.stack(ws_l)
    conv_state_sample = jnp.stack(cs_l)
    return (hp, hs, kv_rows_prompt, win_kv_prompt, conv_state_prompt, kv_rows_sample, win_kv_sample, conv_state_sample)
```

```python
from concourse.bass_utils import run_bass_kernel_spmd

from contextlib import ExitStack
import numpy as np
import concourse.bass as bass
import concourse.mybir as mybir

F32 = mybir.dt.float32
BF16 = mybir.dt.bfloat16
I32 = mybir.dt.int32
U32 = mybir.dt.uint32
AF = mybir.ActivationFunctionType
ALU = mybir.AluOpType
AX = mybir.AxisListType

SAME_ENG_SYNC = True
COMPUTE = ("pe", "dve", "act", "pool")


class Buf:
    def __init__(self, prog, name, t, is_dram=False):
        self.prog = prog
        self.name = name
        self.t = t
        self.is_dram = is_dram
        self.st = {}
        self.whole = [None, []]
        self.excl = False

    def __getitem__(self, idx):
        return self.t[idx]


class Op:
    __slots__ = ("eng", "fn", "deps", "signal", "val", "sem", "chan", "idx", "is_dma")

    def __init__(self, eng, fn):
        self.eng = eng
        self.fn = fn
        self.deps = {}
        self.signal = False
        self.val = None
        self.sem = None
        self.chan = None
        self.is_dma = False


class Prog:
    def __init__(self, nc):
        self.nc = nc
        self.ops = []
        self.stack = ExitStack()
        self.chan_count = {}
        self.nbuf = 0

    def sb(self, name, shape, dtype):
        t = self.stack.enter_context(self.nc.sbuf_tensor("sb_" + name, list(shape), dtype))
        return Buf(self, name, t)

    def ps(self, name, shape, dtype):
        t = self.stack.enter_context(self.nc.psum_tensor("ps_" + name, list(shape), dtype))
        b = Buf(self, name, t)
        b.excl = True
        return b

    def dram(self, name, shape, dtype, kind="Internal"):
        t = self.nc.dram_tensor(name, list(shape), dtype, kind=kind)
        return Buf(self, name, t.ap(), is_dram=True)

    def _norm(self, lst):
        out = []
        for x in lst:
            if x is None:
                continue
            if isinstance(x, Buf):
                out.append((x, None))
            else:
                out.append(x)
        return out

    def op(self, eng, fn, reads=(), writes=(), chan=None):
        o = Op(eng, fn)
        o.idx = len(self.ops)
        if chan is not None:
            o.is_dma = True
            o.chan = chan
            self.chan_count[chan] = self.chan_count.get(chan, 0) + 1
            o.val = 16 * self.chan_count[chan]
            o.signal = True
        reads = self._norm(reads)
        writes = self._norm(writes)
        writes = writes + [(b_, k_) for (b_, k_) in reads if b_.excl]
        reads = [(b_, k_) for (b_, k_) in reads if not b_.excl]

        def add_dep(d):
            if d is None or d is o:
                return
            if d.is_dma:
                v = 16 * self.chan_count[d.chan]
                if d is not o and d.chan == o.chan:
                    v = d.val
                o.deps[d] = max(o.deps.get(d, 0), v)
            else:
                if d.eng == o.eng and not o.is_dma:
                    if d.eng == "pe" or not SAME_ENG_SYNC:
                        return
                o.deps[d] = 0

        for b, k in reads:
            add_dep(b.whole[0])
            if k is None:
                for st in b.st.values():
                    add_dep(st[0])
            elif k in b.st:
                add_dep(b.st[k][0])
        for b, k in writes:
            add_dep(b.whole[0])
            for r in b.whole[1]:
                add_dep(r)
            if k is None:
                for st in b.st.values():
                    add_dep(st[0])
                    for r in st[1]:
                        add_dep(r)
            elif k in b.st:
                add_dep(b.st[k][0])
                for r in b.st[k][1]:
                    add_dep(r)
        for b, k in reads:
            if k is None:
                b.whole[1].append(o)
            else:
                b.st.setdefault(k, [None, []])[1].append(o)
        for b, k in writes:
            if k is None:
                b.whole = [o, []]
                b.st = {}
            else:
                b.st[k] = [o, []]
        self.ops.append(o)
        return o

    def dma(self, eng, out, in_, reads=(), writes=(), chan=None, **kw):
        assert chan is not None
        return self.op(eng, lambda e: e.dma_start(out=out, in_=in_, **kw), reads, writes, chan=chan)

    def emit(self):
        nc = self.nc
        for o in self.ops:
            for d in o.deps:
                d.signal = True
        sems = {}
        for e in COMPUTE:
            sems[e] = self.stack.enter_context(nc.semaphore("s_" + e))
        for c in self.chan_count:
            sems["c_" + c] = self.stack.enter_context(nc.semaphore("c_" + c))
        cnt = {e: 0 for e in COMPUTE}
        for o in self.ops:
            if o.is_dma:
                o.sem = sems["c_" + o.chan]
            else:
                o.sem = sems[o.eng]
                if o.signal:
                    cnt[o.eng] += 1
                    o.val = cnt[o.eng]
        by_eng = {}
        for o in self.ops:
            by_eng.setdefault(o.eng, []).append(o)
        last_chan_eng = {}
        for o in self.ops:
            if o.is_dma:
                last_chan_eng[o.chan] = o.eng
        self.n_inst = {e: len(v) for e, v in by_eng.items()}

        def make_section(ename, ops):
            def section(eng):
                waited = {}
                for o in ops:
                    for d, v in o.deps.items():
                        val = v if d.is_dma else d.val
                        key = id(d.sem)
                        if waited.get(key, 0) >= val:
                            continue
                        eng.wait_ge(d.sem, val)
                        waited[key] = val
                    inst = o.fn(eng)
                    if o.signal:
                        inst.then_inc(o.sem, 16 if o.is_dma else 1)
                for c, e in last_chan_eng.items():
                    if e == ename:
                        eng.wait_ge(sems["c_" + c], 16 * self.chan_count[c])
            return section

        with nc.Block() as block:
            dec = {"pe": block.tensor, "dve": block.vector, "act": block.scalar,
                   "pool": block.gpsimd, "sp": block.sync}
            for ename, ops in by_eng.items():
                dec[ename](make_section(ename, ops))
        self.stack.close()

D = 1024
NT = 64
NSLOT = 16
BIG8 = 240000.0
IN_W = 5912
KV0 = 512
G0 = 1280
CH_COLS = [0, 1304, 1816, 2328, 2840, 3352, 3864, 4376, 4888, 5400]
NWB = 2
NCH = 14


def _mk(P):
    class H:
        pass
    h = H()

    def mm(out, lhsT, rhs, start, stop, reads, writes, **kw):
        return P.op("pe", lambda e: e.matmul(out=out, lhsT=lhsT, rhs=rhs, start=start, stop=stop, **kw), reads, writes)

    def tr(out, in_, ident, reads, writes):
        return P.op("pe", lambda e: e.transpose(out=out, in_=in_, identity=ident), reads, writes)

    def act(out, in_, func, reads, writes, **kw):
        return P.op("act", lambda e: e.activation(out=out, in_=in_, func=func, **kw), reads, writes)

    def tt(eng, out, in0, in1, op, reads, writes):
        return P.op(eng, lambda e: e.tensor_tensor(out=out, in0=in0, in1=in1, op=op), reads, writes)

    def ts(eng, out, in0, s1, s2, op0, op1, reads, writes, **kw):
        if op1 is None:
            return P.op(eng, lambda e: e.tensor_scalar(out=out, in0=in0, scalar1=s1, scalar2=None, op0=op0, **kw), reads, writes)
        return P.op(eng, lambda e: e.tensor_scalar(out=out, in0=in0, scalar1=s1, scalar2=s2, op0=op0, op1=op1, **kw), reads, writes)

    def stt(out, in0, scalar, in1, op0, op1, reads, writes):
        return P.op("dve", lambda e: e.scalar_tensor_tensor(out=out, in0=in0, scalar=scalar, in1=in1, op0=op0, op1=op1), reads, writes)

    def cp(eng, out, in_, reads, writes):
        if eng == "act":
            return P.op("act", lambda e: e.copy(out=out, in_=in_), reads, writes)
        return P.op(eng, lambda e: e.tensor_copy(out=out, in_=in_), reads, writes)

    def ms(eng, ap, val, writes):
        return P.op(eng, lambda e: e.memset(ap, val), (), writes)

    h.mm, h.tr, h.act, h.tt, h.ts, h.stt, h.cp, h.ms = mm, tr, act, tt, ts, stt, cp, ms
    return h


def build(nt=NT, nslot=NSLOT, do_sample=True, dbg=False):
    nc = bass.Bass("TRN2", target_bir_lowering=False)
    P = Prog(nc)
    h = _mk(P)
    mm, tr, act, tt, ts, stt, cp, ms = h.mm, h.tr, h.act, h.tt, h.ts, h.stt, h.cp, h.ms

    def din(name, shape, dt=F32):
        return nc.dram_tensor(name, list(shape), dt, kind="ExternalInput").ap()

    def dout(name, shape, dt=F32):
        return nc.dram_tensor(name, list(shape), dt, kind="ExternalOutput").ap()

    xp = din("xp", [NT * 128, D])
    xmine = din("xmine", [NSLOT * 128, D])
    xprev = din("xprev", [NSLOT * 32, D])
    cp_d = din("cpr", [1, D])
    w_ada = din("w_ada", [D, 3 * D])
    b_ada = din("b_ada", [1, 3 * D])
    g_pre = din("g_pre", [1, D])
    g_post = din("g_post", [1, D])
    w_in = din("w_in", [D, IN_W])
    pe_cmp = din("pe_cmp", [2, 32, 64])
    w_cmp1 = din("w_cmp1", [2, 2048, 256])
    w_cmp2 = din("w_cmp2", [2, 256, 64])
    conv_w = din("conv_w", [3, 512])
    w_br_a = din("w_br_a", [512, D])
    w_br_b = din("w_br_b", [512, D])
    w_out = din("w_out", [D, D])
    ident_d = din("ident", [128, 128])
    ropeA_d = din("ropeA", [128, NT, 16])
    ropeB_d = din("ropeB", [128, NSLOT, 16])
    slotc_d = din("slotc", [NSLOT, 128, 3, 128])
    cmpb_d = din("cmpb", [NSLOT, 128, 4, 128])
    winb_d = din("winb", [128, 8, 128])
    selc_d = din("selc", [128, 3, 4, 128])
    ovl_d = din("ovl", [128, 4, 128])
    shm_d = din("shm", [128, 2, 128])
    shb_d = din("shb", [32, 2, 128])
    upsc_d = din("upsc", [32, NSLOT])

    xs_d = din("xs", [16, D])
    cs_d = din("cs", [16, D])
    ptab_d = din("ptab", [1, 256], I32)
    cache_d = din("cache", [2560 * 128, 512])
    swin_d = din("swin", [16, 512, 256])
    sconv_d = din("sconv", [16, 2, 512])
    ropeS_d = din("ropeS", [16, 16])
    slotS_d = din("slotS", [128, 3, 128])
    smask_d = din("smask", [128, 4, 128])
    onehot_d = din("onehot", [128, 16])
    pcol_d = din("pcol", [128, 1])
    ys = dout("ys", [16, D])
    kvs = dout("kvs", [16, 512])
    wins = dout("wins", [16, 512, 256])
    convs = dout("convs", [16, 2, 512])
    yp = dout("yp", [NSLOT * 128, D])
    kvp = dout("kvp", [NSLOT * 128, 512])
    winp = dout("winp", [128, 256])
    convp = dout("convp", [2, 512])
    dbg_outs = {}

    def dbg_out(name, buf, ap, shape, dt=F32):
        if not dbg:
            return
        d = dout("dbg_" + name, shape, dt)
        P.dma("sp", d, ap, reads=[buf], chan="dbg_" + name)

    wscr = P.dram("wscr", [NCH, 128, 4096], BF16)

    ident = P.sb("ident", [128, 128], F32)
    identb = P.sb("identb", [128, 128], BF16)
    I4 = P.sb("I4", [128, 512], BF16)
    ones1 = P.sb("ones1", [1, 128], F32)
    wkv = P.sb("wkv", [128, 8, 768], BF16)
    wg = P.sb("wg", [128, 8, 24], BF16)
    w1 = P.sb("w1", [128, 32, 256], BF16)
    w2 = P.sb("w2", [128, 2, 2, 64], BF16)
    wch = [P.sb("wch%d" % i, [128, 4096], BF16) for i in range(NWB)]
    KselT = P.sb("KselT", [128, NT * 128], BF16)
    Vsel = P.sb("Vsel", [128, NT, 2, 65], BF16)
    KwinT = P.sb("KwinT", [128, 8 * 128], BF16)
    Vwin = P.sb("Vwin", [128, 8, 2, 65], BF16)
    kcT = P.sb("kcT", [128, 512], BF16)
    vca = P.sb("vca", [128, 4, 2, 65], BF16)
    vcf = P.sb("vcf", [128, 4, 2, 64], F32)
    CT = [P.sb("CT%d" % g, [128, 144], BF16) for g in range(2)]
    xt = [P.sb("xt0", [128, D], F32)] * 2
    xn = P.sb("xn", [128, D], F32)
    hT = [P.sb("hT0", [128, 8, 128], BF16)] * 2
    zkv = P.sb("zkv", [128, 768], F32)
    ctsrc = P.sb("ctsrc", [128, 256], F32)
    ropeA = P.sb("ropeA", [128, NT, 16], F32)
    ropeB = P.sb("ropeB", [128, NSLOT, 16], F32)
    rt = P.sb("rt", [128, 4, 8 * 8], F32)
    ss = P.sb("ss", [128, 8], F32)
    G1col = P.sb("G1col", [128, 8], F32)
    SHcol = P.sb("SHcol", [128, 8], F32)
    GP = P.sb("GP", [128, D], F32)
    cm = P.sb("cm", [128, 4], F32)
    cmb = P.sb("cmb", [128, 4], BF16)
    cmt8 = P.sb("cmt8", [128, 16], F32)
    cmneg = P.sb("cmneg", [128, 128], F32)
    bias1 = P.sb("bias1", [128, 2, 2], F32)
    hsT = P.sb("hsT", [128, 2, 2, 8], BF16)
    hsTp = P.sb("hsTp", [128, 2, 256], BF16)
    sT = P.sb("sT", [128, 8, 17], F32)
    modT = P.sb("modT", [128, 24, 17], F32)
    coltmp = P.sb("coltmp", [128, 32], F32)
    G1S = P.sb("G1S", [128, 8, 16], F32)
    SHS = P.sb("SHS", [128, 8, 16], F32)
    GST = P.sb("GST", [128, 8, 16], F32)
    bcm = P.sb("bcm", [128, 128], F32)
    convwb = P.sb("convwb", [128, 3, 512], F32)
    winb = P.sb("winb", [128, 8, 128], F32)
    selc = P.sb("selc", [128, 3, 4, 128], F32)
    ovl = P.sb("ovl", [128, 4, 128], BF16)
    ovlf = P.sb("ovlf", [128, 4, 128], F32)
    shm = P.sb("shm", [128, 2, 128], F32)
    shb = P.sb("shb", [32, 2, 128], F32)
    upsc = P.sb("upsc", [32, NSLOT], F32)
    pet = P.sb("pet", [32, 2, 64], F32)
    peT = P.sb("peT", [128, 32], BF16)

    xm = P.sb("xm", [128, D], F32)
    ps = [P.ps("b%d" % i, [128, 512], F32) for i in range(8)]

    P.dma("sp", ident[:], ident_d, writes=[ident], chan="c0")
    P.dma("sp", ropeA[:], ropeA_d, writes=[ropeA], chan="c0")
    P.dma("sp", ropeB[:], ropeB_d, writes=[ropeB], chan="c0")
    P.dma("sp", winb[:], winb_d, writes=[winb], chan="c0")
    P.dma("sp", selc[:], selc_d, writes=[selc], chan="c0")
    P.dma("sp", ovlf[:], ovl_d, writes=[ovlf], chan="c0")
    P.dma("sp", shm[:], shm_d, writes=[shm], chan="c0")
    P.dma("sp", shb[:], shb_d, writes=[shb], chan="c0")
    P.dma("sp", upsc[:], upsc_d, writes=[upsc], chan="c0")
    P.dma("sp", pet[:], pe_cmp.rearrange("k r d -> r k d"), writes=[pet], chan="c0")
    cp("dve", identb[:], ident[:], [ident], [identb])
    cp("dve", ovl[:], ovlf[:], [ovlf], [ovl])
    for i in range(4):
        cp("dve", I4[:, i * 128:(i + 1) * 128], ident[:], [ident], [I4])
    ms("dve", ones1[:], 1.0, [ones1])
    ms("pool", Vsel[:], 1.0, [Vsel])
    ms("pool", Vwin[:], 1.0, [Vwin])
    ms("pool", vca[:], 1.0, [vca])
    ms("pool", vcf[:], 0.0, [vcf])
    ms("pool", kcT[:], 0.0, [kcT])
    ms("pool", cm[:], 0.0, [cm])
    for g in range(2):
        ms("pool", CT[g][:], 0.0, [CT[g]])
    ms("pool", KwinT[:], 0.0, [KwinT])
    ms("pool", KselT[:], 0.0, [KselT])

    P.dma("pool", wkv[:], w_in[:, KV0:KV0 + 768].rearrange("(c p) n -> p c n", p=128), writes=[wkv], chan="wk")
    P.dma("pool", wg[:], w_in[:, G0:G0 + 24].rearrange("(c p) n -> p c n", p=128), writes=[wg], chan="wk")
    for kv in range(2):
        P.dma("pool", w1[kv * 64:(kv + 1) * 64, :, :], w_cmp1[kv].rearrange("(r d) n -> d r n", d=64),
              writes=[w1], chan="wk")
        P.dma("pool", w2[:, kv, :, :], w_cmp2[kv].rearrange("(h p) n -> p h n", p=128), writes=[w2], chan="wk")
    for j in range(NCH):
        wb = wch[j % NWB]
        if j < 10:
            src = w_in[:, CH_COLS[j]:CH_COLS[j] + 512].rearrange("(c p) n -> p c n", p=128)
            dst = wb[:].rearrange("p (c n) -> p c n", c=8)
        elif j == 10:
            src = w_br_a.rearrange("(c p) n -> p c n", p=128)
            dst = wb[:].rearrange("p (c n) -> p c n", c=4)
        elif j == 11:
            src = w_br_b.rearrange("(c p) n -> p c n", p=128)
            dst = wb[:].rearrange("p (c n) -> p c n", c=4)
        else:
            o = (j - 12) * 512
            src = w_out[:, o:o + 512].rearrange("(c p) n -> p c n", p=128)
            dst = wb[:].rearrange("p (c n) -> p c n", c=8)
        P.dma("pool", dst, src, writes=[wb], chan="wst%d" % (j % NWB))
        P.dma("sp", wscr[j], wb[:], reads=[wb], writes=[(wscr, j)], chan="wst%d" % (j % NWB))

    P.dma("sp", xn[0:1, :], cp_d, writes=[xn], chan="xnl")
    P.dma("sp", xn[1:17, :], cs_d, writes=[xn], chan="xnl")
    act(xn[0:17, :], xn[0:17, :], AF.Silu, [xn], [xn])
    for k in range(8):
        tr(ps[0][:, k * 17:(k + 1) * 17], xn[0:17, k * 128:(k + 1) * 128], ident[0:17, 0:17], [xn, ident], [ps[0]])
    cp("dve", sT[:].rearrange("p k s -> p (k s)"), ps[0][:, 0:136], [ps[0]], [sT])

    def row_to_cols(row_ap, dst_buf, dst_ap_fn, n):
        for j in range(n):
            tr(ps[0][:, j:j + 1], row_ap[:, j * 128:(j + 1) * 128], ident[0:1, 0:1], [xm, ident], [ps[0]])
        cp("dve", dst_ap_fn, ps[0][:, 0:n], [ps[0]], [dst_buf])

    for j in range(24):
        wa = xt[j % 2]
        P.dma("sp", wa[:].rearrange("p (c n) -> p c n", c=8),
              w_ada[:, j * 128:(j + 1) * 128].rearrange("(c p) n -> p c n", p=128), writes=[wa], chan="xt0")
        for k in range(8):
            mm(ps[1][:, j * 17:(j + 1) * 17], wa[:, k * 128:(k + 1) * 128], sT[:, k, :], k == 0, k == 7, [wa, sT], [ps[1]])
    cp("dve", modT[:].rearrange("p j s -> p (j s)"), ps[1][:, 0:408], [ps[1]], [modT])
    for i3 in range(3):
        P.dma("sp", xm[0:1, :], b_ada[:, i3 * D:(i3 + 1) * D], writes=[xm], chan="xm")
        row_to_cols(xm[0:1, :], coltmp, coltmp[:, i3 * 8:(i3 + 1) * 8], 8)
    tt("dve", modT[:], modT[:], coltmp[:, 0:24].unsqueeze(2).to_broadcast([128, 24, 17]), ALU.add, [modT, coltmp], [modT])
    P.dma("sp", xm[0:1, :], g_pre, writes=[xm], chan="xm")
    row_to_cols(xm[0:1, :], coltmp, coltmp[:, 24:32], 8)
    stt(G1col[:], modT[:, 8:16, 0], 1.0, coltmp[:, 24:32], ALU.add, ALU.mult, [modT, coltmp], [G1col])
    cp("dve", SHcol[:], modT[:, 0:8, 0], [modT], [SHcol])
    ts("dve", G1S[:], modT[:, 8:16, 1:17], 1.0, None, ALU.add, None, [modT], [G1S])
    tt("dve", G1S[:], G1S[:], coltmp[:, 24:32].unsqueeze(2).to_broadcast([128, 8, 16]), ALU.mult, [G1S, coltmp], [G1S])
    cp("dve", SHS[:], modT[:, 0:8, 1:17], [modT], [SHS])
    P.dma("sp", xm[0:1, :], g_post, writes=[xm], chan="xm")
    row_to_cols(xm[0:1, :], coltmp, coltmp[:, 0:8], 8)
    tt("dve", coltmp[:, 8:16], coltmp[:, 0:8], modT[:, 16:24, 0], ALU.mult, [coltmp, modT], [coltmp])
    tt("dve", GST[:], modT[:, 16:24, 1:17], coltmp[:, 0:8].unsqueeze(2).to_broadcast([128, 8, 16]), ALU.mult,
       [modT, coltmp], [GST])
    for c in range(8):
        cp("dve", bcm[:], coltmp[:, 8 + c:9 + c].to_broadcast([128, 128]), [coltmp], [bcm])
        tr(ps[2 + c // 4][:, (c % 4) * 128:(c % 4 + 1) * 128], bcm[:], ident[:], [bcm, ident], [ps[2 + c // 4]])
    cp("dve", GP[:, 0:512], ps[2][:], [ps[2]], [GP])
    cp("dve", GP[:, 512:1024], ps[3][:], [ps[3]], [GP])
    for k in range(3):
        P.dma("sp", xm[0:1, 0:512], conv_w[k:k + 1, :], writes=[xm], chan="xm")
        mm(ps[4][:], ones1[0:1, :], xm[0:1, 0:512], True, True, [ones1, xm], [ps[4]])
        cp("dve", convwb[:, k, :], ps[4][:], [ps[4]], [convwb])
    tr(ps[5][:, 0:32], pet[:].rearrange("r k d -> r (k d)"), ident[0:32, 0:32], [pet, ident], [ps[5]])
    cp("dve", peT[:], ps[5][:, 0:32], [ps[5]], [peT])
    for kv in range(2):
        for half in range(2):
            for r in range(32):
                mm(ps[6 + kv][:, half:half + 1], w1[kv * 64:(kv + 1) * 64, r, half * 128:(half + 1) * 128],
                   peT[kv * 64:(kv + 1) * 64, r:r + 1], r == 0, r == 31, [w1, peT], [ps[6 + kv]])
    for kv in range(2):
        cp("dve", bias1[:, kv, :], ps[6 + kv][:, 0:2], [ps[6 + kv]], [bias1])

    def load_x(buf, src_ap, chan, rows=128):
        P.dma("sp", buf[0:rows, :], src_ap, writes=[buf], chan=chan)

    def norm_T(xb, hTb, rows=128):
        act(xn[0:rows, :], xb[0:rows, :], AF.Square, [xb], [xn, ss], accum_out=ss[0:rows, 0:1])
        ts("dve", ss[0:rows, 1:2], ss[0:rows, 0:1], 1.0 / D, 1e-6, ALU.mult, ALU.add, [ss], [ss])
        act(ss[0:rows, 2:3], ss[0:rows, 1:2], AF.Sqrt, [ss], [ss])
        P.op("dve", lambda e: e.reciprocal(out=ss[0:rows, 3:4], in_=ss[0:rows, 2:3]), [ss], [ss])
        ts("dve", xn[0:rows, :], xb[0:rows, :], ss[0:rows, 3:4], None, ALU.mult, None, [xb, ss], [xn])
        for c in range(8):
            tr(ps[c // 4][:, (c % 4) * 128:(c % 4) * 128 + rows], xn[0:rows, c * 128:(c + 1) * 128],
               ident[0:rows, 0:rows], [xn, ident], [ps[c // 4]])
        for c in range(8):
            act(hTb[:, c, 0:rows], ps[c // 4][:, (c % 4) * 128:(c % 4) * 128 + rows], AF.Identity,
                [ps[c // 4], G1col, SHcol], [hTb], scale=G1col[:, c:c + 1], bias=SHcol[:, c:c + 1])

    def rope_inplace(zb, base_views, tab_ap, rows=128):
        for v in base_views:
            n = v.shape[1]
            x1 = v[:, :, 0:8]
            x2 = v[:, :, 8:16]
            cs = tab_ap[:, 0:8].unsqueeze(1).to_broadcast([rows, n, 8])
            sn = tab_ap[:, 8:16].unsqueeze(1).to_broadcast([rows, n, 8])
            t = [rt[0:rows, i, 0:n * 8].rearrange("p (n e) -> p n e", e=8) for i in range(4)]
            tt("dve", t[0], x1, cs, ALU.mult, [zb], [rt])
            tt("dve", t[1], x2, sn, ALU.mult, [zb], [rt])
            tt("dve", t[2], x1, sn, ALU.mult, [zb], [rt])
            tt("dve", t[3], x2, cs, ALU.mult, [zb], [rt])
            tt("dve", x1, t[0], t[1], ALU.subtract, [rt], [zb])
            tt("dve", x2, t[2], t[3], ALU.add, [rt], [zb])

    def kv_proj(hTb, tab_ap, rows=128):
        for half in range(2):
            for k in range(8):
                mm(ps[2 + half][0:rows, 0:384], hTb[:, k, 0:rows], wkv[:, k, half * 384:(half + 1) * 384], k == 0, k == 7,
                   [hTb, wkv], [ps[2 + half]])
        cp("act", zkv[0:rows, 0:384], ps[2][0:rows, 0:384], [ps[2]], [zkv])
        cp("dve", zkv[0:rows, 384:768], ps[3][0:rows, 0:384], [ps[3]], [zkv])
        views = [zkv[0:rows, o:o + 128].rearrange("p (g d) -> p g d", g=2)[:, :, 0:16] for o in (0, 256, 512)]
        rope_inplace(zkv, views, tab_ap, rows=rows)

    def colmax_update(buf, ap, prt, bi, n):
        P.op("dve", lambda e: e.max(out=cmt8[prt, 0:8], in_=ap), [buf], [cmt8])
        tt("dve", cm[prt, bi:bi + 1], cm[prt, bi:bi + 1], cmt8[prt, 0:1], ALU.max, [cm, cmt8], [cm])
        ts("dve", cmneg[prt, 0:n], ap, -1.0, None, ALU.mult, None, [buf], [cmneg])
        P.op("dve", lambda e: e.max(out=cmt8[prt, 8:16], in_=cmneg[prt, 0:n]), [cmneg], [cmt8])
        tt("dve", cm[prt, bi:bi + 1], cm[prt, bi:bi + 1], cmt8[prt, 8:9], ALU.max, [cm, cmt8], [cm])

    KST = ""

    def phase_a(t):
        xb = xt[t % 2]
        hTb = hT[t % 2]
        norm_T(xb, hTb)
        if t + 1 < nt:
            load_x(xt[(t + 1) % 2], xp[(t + 1) * 128:(t + 2) * 128, :], "xt0")
        kv_proj(hTb, ropeA[:, t, :])
        ingest(t, zkv, True)

    def ingest(t, zb, has_win):
        tr(ps[4][:, 0:128], zb[:, 256:384], ident[:], [zb, ident], [ps[4]])
        if has_win:
            tr(ps[4][:, 128:256], zb[:, 512:640], ident[:], [zb, ident], [ps[4]])
        cp("dve", ctsrc[:].rearrange("p (g k d) -> p g k d", g=2, k=2),
           zb[:, 0:256].rearrange("p (k g d) -> p g k d", k=2, g=2), [zb], [ctsrc])
        for g in range(2):
            tr(ps[4][:, 256 + g * 128:384 + g * 128], ctsrc[:, g * 128:(g + 1) * 128], ident[:], [ctsrc, ident], [ps[4]])
        cp("act", KselT[:, t * 128:(t + 1) * 128], ps[4][:, 0:128], [ps[4]], [(KselT, t)])
        if has_win:
            cp("act", KwinT[:, (t % 8) * 128:(t % 8 + 1) * 128], ps[4][:, 128:256], [ps[4]], [(KwinT, t % 8)])
        for g in range(2):
            cp("act", CT[g][:, 16:144], ps[4][:, 256 + g * 128:384 + g * 128], [ps[4]], [CT[g]])
        for bi, o in (((1, 0), (2, 128)) if has_win else ((1, 0),)):
            colmax_update(ps[4], ps[4][:, o:o + 128], slice(0, 128), bi, 128)
        cp("dve", Vsel[:, t, :, 0:64], zb[:, 384:512].rearrange("p (g d) -> p g d", g=2), [zb], [(Vsel, t)])
        if has_win:
            cp("dve", Vwin[:, t % 8, :, 0:64], zb[:, 640:768].rearrange("p (g d) -> p g d", g=2), [zb], [(Vwin, t % 8)])
        m0 = 1 if t == 0 else 0
        nb = 8 - m0
        i0 = 8 * t - 1 + m0
        kt0 = i0 // 128
        c0 = i0 - kt0 * 128
        for g in range(2):
            for kv in range(2):
                for half in range(2):
                    col = ((g * 2 + kv) * 2 + half) * 8
                    for r in range(32):
                        rhs = CT[g][kv * 64:(kv + 1) * 64, r:r + 16 * 7 + 1:16]
                        mm(ps[5 + 2 * kv][:, col:col + 8], w1[kv * 64:(kv + 1) * 64, r, half * 128:(half + 1) * 128], rhs,
                           r == 0, r == 31, [w1, CT[g]], [ps[5 + 2 * kv]])
            ms("pool", hsTp[:], 0.0, [hsTp])
            for kv in range(2):
                for half in range(2):
                    col = ((g * 2 + kv) * 2 + half) * 8
                    act(hsT[:, kv, half, 0:8], ps[5 + 2 * kv][:, col:col + 8], AF.Silu, [ps[5 + 2 * kv], bias1], [hsT],
                        bias=bias1[:, kv, half:half + 1])
            if KST == "a4":
                continue
            cp("dve", hsTp[:, :, c0:c0 + nb], hsT[:, 1, :, m0:8], [hsT], [hsTp])
            for half in range(2):
                mm(ps[6][g * 64:(g + 1) * 64, 0:8], w2[:, 0, half, :], hsT[:, 0, half, 0:8], half == 0, half == 1,
                   [w2, hsT], [ps[6]])
            cp("act", kcT[g * 64:(g + 1) * 64, i0:i0 + nb], ps[6][g * 64:(g + 1) * 64, m0:8], [ps[6]], [kcT])
            colmax_update(ps[6], ps[6][g * 64:(g + 1) * 64, 0:8], slice(g * 64, (g + 1) * 64), 0, 8)
            if KST == "a5":
                continue
            for w in range(2):
                if c0 + nb <= w * 128 or c0 >= (w + 1) * 128 or kt0 + w > 3:
                    continue
                for half in range(2):
                    mm(ps[7][:, 0:64], hsTp[:, half, w * 128:(w + 1) * 128], w2[:, 1, half, :], half == 0, half == 1,
                       [hsTp, w2], [ps[7]])
                tt("dve", vcf[:, kt0 + w, g, :], vcf[:, kt0 + w, g, :], ps[7][:, 0:64], ALU.add, [vcf, ps[7]], [vcf])
                cp("dve", vca[:, kt0 + w, g, 0:64], vcf[:, kt0 + w, g, :], [vcf], [vca])
        for g in range(2):
            cp("dve", CT[g][:, 0:16], CT[g][:, 128:144], [CT[g]], [CT[g]])


    xpv = P.sb("xpv", [32, D], F32)
    hTm = P.sb("hTm", [128, 8, 128], BF16)
    hTp = P.sb("hTp", [128, 8, 32], BF16)
    qf = P.sb("qf", [128, 512], F32)
    QT = P.sb("QT", [128, 512], BF16)
    aQT = P.sb("aQT", [128, 512], BF16)
    gates = P.sb("gates", [128, 24], F32)
    sa = P.sb("sa", [128, 512], F32)
    cbf = P.sb("cbf", [128, 512], F32)
    ccf = P.sb("ccf", [128, 512], F32)
    uu = P.sb("uu", [128, 512], F32)
    ccp = P.sb("ccp", [32, 512], F32)
    up = P.sb("up", [32, 512], F32)
    co = P.sb("co", [128, 512], F32)
    sgc = P.sb("sgc", [128, 512], F32)
    obT = P.sb("obT", [128, 4, 128], BF16)
    oat = P.sb("oat", [128, 512], F32)
    oaT = P.sb("oaT", [128, 4, 128], BF16)
    mbuf = P.sb("mbuf", [128, D], F32)
    mT = P.sb("mT", [128, 8, 128], BF16)
    yout = mbuf
    PTb = [P.sb("PT%d" % i, [128, 512], BF16) for i in range(3)]
    Ls = P.sb("Ls", [128, 16, 128], BF16)
    Lx = P.sb("Lx", [128, 128], F32)
    Lt = P.sb("Lt", [128, 128], F32)
    Lexp = [P.sb("Lexp%d" % i, [128, 1024], BF16) for i in range(2)]
    slotc = P.sb("slotc", [128, 3, 128], F32)
    cmpb = P.sb("cmpb", [128, 4, 128], F32)
    negm = P.sb("negm", [128, 4], F32)
    sc = [P.sb("sc%d" % i, [128, 128], F32) for i in range(3)]
    Lselb = P.sb("Lselb", [128, 128], BF16)
    m8 = P.sb("m8", [128, 16], F32)
    rc = P.sb("rc", [128, 16], F32)

    chunk_seq = [(n, j) for n in range(nslot + (1 if do_sample else 0)) for j in (0, 1, 2, 3, 4, 5, 10, 6, 7, 11, 8, 9, 12, 13)]
    st = {"issued": 0, "used": 0, "u": 0}

    def issue_chunks(upto):
        while st["issued"] < min(upto, len(chunk_seq)):
            k = st["issued"]
            P.dma("sp", wch[k % NWB][:], wscr[chunk_seq[k][1]], reads=[(wscr, chunk_seq[k][1])], writes=[wch[k % NWB]],
                  chan="wst%d" % (k % NWB))
            st["issued"] += 1

    def next_chunk(expect_j):
        k = st["used"]
        assert chunk_seq[k][1] == expect_j, (chunk_seq[k], expect_j)
        issue_chunks(k + 1)
        st["used"] += 1
        return wch[k % NWB], k

    def zchunk(j, bank, lh=None, rows=128):
        wb, k = next_chunk(j)
        wv = wb[:].rearrange("p (c n) -> p c n", c=8)
        for kc in range(8):
            mm(bank[0:128, :], hTm[:, kc, :], wv[:, kc, :], kc == 0, kc == 7, [hTm, wb], [bank])
        return wb, wv, k

    def attention(n, g, samp=None):
        gr = slice(g * 64, (g + 1) * 64)
        nkt = 4 * n + 4 if samp is None else 17
        nfull = 4 * n if samp is None else 16
        pm = ps[2] if g == 0 else ps[1]
        sbanks = (ps[3], ps[4]) if g == 0 else (ps[0], ps[1])
        for br in range(4):
            for hh in range(4):
                mm(pm[:, br * 4 + hh:br * 4 + hh + 1], aQT[gr, hh * 128:(hh + 1) * 128], cmb[gr, min(br, 2):min(br, 2) + 1], True, True,
                   [aQT, cmb], [pm])
        for br in range(3):
            P.op("dve", lambda e, br=br: e.max(out=m8[:, 0:8], in_=pm[:, br * 4:br * 4 + 8]), [pm], [m8])
            ts("dve", negm[:, br:br + 1], m8[:, 0:1], -1.0, None, ALU.mult, None, [m8], [negm])
        ts("dve", negm[:, 3:4], negm[:, 1:2], -BIG8, None, ALU.add, None, [negm], [negm])

        def unit(Kt, Lap, Lbuf, Vap, Vbuf, Kbuf, obank, first, last, ovl_kt=None):
            u = st["u"]
            st["u"] += 1
            S = sbanks[u % 2]
            PT = PTb[u % 3]
            mm(S[:], Kt, QT[gr, :], True, False, [Kbuf, QT], [S])
            mm(S[:], Lap, I4[:], False, True, [Lbuf, I4], [S])
            act(PT[:], S[:], AF.Exp, [S], [PT], scale=0.125)
            for hh in range(4):
                mm(obank[:, hh * 65:(hh + 1) * 65], PT[:, hh * 128:(hh + 1) * 128], Vap, first and hh == 0, last,
                   [PT, Vbuf], [obank], skip_group_check=True)
            if ovl_kt is not None:
                for hh in range(4):
                    mm(ps[6][:, hh * 128:(hh + 1) * 128], PT[:, hh * 128:(hh + 1) * 128], ovl[:, ovl_kt, :],
                       first and hh == 0, last, [PT, ovl], [ps[6]], skip_group_check=True)

        def finish_branch(obank, br, first_branch):
            ov = obank[:, 0:260].rearrange("p (h e) -> p h e", e=65)
            ts("dve", rc[:, 0:4], ov[:, :, 64], 1e-30, None, ALU.max, None, [obank], [rc])
            P.op("dve", lambda e: e.reciprocal(out=rc[:, 4:8], in_=rc[:, 0:4]), [rc], [rc])
            gv = gates[:, g * 12:(g + 1) * 12].rearrange("p (h b) -> p h b", b=3)[:, :, br]
            tt("dve", rc[:, 8:12], rc[:, 4:8], gv, ALU.mult, [rc, gates], [rc])
            if samp is not None:
                ts("dve", rc[:, 8:12], rc[:, 8:12], onehot[:, samp:samp + 1], None, ALU.mult, None, [rc, onehot], [rc])
                first_branch = False
            for hh in range(4):
                dst = oat[:, g * 256 + hh * 64:g * 256 + (hh + 1) * 64]
                if first_branch:
                    ts("dve", dst, ov[:, hh, 0:64], rc[:, 8 + hh:9 + hh], None, ALU.mult, None, [obank, rc], [oat])
                else:
                    stt(dst, ov[:, hh, 0:64], rc[:, 8 + hh:9 + hh], dst, ALU.mult, ALU.add, [obank, rc, oat], [oat])

        ktmax = min(3, (32 * n + 30) // 128) if samp is None else 0
        for kt in range(ktmax + 1):
            if samp is None:
                ts("dve", Ls[:, kt, :], cmpb[:, kt, :], negm[:, 0:1], None, ALU.add, None, [cmpb, negm], [(Ls, kt)])
            else:
                ts("dve", Ls[:, kt, :], smask[:, 0, :], negm[:, 0:1], None, ALU.add, None, [smask, negm], [(Ls, kt)])
        for kt in range(ktmax + 1):
            unit(kcT[gr, kt * 128:(kt + 1) * 128], Ls[:, kt, :], (Ls, kt), vca[:, kt, g, :], vca, kcT, ps[5],
                 kt == 0, kt == ktmax, ovl_kt=kt)
        ov = ps[5][:, 0:260].rearrange("p (h e) -> p h e", e=65)
        ts("dve", rc[:, 12:16], ov[:, :, 64], 1e-30, None, ALU.max, None, [ps[5]], [rc])
        P.op("dve", lambda e: e.reciprocal(out=rc[:, 12:16], in_=rc[:, 12:16]), [rc], [rc])
        ts("dve", sc[0][:], ps[6][:, 0:128], rc[:, 12:13], None, ALU.mult, None, [ps[6], rc], [sc[0]])
        for hh in range(1, 4):
            stt(sc[0][:], ps[6][:, hh * 128:(hh + 1) * 128], rc[:, 12 + hh:13 + hh], sc[0][:], ALU.mult, ALU.add,
                [ps[6], rc, sc[0]], [sc[0]])
        finish_branch(ps[5], 0, True)
        tt("dve", sc[0][:], sc[0][:], slotc[:, 0, :], ALU.mult, [sc[0], slotc], [sc[0]])
        tt("dve", sc[0][:], sc[0][:], slotc[:, 1, :], ALU.add, [sc[0], slotc], [sc[0]])
        tt("dve", sc[0][:], sc[0][:], slotc[:, 2, :], ALU.max, [sc[0], slotc], [sc[0]])
        ms("dve", sc[0][:, 0:1], 1e6, [sc[0]])
        P.op("dve", lambda e: e.max(out=m8[:, 0:8], in_=sc[0][:]), [sc[0]], [m8])
        P.op("dve", lambda e: e.match_replace(out=sc[1][:], in_to_replace=m8[:, 0:8], in_values=sc[0][:], imm_value=-1e30),
             [sc[0], m8], [sc[1]])
        P.op("dve", lambda e: e.max(out=m8[:, 8:16], in_=sc[1][:]), [sc[1]], [m8])
        ts("dve", sc[2][:], sc[0][:], m8[:, 15:16], BIG8, ALU.is_ge, ALU.mult, [sc[0], m8], [sc[2]])
        ts("dve", Lselb[:], sc[2][:], negm[:, 3:4], None, ALU.add, None, [sc[2], negm], [Lselb])
        wk = [k for k in range(8) if 4 * n - 4 + k >= 0] if samp is None else [0, 1, 2, 3, 4]
        for k in wk:
            if samp is None:
                ts("dve", Ls[:, 4 + k, :], winb[:, k, :], negm[:, 2:3], None, ALU.add, None, [winb, negm], [(Ls, 4 + k)])
            else:
                mi = (2, 3, 3, 3, 1)[k]
                ts("dve", Ls[:, 4 + k, :], smask[:, mi, :], negm[:, 2:3], None, ALU.add, None, [smask, negm], [(Ls, 4 + k)])
        for k in wk:
            kt = 4 * n - 4 + k if samp is None else k
            unit(KwinT[gr, (kt % 8) * 128:(kt % 8 + 1) * 128], Ls[:, 4 + k, :], (Ls, 4 + k), Vwin[:, kt % 8, g, :],
                 (Vwin, kt % 8), (KwinT, kt % 8), ps[7], k == wk[0], k == wk[-1])
        finish_branch(ps[7], 2, False)
        for kt in range(nkt):
            first, last = kt == 0, kt == nkt - 1
            if kt < nfull:
                c, o = divmod(kt, 8)
                if o == 0:
                    nb16 = min(16, 2 * nfull - 16 * c)
                    cp("pool", Lexp[c % 2][:, 0:nb16 * 64].rearrange("p (j e) -> p j e", e=64),
                       Lselb[:, 16 * c:16 * c + nb16].unsqueeze(2).to_broadcast([128, nb16, 64]), [Lselb], [Lexp[c % 2]])
                Lap, Lbuf = Lexp[c % 2][:, o * 128:(o + 1) * 128], Lexp[c % 2]
            elif samp is not None:
                ts("dve", Ls[:, 12, :], smask[:, 1, :], negm[:, 1:2], None, ALU.add, None, [smask, negm], [(Ls, 12)])
                Lap, Lbuf = Ls[:, 12, :], (Ls, 12)
            else:
                kr = kt - 4 * n
                cp("dve", Lx[:].rearrange("p (j e) -> p j e", e=64),
                   Lselb[:, 2 * kt:2 * kt + 2].unsqueeze(2).to_broadcast([128, 2, 64]), [Lselb], [Lx])
                stt(Lt[:], selc[:, 1, kr, :], negm[:, 1:2], selc[:, 2, kr, :], ALU.mult, ALU.add, [selc, negm], [Lt])
                tt("dve", Lx[:], Lx[:], selc[:, 0, kr, :], ALU.mult, [Lx, selc], [Lx])
                tt("dve", Ls[:, 12 + kr, :], Lx[:], Lt[:], ALU.add, [Lx, Lt], [(Ls, 12 + kr)])
                Lap, Lbuf = Ls[:, 12 + kr, :], (Ls, 12 + kr)
            unit(KselT[gr, kt * 128:(kt + 1) * 128], Lap, Lbuf, Vsel[:, kt, g, :], (Vsel, kt), (KselT, kt), ps[5],
                 first, last)
        finish_branch(ps[5], 1, False)

    def phase_b(n):
        load_x(xm, xmine[n * 128:(n + 1) * 128, :], "xm")
        load_x(xpv, xprev[n * 32:(n + 1) * 32, :], "xpv", rows=32)
        P.dma("sp", slotc[:], slotc_d[n], writes=[slotc], chan="slotc")
        P.dma("sp", cmpb[:], cmpb_d[n], writes=[cmpb], chan="cmpb")
        norm_T(xm, hTm)
        norm_T(xpv, hTp, rows=32)
        kv_proj(hTm, ropeB[:, n, :])
        P.dma("sp", kvp[n * 128:(n + 1) * 128, :], zkv[:, 0:512], reads=[zkv], chan="zkvo")
        if n == nslot - 1:
            P.dma("sp", winp, zkv[:, 512:768], reads=[zkv], chan="zkvo")
        cp("dve", cmb[:], cm[:], [cm], [cmb])
        zchunk(0, ps[0])
        cp("act", qf[:], ps[0][:], [ps[0]], [qf])
        rope_inplace(qf, [qf[:].rearrange("p (h d) -> p h d", d=64)[:, :, 0:16]], ropeB[:, n, :])
        cp("pool", sgc[:].rearrange("p (h g d) -> p h g d", h=4, g=2),
           qf[:].rearrange("p (g h d) -> p h g d", g=2, h=4), [qf], [sgc])
        for jj in range(4):
            tr(ps[2][:, jj * 128:(jj + 1) * 128], sgc[:, jj * 128:(jj + 1) * 128], ident[:], [sgc, ident], [ps[2]])
        cp("act", QT[:], ps[2][:], [ps[2]], [QT])
        act(aQT[:], ps[2][:], AF.Abs, [ps[2]], [aQT])
        for kc in range(8):
            mm(ps[1][:, 0:24], hTm[:, kc, :], wg[:, kc, :], kc == 0, kc == 7, [hTm, wg], [ps[1]])
        act(gates[:], ps[1][:, 0:24], AF.Sigmoid, [ps[1]], [gates])
        zchunk(1, ps[0])
        act(sa[:], ps[0][:], AF.Silu, [ps[0]], [sa])
        zchunk(2, ps[1])
        cp("act", cbf[:], ps[1][:], [ps[1]], [cbf])
        wb, wv, _ = zchunk(3, ps[0])
        for kc in range(8):
            mm(ps[2][0:32, :], hTp[:, kc, :], wv[:, kc, :], kc == 0, kc == 7, [hTp, wb], [ps[2]])
        cp("act", ccf[:], ps[0][:], [ps[0]], [ccf])
        cp("act", ccp[:], ps[2][0:32, :], [ps[2]], [ccp])
        wb, wv, _ = zchunk(4, ps[1])
        for kc in range(8):
            mm(ps[2][0:32, :], hTp[:, kc, :], wv[:, kc, :], kc == 0, kc == 7, [hTp, wb], [ps[2]])
        tt("dve", uu[:], ccf[:], ps[1][:], ALU.mult, [ccf, ps[1]], [uu])
        stt(up[:], ps[2][0:32, :], upsc[:, n:n + 1], ccp[:], ALU.mult, ALU.mult, [ps[2], upsc, ccp], [up])
        if n == nslot - 1:
            P.dma("sp", convp, uu[126:128, :], reads=[uu], chan="uuo")
        for s_ in range(2):
            mm(ps[2 + s_][:], shm[:, s_, :], uu[:], True, False, [shm, uu], [ps[2 + s_]])
            mm(ps[2 + s_][:], shb[:, s_, :], up[:], False, True, [shb, up], [ps[2 + s_]])
        tt("dve", co[:], uu[:], convwb[:, 2, :], ALU.mult, [uu, convwb], [co])
        tt("dve", qf[:], ps[2][:], convwb[:, 1, :], ALU.mult, [ps[2], convwb], [qf])
        tt("dve", co[:], co[:], qf[:], ALU.add, [co, qf], [co])
        tt("dve", qf[:], ps[3][:], convwb[:, 0, :], ALU.mult, [ps[3], convwb], [qf])
        tt("dve", co[:], co[:], qf[:], ALU.add, [co, qf], [co])
        zchunk(5, ps[0])
        act(sgc[:], ps[0][:], AF.Silu, [ps[0]], [sgc])
        tt("dve", co[:], co[:], cbf[:], ALU.mult, [co, cbf], [co])
        tt("dve", co[:], co[:], sgc[:], ALU.mult, [co, sgc], [co])
        for c in range(4):
            tr(ps[1][:, c * 128:(c + 1) * 128], co[:, c * 128:(c + 1) * 128], ident[:], [co, ident], [ps[1]])
        cp("act", obT[:].rearrange("p c n -> p (c n)"), ps[1][:], [ps[1]], [obT])
        for g in range(2):
            attention(n, g)
        tt("dve", oat[:], oat[:], sa[:], ALU.mult, [oat, sa], [oat])
        for c in range(4):
            tr(ps[2][:, c * 128:(c + 1) * 128], oat[:, c * 128:(c + 1) * 128], ident[:], [oat, ident], [ps[2]])
        cp("act", oaT[:].rearrange("p c n -> p (c n)"), ps[2][:], [ps[2]], [oaT])
        for (jw, srcT, jg) in ((10, oaT, (6, 7)), (11, obT, (8, 9))):
            wb, k = next_chunk(jw)
            wv = wb[:].rearrange("p (c n) -> p c n", c=4)
            for half in range(2):
                for kc in range(4):
                    mm(ps[0 + half][:], srcT[:, kc, :], wv[:, kc, half * 512:(half + 1) * 512], kc == 0, kc == 3,
                       [srcT, wb], [ps[half]])
            for half in range(2):
                zchunk(jg[half], ps[2 + half])
                act(sgc[:], ps[2 + half][:], AF.Sigmoid, [ps[2 + half]], [sgc])
                dst = mbuf[:, half * 512:(half + 1) * 512]
                if jw == 10:
                    tt("dve", dst, sgc[:], ps[half][:], ALU.mult, [sgc, ps[half]], [mbuf])
                else:
                    tt("dve", qf[:], sgc[:], ps[half][:], ALU.mult, [sgc, ps[half]], [qf])
                    tt("dve", dst, dst, qf[:], ALU.add, [mbuf, qf], [mbuf])
        for c in range(8):
            tr(ps[4 + c // 4][:, (c % 4) * 128:(c % 4 + 1) * 128], mbuf[:, c * 128:(c + 1) * 128], ident[:],
               [mbuf, ident], [ps[4 + c // 4]])
        cp("act", mT[:, 0:4, :].rearrange("p c n -> p (c n)"), ps[4][:], [ps[4]], [mT])
        cp("dve", mT[:, 4:8, :].rearrange("p c n -> p (c n)"), ps[5][:], [ps[5]], [mT])
        for half in range(2):
            wb, k = next_chunk(12 + half)
            wv = wb[:].rearrange("p (c n) -> p c n", c=8)
            for kc in range(8):
                mm(ps[6 + half][:], mT[:, kc, :], wv[:, kc, :], kc == 0, kc == 7, [mT, wb], [ps[6 + half]])
        issue_chunks(st["used"] + NWB)
        for half in range(2):
            act(qf[:], ps[6 + half][:], AF.Square, [ps[6 + half]], [qf, ss],
                accum_out=ss[:, 4 + half:5 + half])
        tt("dve", ss[:, 6:7], ss[:, 4:5], ss[:, 5:6], ALU.add, [ss], [ss])
        ts("dve", ss[:, 6:7], ss[:, 6:7], 1.0 / D, 1e-6, ALU.mult, ALU.add, [ss], [ss])
        act(ss[:, 7:8], ss[:, 6:7], AF.Sqrt, [ss], [ss])
        P.op("dve", lambda e: e.reciprocal(out=ss[:, 6:7], in_=ss[:, 7:8]), [ss], [ss])
        for half in range(2):
            sl = slice(half * 512, (half + 1) * 512)
            stt(yout[:, sl], ps[6 + half][:], ss[:, 6:7], GP[:, sl], ALU.mult, ALU.mult, [ps[6 + half], ss, GP], [yout])
        tt("dve", yout[:], yout[:], xm[:], ALU.add, [yout, xm], [yout])
        P.dma("sp", yp[n * 128:(n + 1) * 128, :], yout[:], reads=[yout], chan="yout")


    pgb = [P.sb("pgb%d" % i, [128, 512], F32) for i in range(2)]
    idxf = P.sb("idxf", [128, 256], F32)
    idxi = P.sb("idxi", [128, 256], I32)
    pcol = P.sb("pcol", [128, 1], F32)
    smask = P.sb("smask", [128, 4, 128], F32)
    onehot = P.sb("onehot", [128, 16], F32)
    ropeS = P.sb("ropeS", [16, 16], F32)
    qsT = P.sb("qsT", [128, 4, 16], BF16)
    aqsT = P.sb("aqsT", [128, 4, 16], BF16)
    swt = P.sb("swt", [128, 256], F32)

    def phase_s():
        R = 16
        P.dma("sp", slotc[:], slotS_d, writes=[slotc], chan="slotc")
        P.dma("sp", smask[:], smask_d, writes=[smask], chan="c1")
        P.dma("sp", onehot[:], onehot_d, writes=[onehot], chan="c1")
        P.dma("sp", ropeS[:], ropeS_d, writes=[ropeS], chan="c1")
        P.dma("sp", pcol[:], pcol_d, writes=[pcol], chan="c1")
        P.dma("sp", idxi[:], ptab_d.to_broadcast([128, 256]), writes=[idxi], chan="c1")
        cp("dve", idxf[:], idxi[:], [idxi], [idxf])
        ts("dve", idxf[:], idxf[:], 128.0, pcol[:, 0:1], ALU.mult, ALU.add, [idxf, pcol], [idxf])
        cp("dve", idxi[:], idxf[:], [idxf], [idxi])
        load_x(xm, xs_d, "xm", rows=R)
        act(xn[0:R, :], xm[0:R, :], AF.Square, [xm], [xn, ss], accum_out=ss[0:R, 0:1])
        ts("dve", ss[0:R, 1:2], ss[0:R, 0:1], 1.0 / D, 1e-6, ALU.mult, ALU.add, [ss], [ss])
        act(ss[0:R, 2:3], ss[0:R, 1:2], AF.Sqrt, [ss], [ss])
        P.op("dve", lambda e: e.reciprocal(out=ss[0:R, 3:4], in_=ss[0:R, 2:3]), [ss], [ss])
        ts("dve", xn[0:R, :], xm[0:R, :], ss[0:R, 3:4], None, ALU.mult, None, [xm, ss], [xn])
        for c in range(8):
            tr(ps[0][:, c * R:(c + 1) * R], xn[0:R, c * 128:(c + 1) * 128], ident[0:R, 0:R], [xn, ident], [ps[0]])
        tt("dve", bcm[:], ps[0][:, 0:128], G1S[:].rearrange("p c s -> p (c s)"), ALU.mult, [ps[0], G1S], [bcm])
        tt("dve", hTm[:, :, 0:R], bcm[:].rearrange("p (c s) -> p c s", s=R), SHS[:], ALU.add, [bcm, SHS], [hTm])
        kv_proj(hTm, ropeS[:, :], rows=R)
        P.dma("sp", kvs, zkv[0:R, 0:512], reads=[zkv], chan="zkvo")
        P.dma("sp", wins[:, 511, :], zkv[0:R, 512:768], reads=[zkv], chan="zkvo")
        P.dma("sp", convs[:, 0, :], sconv_d[:, 1, :], chan="d2d")
        tr(ps[4][:, 0:R], zkv[0:R, 256:384], ident[0:R, 0:R], [zkv, ident], [ps[4]])
        tr(ps[4][:, R:2 * R], zkv[0:R, 512:640], ident[0:R, 0:R], [zkv, ident], [ps[4]])
        cp("act", KselT[:, 2048:2048 + R], ps[4][:, 0:R], [ps[4]], [(KselT, 16)])
        cp("act", KwinT[:, 512:512 + R], ps[4][:, R:2 * R], [ps[4]], [(KwinT, 4)])
        colmax_update(ps[4], ps[4][:, 0:R], slice(0, 128), 1, R)
        colmax_update(ps[4], ps[4][:, R:2 * R], slice(0, 128), 2, R)
        cp("dve", Vsel[0:R, 16, :, 0:64], zkv[0:R, 384:512].rearrange("p (g d) -> p g d", g=2), [zkv], [(Vsel, 16)])
        cp("dve", Vwin[0:R, 4, :, 0:64], zkv[0:R, 640:768].rearrange("p (g d) -> p g d", g=2), [zkv], [(Vwin, 4)])
        cp("dve", cmb[:], cm[:], [cm], [cmb])
        zchunk(0, ps[0])
        cp("act", qf[0:R, :], ps[0][0:R, :], [ps[0]], [qf])
        rope_inplace(qf, [qf[0:R, :].rearrange("p (h d) -> p h d", d=64)[:, :, 0:16]], ropeS[:, :], rows=R)
        cp("pool", sgc[0:R, :].rearrange("p (h g d) -> p h g d", h=4, g=2),
           qf[0:R, :].rearrange("p (g h d) -> p h g d", g=2, h=4), [qf], [sgc])
        for jj in range(4):
            tr(ps[2][:, jj * R:(jj + 1) * R], sgc[0:R, jj * 128:(jj + 1) * 128], ident[0:R, 0:R], [sgc, ident], [ps[2]])
        cp("act", qsT[:].rearrange("p j s -> p (j s)"), ps[2][:, 0:4 * R], [ps[2]], [qsT])
        act(aqsT[:].rearrange("p j s -> p (j s)"), ps[2][:, 0:4 * R], AF.Abs, [ps[2]], [aqsT])
        for kc in range(8):
            mm(ps[1][:, 0:24], hTm[:, kc, :], wg[:, kc, :], kc == 0, kc == 7, [hTm, wg], [ps[1]])
        act(gates[0:R, :], ps[1][0:R, 0:24], AF.Sigmoid, [ps[1]], [gates])
        zchunk(1, ps[0])
        act(sa[0:R, :], ps[0][0:R, :], AF.Silu, [ps[0]], [sa])
        zchunk(2, ps[1])
        cp("act", cbf[0:R, :], ps[1][0:R, :], [ps[1]], [cbf])
        zchunk(3, ps[0])
        cp("act", ccf[0:R, :], ps[0][0:R, :], [ps[0]], [ccf])
        zchunk(4, ps[1])
        tt("dve", uu[0:R, :], ccf[0:R, :], ps[1][0:R, :], ALU.mult, [ccf, ps[1]], [uu])
        P.dma("sp", convs[:, 1, :], uu[0:R, :], reads=[uu], chan="uuo")
        P.dma("sp", ccp[0:R, :], sconv_d[:, 0, :], writes=[ccp], chan="scv")
        P.dma("sp", up[0:R, :], sconv_d[:, 1, :], writes=[up], chan="scv")
        tt("dve", co[0:R, :], uu[0:R, :], convwb[0:R, 2, :], ALU.mult, [uu, convwb], [co])
        tt("dve", qf[0:R, :], up[0:R, :], convwb[0:R, 1, :], ALU.mult, [up, convwb], [qf])
        tt("dve", co[0:R, :], co[0:R, :], qf[0:R, :], ALU.add, [co, qf], [co])
        tt("dve", qf[0:R, :], ccp[0:R, :], convwb[0:R, 0, :], ALU.mult, [ccp, convwb], [qf])
        tt("dve", co[0:R, :], co[0:R, :], qf[0:R, :], ALU.add, [co, qf], [co])
        zchunk(5, ps[0])
        act(sgc[0:R, :], ps[0][0:R, :], AF.Silu, [ps[0]], [sgc])
        tt("dve", co[0:R, :], co[0:R, :], cbf[0:R, :], ALU.mult, [co, cbf], [co])
        tt("dve", co[0:R, :], co[0:R, :], sgc[0:R, :], ALU.mult, [co, sgc], [co])
        for c in range(4):
            tr(ps[1][:, c * R:(c + 1) * R], co[0:R, c * 128:(c + 1) * 128], ident[0:R, 0:R], [co, ident], [ps[1]])
        cp("act", obT[:, :, 0:R], ps[1][:, 0:4 * R].rearrange("p (c s) -> p c s", s=R), [ps[1]], [obT])
        ms("pool", oat[:], 0.0, [oat])
        ms("pool", QT[:], 0.0, [QT])
        ms("pool", aQT[:], 0.0, [aQT])
        k = 0
        for sm in range(R):
            ms("pool", vcf[:], 0.0, [vcf])
            for pgi in range(16):
                pb = pgb[k % 2]
                col = sm * 16 + pgi
                P.op("pool", lambda e, pb=pb, col=col: e.indirect_dma_start(
                    out=pb[:], out_offset=None, in_=cache_d,
                    in_offset=bass.IndirectOffsetOnAxis(ap=idxi[:, col:col + 1], axis=0)),
                    reads=[idxi], writes=[pb], chan="pgb%d" % (k % 2))
                ingest(pgi, pb, False)
                k += 1
            for i in range(4):
                P.dma("sp", swt[:], swin_d[sm, i * 128:(i + 1) * 128, :], writes=[swt], chan="swt")
                tr(ps[4][:, 0:128], swt[:, 0:128], ident[:], [swt, ident], [ps[4]])
                cp("act", KwinT[:, i * 128:(i + 1) * 128], ps[4][:, 0:128], [ps[4]], [(KwinT, i)])
                colmax_update(ps[4], ps[4][:, 0:128], slice(0, 128), 2, 128)
                cp("dve", Vwin[:, i, :, 0:64], swt[:, 128:256].rearrange("p (g d) -> p g d", g=2), [swt], [(Vwin, i)])
            P.dma("sp", wins[sm, 0:511, :], swin_d[sm, 1:512, :], chan="d2d")
            cp("dve", cmb[:], cm[:], [cm], [cmb])
            QTv = QT[:].rearrange("p (j q) -> p j q", q=128)
            aQTv = aQT[:].rearrange("p (j q) -> p j q", q=128)
            if sm > 0:
                ms("dve", QTv[:, :, sm - 1], 0.0, [QT])
                ms("dve", aQTv[:, :, sm - 1], 0.0, [aQT])
            cp("dve", QTv[:, :, sm], qsT[:, :, sm], [qsT], [QT])
            cp("dve", aQTv[:, :, sm], aqsT[:, :, sm], [aqsT], [aQT])
            for g in range(2):
                attention(0, g, samp=sm)
        tt("dve", oat[0:R, :], oat[0:R, :], sa[0:R, :], ALU.mult, [oat, sa], [oat])
        for c in range(4):
            tr(ps[2][:, c * R:(c + 1) * R], oat[0:R, c * 128:(c + 1) * 128], ident[0:R, 0:R], [oat, ident], [ps[2]])
        cp("act", oaT[:, :, 0:R], ps[2][:, 0:4 * R].rearrange("p (c s) -> p c s", s=R), [ps[2]], [oaT])
        for (jw, srcT, jg) in ((10, oaT, (6, 7)), (11, obT, (8, 9))):
            wb, kk = next_chunk(jw)
            wv = wb[:].rearrange("p (c n) -> p c n", c=4)
            for half in range(2):
                for kc in range(4):
                    mm(ps[0 + half][:], srcT[:, kc, :], wv[:, kc, half * 512:(half + 1) * 512], kc == 0, kc == 3,
                       [srcT, wb], [ps[half]])
            for half in range(2):
                zchunk(jg[half], ps[2 + half])
                act(sgc[0:R, :], ps[2 + half][0:R, :], AF.Sigmoid, [ps[2 + half]], [sgc])
                dst = mbuf[0:R, half * 512:(half + 1) * 512]
                if jw == 10:
                    tt("dve", dst, sgc[0:R, :], ps[half][0:R, :], ALU.mult, [sgc, ps[half]], [mbuf])
                else:
                    tt("dve", qf[0:R, :], sgc[0:R, :], ps[half][0:R, :], ALU.mult, [sgc, ps[half]], [qf])
                    tt("dve", dst, dst, qf[0:R, :], ALU.add, [mbuf, qf], [mbuf])
        for c in range(8):
            tr(ps[4 + c // 4][:, (c % 4) * R:(c % 4 + 1) * R], mbuf[0:R, c * 128:(c + 1) * 128], ident[0:R, 0:R],
               [mbuf, ident], [ps[4 + c // 4]])
        cp("act", mT[:, 0:4, 0:R], ps[4][:, 0:4 * R].rearrange("p (c s) -> p c s", s=R), [ps[4]], [mT])
        cp("dve", mT[:, 4:8, 0:R], ps[5][:, 0:4 * R].rearrange("p (c s) -> p c s", s=R), [ps[5]], [mT])
        for half in range(2):
            wb, kk = next_chunk(12 + half)
            wv = wb[:].rearrange("p (c n) -> p c n", c=8)
            for kc in range(8):
                mm(ps[6 + half][:], mT[:, kc, :], wv[:, kc, :], kc == 0, kc == 7, [mT, wb], [ps[6 + half]])
        for c in range(8):
            tr(ps[2 + c // 4][0:R, (c % 4) * 128:(c % 4 + 1) * 128], GST[:, c, :], ident[:], [GST, ident], [ps[2 + c // 4]])
        cp("dve", GP[0:R, 0:512], ps[2][0:R, :], [ps[2]], [GP])
        cp("dve", GP[0:R, 512:1024], ps[3][0:R, :], [ps[3]], [GP])
        for half in range(2):
            act(qf[0:R, :], ps[6 + half][0:R, :], AF.Square, [ps[6 + half]], [qf, ss], accum_out=ss[0:R, 4 + half:5 + half])
        tt("dve", ss[0:R, 6:7], ss[0:R, 4:5], ss[0:R, 5:6], ALU.add, [ss], [ss])
        ts("dve", ss[0:R, 6:7], ss[0:R, 6:7], 1.0 / D, 1e-6, ALU.mult, ALU.add, [ss], [ss])
        act(ss[0:R, 7:8], ss[0:R, 6:7], AF.Sqrt, [ss], [ss])
        P.op("dve", lambda e: e.reciprocal(out=ss[0:R, 6:7], in_=ss[0:R, 7:8]), [ss], [ss])
        for half in range(2):
            sl = slice(half * 512, (half + 1) * 512)
            stt(mbuf[0:R, sl], ps[6 + half][0:R, :], ss[0:R, 6:7], GP[0:R, sl], ALU.mult, ALU.mult, [ps[6 + half], ss, GP], [mbuf])
        tt("dve", mbuf[0:R, :], mbuf[0:R, :], xm[0:R, :], ALU.add, [mbuf, xm], [mbuf])
        P.dma("sp", ys, mbuf[0:R, :], reads=[mbuf], chan="yout")

    stop = ""
    if stop != "p0":
        load_x(xt[0], xp[0:128, :], "xt0")
        for t in range(nt):
            phase_a(t)
            if stop.startswith("a"):
                continue
            if t % 4 == 3 and t // 4 < nslot:
                phase_b(t // 4)
    if do_sample:
        phase_s()

    P.emit()
    return nc, P


def make_consts(r):
    f = np.float32
    c = {}
    c["ident"] = np.eye(128, dtype=f)
    inv = (np.float32(500000.0) ** (-(np.arange(8, dtype=f)) / np.float32(8))).astype(f)
    p = np.arange(128)

    def rope_tab(pos):
        ang = (pos.astype(f)[..., None] * inv).astype(f)
        return np.concatenate([np.cos(ang.astype(np.float64)), np.sin(ang.astype(np.float64))], -1).astype(f)

    c["ropeA"] = rope_tab(np.arange(NT)[None, :] * 128 + p[:, None])
    tn = 4 * np.arange(NSLOT) + r
    c["ropeB"] = rope_tab(tn[None, :] * 128 + p[:, None])
    j = np.arange(128)
    slotc = np.zeros((NSLOT, 128, 3, 128), f)
    cmpb = np.zeros((NSLOT, 128, 4, 128), f)
    cidx = (np.arange(4)[:, None] * 128 + np.arange(128)[None, :])
    for n in range(NSLOT):
        b = 4 * n + r
        cur = 2 * b + (p >= 64)
        A = (j[None, :] <= cur[:, None]).astype(f)
        slotc[n, :, 0] = A
        slotc[n, :, 1] = A - 1
        slotc[n, :, 2] = np.where((j[None, :] == cur[:, None]) | (j[None, :] == cur[:, None] - 1), 1e6, -2.0)
        valid = (16 * cidx[None] + 31 <= (128 * b + p)[:, None, None]) & (cidx[None] < 511)
        cmpb[n] = np.where(valid, 0.0, -BIG8)
    c["slotc"] = slotc
    c["cmpb"] = cmpb
    q = p[:, None]
    k = p[None, :]
    winb = np.zeros((128, 8, 128), f)
    for kr in range(8):
        dt = 128 * (r + 4 - kr) + q - k
        winb[:, kr] = np.where((dt >= 0) & (dt < 512), 0.0, -BIG8)
    c["winb"] = winb
    selc = np.zeros((128, 3, 4, 128), f)
    for kr in range(4):
        if kr < r:
            selc[:, 0, kr] = 1.0
        elif kr == r:
            selc[:, 1, kr] = 1.0
            selc[:, 2, kr] = np.where(k <= q, 0.0, -BIG8)
        else:
            selc[:, 1, kr] = 1.0
            selc[:, 2, kr] = -BIG8
    c["selc"] = selc
    cs = 16 * cidx
    ov = (cs[:, :, None] < 64 * j[None, None, :] + 64) & (cs[:, :, None] + 32 > 64 * j[None, None, :]) & (cidx[:, :, None] < 511)
    c["ovl"] = np.ascontiguousarray(ov.transpose(1, 0, 2)).astype(f)
    shm = np.zeros((128, 2, 128), f)
    shb = np.zeros((32, 2, 128), f)
    for s in (1, 2):
        for m in range(128):
            kk = m - s
            if kk >= 0:
                shm[kk, s - 1, m] = 1.0
            else:
                shb[32 + kk, s - 1, m] = 1.0
    c["shm"] = shm
    c["shb"] = shb
    ups = np.ones((32, NSLOT), f)
    if r == 0:
        ups[:, 0] = 0.0
    c["upsc"] = ups
    c["ropeS"] = np.repeat(rope_tab(np.array([2048])), 16, axis=0)
    slotS = np.zeros((128, 3, 128), f)
    A = (j <= 32).astype(f)
    slotS[:, 0] = A[None, :]
    slotS[:, 1] = A[None, :] - 1
    slotS[:, 2] = np.where((j == 31) | (j == 32), 1e6, -2.0)[None, :]
    c["slotS"] = slotS
    smask = np.zeros((128, 4, 128), f)
    smask[:, 0] = np.where(k <= 126, 0.0, -BIG8)
    smask[:, 1] = np.where(k == q, 0.0, -BIG8)
    smask[:, 2] = np.where(k >= 1, 0.0, -BIG8)
    c["smask"] = smask
    oh = np.zeros((128, 16), f)
    oh[np.arange(16), np.arange(16)] = 1.0
    c["onehot"] = oh
    c["pcol"] = np.arange(128, dtype=f).reshape(128, 1)
    return c


_CACHE = {}


def kernel(x_prompt, x_sample, cache_kv_pages, state_win_kv, state_conv, page_table, c_prompt, c_sample,
           w_ada, b_ada, g_pre, g_post, w_in, pe_cmp, w_cmp1, w_cmp2, conv_w, w_br_a, w_br_b, w_out,
           _nt=NT, _nslot=NSLOT):
    f = np.float32
    key = (_nt, _nslot)
    if key not in _CACHE:
        _CACHE[key] = build(_nt, _nslot)
    nc, P = _CACHE[key]
    in_maps = []
    cache_flat = np.ascontiguousarray(cache_kv_pages[0], dtype=f).reshape(-1, 512)
    for c in range(8):
        bi, r = c // 4, c % 4
        xb = np.ascontiguousarray(x_prompt[bi], dtype=f)
        tiles = xb.reshape(NT, 128, D)
        tn = 4 * np.arange(NSLOT) + r
        xmine = np.ascontiguousarray(tiles[tn]).reshape(NSLOT * 128, D)
        xprev = np.zeros((NSLOT, 32, D), f)
        for n in range(NSLOT):
            if tn[n] > 0:
                xprev[n] = xb[tn[n] * 128 - 32:tn[n] * 128]
        m = {
            "xp": xb, "xmine": xmine, "xprev": xprev.reshape(NSLOT * 32, D),
            "cpr": np.ascontiguousarray(c_prompt[bi:bi + 1], dtype=f),
            "w_ada": np.ascontiguousarray(w_ada[0]), "b_ada": np.ascontiguousarray(b_ada), "g_pre": np.ascontiguousarray(g_pre),
            "g_post": np.ascontiguousarray(g_post), "w_in": np.ascontiguousarray(w_in[0]), "pe_cmp": np.ascontiguousarray(pe_cmp[0]),
            "w_cmp1": np.ascontiguousarray(w_cmp1[0]), "w_cmp2": np.ascontiguousarray(w_cmp2[0]),
            "conv_w": np.ascontiguousarray(conv_w[0]), "w_br_a": np.ascontiguousarray(w_br_a[0]),
            "w_br_b": np.ascontiguousarray(w_br_b[0]), "w_out": np.ascontiguousarray(w_out[0]),
        }
        m.update(make_consts(r))
        sl = slice(16 * c, 16 * c + 16)
        m["xs"] = np.ascontiguousarray(x_sample[sl, 0, :], dtype=f)
        m["cs"] = np.ascontiguousarray(c_sample[sl], dtype=f)
        m["ptab"] = np.ascontiguousarray(page_table[sl], dtype=np.int32).reshape(1, 256)
        m["cache"] = cache_flat
        m["swin"] = np.ascontiguousarray(state_win_kv[0, sl], dtype=f).reshape(16, 512, 256)
        m["sconv"] = np.ascontiguousarray(state_conv[0, sl], dtype=f)
        in_maps.append(m)
    res = run_bass_kernel_spmd(nc, in_maps, core_ids=list(range(8)))
    B, T = x_prompt.shape[0], x_prompt.shape[1]
    y_prompt = np.zeros((B, T, D), f)
    kv_rows_prompt = np.zeros((1, B, T, 4, 2, 64), f)
    win_kv_prompt = np.zeros((1, B, 512, 2, 2, 64), f)
    conv_state_prompt = np.zeros((1, B, 2, 512), f)
    for c in range(8):
        bi, r = c // 4, c % 4
        o = res.results[c]
        for n in range(NSLOT):
            t = 4 * n + r
            y_prompt[bi, t * 128:(t + 1) * 128] = o["yp"][n * 128:(n + 1) * 128]
            kv_rows_prompt[0, bi, t * 128:(t + 1) * 128] = o["kvp"][n * 128:(n + 1) * 128].reshape(128, 4, 2, 64)
        win_kv_prompt[0, bi, r * 128:(r + 1) * 128] = o["winp"].reshape(128, 2, 2, 64)
        if r == 3:
            conv_state_prompt[0, bi] = o["convp"]
    nS = x_sample.shape[0]
    y_sample = np.zeros((nS, 1, D), f)
    kv_rows_sample = np.zeros((1, nS, 1, 4, 2, 64), f)
    win_kv_sample = np.zeros((1, nS, 512, 2, 2, 64), f)
    conv_state_sample = np.zeros((1, nS, 2, 512), f)
    for c in range(8):
        o = res.results[c]
        sl = slice(16 * c, 16 * c + 16)
        y_sample[sl, 0] = o["ys"]
        kv_rows_sample[0, sl, 0] = o["kvs"].reshape(16, 4, 2, 64)
        win_kv_sample[0, sl] = o["wins"].reshape(16, 512, 2, 2, 64)
        conv_state_sample[0, sl] = o["convs"]
    return (y_prompt, y_sample, kv_rows_prompt, win_kv_prompt, conv_state_prompt, kv_rows_sample, win_kv_sample,
            conv_state_sample)
```

```python
from concourse.bass_utils import run_bass_kernel_spmd

from contextlib import ExitStack
import numpy as np
import concourse.bass as bass
import concourse.mybir as mybir

F32 = mybir.dt.float32
BF16 = mybir.dt.bfloat16
I32 = mybir.dt.int32
U32 = mybir.dt.uint32
AF = mybir.ActivationFunctionType
ALU = mybir.AluOpType
AX = mybir.AxisListType

SAME_ENG_SYNC = True
COMPUTE = ("pe", "dve", "act", "pool")


class Buf:
    def __init__(self, prog, name, t, is_dram=False):
        self.prog = prog
        self.name = name
        self.t = t
        self.is_dram = is_dram
        self.st = {}
        self.whole = [None, []]
        self.excl = False

    def __getitem__(self, idx):
        return self.t[idx]


class Op:
    __slots__ = ("eng", "fn", "deps", "signal", "val", "sem", "chan", "idx", "is_dma")

    def __init__(self, eng, fn):
        self.eng = eng
        self.fn = fn
        self.deps = {}
        self.signal = False
        self.val = None
        self.sem = None
        self.chan = None
        self.is_dma = False


class Prog:
    def __init__(self, nc):
        self.nc = nc
        self.ops = []
        self.stack = ExitStack()
        self.chan_count = {}
        self.nbuf = 0

    def sb(self, name, shape, dtype):
        t = self.stack.enter_context(self.nc.sbuf_tensor("sb_" + name, list(shape), dtype))
        return Buf(self, name, t)

    def ps(self, name, shape, dtype):
        t = self.stack.enter_context(self.nc.psum_tensor("ps_" + name, list(shape), dtype))
        b = Buf(self, name, t)
        b.excl = True
        return b

    def dram(self, name, shape, dtype, kind="Internal"):
        t = self.nc.dram_tensor(name, list(shape), dtype, kind=kind)
        return Buf(self, name, t.ap(), is_dram=True)

    def _norm(self, lst):
        out = []
        for x in lst:
            if x is None:
                continue
            if isinstance(x, Buf):
                out.append((x, None))
            else:
                out.append(x)
        return out

    def op(self, eng, fn, reads=(), writes=(), chan=None):
        o = Op(eng, fn)
        o.idx = len(self.ops)
        if chan is not None:
            o.is_dma = True
            o.chan = chan
            self.chan_count[chan] = self.chan_count.get(chan, 0) + 1
            o.val = 16 * self.chan_count[chan]
            o.signal = True
        reads = self._norm(reads)
        writes = self._norm(writes)
        writes = writes + [(b_, k_) for (b_, k_) in reads if b_.excl]
        reads = [(b_, k_) for (b_, k_) in reads if not b_.excl]

        def add_dep(d):
            if d is None or d is o:
                return
            if d.is_dma:
                v = 16 * self.chan_count[d.chan]
                if d is not o and d.chan == o.chan:
                    v = d.val
                o.deps[d] = max(o.deps.get(d, 0), v)
            else:
                if d.eng == o.eng and not o.is_dma:
                    if d.eng == "pe" or not SAME_ENG_SYNC:
                        return
                o.deps[d] = 0

        for b, k in reads:
            add_dep(b.whole[0])
            if k is None:
                for st in b.st.values():
                    add_dep(st[0])
            elif k in b.st:
                add_dep(b.st[k][0])
        for b, k in writes:
            add_dep(b.whole[0])
            for r in b.whole[1]:
                add_dep(r)
            if k is None:
                for st in b.st.values():
                    add_dep(st[0])
                    for r in st[1]:
                        add_dep(r)
            elif k in b.st:
                add_dep(b.st[k][0])
                for r in b.st[k][1]:
                    add_dep(r)
        for b, k in reads:
            if k is None:
                b.whole[1].append(o)
            else:
                b.st.setdefault(k, [None, []])[1].append(o)
        for b, k in writes:
            if k is None:
                b.whole = [o, []]
                b.st = {}
            else:
                b.st[k] = [o, []]
        self.ops.append(o)
        return o

    def dma(self, eng, out, in_, reads=(), writes=(), chan=None, **kw):
        assert chan is not None
        return self.op(eng, lambda e: e.dma_start(out=out, in_=in_, **kw), reads, writes, chan=chan)

    def emit(self):
        nc = self.nc
        for o in self.ops:
            for d in o.deps:
                d.signal = True
        sems = {}
        for e in COMPUTE:
            sems[e] = self.stack.enter_context(nc.semaphore("s_" + e))
        for c in self.chan_count:
            sems["c_" + c] = self.stack.enter_context(nc.semaphore("c_" + c))
        cnt = {e: 0 for e in COMPUTE}
        for o in self.ops:
            if o.is_dma:
                o.sem = sems["c_" + o.chan]
            else:
                o.sem = sems[o.eng]
                if o.signal:
                    cnt[o.eng] += 1
                    o.val = cnt[o.eng]
        by_eng = {}
        for o in self.ops:
            by_eng.setdefault(o.eng, []).append(o)
        last_chan_eng = {}
        for o in self.ops:
            if o.is_dma:
                last_chan_eng[o.chan] = o.eng
        self.n_inst = {e: len(v) for e, v in by_eng.items()}

        def make_section(ename, ops):
            def section(eng):
                waited = {}
                for o in ops:
                    for d, v in o.deps.items():
                        val = v if d.is_dma else d.val
                        key = id(d.sem)
                        if waited.get(key, 0) >= val:
                            continue
                        eng.wait_ge(d.sem, val)
                        waited[key] = val
                    inst = o.fn(eng)
                    if o.signal:
                        inst.then_inc(o.sem, 16 if o.is_dma else 1)
                for c, e in last_chan_eng.items():
                    if e == ename:
                        eng.wait_ge(sems["c_" + c], 16 * self.chan_count[c])
            return section

        with nc.Block() as block:
            dec = {"pe": block.tensor, "dve": block.vector, "act": block.scalar,
                   "pool": block.gpsimd, "sp": block.sync}
            for ename, ops in by_eng.items():
                dec[ename](make_section(ename, ops))
        self.stack.close()

D = 1024
NT = 64
NSLOT = 16
BIG8 = 240000.0
IN_W = 5912
KV0 = 512
G0 = 1280
CH_COLS = [0, 1304, 1816, 2328, 2840, 3352, 3864, 4376, 4888, 5400]
NWB = 2
NCH = 14


def _mk(P):
    class H:
        pass
    h = H()

    def mm(out, lhsT, rhs, start, stop, reads, writes, **kw):
        return P.op("pe", lambda e: e.matmul(out=out, lhsT=lhsT, rhs=rhs, start=start, stop=stop, **kw), reads, writes)

    def tr(out, in_, ident, reads, writes):
        return P.op("pe", lambda e: e.transpose(out=out, in_=in_, identity=ident), reads, writes)

    def act(out, in_, func, reads, writes, **kw):
        return P.op("act", lambda e: e.activation(out=out, in_=in_, func=func, **kw), reads, writes)

    def tt(eng, out, in0, in1, op, reads, writes):
        return P.op(eng, lambda e: e.tensor_tensor(out=out, in0=in0, in1=in1, op=op), reads, writes)

    def ts(eng, out, in0, s1, s2, op0, op1, reads, writes, **kw):
        if op1 is None:
            return P.op(eng, lambda e: e.tensor_scalar(out=out, in0=in0, scalar1=s1, scalar2=None, op0=op0, **kw), reads, writes)
        return P.op(eng, lambda e: e.tensor_scalar(out=out, in0=in0, scalar1=s1, scalar2=s2, op0=op0, op1=op1, **kw), reads, writes)

    def stt(out, in0, scalar, in1, op0, op1, reads, writes):
        return P.op("dve", lambda e: e.scalar_tensor_tensor(out=out, in0=in0, scalar=scalar, in1=in1, op0=op0, op1=op1), reads, writes)

    def cp(eng, out, in_, reads, writes):
        if eng == "act":
            return P.op("act", lambda e: e.copy(out=out, in_=in_), reads, writes)
        return P.op(eng, lambda e: e.tensor_copy(out=out, in_=in_), reads, writes)

    def ms(eng, ap, val, writes):
        return P.op(eng, lambda e: e.memset(ap, val), (), writes)

    h.mm, h.tr, h.act, h.tt, h.ts, h.stt, h.cp, h.ms = mm, tr, act, tt, ts, stt, cp, ms
    return h


def build(nt=NT, nslot=NSLOT, do_sample=True, dbg=False):
    nc = bass.Bass("TRN2", target_bir_lowering=False)
    P = Prog(nc)
    h = _mk(P)
    mm, tr, act, tt, ts, stt, cp, ms = h.mm, h.tr, h.act, h.tt, h.ts, h.stt, h.cp, h.ms

    def din(name, shape, dt=F32):
        return nc.dram_tensor(name, list(shape), dt, kind="ExternalInput").ap()

    def dout(name, shape, dt=F32):
        return nc.dram_tensor(name, list(shape), dt, kind="ExternalOutput").ap()

    xp = din("xp", [NT * 128, D])
    xmine = din("xmine", [NSLOT * 128, D])
    xprev = din("xprev", [NSLOT * 32, D])
    cp_d = din("cpr", [1, D])
    w_ada = din("w_ada", [D, 3 * D])
    b_ada = din("b_ada", [1, 3 * D])
    g_pre = din("g_pre", [1, D])
    g_post = din("g_post", [1, D])
    w_in = din("w_in", [D, IN_W])
    pe_cmp = din("pe_cmp", [2, 32, 64])
    w_cmp1 = din("w_cmp1", [2, 2048, 256])
    w_cmp2 = din("w_cmp2", [2, 256, 64])
    conv_w = din("conv_w", [3, 512])
    w_br_a = din("w_br_a", [512, D])
    w_br_b = din("w_br_b", [512, D])
    w_out = din("w_out", [D, D])
    ident_d = din("ident", [128, 128])
    ropeA_d = din("ropeA", [128, NT, 16])
    ropeB_d = din("ropeB", [128, NSLOT, 16])
    slotc_d = din("slotc", [NSLOT, 128, 3, 128])
    cmpb_d = din("cmpb", [NSLOT, 128, 4, 128])
    winb_d = din("winb", [128, 8, 128])
    selc_d = din("selc", [128, 3, 4, 128])
    ovl_d = din("ovl", [128, 4, 128])
    shm_d = din("shm", [128, 2, 128])
    shb_d = din("shb", [32, 2, 128])
    upsc_d = din("upsc", [32, NSLOT])

    xs_d = din("xs", [16, D])
    cs_d = din("cs", [16, D])
    ptab_d = din("ptab", [1, 256], I32)
    cache_d = din("cache", [2560 * 128, 512])
    swin_d = din("swin", [16, 512, 256])
    sconv_d = din("sconv", [16, 2, 512])
    ropeS_d = din("ropeS", [16, 16])
    slotS_d = din("slotS", [128, 3, 128])
    smask_d = din("smask", [128, 4, 128])
    onehot_d = din("onehot", [128, 16])
    pcol_d = din("pcol", [128, 1])
    ys = dout("ys", [16, D])
    kvs = dout("kvs", [16, 512])
    wins = dout("wins", [16, 512, 256])
    convs = dout("convs", [16, 2, 512])
    yp = dout("yp", [NSLOT * 128, D])
    kvp = dout("kvp", [NSLOT * 128, 512])
    winp = dout("winp", [128, 256])
    convp = dout("convp", [2, 512])
    dbg_outs = {}

    def dbg_out(name, buf, ap, shape, dt=F32):
        if not dbg:
            return
        d = dout("dbg_" + name, shape, dt)
        P.dma("sp", d, ap, reads=[buf], chan="dbg_" + name)

    wscr = P.dram("wscr", [NCH, 128, 4096], BF16)

    ident = P.sb("ident", [128, 128], F32)
    identb = P.sb("identb", [128, 128], BF16)
    I4 = P.sb("I4", [128, 512], BF16)
    ones1 = P.sb("ones1", [1, 128], F32)
    wkv = P.sb("wkv", [128, 8, 768], BF16)
    wg = P.sb("wg", [128, 8, 24], BF16)
    w1 = P.sb("w1", [128, 32, 256], BF16)
    w2 = P.sb("w2", [128, 2, 2, 64], BF16)
    wch = [P.sb("wch%d" % i, [128, 4096], BF16) for i in range(NWB)]
    KselT = P.sb("KselT", [128, NT * 128], BF16)
    Vsel = P.sb("Vsel", [128, NT, 2, 65], BF16)
    KwinT = P.sb("KwinT", [128, 8 * 128], BF16)
    Vwin = P.sb("Vwin", [128, 8, 2, 65], BF16)
    kcT = P.sb("kcT", [128, 512], BF16)
    vca = P.sb("vca", [128, 4, 2, 65], BF16)
    vcf = P.sb("vcf", [128, 4, 2, 64], F32)
    CT = [P.sb("CT%d" % g, [128, 144], BF16) for g in range(2)]
    xt = [P.sb("xt0", [128, D], F32)] * 2
    xn = P.sb("xn", [128, D], F32)
    hT = [P.sb("hT0", [128, 8, 128], BF16)] * 2
    zkv = P.sb("zkv", [128, 768], F32)
    ctsrc = P.sb("ctsrc", [128, 256], F32)
    ropeA = P.sb("ropeA", [128, NT, 16], F32)
    ropeB = P.sb("ropeB", [128, NSLOT, 16], F32)
    rt = P.sb("rt", [128, 4, 8 * 8], F32)
    ss = P.sb("ss", [128, 8], F32)
    G1col = P.sb("G1col", [128, 8], F32)
    SHcol = P.sb("SHcol", [128, 8], F32)
    GP = P.sb("GP", [128, D], F32)
    cm = P.sb("cm", [128, 4], F32)
    cmb = P.sb("cmb", [128, 4], BF16)
    cmt8 = P.sb("cmt8", [128, 16], F32)
    cmneg = P.sb("cmneg", [128, 128], F32)
    bias1 = P.sb("bias1", [128, 2, 2], F32)
    hsT = P.sb("hsT", [128, 2, 2, 8], BF16)
    hsTp = P.sb("hsTp", [128, 2, 256], BF16)
    sT = P.sb("sT", [128, 8, 17], F32)
    modT = P.sb("modT", [128, 24, 17], F32)
    coltmp = P.sb("coltmp", [128, 32], F32)
    G1S = P.sb("G1S", [128, 8, 16], F32)
    SHS = P.sb("SHS", [128, 8, 16], F32)
    GST = P.sb("GST", [128, 8, 16], F32)
    bcm = P.sb("bcm", [128, 128], F32)
    convwb = P.sb("convwb", [128, 3, 512], F32)
    winb = P.sb("winb", [128, 8, 128], F32)
    selc = P.sb("selc", [128, 3, 4, 128], F32)
    ovl = P.sb("ovl", [128, 4, 128], BF16)
    ovlf = P.sb("ovlf", [128, 4, 128], F32)
    shm = P.sb("shm", [128, 2, 128], F32)
    shb = P.sb("shb", [32, 2, 128], F32)
    upsc = P.sb("upsc", [32, NSLOT], F32)
    pet = P.sb("pet", [32, 2, 64], F32)
    peT = P.sb("peT", [128, 32], BF16)

    xm = P.sb("xm", [128, D], F32)
    ps = [P.ps("b%d" % i, [128, 512], F32) for i in range(8)]

    P.dma("sp", ident[:], ident_d, writes=[ident], chan="c0")
    P.dma("sp", ropeA[:], ropeA_d, writes=[ropeA], chan="c0")
    P.dma("sp", ropeB[:], ropeB_d, writes=[ropeB], chan="c0")
    P.dma("sp", winb[:], winb_d, writes=[winb], chan="c0")
    P.dma("sp", selc[:], selc_d, writes=[selc], chan="c0")
    P.dma("sp", ovlf[:], ovl_d, writes=[ovlf], chan="c0")
    P.dma("sp", shm[:], shm_d, writes=[shm], chan="c0")
    P.dma("sp", shb[:], shb_d, writes=[shb], chan="c0")
    P.dma("sp", upsc[:], upsc_d, writes=[upsc], chan="c0")
    P.dma("sp", pet[:], pe_cmp.rearrange("k r d -> r k d"), writes=[pet], chan="c0")
    cp("dve", identb[:], ident[:], [ident], [identb])
    cp("dve", ovl[:], ovlf[:], [ovlf], [ovl])
    for i in range(4):
        cp("dve", I4[:, i * 128:(i + 1) * 128], ident[:], [ident], [I4])
    ms("dve", ones1[:], 1.0, [ones1])
    ms("pool", Vsel[:], 1.0, [Vsel])
    ms("pool", Vwin[:], 1.0, [Vwin])
    ms("pool", vca[:], 1.0, [vca])
    ms("pool", vcf[:], 0.0, [vcf])
    ms("pool", kcT[:], 0.0, [kcT])
    ms("pool", cm[:], 0.0, [cm])
    for g in range(2):
        ms("pool", CT[g][:], 0.0, [CT[g]])
    ms("pool", KwinT[:], 0.0, [KwinT])
    ms("pool", KselT[:], 0.0, [KselT])

    P.dma("pool", wkv[:], w_in[:, KV0:KV0 + 768].rearrange("(c p) n -> p c n", p=128), writes=[wkv], chan="wk")
    P.dma("pool", wg[:], w_in[:, G0:G0 + 24].rearrange("(c p) n -> p c n", p=128), writes=[wg], chan="wk")
    for kv in range(2):
        P.dma("pool", w1[kv * 64:(kv + 1) * 64, :, :], w_cmp1[kv].rearrange("(r d) n -> d r n", d=64),
              writes=[w1], chan="wk")
        P.dma("pool", w2[:, kv, :, :], w_cmp2[kv].rearrange("(h p) n -> p h n", p=128), writes=[w2], chan="wk")
    for j in range(NCH):
        wb = wch[j % NWB]
        if j < 10:
            src = w_in[:, CH_COLS[j]:CH_COLS[j] + 512].rearrange("(c p) n -> p c n", p=128)
            dst = wb[:].rearrange("p (c n) -> p c n", c=8)
        elif j == 10:
            src = w_br_a.rearrange("(c p) n -> p c n", p=128)
            dst = wb[:].rearrange("p (c n) -> p c n", c=4)
        elif j == 11:
            src = w_br_b.rearrange("(c p) n -> p c n", p=128)
            dst = wb[:].rearrange("p (c n) -> p c n", c=4)
        else:
            o = (j - 12) * 512
            src = w_out[:, o:o + 512].rearrange("(c p) n -> p c n", p=128)
            dst = wb[:].rearrange("p (c n) -> p c n", c=8)
        P.dma("pool", dst, src, writes=[wb], chan="wst%d" % (j % NWB))
        P.dma("sp", wscr[j], wb[:], reads=[wb], writes=[(wscr, j)], chan="wst%d" % (j % NWB))

    P.dma("sp", xn[0:1, :], cp_d, writes=[xn], chan="xnl")
    P.dma("sp", xn[1:17, :], cs_d, writes=[xn], chan="xnl")
    act(xn[0:17, :], xn[0:17, :], AF.Silu, [xn], [xn])
    for k in range(8):
        tr(ps[0][:, k * 17:(k + 1) * 17], xn[0:17, k * 128:(k + 1) * 128], ident[0:17, 0:17], [xn, ident], [ps[0]])
    cp("dve", sT[:].rearrange("p k s -> p (k s)"), ps[0][:, 0:136], [ps[0]], [sT])

    def row_to_cols(row_ap, dst_buf, dst_ap_fn, n):
        for j in range(n):
            tr(ps[0][:, j:j + 1], row_ap[:, j * 128:(j + 1) * 128], ident[0:1, 0:1], [xm, ident], [ps[0]])
        cp("dve", dst_ap_fn, ps[0][:, 0:n], [ps[0]], [dst_buf])

    for j in range(24):
        wa = xt[j % 2]
        P.dma("sp", wa[:].rearrange("p (c n) -> p c n", c=8),
              w_ada[:, j * 128:(j + 1) * 128].rearrange("(c p) n -> p c n", p=128), writes=[wa], chan="xt0")
        for k in range(8):
            mm(ps[1][:, j * 17:(j + 1) * 17], wa[:, k * 128:(k + 1) * 128], sT[:, k, :], k == 0, k == 7, [wa, sT], [ps[1]])
    cp("dve", modT[:].rearrange("p j s -> p (j s)"), ps[1][:, 0:408], [ps[1]], [modT])
    for i3 in range(3):
        P.dma("sp", xm[0:1, :], b_ada[:, i3 * D:(i3 + 1) * D], writes=[xm], chan="xm")
        row_to_cols(xm[0:1, :], coltmp, coltmp[:, i3 * 8:(i3 + 1) * 8], 8)
    tt("dve", modT[:], modT[:], coltmp[:, 0:24].unsqueeze(2).to_broadcast([128, 24, 17]), ALU.add, [modT, coltmp], [modT])
    P.dma("sp", xm[0:1, :], g_pre, writes=[xm], chan="xm")
    row_to_cols(xm[0:1, :], coltmp, coltmp[:, 24:32], 8)
    stt(G1col[:], modT[:, 8:16, 0], 1.0, coltmp[:, 24:32], ALU.add, ALU.mult, [modT, coltmp], [G1col])
    cp("dve", SHcol[:], modT[:, 0:8, 0], [modT], [SHcol])
    ts("dve", G1S[:], modT[:, 8:16, 1:17], 1.0, None, ALU.add, None, [modT], [G1S])
    tt("dve", G1S[:], G1S[:], coltmp[:, 24:32].unsqueeze(2).to_broadcast([128, 8, 16]), ALU.mult, [G1S, coltmp], [G1S])
    cp("dve", SHS[:], modT[:, 0:8, 1:17], [modT], [SHS])
    P.dma("sp", xm[0:1, :], g_post, writes=[xm], chan="xm")
    row_to_cols(xm[0:1, :], coltmp, coltmp[:, 0:8], 8)
    tt("dve", coltmp[:, 8:16], coltmp[:, 0:8], modT[:, 16:24, 0], ALU.mult, [coltmp, modT], [coltmp])
    tt("dve", GST[:], modT[:, 16:24, 1:17], coltmp[:, 0:8].unsqueeze(2).to_broadcast([128, 8, 16]), ALU.mult,
       [modT, coltmp], [GST])
    for c in range(8):
        cp("dve", bcm[:], coltmp[:, 8 + c:9 + c].to_broadcast([128, 128]), [coltmp], [bcm])
        tr(ps[2 + c // 4][:, (c % 4) * 128:(c % 4 + 1) * 128], bcm[:], ident[:], [bcm, ident], [ps[2 + c // 4]])
    cp("dve", GP[:, 0:512], ps[2][:], [ps[2]], [GP])
    cp("dve", GP[:, 512:1024], ps[3][:], [ps[3]], [GP])
    for k in range(3):
        P.dma("sp", xm[0:1, 0:512], conv_w[k:k + 1, :], writes=[xm], chan="xm")
        mm(ps[4][:], ones1[0:1, :], xm[0:1, 0:512], True, True, [ones1, xm], [ps[4]])
        cp("dve", convwb[:, k, :], ps[4][:], [ps[4]], [convwb])
    tr(ps[5][:, 0:32], pet[:].rearrange("r k d -> r (k d)"), ident[0:32, 0:32], [pet, ident], [ps[5]])
    cp("dve", peT[:], ps[5][:, 0:32], [ps[5]], [peT])
    for kv in range(2):
        for half in range(2):
            for r in range(32):
                mm(ps[6 + kv][:, half:half + 1], w1[kv * 64:(kv + 1) * 64, r, half * 128:(half + 1) * 128],
                   peT[kv * 64:(kv + 1) * 64, r:r + 1], r == 0, r == 31, [w1, peT], [ps[6 + kv]])
    for kv in range(2):
        cp("dve", bias1[:, kv, :], ps[6 + kv][:, 0:2], [ps[6 + kv]], [bias1])

    def load_x(buf, src_ap, chan, rows=128):
        P.dma("sp", buf[0:rows, :], src_ap, writes=[buf], chan=chan)

    def norm_T(xb, hTb, rows=128):
        act(xn[0:rows, :], xb[0:rows, :], AF.Square, [xb], [xn, ss], accum_out=ss[0:rows, 0:1])
        ts("dve", ss[0:rows, 1:2], ss[0:rows, 0:1], 1.0 / D, 1e-6, ALU.mult, ALU.add, [ss], [ss])
        act(ss[0:rows, 2:3], ss[0:rows, 1:2], AF.Sqrt, [ss], [ss])
        P.op("dve", lambda e: e.reciprocal(out=ss[0:rows, 3:4], in_=ss[0:rows, 2:3]), [ss], [ss])
        ts("dve", xn[0:rows, :], xb[0:rows, :], ss[0:rows, 3:4], None, ALU.mult, None, [xb, ss], [xn])
        for c in range(8):
            tr(ps[c // 4][:, (c % 4) * 128:(c % 4) * 128 + rows], xn[0:rows, c * 128:(c + 1) * 128],
               ident[0:rows, 0:rows], [xn, ident], [ps[c // 4]])
        for c in range(8):
            act(hTb[:, c, 0:rows], ps[c // 4][:, (c % 4) * 128:(c % 4) * 128 + rows], AF.Identity,
                [ps[c // 4], G1col, SHcol], [hTb], scale=G1col[:, c:c + 1], bias=SHcol[:, c:c + 1])

    def rope_inplace(zb, base_views, tab_ap, rows=128):
        for v in base_views:
            n = v.shape[1]
            x1 = v[:, :, 0:8]
            x2 = v[:, :, 8:16]
            cs = tab_ap[:, 0:8].unsqueeze(1).to_broadcast([rows, n, 8])
            sn = tab_ap[:, 8:16].unsqueeze(1).to_broadcast([rows, n, 8])
            t = [rt[0:rows, i, 0:n * 8].rearrange("p (n e) -> p n e", e=8) for i in range(4)]
            tt("dve", t[0], x1, cs, ALU.mult, [zb], [rt])
            tt("dve", t[1], x2, sn, ALU.mult, [zb], [rt])
            tt("dve", t[2], x1, sn, ALU.mult, [zb], [rt])
            tt("dve", t[3], x2, cs, ALU.mult, [zb], [rt])
            tt("dve", x1, t[0], t[1], ALU.subtract, [rt], [zb])
            tt("dve", x2, t[2], t[3], ALU.add, [rt], [zb])

    def kv_proj(hTb, tab_ap, rows=128):
        for half in range(2):
            for k in range(8):
                mm(ps[2 + half][0:rows, 0:384], hTb[:, k, 0:rows], wkv[:, k, half * 384:(half + 1) * 384], k == 0, k == 7,
                   [hTb, wkv], [ps[2 + half]])
        cp("act", zkv[0:rows, 0:384], ps[2][0:rows, 0:384], [ps[2]], [zkv])
        cp("dve", zkv[0:rows, 384:768], ps[3][0:rows, 0:384], [ps[3]], [zkv])
        views = [zkv[0:rows, o:o + 128].rearrange("p (g d) -> p g d", g=2)[:, :, 0:16] for o in (0, 256, 512)]
        rope_inplace(zkv, views, tab_ap, rows=rows)

    def colmax_update(buf, ap, prt, bi, n):
        P.op("dve", lambda e: e.max(out=cmt8[prt, 0:8], in_=ap), [buf], [cmt8])
        tt("dve", cm[prt, bi:bi + 1], cm[prt, bi:bi + 1], cmt8[prt, 0:1], ALU.max, [cm, cmt8], [cm])
        ts("dve", cmneg[prt, 0:n], ap, -1.0, None, ALU.mult, None, [buf], [cmneg])
        P.op("dve", lambda e: e.max(out=cmt8[prt, 8:16], in_=cmneg[prt, 0:n]), [cmneg], [cmt8])
        tt("dve", cm[prt, bi:bi + 1], cm[prt, bi:bi + 1], cmt8[prt, 8:9], ALU.max, [cm, cmt8], [cm])

    KST = ""

    def phase_a(t):
        xb = xt[t % 2]
        hTb = hT[t % 2]
        norm_T(xb, hTb)
        if t + 1 < nt:
            load_x(xt[(t + 1) % 2], xp[(t + 1) * 128:(t + 2) * 128, :], "xt0")
        kv_proj(hTb, ropeA[:, t, :])
        ingest(t, zkv, True)

    def ingest(t, zb, has_win):
        tr(ps[4][:, 0:128], zb[:, 256:384], ident[:], [zb, ident], [ps[4]])
        if has_win:
            tr(ps[4][:, 128:256], zb[:, 512:640], ident[:], [zb, ident], [ps[4]])
        cp("dve", ctsrc[:].rearrange("p (g k d) -> p g k d", g=2, k=2),
           zb[:, 0:256].rearrange("p (k g d) -> p g k d", k=2, g=2), [zb], [ctsrc])
        for g in range(2):
            tr(ps[4][:, 256 + g * 128:384 + g * 128], ctsrc[:, g * 128:(g + 1) * 128], ident[:], [ctsrc, ident], [ps[4]])
        cp("act", KselT[:, t * 128:(t + 1) * 128], ps[4][:, 0:128], [ps[4]], [(KselT, t)])
        if has_win:
            cp("act", KwinT[:, (t % 8) * 128:(t % 8 + 1) * 128], ps[4][:, 128:256], [ps[4]], [(KwinT, t % 8)])
        for g in range(2):
            cp("act", CT[g][:, 16:144], ps[4][:, 256 + g * 128:384 + g * 128], [ps[4]], [CT[g]])
        for bi, o in (((1, 0), (2, 128)) if has_win else ((1, 0),)):
            colmax_update(ps[4], ps[4][:, o:o + 128], slice(0, 128), bi, 128)
        cp("dve", Vsel[:, t, :, 0:64], zb[:, 384:512].rearrange("p (g d) -> p g d", g=2), [zb], [(Vsel, t)])
        if has_win:
            cp("dve", Vwin[:, t % 8, :, 0:64], zb[:, 640:768].rearrange("p (g d) -> p g d", g=2), [zb], [(Vwin, t % 8)])
        m0 = 1 if t == 0 else 0
        nb = 8 - m0
        i0 = 8 * t - 1 + m0
        kt0 = i0 // 128
        c0 = i0 - kt0 * 128
        for g in range(2):
            for kv in range(2):
                for half in range(2):
                    col = ((g * 2 + kv) * 2 + half) * 8
                    for r in range(32):
                        rhs = CT[g][kv * 64:(kv + 1) * 64, r:r + 16 * 7 + 1:16]
                        mm(ps[5 + 2 * kv][:, col:col + 8], w1[kv * 64:(kv + 1) * 64, r, half * 128:(half + 1) * 128], rhs,
                           r == 0, r == 31, [w1, CT[g]], [ps[5 + 2 * kv]])
            ms("pool", hsTp[:], 0.0, [hsTp])
            for kv in range(2):
                for half in range(2):
                    col = ((g * 2 + kv) * 2 + half) * 8
                    act(hsT[:, kv, half, 0:8], ps[5 + 2 * kv][:, col:col + 8], AF.Silu, [ps[5 + 2 * kv], bias1], [hsT],
                        bias=bias1[:, kv, half:half + 1])
            if KST == "a4":
                continue
            cp("dve", hsTp[:, :, c0:c0 + nb], hsT[:, 1, :, m0:8], [hsT], [hsTp])
            for half in range(2):
                mm(ps[6][g * 64:(g + 1) * 64, 0:8], w2[:, 0, half, :], hsT[:, 0, half, 0:8], half == 0, half == 1,
                   [w2, hsT], [ps[6]])
            cp("act", kcT[g * 64:(g + 1) * 64, i0:i0 + nb], ps[6][g * 64:(g + 1) * 64, m0:8], [ps[6]], [kcT])
            colmax_update(ps[6], ps[6][g * 64:(g + 1) * 64, 0:8], slice(g * 64, (g + 1) * 64), 0, 8)
            if KST == "a5":
                continue
            for w in range(2):
                if c0 + nb <= w * 128 or c0 >= (w + 1) * 128 or kt0 + w > 3:
                    continue
                for half in range(2):
                    mm(ps[7][:, 0:64], hsTp[:, half, w * 128:(w + 1) * 128], w2[:, 1, half, :], half == 0, half == 1,
                       [hsTp, w2], [ps[7]])
                tt("dve", vcf[:, kt0 + w, g, :], vcf[:, kt0 + w, g, :], ps[7][:, 0:64], ALU.add, [vcf, ps[7]], [vcf])
                cp("dve", vca[:, kt0 + w, g, 0:64], vcf[:, kt0 + w, g, :], [vcf], [vca])
        for g in range(2):
            cp("dve", CT[g][:, 0:16], CT[g][:, 128:144], [CT[g]], [CT[g]])


    xpv = P.sb("xpv", [32, D], F32)
    hTm = P.sb("hTm", [128, 8, 128], BF16)
    hTp = P.sb("hTp", [128, 8, 32], BF16)
    qf = P.sb("qf", [128, 512], F32)
    QT = P.sb("QT", [128, 512], BF16)
    aQT = P.sb("aQT", [128, 512], BF16)
    gates = P.sb("gates", [128, 24], F32)
    sa = P.sb("sa", [128, 512], F32)
    cbf = P.sb("cbf", [128, 512], F32)
    ccf = P.sb("ccf", [128, 512], F32)
    uu = P.sb("uu", [128, 512], F32)
    ccp = P.sb("ccp", [32, 512], F32)
    up = P.sb("up", [32, 512], F32)
    co = P.sb("co", [128, 512], F32)
    sgc = P.sb("sgc", [128, 512], F32)
    obT = P.sb("obT", [128, 4, 128], BF16)
    oat = P.sb("oat", [128, 512], F32)
    oaT = P.sb("oaT", [128, 4, 128], BF16)
    mbuf = P.sb("mbuf", [128, D], F32)
    mT = P.sb("mT", [128, 8, 128], BF16)
    yout = mbuf
    PTb = [P.sb("PT%d" % i, [128, 512], BF16) for i in range(3)]
    Ls = P.sb("Ls", [128, 16, 128], BF16)
    Lx = P.sb("Lx", [128, 128], F32)
    Lt = P.sb("Lt", [128, 128], F32)
    Lexp = [P.sb("Lexp%d" % i, [128, 1024], BF16) for i in range(2)]
    slotc = P.sb("slotc", [128, 3, 128], F32)
    cmpb = P.sb("cmpb", [128, 4, 128], F32)
    negm = P.sb("negm", [128, 4], F32)
    sc = [P.sb("sc%d" % i, [128, 128], F32) for i in range(3)]
    Lselb = P.sb("Lselb", [128, 128], BF16)
    m8 = P.sb("m8", [128, 16], F32)
    rc = P.sb("rc", [128, 16], F32)

    chunk_seq = [(n, j) for n in range(nslot + (1 if do_sample else 0)) for j in (0, 1, 2, 3, 4, 5, 10, 6, 7, 11, 8, 9, 12, 13)]
    st = {"issued": 0, "used": 0, "u": 0}

    def issue_chunks(upto):
        while st["issued"] < min(upto, len(chunk_seq)):
            k = st["issued"]
            P.dma("sp", wch[k % NWB][:], wscr[chunk_seq[k][1]], reads=[(wscr, chunk_seq[k][1])], writes=[wch[k % NWB]],
                  chan="wst%d" % (k % NWB))
            st["issued"] += 1

    def next_chunk(expect_j):
        k = st["used"]
        assert chunk_seq[k][1] == expect_j, (chunk_seq[k], expect_j)
        issue_chunks(k + 1)
        st["used"] += 1
        return wch[k % NWB], k

    def zchunk(j, bank, lh=None, rows=128):
        wb, k = next_chunk(j)
        wv = wb[:].rearrange("p (c n) -> p c n", c=8)
        for kc in range(8):
            mm(bank[0:128, :], hTm[:, kc, :], wv[:, kc, :], kc == 0, kc == 7, [hTm, wb], [bank])
        return wb, wv, k

    def attention(n, g, samp=None):
        gr = slice(g * 64, (g + 1) * 64)
        nkt = 4 * n + 4 if samp is None else 17
        nfull = 4 * n if samp is None else 16
        pm = ps[2] if g == 0 else ps[1]
        sbanks = (ps[3], ps[4]) if g == 0 else (ps[0], ps[1])
        for br in range(4):
            for hh in range(4):
                mm(pm[:, br * 4 + hh:br * 4 + hh + 1], aQT[gr, hh * 128:(hh + 1) * 128], cmb[gr, min(br, 2):min(br, 2) + 1], True, True,
                   [aQT, cmb], [pm])
        for br in range(3):
            P.op("dve", lambda e, br=br: e.max(out=m8[:, 0:8], in_=pm[:, br * 4:br * 4 + 8]), [pm], [m8])
            ts("dve", negm[:, br:br + 1], m8[:, 0:1], -1.0, None, ALU.mult, None, [m8], [negm])
        ts("dve", negm[:, 3:4], negm[:, 1:2], -BIG8, None, ALU.add, None, [negm], [negm])

        def unit(Kt, Lap, Lbuf, Vap, Vbuf, Kbuf, obank, first, last, ovl_kt=None):
            u = st["u"]
            st["u"] += 1
            S = sbanks[u % 2]
            PT = PTb[u % 3]
            mm(S[:], Kt, QT[gr, :], True, False, [Kbuf, QT], [S])
            mm(S[:], Lap, I4[:], False, True, [Lbuf, I4], [S])
            act(PT[:], S[:], AF.Exp, [S], [PT], scale=0.125)
            for hh in range(4):
                mm(obank[:, hh * 65:(hh + 1) * 65], PT[:, hh * 128:(hh + 1) * 128], Vap, first and hh == 0, last,
                   [PT, Vbuf], [obank], skip_group_check=True)
            if ovl_kt is not None:
                for hh in range(4):
                    mm(ps[6][:, hh * 128:(hh + 1) * 128], PT[:, hh * 128:(hh + 1) * 128], ovl[:, ovl_kt, :],
                       first and hh == 0, last, [PT, ovl], [ps[6]], skip_group_check=True)

        def finish_branch(obank, br, first_branch):
            ov = obank[:, 0:260].rearrange("p (h e) -> p h e", e=65)
            ts("dve", rc[:, 0:4], ov[:, :, 64], 1e-30, None, ALU.max, None, [obank], [rc])
            P.op("dve", lambda e: e.reciprocal(out=rc[:, 4:8], in_=rc[:, 0:4]), [rc], [rc])
            gv = gates[:, g * 12:(g + 1) * 12].rearrange("p (h b) -> p h b", b=3)[:, :, br]
            tt("dve", rc[:, 8:12], rc[:, 4:8], gv, ALU.mult, [rc, gates], [rc])
            if samp is not None:
                ts("dve", rc[:, 8:12], rc[:, 8:12], onehot[:, samp:samp + 1], None, ALU.mult, None, [rc, onehot], [rc])
                first_branch = False
            for hh in range(4):
                dst = oat[:, g * 256 + hh * 64:g * 256 + (hh + 1) * 64]
                if first_branch:
                    ts("dve", dst, ov[:, hh, 0:64], rc[:, 8 + hh:9 + hh], None, ALU.mult, None, [obank, rc], [oat])
                else:
                    stt(dst, ov[:, hh, 0:64], rc[:, 8 + hh:9 + hh], dst, ALU.mult, ALU.add, [obank, rc, oat], [oat])

        ktmax = min(3, (32 * n + 30) // 128) if samp is None else 0
        for kt in range(ktmax + 1):
            if samp is None:
                ts("dve", Ls[:, kt, :], cmpb[:, kt, :], negm[:, 0:1], None, ALU.add, None, [cmpb, negm], [(Ls, kt)])
            else:
                ts("dve", Ls[:, kt, :], smask[:, 0, :], negm[:, 0:1], None, ALU.add, None, [smask, negm], [(Ls, kt)])
        for kt in range(ktmax + 1):
            unit(kcT[gr, kt * 128:(kt + 1) * 128], Ls[:, kt, :], (Ls, kt), vca[:, kt, g, :], vca, kcT, ps[5],
                 kt == 0, kt == ktmax, ovl_kt=kt)
        ov = ps[5][:, 0:260].rearrange("p (h e) -> p h e", e=65)
        ts("dve", rc[:, 12:16], ov[:, :, 64], 1e-30, None, ALU.max, None, [ps[5]], [rc])
        P.op("dve", lambda e: e.reciprocal(out=rc[:, 12:16], in_=rc[:, 12:16]), [rc], [rc])
        ts("dve", sc[0][:], ps[6][:, 0:128], rc[:, 12:13], None, ALU.mult, None, [ps[6], rc], [sc[0]])
        for hh in range(1, 4):
            stt(sc[0][:], ps[6][:, hh * 128:(hh + 1) * 128], rc[:, 12 + hh:13 + hh], sc[0][:], ALU.mult, ALU.add,
                [ps[6], rc, sc[0]], [sc[0]])
        finish_branch(ps[5], 0, True)
        tt("dve", sc[0][:], sc[0][:], slotc[:, 0, :], ALU.mult, [sc[0], slotc], [sc[0]])
        tt("dve", sc[0][:], sc[0][:], slotc[:, 1, :], ALU.add, [sc[0], slotc], [sc[0]])
        tt("dve", sc[0][:], sc[0][:], slotc[:, 2, :], ALU.max, [sc[0], slotc], [sc[0]])
        ms("dve", sc[0][:, 0:1], 1e6, [sc[0]])
        P.op("dve", lambda e: e.max(out=m8[:, 0:8], in_=sc[0][:]), [sc[0]], [m8])
        P.op("dve", lambda e: e.match_replace(out=sc[1][:], in_to_replace=m8[:, 0:8], in_values=sc[0][:], imm_value=-1e30),
             [sc[0], m8], [sc[1]])
        P.op("dve", lambda e: e.max(out=m8[:, 8:16], in_=sc[1][:]), [sc[1]], [m8])
        ts("dve", sc[2][:], sc[0][:], m8[:, 15:16], BIG8, ALU.is_ge, ALU.mult, [sc[0], m8], [sc[2]])
        ts("dve", Lselb[:], sc[2][:], negm[:, 3:4], None, ALU.add, None, [sc[2], negm], [Lselb])
        wk = [k for k in range(8) if 4 * n - 4 + k >= 0] if samp is None else [0, 1, 2, 3, 4]
        for k in wk:
            if samp is None:
                ts("dve", Ls[:, 4 + k, :], winb[:, k, :], negm[:, 2:3], None, ALU.add, None, [winb, negm], [(Ls, 4 + k)])
            else:
                mi = (2, 3, 3, 3, 1)[k]
                ts("dve", Ls[:, 4 + k, :], smask[:, mi, :], negm[:, 2:3], None, ALU.add, None, [smask, negm], [(Ls, 4 + k)])
        for k in wk:
            kt = 4 * n - 4 + k if samp is None else k
            unit(KwinT[gr, (kt % 8) * 128:(kt % 8 + 1) * 128], Ls[:, 4 + k, :], (Ls, 4 + k), Vwin[:, kt % 8, g, :],
                 (Vwin, kt % 8), (KwinT, kt % 8), ps[7], k == wk[0], k == wk[-1])
        finish_branch(ps[7], 2, False)
        for kt in range(nkt):
            first, last = kt == 0, kt == nkt - 1
            if kt < nfull:
                c, o = divmod(kt, 8)
                if o == 0:
                    nb16 = min(16, 2 * nfull - 16 * c)
                    cp("pool", Lexp[c % 2][:, 0:nb16 * 64].rearrange("p (j e) -> p j e", e=64),
                       Lselb[:, 16 * c:16 * c + nb16].unsqueeze(2).to_broadcast([128, nb16, 64]), [Lselb], [Lexp[c % 2]])
                Lap, Lbuf = Lexp[c % 2][:, o * 128:(o + 1) * 128], Lexp[c % 2]
            elif samp is not None:
                ts("dve", Ls[:, 12, :], smask[:, 1, :], negm[:, 1:2], None, ALU.add, None, [smask, negm], [(Ls, 12)])
                Lap, Lbuf = Ls[:, 12, :], (Ls, 12)
            else:
                kr = kt - 4 * n
                cp("dve", Lx[:].rearrange("p (j e) -> p j e", e=64),
                   Lselb[:, 2 * kt:2 * kt + 2].unsqueeze(2).to_broadcast([128, 2, 64]), [Lselb], [Lx])
                stt(Lt[:], selc[:, 1, kr, :], negm[:, 1:2], selc[:, 2, kr, :], ALU.mult, ALU.add, [selc, negm], [Lt])
                tt("dve", Lx[:], Lx[:], selc[:, 0, kr, :], ALU.mult, [Lx, selc], [Lx])
                tt("dve", Ls[:, 12 + kr, :], Lx[:], Lt[:], ALU.add, [Lx, Lt], [(Ls, 12 + kr)])
                Lap, Lbuf = Ls[:, 12 + kr, :], (Ls, 12 + kr)
            unit(KselT[gr, kt * 128:(kt + 1) * 128], Lap, Lbuf, Vsel[:, kt, g, :], (Vsel, kt), (KselT, kt), ps[5],
                 first, last)
        finish_branch(ps[5], 1, False)

    def phase_b(n):
        load_x(xm, xmine[n * 128:(n + 1) * 128, :], "xm")
        load_x(xpv, xprev[n * 32:(n + 1) * 32, :], "xpv", rows=32)
        P.dma("sp", slotc[:], slotc_d[n], writes=[slotc], chan="slotc")
        P.dma("sp", cmpb[:], cmpb_d[n], writes=[cmpb], chan="cmpb")
        norm_T(xm, hTm)
        norm_T(xpv, hTp, rows=32)
        kv_proj(hTm, ropeB[:, n, :])
        P.dma("sp", kvp[n * 128:(n + 1) * 128, :], zkv[:, 0:512], reads=[zkv], chan="zkvo")
        if n == nslot - 1:
            P.dma("sp", winp, zkv[:, 512:768], reads=[zkv], chan="zkvo")
        cp("dve", cmb[:], cm[:], [cm], [cmb])
        zchunk(0, ps[0])
        cp("act", qf[:], ps[0][:], [ps[0]], [qf])
        rope_inplace(qf, [qf[:].rearrange("p (h d) -> p h d", d=64)[:, :, 0:16]], ropeB[:, n, :])
        cp("pool", sgc[:].rearrange("p (h g d) -> p h g d", h=4, g=2),
           qf[:].rearrange("p (g h d) -> p h g d", g=2, h=4), [qf], [sgc])
        for jj in range(4):
            tr(ps[2][:, jj * 128:(jj + 1) * 128], sgc[:, jj * 128:(jj + 1) * 128], ident[:], [sgc, ident], [ps[2]])
        cp("act", QT[:], ps[2][:], [ps[2]], [QT])
        act(aQT[:], ps[2][:], AF.Abs, [ps[2]], [aQT])
        for kc in range(8):
            mm(ps[1][:, 0:24], hTm[:, kc, :], wg[:, kc, :], kc == 0, kc == 7, [hTm, wg], [ps[1]])
        act(gates[:], ps[1][:, 0:24], AF.Sigmoid, [ps[1]], [gates])
        zchunk(1, ps[0])
        act(sa[:], ps[0][:], AF.Silu, [ps[0]], [sa])
        zchunk(2, ps[1])
        cp("act", cbf[:], ps[1][:], [ps[1]], [cbf])
        wb, wv, _ = zchunk(3, ps[0])
        for kc in range(8):
            mm(ps[2][0:32, :], hTp[:, kc, :], wv[:, kc, :], kc == 0, kc == 7, [hTp, wb], [ps[2]])
        cp("act", ccf[:], ps[0][:], [ps[0]], [ccf])
        cp("act", ccp[:], ps[2][0:32, :], [ps[2]], [ccp])
        wb, wv, _ = zchunk(4, ps[1])
        for kc in range(8):
            mm(ps[2][0:32, :], hTp[:, kc, :], wv[:, kc, :], kc == 0, kc == 7, [hTp, wb], [ps[2]])
        tt("dve", uu[:], ccf[:], ps[1][:], ALU.mult, [ccf, ps[1]], [uu])
        stt(up[:], ps[2][0:32, :], upsc[:, n:n + 1], ccp[:], ALU.mult, ALU.mult, [ps[2], upsc, ccp], [up])
        if n == nslot - 1:
            P.dma("sp", convp, uu[126:128, :], reads=[uu], chan="uuo")
        for s_ in range(2):
            mm(ps[2 + s_][:], shm[:, s_, :], uu[:], True, False, [shm, uu], [ps[2 + s_]])
            mm(ps[2 + s_][:], shb[:, s_, :], up[:], False, True, [shb, up], [ps[2 + s_]])
        tt("dve", co[:], uu[:], convwb[:, 2, :], ALU.mult, [uu, convwb], [co])
        tt("dve", qf[:], ps[2][:], convwb[:, 1, :], ALU.mult, [ps[2], convwb], [qf])
        tt("dve", co[:], co[:], qf[:], ALU.add, [co, qf], [co])
        tt("dve", qf[:], ps[3][:], convwb[:, 0, :], ALU.mult, [ps[3], convwb], [qf])
        tt("dve", co[:], co[:], qf[:], ALU.add, [co, qf], [co])
        zchunk(5, ps[0])
        act(sgc[:], ps[0][:], AF.Silu, [ps[0]], [sgc])
        tt("dve", co[:], co[:], cbf[:], ALU.mult, [co, cbf], [co])
        tt("dve", co[:], co[:], sgc[:], ALU.mult, [co, sgc], [co])
        for c in range(4):
            tr(ps[1][:, c * 128:(c + 1) * 128], co[:, c * 128:(c + 1) * 128], ident[:], [co, ident], [ps[1]])
        cp("act", obT[:].rearrange("p c n -> p (c n)"), ps[1][:], [ps[1]], [obT])
        for g in range(2):
            attention(n, g)
        tt("dve", oat[:], oat[:], sa[:], ALU.mult, [oat, sa], [oat])
        for c in range(4):
            tr(ps[2][:, c * 128:(c + 1) * 128], oat[:, c * 128:(c + 1) * 128], ident[:], [oat, ident], [ps[2]])
        cp("act", oaT[:].rearrange("p c n -> p (c n)"), ps[2][:], [ps[2]], [oaT])
        for (jw, srcT, jg) in ((10, oaT, (6, 7)), (11, obT, (8, 9))):
            wb, k = next_chunk(jw)
            wv = wb[:].rearrange("p (c n) -> p c n", c=4)
            for half in range(2):
                for kc in range(4):
                    mm(ps[0 + half][:], srcT[:, kc, :], wv[:, kc, half * 512:(half + 1) * 512], kc == 0, kc == 3,
                       [srcT, wb], [ps[half]])
            for half in range(2):
                zchunk(jg[half], ps[2 + half])
                act(sgc[:], ps[2 + half][:], AF.Sigmoid, [ps[2 + half]], [sgc])
                dst = mbuf[:, half * 512:(half + 1) * 512]
                if jw == 10:
                    tt("dve", dst, sgc[:], ps[half][:], ALU.mult, [sgc, ps[half]], [mbuf])
                else:
                    tt("dve", qf[:], sgc[:], ps[half][:], ALU.mult, [sgc, ps[half]], [qf])
                    tt("dve", dst, dst, qf[:], ALU.add, [mbuf, qf], [mbuf])
        for c in range(8):
            tr(ps[4 + c // 4][:, (c % 4) * 128:(c % 4 + 1) * 128], mbuf[:, c * 128:(c + 1) * 128], ident[:],
               [mbuf, ident], [ps[4 + c // 4]])
        cp("act", mT[:, 0:4, :].rearrange("p c n -> p (c n)"), ps[4][:], [ps[4]], [mT])
        cp("dve", mT[:, 4:8, :].rearrange("p c n -> p (c n)"), ps[5][:], [ps[5]], [mT])
        for half in range(2):
            wb, k = next_chunk(12 + half)
            wv = wb[:].rearrange("p (c n) -> p c n", c=8)
            for kc in range(8):
                mm(ps[6 + half][:], mT[:, kc, :], wv[:, kc, :], kc == 0, kc == 7, [mT, wb], [ps[6 + half]])
        issue_chunks(st["used"] + NWB)
        for half in range(2):
            act(qf[:], ps[6 + half][:], AF.Square, [ps[6 + half]], [qf, ss],
                accum_out=ss[:, 4 + half:5 + half])
        tt("dve", ss[:, 6:7], ss[:, 4:5], ss[:, 5:6], ALU.add, [ss], [ss])
        ts("dve", ss[:, 6:7], ss[:, 6:7], 1.0 / D, 1e-6, ALU.mult, ALU.add, [ss], [ss])
        act(ss[:, 7:8], ss[:, 6:7], AF.Sqrt, [ss], [ss])
        P.op("dve", lambda e: e.reciprocal(out=ss[:, 6:7], in_=ss[:, 7:8]), [ss], [ss])
        for half in range(2):
            sl = slice(half * 512, (half + 1) * 512)
            stt(yout[:, sl], ps[6 + half][:], ss[:, 6:7], GP[:, sl], ALU.mult, ALU.mult, [ps[6 + half], ss, GP], [yout])
        tt("dve", yout[:], yout[:], xm[:], ALU.add, [yout, xm], [yout])
        P.dma("sp", yp[n * 128:(n + 1) * 128, :], yout[:], reads=[yout], chan="yout")


    pgb = [P.sb("pgb%d" % i, [128, 512], F32) for i in range(2)]
    idxf = P.sb("idxf", [128, 256], F32)
    idxi = P.sb("idxi", [128, 256], I32)
    pcol = P.sb("pcol", [128, 1], F32)
    smask = P.sb("smask", [128, 4, 128], F32)
    onehot = P.sb("onehot", [128, 16], F32)
    ropeS = P.sb("ropeS", [16, 16], F32)
    qsT = P.sb("qsT", [128, 4, 16], BF16)
    aqsT = P.sb("aqsT", [128, 4, 16], BF16)
    swt = P.sb("swt", [128, 256], F32)

    def phase_s():
        R = 16
        P.dma("sp", slotc[:], slotS_d, writes=[slotc], chan="slotc")
        P.dma("sp", smask[:], smask_d, writes=[smask], chan="c1")
        P.dma("sp", onehot[:], onehot_d, writes=[onehot], chan="c1")
        P.dma("sp", ropeS[:], ropeS_d, writes=[ropeS], chan="c1")
        P.dma("sp", pcol[:], pcol_d, writes=[pcol], chan="c1")
        P.dma("sp", idxi[:], ptab_d.to_broadcast([128, 256]), writes=[idxi], chan="c1")
        cp("dve", idxf[:], idxi[:], [idxi], [idxf])
        ts("dve", idxf[:], idxf[:], 128.0, pcol[:, 0:1], ALU.mult, ALU.add, [idxf, pcol], [idxf])
        cp("dve", idxi[:], idxf[:], [idxf], [idxi])
        load_x(xm, xs_d, "xm", rows=R)
        act(xn[0:R, :], xm[0:R, :], AF.Square, [xm], [xn, ss], accum_out=ss[0:R, 0:1])
        ts("dve", ss[0:R, 1:2], ss[0:R, 0:1], 1.0 / D, 1e-6, ALU.mult, ALU.add, [ss], [ss])
        act(ss[0:R, 2:3], ss[0:R, 1:2], AF.Sqrt, [ss], [ss])
        P.op("dve", lambda e: e.reciprocal(out=ss[0:R, 3:4], in_=ss[0:R, 2:3]), [ss], [ss])
        ts("dve", xn[0:R, :], xm[0:R, :], ss[0:R, 3:4], None, ALU.mult, None, [xm, ss], [xn])
        for c in range(8):
            tr(ps[0][:, c * R:(c + 1) * R], xn[0:R, c * 128:(c + 1) * 128], ident[0:R, 0:R], [xn, ident], [ps[0]])
        tt("dve", bcm[:], ps[0][:, 0:128], G1S[:].rearrange("p c s -> p (c s)"), ALU.mult, [ps[0], G1S], [bcm])
        tt("dve", hTm[:, :, 0:R], bcm[:].rearrange("p (c s) -> p c s", s=R), SHS[:], ALU.add, [bcm, SHS], [hTm])
        kv_proj(hTm, ropeS[:, :], rows=R)
        P.dma("sp", kvs, zkv[0:R, 0:512], reads=[zkv], chan="zkvo")
        P.dma("sp", wins[:, 511, :], zkv[0:R, 512:768], reads=[zkv], chan="zkvo")
        P.dma("sp", convs[:, 0, :], sconv_d[:, 1, :], chan="d2d")
        tr(ps[4][:, 0:R], zkv[0:R, 256:384], ident[0:R, 0:R], [zkv, ident], [ps[4]])
        tr(ps[4][:, R:2 * R], zkv[0:R, 512:640], ident[0:R, 0:R], [zkv, ident], [ps[4]])
        cp("act", KselT[:, 2048:2048 + R], ps[4][:, 0:R], [ps[4]], [(KselT, 16)])
        cp("act", KwinT[:, 512:512 + R], ps[4][:, R:2 * R], [ps[4]], [(KwinT, 4)])
        colmax_update(ps[4], ps[4][:, 0:R], slice(0, 128), 1, R)
        colmax_update(ps[4], ps[4][:, R:2 * R], slice(0, 128), 2, R)
        cp("dve", Vsel[0:R, 16, :, 0:64], zkv[0:R, 384:512].rearrange("p (g d) -> p g d", g=2), [zkv], [(Vsel, 16)])
        cp("dve", Vwin[0:R, 4, :, 0:64], zkv[0:R, 640:768].rearrange("p (g d) -> p g d", g=2), [zkv], [(Vwin, 4)])
        cp("dve", cmb[:], cm[:], [cm], [cmb])
        zchunk(0, ps[0])
        cp("act", qf[0:R, :], ps[0][0:R, :], [ps[0]], [qf])
        rope_inplace(qf, [qf[0:R, :].rearrange("p (h d) -> p h d", d=64)[:, :, 0:16]], ropeS[:, :], rows=R)
        cp("pool", sgc[0:R, :].rearrange("p (h g d) -> p h g d", h=4, g=2),
           qf[0:R, :].rearrange("p (g h d) -> p h g d", g=2, h=4), [qf], [sgc])
        for jj in range(4):
            tr(ps[2][:, jj * R:(jj + 1) * R], sgc[0:R, jj * 128:(jj + 1) * 128], ident[0:R, 0:R], [sgc, ident], [ps[2]])
        cp("act", qsT[:].rearrange("p j s -> p (j s)"), ps[2][:, 0:4 * R], [ps[2]], [qsT])
        act(aqsT[:].rearrange("p j s -> p (j s)"), ps[2][:, 0:4 * R], AF.Abs, [ps[2]], [aqsT])
        for kc in range(8):
            mm(ps[1][:, 0:24], hTm[:, kc, :], wg[:, kc, :], kc == 0, kc == 7, [hTm, wg], [ps[1]])
        act(gates[0:R, :], ps[1][0:R, 0:24], AF.Sigmoid, [ps[1]], [gates])
        zchunk(1, ps[0])
        act(sa[0:R, :], ps[0][0:R, :], AF.Silu, [ps[0]], [sa])
        zchunk(2, ps[1])
        cp("act", cbf[0:R, :], ps[1][0:R, :], [ps[1]], [cbf])
        zchunk(3, ps[0])
        cp("act", ccf[0:R, :], ps[0][0:R, :], [ps[0]], [ccf])
        zchunk(4, ps[1])
        tt("dve", uu[0:R, :], ccf[0:R, :], ps[1][0:R, :], ALU.mult, [ccf, ps[1]], [uu])
        P.dma("sp", convs[:, 1, :], uu[0:R, :], reads=[uu], chan="uuo")
        P.dma("sp", ccp[0:R, :], sconv_d[:, 0, :], writes=[ccp], chan="scv")
        P.dma("sp", up[0:R, :], sconv_d[:, 1, :], writes=[up], chan="scv")
        tt("dve", co[0:R, :], uu[0:R, :], convwb[0:R, 2, :], ALU.mult, [uu, convwb], [co])
        tt("dve", qf[0:R, :], up[0:R, :], convwb[0:R, 1, :], ALU.mult, [up, convwb], [qf])
        tt("dve", co[0:R, :], co[0:R, :], qf[0:R, :], ALU.add, [co, qf], [co])
        tt("dve", qf[0:R, :], ccp[0:R, :], convwb[0:R, 0, :], ALU.mult, [ccp, convwb], [qf])
        tt("dve", co[0:R, :], co[0:R, :], qf[0:R, :], ALU.add, [co, qf], [co])
        zchunk(5, ps[0])
        act(sgc[0:R, :], ps[0][0:R, :], AF.Silu, [ps[0]], [sgc])
        tt("dve", co[0:R, :], co[0:R, :], cbf[0:R, :], ALU.mult, [co, cbf], [co])
        tt("dve", co[0:R, :], co[0:R, :], sgc[0:R, :], ALU.mult, [co, sgc], [co])
        for c in range(4):
            tr(ps[1][:, c * R:(c + 1) * R], co[0:R, c * 128:(c + 1) * 128], ident[0:R, 0:R], [co, ident], [ps[1]])
        cp("act", obT[:, :, 0:R], ps[1][:, 0:4 * R].rearrange("p (c s) -> p c s", s=R), [ps[1]], [obT])
        ms("pool", hsTp[:, 0, 0:2], 0.0, [hsTp, KselT])
        ms("pool", oat[:], 0.0, [oat])
        ms("pool", QT[:], 0.0, [QT])
        ms("pool", aQT[:], 0.0, [aQT])
        k = 0
        for sm in range(R):
            CTb = [KselT[:, 4096 + g_ * 2048:4096 + (g_ + 1) * 2048] for g_ in range(2)]
            for q4 in range(4):
                for pi in range(4):
                    pgi = q4 * 4 + pi
                    pb = pgb[k % 2]
                    col = sm * 16 + pgi
                    P.op("pool", lambda e, pb=pb, col=col: e.indirect_dma_start(
                        out=pb[:], out_offset=None, in_=cache_d,
                        in_offset=bass.IndirectOffsetOnAxis(ap=idxi[:, col:col + 1], axis=0)),
                        reads=[idxi], writes=[pb], chan="pgb%d" % (k % 2))
                    k += 1
                    tr(ps[4][:, pi * 128:(pi + 1) * 128], pb[:, 256:384], ident[:], [pb, ident], [ps[4]])
                    cp("dve", ctsrc[:].rearrange("p (g k d) -> p g k d", g=2, k=2),
                       pb[:, 0:256].rearrange("p (k g d) -> p g k d", k=2, g=2), [pb], [ctsrc])
                    for g_ in range(2):
                        tr(ps[5 + g_][:, pi * 128:(pi + 1) * 128], ctsrc[:, g_ * 128:(g_ + 1) * 128], ident[:],
                           [ctsrc, ident], [ps[5 + g_]])
                    cp("dve", Vsel[:, pgi, :, 0:64], pb[:, 384:512].rearrange("p (g d) -> p g d", g=2), [pb], [(Vsel, pgi)])
                cp("act", KselT[:, q4 * 512:(q4 + 1) * 512], ps[4][:], [ps[4]], [(KselT, 4 * q4 + i_) for i_ in range(4)])
                P.op("dve", lambda e: e.max(out=cmt8[:, 0:8], in_=ps[4][:]), [ps[4]], [cmt8])
                tt("dve", cm[:, 1:2], cm[:, 1:2], cmt8[:, 0:1], ALU.max, [cm, cmt8], [cm])
                ts("dve", sgc[:], ps[4][:], -1.0, None, ALU.mult, None, [ps[4]], [sgc])
                P.op("dve", lambda e: e.max(out=cmt8[:, 8:16], in_=sgc[:]), [sgc], [cmt8])
                tt("dve", cm[:, 1:2], cm[:, 1:2], cmt8[:, 8:9], ALU.max, [cm, cmt8], [cm])
                for g_ in range(2):
                    cp("act", CTb[g_][:, q4 * 512:(q4 + 1) * 512], ps[5 + g_][:], [ps[5 + g_]], [(KselT, "ctb%d" % g_)])
            hsv = Lexp[1][:, 0:512].rearrange("p (k h n) -> p k h n", k=2, h=2)
            for g_ in range(2):
                for kv in range(2):
                    for half in range(2):
                        for r in range(32):
                            rhs = CTb[g_][kv * 64:(kv + 1) * 64, r:r + 16 * 126 + 1:16]
                            mm(ps[5 + 2 * kv][:, half * 128:half * 128 + 127],
                               w1[kv * 64:(kv + 1) * 64, r, half * 128:(half + 1) * 128], rhs, r == 0, r == 31,
                               [w1, (KselT, "ctb%d" % g_)], [ps[5 + 2 * kv]])
                for kv in range(2):
                    for half in range(2):
                        act(hsv[:, kv, half, 0:127], ps[5 + 2 * kv][:, half * 128:half * 128 + 127], AF.Silu,
                            [ps[5 + 2 * kv], bias1], [Lexp[1]], bias=bias1[:, kv, half:half + 1])
                for half in range(2):
                    mm(ps[6][g_ * 64:(g_ + 1) * 64, 0:127], w2[:, 0, half, :], hsv[:, 0, half, 0:127], half == 0, half == 1,
                       [w2, Lexp[1]], [ps[6]])
                cp("act", kcT[g_ * 64:(g_ + 1) * 64, 0:127], ps[6][g_ * 64:(g_ + 1) * 64, 0:127], [ps[6]], [kcT])
                colmax_update(ps[6], ps[6][g_ * 64:(g_ + 1) * 64, 0:127], slice(g_ * 64, (g_ + 1) * 64), 0, 127)
                for half in range(2):
                    mm(ps[4][0:127, 0:64], hsv[:, 1, half, 0:127], w2[:, 1, half, :], half == 0, half == 1,
                       [Lexp[1], w2], [ps[4]])
                cp("dve", vca[0:127, 0, g_, 0:64], ps[4][0:127, 0:64], [ps[4]], [vca])
            for i in range(4):
                P.dma("sp", swt[:], swin_d[sm, i * 128:(i + 1) * 128, :], writes=[swt], chan="swt")
                tr(ps[4][:, 0:128], swt[:, 0:128], ident[:], [swt, ident], [ps[4]])
                cp("act", KwinT[:, i * 128:(i + 1) * 128], ps[4][:, 0:128], [ps[4]], [(KwinT, i)])
                colmax_update(ps[4], ps[4][:, 0:128], slice(0, 128), 2, 128)
                cp("dve", Vwin[:, i, :, 0:64], swt[:, 128:256].rearrange("p (g d) -> p g d", g=2), [swt], [(Vwin, i)])
            P.dma("sp", wins[sm, 0:511, :], swin_d[sm, 1:512, :], chan="d2d")
            cp("dve", cmb[:], cm[:], [cm], [cmb])
            QTv = QT[:].rearrange("p (j q) -> p j q", q=128)
            aQTv = aQT[:].rearrange("p (j q) -> p j q", q=128)
            if sm > 0:
                ms("dve", QTv[:, :, sm - 1], 0.0, [QT])
                ms("dve", aQTv[:, :, sm - 1], 0.0, [aQT])
            cp("dve", QTv[:, :, sm], qsT[:, :, sm], [qsT], [QT])
            cp("dve", aQTv[:, :, sm], aqsT[:, :, sm], [aqsT], [aQT])
            for g in range(2):
                attention(0, g, samp=sm)
        tt("dve", oat[0:R, :], oat[0:R, :], sa[0:R, :], ALU.mult, [oat, sa], [oat])
        for c in range(4):
            tr(ps[2][:, c * R:(c + 1) * R], oat[0:R, c * 128:(c + 1) * 128], ident[0:R, 0:R], [oat, ident], [ps[2]])
        cp("act", oaT[:, :, 0:R], ps[2][:, 0:4 * R].rearrange("p (c s) -> p c s", s=R), [ps[2]], [oaT])
        for (jw, srcT, jg) in ((10, oaT, (6, 7)), (11, obT, (8, 9))):
            wb, kk = next_chunk(jw)
            wv = wb[:].rearrange("p (c n) -> p c n", c=4)
            for half in range(2):
                for kc in range(4):
                    mm(ps[0 + half][:], srcT[:, kc, :], wv[:, kc, half * 512:(half + 1) * 512], kc == 0, kc == 3,
                       [srcT, wb], [ps[half]])
            for half in range(2):
                zchunk(jg[half], ps[2 + half])
                act(sgc[0:R, :], ps[2 + half][0:R, :], AF.Sigmoid, [ps[2 + half]], [sgc])
                dst = mbuf[0:R, half * 512:(half + 1) * 512]
                if jw == 10:
                    tt("dve", dst, sgc[0:R, :], ps[half][0:R, :], ALU.mult, [sgc, ps[half]], [mbuf])
                else:
                    tt("dve", qf[0:R, :], sgc[0:R, :], ps[half][0:R, :], ALU.mult, [sgc, ps[half]], [qf])
                    tt("dve", dst, dst, qf[0:R, :], ALU.add, [mbuf, qf], [mbuf])
        for c in range(8):
            tr(ps[4 + c // 4][:, (c % 4) * R:(c % 4 + 1) * R], mbuf[0:R, c * 128:(c + 1) * 128], ident[0:R, 0:R],
               [mbuf, ident], [ps[4 + c // 4]])
        cp("act", mT[:, 0:4, 0:R], ps[4][:, 0:4 * R].rearrange("p (c s) -> p c s", s=R), [ps[4]], [mT])
        cp("dve", mT[:, 4:8, 0:R], ps[5][:, 0:4 * R].rearrange("p (c s) -> p c s", s=R), [ps[5]], [mT])
        for half in range(2):
            wb, kk = next_chunk(12 + half)
            wv = wb[:].rearrange("p (c n) -> p c n", c=8)
            for kc in range(8):
                mm(ps[6 + half][:], mT[:, kc, :], wv[:, kc, :], kc == 0, kc == 7, [mT, wb], [ps[6 + half]])
        for c in range(8):
            tr(ps[2 + c // 4][0:R, (c % 4) * 128:(c % 4 + 1) * 128], GST[:, c, :], ident[:], [GST, ident], [ps[2 + c // 4]])
        cp("dve", GP[0:R, 0:512], ps[2][0:R, :], [ps[2]], [GP])
        cp("dve", GP[0:R, 512:1024], ps[3][0:R, :], [ps[3]], [GP])
        for half in range(2):
            act(qf[0:R, :], ps[6 + half][0:R, :], AF.Square, [ps[6 + half]], [qf, ss], accum_out=ss[0:R, 4 + half:5 + half])
        tt("dve", ss[0:R, 6:7], ss[0:R, 4:5], ss[0:R, 5:6], ALU.add, [ss], [ss])
        ts("dve", ss[0:R, 6:7], ss[0:R, 6:7], 1.0 / D, 1e-6, ALU.mult, ALU.add, [ss], [ss])
        act(ss[0:R, 7:8], ss[0:R, 6:7], AF.Sqrt, [ss], [ss])
        P.op("dve", lambda e: e.reciprocal(out=ss[0:R, 6:7], in_=ss[0:R, 7:8]), [ss], [ss])
        for half in range(2):
            sl = slice(half * 512, (half + 1) * 512)
            stt(mbuf[0:R, sl], ps[6 + half][0:R, :], ss[0:R, 6:7], GP[0:R, sl], ALU.mult, ALU.mult, [ps[6 + half], ss, GP], [mbuf])
        tt("dve", mbuf[0:R, :], mbuf[0:R, :], xm[0:R, :], ALU.add, [mbuf, xm], [mbuf])
        P.dma("sp", ys, mbuf[0:R, :], reads=[mbuf], chan="yout")

    stop = ""
    if stop != "p0":
        load_x(xt[0], xp[0:128, :], "xt0")
        for t in range(nt):
            phase_a(t)
            if stop.startswith("a"):
                continue
            if t % 4 == 3 and t // 4 < nslot:
                phase_b(t // 4)
    if do_sample:
        phase_s()

    P.emit()
    return nc, P


def make_consts(r):
    f = np.float32
    c = {}
    c["ident"] = np.eye(128, dtype=f)
    inv = (np.float32(500000.0) ** (-(np.arange(8, dtype=f)) / np.float32(8))).astype(f)
    p = np.arange(128)

    def rope_tab(pos):
        ang = (pos.astype(f)[..., None] * inv).astype(f)
        return np.concatenate([np.cos(ang.astype(np.float64)), np.sin(ang.astype(np.float64))], -1).astype(f)

    c["ropeA"] = rope_tab(np.arange(NT)[None, :] * 128 + p[:, None])
    tn = 4 * np.arange(NSLOT) + r
    c["ropeB"] = rope_tab(tn[None, :] * 128 + p[:, None])
    j = np.arange(128)
    slotc = np.zeros((NSLOT, 128, 3, 128), f)
    cmpb = np.zeros((NSLOT, 128, 4, 128), f)
    cidx = (np.arange(4)[:, None] * 128 + np.arange(128)[None, :])
    for n in range(NSLOT):
        b = 4 * n + r
        cur = 2 * b + (p >= 64)
        A = (j[None, :] <= cur[:, None]).astype(f)
        slotc[n, :, 0] = A
        slotc[n, :, 1] = A - 1
        slotc[n, :, 2] = np.where((j[None, :] == cur[:, None]) | (j[None, :] == cur[:, None] - 1), 1e6, -2.0)
        valid = (16 * cidx[None] + 31 <= (128 * b + p)[:, None, None]) & (cidx[None] < 511)
        cmpb[n] = np.where(valid, 0.0, -BIG8)
    c["slotc"] = slotc
    c["cmpb"] = cmpb
    q = p[:, None]
    k = p[None, :]
    winb = np.zeros((128, 8, 128), f)
    for kr in range(8):
        dt = 128 * (r + 4 - kr) + q - k
        winb[:, kr] = np.where((dt >= 0) & (dt < 512), 0.0, -BIG8)
    c["winb"] = winb
    selc = np.zeros((128, 3, 4, 128), f)
    for kr in range(4):
        if kr < r:
            selc[:, 0, kr] = 1.0
        elif kr == r:
            selc[:, 1, kr] = 1.0
            selc[:, 2, kr] = np.where(k <= q, 0.0, -BIG8)
        else:
            selc[:, 1, kr] = 1.0
            selc[:, 2, kr] = -BIG8
    c["selc"] = selc
    cs = 16 * cidx
    ov = (cs[:, :, None] < 64 * j[None, None, :] + 64) & (cs[:, :, None] + 32 > 64 * j[None, None, :]) & (cidx[:, :, None] < 511)
    c["ovl"] = np.ascontiguousarray(ov.transpose(1, 0, 2)).astype(f)
    shm = np.zeros((128, 2, 128), f)
    shb = np.zeros((32, 2, 128), f)
    for s in (1, 2):
        for m in range(128):
            kk = m - s
            if kk >= 0:
                shm[kk, s - 1, m] = 1.0
            else:
                shb[32 + kk, s - 1, m] = 1.0
    c["shm"] = shm
    c["shb"] = shb
    ups = np.ones((32, NSLOT), f)
    if r == 0:
        ups[:, 0] = 0.0
    c["upsc"] = ups
    c["ropeS"] = np.repeat(rope_tab(np.array([2048])), 16, axis=0)
    slotS = np.zeros((128, 3, 128), f)
    A = (j <= 32).astype(f)
    slotS[:, 0] = A[None, :]
    slotS[:, 1] = A[None, :] - 1
    slotS[:, 2] = np.where((j == 31) | (j == 32), 1e6, -2.0)[None, :]
    c["slotS"] = slotS
    smask = np.zeros((128, 4, 128), f)
    smask[:, 0] = np.where(k <= 126, 0.0, -BIG8)
    smask[:, 1] = np.where(k == q, 0.0, -BIG8)
    smask[:, 2] = np.where(k >= 1, 0.0, -BIG8)
    c["smask"] = smask
    oh = np.zeros((128, 16), f)
    oh[np.arange(16), np.arange(16)] = 1.0
    c["onehot"] = oh
    c["pcol"] = np.arange(128, dtype=f).reshape(128, 1)
    return c


_CACHE = {}


def kernel(x_prompt, x_sample, cache_kv_pages, state_win_kv, state_conv, page_table, c_prompt, c_sample,
           w_ada, b_ada, g_pre, g_post, w_in, pe_cmp, w_cmp1, w_cmp2, conv_w, w_br_a, w_br_b, w_out,
           _nt=NT, _nslot=NSLOT):
    f = np.float32
    key = (_nt, _nslot)
    if key not in _CACHE:
        _CACHE[key] = build(_nt, _nslot)
    nc, P = _CACHE[key]
    in_maps = []
    cache_flat = np.ascontiguousarray(cache_kv_pages[0], dtype=f).reshape(-1, 512)
    for c in range(8):
        bi, r = c // 4, c % 4
        xb = np.ascontiguousarray(x_prompt[bi], dtype=f)
        tiles = xb.reshape(NT, 128, D)
        tn = 4 * np.arange(NSLOT) + r
        xmine = np.ascontiguousarray(tiles[tn]).reshape(NSLOT * 128, D)
        xprev = np.zeros((NSLOT, 32, D), f)
        for n in range(NSLOT):
            if tn[n] > 0:
                xprev[n] = xb[tn[n] * 128 - 32:tn[n] * 128]
        m = {
            "xp": xb, "xmine": xmine, "xprev": xprev.reshape(NSLOT * 32, D),
            "cpr": np.ascontiguousarray(c_prompt[bi:bi + 1], dtype=f),
            "w_ada": np.ascontiguousarray(w_ada[0]), "b_ada": np.ascontiguousarray(b_ada), "g_pre": np.ascontiguousarray(g_pre),
            "g_post": np.ascontiguousarray(g_post), "w_in": np.ascontiguousarray(w_in[0]), "pe_cmp": np.ascontiguousarray(pe_cmp[0]),
            "w_cmp1": np.ascontiguousarray(w_cmp1[0]), "w_cmp2": np.ascontiguousarray(w_cmp2[0]),
            "conv_w": np.ascontiguousarray(conv_w[0]), "w_br_a": np.ascontiguousarray(w_br_a[0]),
            "w_br_b": np.ascontiguousarray(w_br_b[0]), "w_out": np.ascontiguousarray(w_out[0]),
        }
        m.update(make_consts(r))
        sl = slice(16 * c, 16 * c + 16)
        m["xs"] = np.ascontiguousarray(x_sample[sl, 0, :], dtype=f)
        m["cs"] = np.ascontiguousarray(c_sample[sl], dtype=f)
        m["ptab"] = np.ascontiguousarray(page_table[sl], dtype=np.int32).reshape(1, 256)
        m["cache"] = cache_flat
        m["swin"] = np.ascontiguousarray(state_win_kv[0, sl], dtype=f).reshape(16, 512, 256)
        m["sconv"] = np.ascontiguousarray(state_conv[0, sl], dtype=f)
        in_maps.append(m)
    res = run_bass_kernel_spmd(nc, in_maps, core_ids=list(range(8)))
    B, T = x_prompt.shape[0], x_prompt.shape[1]
    y_prompt = np.zeros((B, T, D), f)
    kv_rows_prompt = np.zeros((1, B, T, 4, 2, 64), f)
    win_kv_prompt = np.zeros((1, B, 512, 2, 2, 64), f)
    conv_state_prompt = np.zeros((1, B, 2, 512), f)
    for c in range(8):
        bi, r = c // 4, c % 4
        o = res.results[c]
        for n in range(NSLOT):
            t = 4 * n + r
            y_prompt[bi, t * 128:(t + 1) * 128] = o["yp"][n * 128:(n + 1) * 128]
            kv_rows_prompt[0, bi, t * 128:(t + 1) * 128] = o["kvp"][n * 128:(n + 1) * 128].reshape(128, 4, 2, 64)
        win_kv_prompt[0, bi, r * 128:(r + 1) * 128] = o["winp"].reshape(128, 2, 2, 64)
        if r == 3:
            conv_state_prompt[0, bi] = o["convp"]
    nS = x_sample.shape[0]
    y_sample = np.zeros((nS, 1, D), f)
    kv_rows_sample = np.zeros((1, nS, 1, 4, 2, 64), f)
    win_kv_sample = np.zeros((1, nS, 512, 2, 2, 64), f)
    conv_state_sample = np.zeros((1, nS, 2, 512), f)
    for c in range(8):
        o = res.results[c]
        sl = slice(16 * c, 16 * c + 16)
        y_sample[sl, 0] = o["ys"]
        kv_rows_sample[0, sl, 0] = o["kvs"].reshape(16, 4, 2, 64)
        win_kv_sample[0, sl] = o["wins"].reshape(16, 512, 2, 2, 64)
        conv_state_sample[0, sl] = o["convs"]
    return (y_prompt, y_sample, kv_rows_prompt, win_kv_prompt, conv_state_prompt, kv_rows_sample, win_kv_sample,
            conv_state_sample)
```

```python
from concourse.bass_utils import run_bass_kernel_spmd

from contextlib import ExitStack
import numpy as np
import concourse.bass as bass
import concourse.mybir as mybir

F32 = mybir.dt.float32
BF16 = mybir.dt.bfloat16
I32 = mybir.dt.int32
U32 = mybir.dt.uint32
AF = mybir.ActivationFunctionType
ALU = mybir.AluOpType
AX = mybir.AxisListType

SAME_ENG_SYNC = True
COMPUTE = ("pe", "dve", "act", "pool")


class Buf:
    def __init__(self, prog, name, t, is_dram=False):
        self.prog = prog
        self.name = name
        self.t = t
        self.is_dram = is_dram
        self.st = {}
        self.whole = [None, []]
        self.excl = False

    def __getitem__(self, idx):
        return self.t[idx]


class Op:
    __slots__ = ("eng", "fn", "deps", "signal", "val", "sem", "chan", "idx", "is_dma")

    def __init__(self, eng, fn):
        self.eng = eng
        self.fn = fn
        self.deps = {}
        self.signal = False
        self.val = None
        self.sem = None
        self.chan = None
        self.is_dma = False


class Prog:
    def __init__(self, nc):
        self.nc = nc
        self.ops = []
        self.stack = ExitStack()
        self.chan_count = {}
        self.nbuf = 0

    def sb(self, name, shape, dtype):
        t = self.stack.enter_context(self.nc.sbuf_tensor("sb_" + name, list(shape), dtype))
        return Buf(self, name, t)

    def ps(self, name, shape, dtype):
        t = self.stack.enter_context(self.nc.psum_tensor("ps_" + name, list(shape), dtype))
        b = Buf(self, name, t)
        b.excl = True
        return b

    def dram(self, name, shape, dtype, kind="Internal"):
        t = self.nc.dram_tensor(name, list(shape), dtype, kind=kind)
        return Buf(self, name, t.ap(), is_dram=True)

    def _norm(self, lst):
        out = []
        for x in lst:
            if x is None:
                continue
            if isinstance(x, Buf):
                out.append((x, None))
            else:
                out.append(x)
        return out

    def op(self, eng, fn, reads=(), writes=(), chan=None):
        o = Op(eng, fn)
        o.idx = len(self.ops)
        if chan is not None:
            o.is_dma = True
            o.chan = chan
            self.chan_count[chan] = self.chan_count.get(chan, 0) + 1
            o.val = 16 * self.chan_count[chan]
            o.signal = True
        reads = self._norm(reads)
        writes = self._norm(writes)
        writes = writes + [(b_, k_) for (b_, k_) in reads if b_.excl]
        reads = [(b_, k_) for (b_, k_) in reads if not b_.excl]

        def add_dep(d):
            if d is None or d is o:
                return
            if d.is_dma:
                v = 16 * self.chan_count[d.chan]
                if d is not o and d.chan == o.chan:
                    v = d.val
                o.deps[d] = max(o.deps.get(d, 0), v)
            else:
                if d.eng == o.eng and not o.is_dma:
                    if d.eng == "pe" or not SAME_ENG_SYNC:
                        return
                o.deps[d] = 0

        for b, k in reads:
            add_dep(b.whole[0])
            if k is None:
                for st in b.st.values():
                    add_dep(st[0])
            elif k in b.st:
                add_dep(b.st[k][0])
        for b, k in writes:
            add_dep(b.whole[0])
            for r in b.whole[1]:
                add_dep(r)
            if k is None:
                for st in b.st.values():
                    add_dep(st[0])
                    for r in st[1]:
                        add_dep(r)
            elif k in b.st:
                add_dep(b.st[k][0])
                for r in b.st[k][1]:
                    add_dep(r)
        for b, k in reads:
            if k is None:
                b.whole[1].append(o)
            else:
                b.st.setdefault(k, [None, []])[1].append(o)
        for b, k in writes:
            if k is None:
                b.whole = [o, []]
                b.st = {}
            else:
                b.st[k] = [o, []]
        self.ops.append(o)
        return o

    def dma(self, eng, out, in_, reads=(), writes=(), chan=None, **kw):
        assert chan is not None
        return self.op(eng, lambda e: e.dma_start(out=out, in_=in_, **kw), reads, writes, chan=chan)

    def emit(self):
        nc = self.nc
        for o in self.ops:
            for d in o.deps:
                d.signal = True
        sems = {}
        for e in COMPUTE:
            sems[e] = self.stack.enter_context(nc.semaphore("s_" + e))
        for c in self.chan_count:
            sems["c_" + c] = self.stack.enter_context(nc.semaphore("c_" + c))
        cnt = {e: 0 for e in COMPUTE}
        for o in self.ops:
            if o.is_dma:
                o.sem = sems["c_" + o.chan]
            else:
                o.sem = sems[o.eng]
                if o.signal:
                    cnt[o.eng] += 1
                    o.val = cnt[o.eng]
        by_eng = {}
        for o in self.ops:
            by_eng.setdefault(o.eng, []).append(o)
        last_chan_eng = {}
        for o in self.ops:
            if o.is_dma:
                last_chan_eng[o.chan] = o.eng
        self.n_inst = {e: len(v) for e, v in by_eng.items()}

        def make_section(ename, ops):
            def section(eng):
                waited = {}
                for o in ops:
                    for d, v in o.deps.items():
                        val = v if d.is_dma else d.val
                        key = id(d.sem)
                        if waited.get(key, 0) >= val:
                            continue
                        eng.wait_ge(d.sem, val)
                        waited[key] = val
                    inst = o.fn(eng)
                    if o.signal:
                        inst.then_inc(o.sem, 16 if o.is_dma else 1)
                for c, e in last_chan_eng.items():
                    if e == ename:
                        eng.wait_ge(sems["c_" + c], 16 * self.chan_count[c])
            return section

        with nc.Block() as block:
            dec = {"pe": block.tensor, "dve": block.vector, "act": block.scalar,
                   "pool": block.gpsimd, "sp": block.sync}
            for ename, ops in by_eng.items():
                dec[ename](make_section(ename, ops))
        self.stack.close()

D = 1024
NT = 64
NSLOT = 16
BIG8 = 240000.0
IN_W = 5912
KV0 = 512
G0 = 1280
CH_COLS = [0, 1304, 1816, 2328, 2840, 3352, 3864, 4376, 4888, 5400]
NWB = 2
NCH = 14


def _mk(P):
    class H:
        pass
    h = H()

    def mm(out, lhsT, rhs, start, stop, reads, writes, **kw):
        return P.op("pe", lambda e: e.matmul(out=out, lhsT=lhsT, rhs=rhs, start=start, stop=stop, **kw), reads, writes)

    def tr(out, in_, ident, reads, writes):
        return P.op("pe", lambda e: e.transpose(out=out, in_=in_, identity=ident), reads, writes)

    def act(out, in_, func, reads, writes, **kw):
        return P.op("act", lambda e: e.activation(out=out, in_=in_, func=func, **kw), reads, writes)

    def tt(eng, out, in0, in1, op, reads, writes):
        return P.op(eng, lambda e: e.tensor_tensor(out=out, in0=in0, in1=in1, op=op), reads, writes)

    def ts(eng, out, in0, s1, s2, op0, op1, reads, writes, **kw):
        if op1 is None:
            return P.op(eng, lambda e: e.tensor_scalar(out=out, in0=in0, scalar1=s1, scalar2=None, op0=op0, **kw), reads, writes)
        return P.op(eng, lambda e: e.tensor_scalar(out=out, in0=in0, scalar1=s1, scalar2=s2, op0=op0, op1=op1, **kw), reads, writes)

    def stt(out, in0, scalar, in1, op0, op1, reads, writes):
        return P.op("dve", lambda e: e.scalar_tensor_tensor(out=out, in0=in0, scalar=scalar, in1=in1, op0=op0, op1=op1), reads, writes)

    def cp(eng, out, in_, reads, writes):
        if eng == "act":
            return P.op("act", lambda e: e.copy(out=out, in_=in_), reads, writes)
        return P.op(eng, lambda e: e.tensor_copy(out=out, in_=in_), reads, writes)

    def ms(eng, ap, val, writes):
        return P.op(eng, lambda e: e.memset(ap, val), (), writes)

    h.mm, h.tr, h.act, h.tt, h.ts, h.stt, h.cp, h.ms = mm, tr, act, tt, ts, stt, cp, ms
    return h


def build(nt=NT, nslot=NSLOT, do_sample=True, dbg=False):
    nc = bass.Bass("TRN2", target_bir_lowering=False)
    P = Prog(nc)
    h = _mk(P)
    mm, tr, act, tt, ts, stt, cp, ms = h.mm, h.tr, h.act, h.tt, h.ts, h.stt, h.cp, h.ms

    def din(name, shape, dt=F32):
        return nc.dram_tensor(name, list(shape), dt, kind="ExternalInput").ap()

    def dout(name, shape, dt=F32):
        return nc.dram_tensor(name, list(shape), dt, kind="ExternalOutput").ap()

    xp = din("xp", [NT * 128, D])
    xmine = din("xmine", [NSLOT * 128, D])
    xprev = din("xprev", [NSLOT * 32, D])
    cp_d = din("cpr", [1, D])
    w_ada = din("w_ada", [D, 3 * D])
    b_ada = din("b_ada", [1, 3 * D])
    g_pre = din("g_pre", [1, D])
    g_post = din("g_post", [1, D])
    w_in = din("w_in", [D, IN_W])
    pe_cmp = din("pe_cmp", [2, 32, 64])
    w_cmp1 = din("w_cmp1", [2, 2048, 256])
    w_cmp2 = din("w_cmp2", [2, 256, 64])
    conv_w = din("conv_w", [3, 512])
    w_br_a = din("w_br_a", [512, D])
    w_br_b = din("w_br_b", [512, D])
    w_out = din("w_out", [D, D])
    ident_d = din("ident", [128, 128])
    ropeA_d = din("ropeA", [128, NT, 16])
    ropeB_d = din("ropeB", [128, NSLOT, 16])
    slotc_d = din("slotc", [NSLOT, 128, 3, 128])
    cmpb_d = din("cmpb", [NSLOT, 128, 4, 128])
    winb_d = din("winb", [128, 8, 128])
    selc_d = din("selc", [128, 3, 4, 128])
    ovl_d = din("ovl", [128, 4, 128])
    shm_d = din("shm", [128, 2, 128])
    shb_d = din("shb", [32, 2, 128])
    upsc_d = din("upsc", [32, NSLOT])

    xs_d = din("xs", [16, D])
    cs_d = din("cs", [16, D])
    ptab_d = din("ptab", [1, 256], I32)
    cache_d = din("cache", [2560 * 128, 512])
    swin_d = din("swin", [16, 512, 256])
    sconv_d = din("sconv", [16, 2, 512])
    ropeS_d = din("ropeS", [16, 16])
    slotS_d = din("slotS", [128, 3, 128])
    smask_d = din("smask", [128, 4, 128])
    onehot_d = din("onehot", [128, 16])
    pcol_d = din("pcol", [128, 1])
    ys = dout("ys", [16, D])
    kvs = dout("kvs", [16, 512])
    wins = dout("wins", [16, 512, 256])
    convs = dout("convs", [16, 2, 512])
    yp = dout("yp", [NSLOT * 128, D])
    kvp = dout("kvp", [NSLOT * 128, 512])
    winp = dout("winp", [128, 256])
    convp = dout("convp", [2, 512])
    dbg_outs = {}

    def dbg_out(name, buf, ap, shape, dt=F32):
        if not dbg:
            return
        d = dout("dbg_" + name, shape, dt)
        P.dma("sp", d, ap, reads=[buf], chan="dbg_" + name)

    wscr = P.dram("wscr", [NCH, 128, 4096], BF16)

    ident = P.sb("ident", [128, 128], F32)
    identb = P.sb("identb", [128, 128], BF16)
    I4 = P.sb("I4", [128, 512], BF16)
    ones1 = P.sb("ones1", [1, 128], F32)
    wkv = P.sb("wkv", [128, 8, 768], BF16)
    wg = P.sb("wg", [128, 8, 24], BF16)
    w1 = P.sb("w1", [128, 32, 256], BF16)
    w2 = P.sb("w2", [128, 2, 2, 64], BF16)
    wch = [P.sb("wch%d" % i, [128, 4096], BF16) for i in range(NWB)]
    KselT = P.sb("KselT", [128, NT * 128], BF16)
    Vsel = P.sb("Vsel", [128, NT, 2, 65], BF16)
    KwinT = P.sb("KwinT", [128, 8 * 128], BF16)
    Vwin = P.sb("Vwin", [128, 8, 2, 65], BF16)
    kcT = P.sb("kcT", [128, 512], BF16)
    vca = P.sb("vca", [128, 4, 2, 65], BF16)
    vcf = P.sb("vcf", [128, 4, 2, 64], F32)
    CT = [P.sb("CT%d" % g, [128, 144], BF16) for g in range(2)]
    xt = [P.sb("xt0", [128, D], F32)] * 2
    xn = P.sb("xn", [128, D], F32)
    hT = [P.sb("hT0", [128, 8, 128], BF16)] * 2
    zkv = P.sb("zkv", [128, 768], F32)
    ctsrc = P.sb("ctsrc", [128, 256], F32)
    ropeA = P.sb("ropeA", [128, NT, 16], F32)
    ropeB = P.sb("ropeB", [128, NSLOT, 16], F32)
    rt = P.sb("rt", [128, 4, 8 * 8], F32)
    ss = P.sb("ss", [128, 8], F32)
    G1col = P.sb("G1col", [128, 8], F32)
    SHcol = P.sb("SHcol", [128, 8], F32)
    GP = P.sb("GP", [128, D], F32)
    cm = P.sb("cm", [128, 4], F32)
    cmb = P.sb("cmb", [128, 4], BF16)
    cmt8 = P.sb("cmt8", [128, 16], F32)
    cmneg = P.sb("cmneg", [128, 128], F32)
    bias1 = P.sb("bias1", [128, 2, 2], F32)
    hsT = P.sb("hsT", [128, 2, 2, 8], BF16)
    hsTp = P.sb("hsTp", [128, 2, 256], BF16)
    sT = P.sb("sT", [128, 8, 17], F32)
    modT = P.sb("modT", [128, 24, 17], F32)
    coltmp = P.sb("coltmp", [128, 32], F32)
    G1S = P.sb("G1S", [128, 8, 16], F32)
    SHS = P.sb("SHS", [128, 8, 16], F32)
    GST = P.sb("GST", [128, 8, 16], F32)
    bcm = P.sb("bcm", [128, 128], F32)
    convwb = P.sb("convwb", [128, 3, 512], F32)
    winb = P.sb("winb", [128, 8, 128], F32)
    selc = P.sb("selc", [128, 3, 4, 128], F32)
    ovl = P.sb("ovl", [128, 4, 128], BF16)
    ovlf = P.sb("ovlf", [128, 4, 128], F32)
    shm = P.sb("shm", [128, 2, 128], F32)
    shb = P.sb("shb", [32, 2, 128], F32)
    upsc = P.sb("upsc", [32, NSLOT], F32)
    pet = P.sb("pet", [32, 2, 64], F32)
    peT = P.sb("peT", [128, 32], BF16)

    xm = P.sb("xm", [128, D], F32)
    ps = [P.ps("b%d" % i, [128, 512], F32) for i in range(8)]

    P.dma("sp", ident[:], ident_d, writes=[ident], chan="c0")
    P.dma("sp", ropeA[:], ropeA_d, writes=[ropeA], chan="c0")
    P.dma("sp", ropeB[:], ropeB_d, writes=[ropeB], chan="c0")
    P.dma("sp", winb[:], winb_d, writes=[winb], chan="c0")
    P.dma("sp", selc[:], selc_d, writes=[selc], chan="c0")
    P.dma("sp", ovlf[:], ovl_d, writes=[ovlf], chan="c0")
    P.dma("sp", shm[:], shm_d, writes=[shm], chan="c0")
    P.dma("sp", shb[:], shb_d, writes=[shb], chan="c0")
    P.dma("sp", upsc[:], upsc_d, writes=[upsc], chan="c0")
    P.dma("sp", pet[:], pe_cmp.rearrange("k r d -> r k d"), writes=[pet], chan="c0")
    cp("dve", identb[:], ident[:], [ident], [identb])
    cp("dve", ovl[:], ovlf[:], [ovlf], [ovl])
    for i in range(4):
        cp("dve", I4[:, i * 128:(i + 1) * 128], ident[:], [ident], [I4])
    ms("dve", ones1[:], 1.0, [ones1])
    ms("pool", Vsel[:], 1.0, [Vsel])
    ms("pool", Vwin[:], 1.0, [Vwin])
    ms("pool", vca[:], 1.0, [vca])
    ms("pool", vcf[:], 0.0, [vcf])
    ms("pool", kcT[:], 0.0, [kcT])
    ms("pool", cm[:], 0.0, [cm])
    for g in range(2):
        ms("pool", CT[g][:], 0.0, [CT[g]])
    ms("pool", KwinT[:], 0.0, [KwinT])
    ms("pool", KselT[:], 0.0, [KselT])

    P.dma("pool", wkv[:], w_in[:, KV0:KV0 + 768].rearrange("(c p) n -> p c n", p=128), writes=[wkv], chan="wk")
    P.dma("pool", wg[:], w_in[:, G0:G0 + 24].rearrange("(c p) n -> p c n", p=128), writes=[wg], chan="wk")
    for kv in range(2):
        P.dma("pool", w1[kv * 64:(kv + 1) * 64, :, :], w_cmp1[kv].rearrange("(r d) n -> d r n", d=64),
              writes=[(w1, kv)], chan="wk")
        P.dma("pool", w2[:, kv, :, :], w_cmp2[kv].rearrange("(h p) n -> p h n", p=128), writes=[(w2, kv)], chan="wk")
    for j in range(NCH):
        wb = wch[j % NWB]
        if j < 10:
            src = w_in[:, CH_COLS[j]:CH_COLS[j] + 512].rearrange("(c p) n -> p c n", p=128)
            dst = wb[:].rearrange("p (c n) -> p c n", c=8)
        elif j == 10:
            src = w_br_a.rearrange("(c p) n -> p c n", p=128)
            dst = wb[:].rearrange("p (c n) -> p c n", c=4)
        elif j == 11:
            src = w_br_b.rearrange("(c p) n -> p c n", p=128)
            dst = wb[:].rearrange("p (c n) -> p c n", c=4)
        else:
            o = (j - 12) * 512
            src = w_out[:, o:o + 512].rearrange("(c p) n -> p c n", p=128)
            dst = wb[:].rearrange("p (c n) -> p c n", c=8)
        P.dma("pool", dst, src, writes=[wb], chan="wst%d" % (j % NWB))
        P.dma("sp", wscr[j], wb[:], reads=[wb], writes=[(wscr, j)], chan="wso%d" % (j % NWB))

    P.dma("sp", xn[0:1, :], cp_d, writes=[xn], chan="xnl")
    P.dma("sp", xn[1:17, :], cs_d, writes=[xn], chan="xnl")
    act(xn[0:17, :], xn[0:17, :], AF.Silu, [xn], [xn])
    for k in range(8):
        tr(ps[0][:, k * 17:(k + 1) * 17], xn[0:17, k * 128:(k + 1) * 128], ident[0:17, 0:17], [xn, ident], [ps[0]])
    cp("dve", sT[:].rearrange("p k s -> p (k s)"), ps[0][:, 0:136], [ps[0]], [sT])

    def row_to_cols(row_ap, dst_buf, dst_ap_fn, n):
        for j in range(n):
            tr(ps[0][:, j:j + 1], row_ap[:, j * 128:(j + 1) * 128], ident[0:1, 0:1], [xm, ident], [ps[0]])
        cp("dve", dst_ap_fn, ps[0][:, 0:n], [ps[0]], [dst_buf])

    for j in range(24):
        wa = xt[j % 2]
        P.dma("sp", wa[:].rearrange("p (c n) -> p c n", c=8),
              w_ada[:, j * 128:(j + 1) * 128].rearrange("(c p) n -> p c n", p=128), writes=[wa], chan="xt0")
        for k in range(8):
            mm(ps[1][:, j * 17:(j + 1) * 17], wa[:, k * 128:(k + 1) * 128], sT[:, k, :], k == 0, k == 7, [wa, sT], [ps[1]])
    cp("dve", modT[:].rearrange("p j s -> p (j s)"), ps[1][:, 0:408], [ps[1]], [modT])
    for i3 in range(3):
        P.dma("sp", xm[0:1, :], b_ada[:, i3 * D:(i3 + 1) * D], writes=[xm], chan="xm")
        row_to_cols(xm[0:1, :], coltmp, coltmp[:, i3 * 8:(i3 + 1) * 8], 8)
    tt("dve", modT[:], modT[:], coltmp[:, 0:24].unsqueeze(2).to_broadcast([128, 24, 17]), ALU.add, [modT, coltmp], [modT])
    P.dma("sp", xm[0:1, :], g_pre, writes=[xm], chan="xm")
    row_to_cols(xm[0:1, :], coltmp, coltmp[:, 24:32], 8)
    stt(G1col[:], modT[:, 8:16, 0], 1.0, coltmp[:, 24:32], ALU.add, ALU.mult, [modT, coltmp], [G1col])
    cp("dve", SHcol[:], modT[:, 0:8, 0], [modT], [SHcol])
    ts("dve", G1S[:], modT[:, 8:16, 1:17], 1.0, None, ALU.add, None, [modT], [G1S])
    tt("dve", G1S[:], G1S[:], coltmp[:, 24:32].unsqueeze(2).to_broadcast([128, 8, 16]), ALU.mult, [G1S, coltmp], [G1S])
    cp("dve", SHS[:], modT[:, 0:8, 1:17], [modT], [SHS])
    P.dma("sp", xm[0:1, :], g_post, writes=[xm], chan="xm")
    row_to_cols(xm[0:1, :], coltmp, coltmp[:, 0:8], 8)
    tt("dve", coltmp[:, 8:16], coltmp[:, 0:8], modT[:, 16:24, 0], ALU.mult, [coltmp, modT], [coltmp])
    tt("dve", GST[:], modT[:, 16:24, 1:17], coltmp[:, 0:8].unsqueeze(2).to_broadcast([128, 8, 16]), ALU.mult,
       [modT, coltmp], [GST])
    for c in range(8):
        cp("dve", bcm[:], coltmp[:, 8 + c:9 + c].to_broadcast([128, 128]), [coltmp], [bcm])
        tr(ps[2 + c // 4][:, (c % 4) * 128:(c % 4 + 1) * 128], bcm[:], ident[:], [bcm, ident], [ps[2 + c // 4]])
    cp("dve", GP[:, 0:512], ps[2][:], [ps[2]], [GP])
    cp("dve", GP[:, 512:1024], ps[3][:], [ps[3]], [GP])
    for k in range(3):
        P.dma("sp", xm[0:1, 0:512], conv_w[k:k + 1, :], writes=[xm], chan="xm")
        mm(ps[4][:], ones1[0:1, :], xm[0:1, 0:512], True, True, [ones1, xm], [ps[4]])
        cp("dve", convwb[:, k, :], ps[4][:], [ps[4]], [convwb])
    tr(ps[5][:, 0:32], pet[:].rearrange("r k d -> r (k d)"), ident[0:32, 0:32], [pet, ident], [ps[5]])
    cp("dve", peT[:], ps[5][:, 0:32], [ps[5]], [peT])
    for kv in range(2):
        for half in range(2):
            for r in range(32):
                mm(ps[6 + kv][:, half:half + 1], w1[kv * 64:(kv + 1) * 64, r, half * 128:(half + 1) * 128],
                   peT[kv * 64:(kv + 1) * 64, r:r + 1], r == 0, r == 31, [w1, peT], [ps[6 + kv]])
    for kv in range(2):
        cp("dve", bias1[:, kv, :], ps[6 + kv][:, 0:2], [ps[6 + kv]], [bias1])

    def load_x(buf, src_ap, chan, rows=128):
        P.dma("sp", buf[0:rows, :], src_ap, writes=[buf], chan=chan)

    def norm_T(xb, hTb, rows=128):
        act(xn[0:rows, :], xb[0:rows, :], AF.Square, [xb], [xn, ss], accum_out=ss[0:rows, 0:1])
        ts("dve", ss[0:rows, 1:2], ss[0:rows, 0:1], 1.0 / D, 1e-6, ALU.mult, ALU.add, [ss], [ss])
        act(ss[0:rows, 2:3], ss[0:rows, 1:2], AF.Sqrt, [ss], [ss])
        P.op("dve", lambda e: e.reciprocal(out=ss[0:rows, 3:4], in_=ss[0:rows, 2:3]), [ss], [ss])
        ts("dve", xn[0:rows, :], xb[0:rows, :], ss[0:rows, 3:4], None, ALU.mult, None, [xb, ss], [xn])
        for c in range(8):
            tr(ps[c // 4][:, (c % 4) * 128:(c % 4) * 128 + rows], xn[0:rows, c * 128:(c + 1) * 128],
               ident[0:rows, 0:rows], [xn, ident], [ps[c // 4]])
        for c in range(8):
            act(hTb[:, c, 0:rows], ps[c // 4][:, (c % 4) * 128:(c % 4) * 128 + rows], AF.Identity,
                [ps[c // 4], G1col, SHcol], [hTb], scale=G1col[:, c:c + 1], bias=SHcol[:, c:c + 1])

    def rope_inplace(zb, base_views, tab_ap, rows=128):
        for v in base_views:
            n = v.shape[1]
            x1 = v[:, :, 0:8]
            x2 = v[:, :, 8:16]
            cs = tab_ap[:, 0:8].unsqueeze(1).to_broadcast([rows, n, 8])
            sn = tab_ap[:, 8:16].unsqueeze(1).to_broadcast([rows, n, 8])
            t = [rt[0:rows, i, 0:n * 8].rearrange("p (n e) -> p n e", e=8) for i in range(4)]
            tt("dve", t[0], x1, cs, ALU.mult, [zb], [rt])
            tt("dve", t[1], x2, sn, ALU.mult, [zb], [rt])
            tt("dve", t[2], x1, sn, ALU.mult, [zb], [rt])
            tt("dve", t[3], x2, cs, ALU.mult, [zb], [rt])
            tt("dve", x1, t[0], t[1], ALU.subtract, [rt], [zb])
            tt("dve", x2, t[2], t[3], ALU.add, [rt], [zb])

    def kv_proj(hTb, tab_ap, rows=128):
        for half in range(2):
            for k in range(8):
                mm(ps[2 + half][0:rows, 0:384], hTb[:, k, 0:rows], wkv[:, k, half * 384:(half + 1) * 384], k == 0, k == 7,
                   [hTb, wkv], [ps[2 + half]])
        cp("act", zkv[0:rows, 0:384], ps[2][0:rows, 0:384], [ps[2]], [zkv])
        cp("dve", zkv[0:rows, 384:768], ps[3][0:rows, 0:384], [ps[3]], [zkv])
        views = [zkv[0:rows, o:o + 128].rearrange("p (g d) -> p g d", g=2)[:, :, 0:16] for o in (0, 256, 512)]
        rope_inplace(zkv, views, tab_ap, rows=rows)

    def colmax_update(buf, ap, prt, bi, n):
        P.op("dve", lambda e: e.max(out=cmt8[prt, 0:8], in_=ap), [buf], [cmt8])
        tt("dve", cm[prt, bi:bi + 1], cm[prt, bi:bi + 1], cmt8[prt, 0:1], ALU.max, [cm, cmt8], [cm])
        ts("dve", cmneg[prt, 0:n], ap, -1.0, None, ALU.mult, None, [buf], [cmneg])
        P.op("dve", lambda e: e.max(out=cmt8[prt, 8:16], in_=cmneg[prt, 0:n]), [cmneg], [cmt8])
        tt("dve", cm[prt, bi:bi + 1], cm[prt, bi:bi + 1], cmt8[prt, 8:9], ALU.max, [cm, cmt8], [cm])

    KST = ""

    def phase_a(t):
        xb = xt[t % 2]
        hTb = hT[t % 2]
        norm_T(xb, hTb)
        if t + 1 < nt:
            load_x(xt[(t + 1) % 2], xp[(t + 1) * 128:(t + 2) * 128, :], "xt0")
        kv_proj(hTb, ropeA[:, t, :])
        ingest(t, zkv, True)

    def ingest(t, zb, has_win):
        tr(ps[4][:, 0:128], zb[:, 256:384], ident[:], [zb, ident], [ps[4]])
        if has_win:
            tr(ps[4][:, 128:256], zb[:, 512:640], ident[:], [zb, ident], [ps[4]])
        cp("dve", ctsrc[:].rearrange("p (g k d) -> p g k d", g=2, k=2),
           zb[:, 0:256].rearrange("p (k g d) -> p g k d", k=2, g=2), [zb], [ctsrc])
        for g in range(2):
            tr(ps[4][:, 256 + g * 128:384 + g * 128], ctsrc[:, g * 128:(g + 1) * 128], ident[:], [ctsrc, ident], [ps[4]])
        cp("act", KselT[:, t * 128:(t + 1) * 128], ps[4][:, 0:128], [ps[4]], [(KselT, t)])
        if has_win:
            cp("act", KwinT[:, (t % 8) * 128:(t % 8 + 1) * 128], ps[4][:, 128:256], [ps[4]], [(KwinT, t % 8)])
        for g in range(2):
            cp("act", CT[g][:, 16:144], ps[4][:, 256 + g * 128:384 + g * 128], [ps[4]], [CT[g]])
        for bi, o in (((1, 0), (2, 128)) if has_win else ((1, 0),)):
            colmax_update(ps[4], ps[4][:, o:o + 128], slice(0, 128), bi, 128)
        cp("dve", Vsel[:, t, :, 0:64], zb[:, 384:512].rearrange("p (g d) -> p g d", g=2), [zb], [(Vsel, t)])
        if has_win:
            cp("dve", Vwin[:, t % 8, :, 0:64], zb[:, 640:768].rearrange("p (g d) -> p g d", g=2), [zb], [(Vwin, t % 8)])
        m0 = 1 if t == 0 else 0
        nb = 8 - m0
        i0 = 8 * t - 1 + m0
        kt0 = i0 // 128
        c0 = i0 - kt0 * 128
        for g in range(2):
            for kv in range(2):
                for half in range(2):
                    col = ((g * 2 + kv) * 2 + half) * 8
                    for r in range(32):
                        rhs = CT[g][kv * 64:(kv + 1) * 64, r:r + 16 * 7 + 1:16]
                        mm(ps[5 + 2 * kv][:, col:col + 8], w1[kv * 64:(kv + 1) * 64, r, half * 128:(half + 1) * 128], rhs,
                           r == 0, r == 31, [w1, CT[g]], [ps[5 + 2 * kv]])
            ms("pool", hsTp[:], 0.0, [hsTp])
            for kv in range(2):
                for half in range(2):
                    col = ((g * 2 + kv) * 2 + half) * 8
                    act(hsT[:, kv, half, 0:8], ps[5 + 2 * kv][:, col:col + 8], AF.Silu, [ps[5 + 2 * kv], bias1], [hsT],
                        bias=bias1[:, kv, half:half + 1])
            if KST == "a4":
                continue
            cp("dve", hsTp[:, :, c0:c0 + nb], hsT[:, 1, :, m0:8], [hsT], [hsTp])
            for half in range(2):
                mm(ps[6][g * 64:(g + 1) * 64, 0:8], w2[:, 0, half, :], hsT[:, 0, half, 0:8], half == 0, half == 1,
                   [w2, hsT], [ps[6]])
            cp("act", kcT[g * 64:(g + 1) * 64, i0:i0 + nb], ps[6][g * 64:(g + 1) * 64, m0:8], [ps[6]], [kcT])
            colmax_update(ps[6], ps[6][g * 64:(g + 1) * 64, 0:8], slice(g * 64, (g + 1) * 64), 0, 8)
            if KST == "a5":
                continue
            for w in range(2):
                if c0 + nb <= w * 128 or c0 >= (w + 1) * 128 or kt0 + w > 3:
                    continue
                for half in range(2):
                    mm(ps[7][:, 0:64], hsTp[:, half, w * 128:(w + 1) * 128], w2[:, 1, half, :], half == 0, half == 1,
                       [hsTp, w2], [ps[7]])
                tt("dve", vcf[:, kt0 + w, g, :], vcf[:, kt0 + w, g, :], ps[7][:, 0:64], ALU.add, [vcf, ps[7]], [vcf])
                cp("dve", vca[:, kt0 + w, g, 0:64], vcf[:, kt0 + w, g, :], [vcf], [vca])
        for g in range(2):
            cp("dve", CT[g][:, 0:16], CT[g][:, 128:144], [CT[g]], [CT[g]])


    xpv = P.sb("xpv", [32, D], F32)
    hTm = P.sb("hTm", [128, 8, 128], BF16)
    hTp = P.sb("hTp", [128, 8, 32], BF16)
    qf = P.sb("qf", [128, 512], F32)
    QT = P.sb("QT", [128, 512], BF16)
    aQT = P.sb("aQT", [128, 512], BF16)
    gates = P.sb("gates", [128, 24], F32)
    sa = P.sb("sa", [128, 512], F32)
    cbf = P.sb("cbf", [128, 512], F32)
    ccf = P.sb("ccf", [128, 512], F32)
    uu = P.sb("uu", [128, 512], F32)
    ccp = P.sb("ccp", [32, 512], F32)
    up = P.sb("up", [32, 512], F32)
    co = P.sb("co", [128, 512], F32)
    sgc = P.sb("sgc", [128, 512], F32)
    obT = P.sb("obT", [128, 4, 128], BF16)
    oat = P.sb("oat", [128, 512], F32)
    oaT = P.sb("oaT", [128, 4, 128], BF16)
    mbuf = P.sb("mbuf", [128, D], F32)
    mT = P.sb("mT", [128, 8, 128], BF16)
    yout = mbuf
    PTb = [P.sb("PT%d" % i, [128, 512], BF16) for i in range(3)]
    Ls = P.sb("Ls", [128, 16, 128], BF16)
    Lx = P.sb("Lx", [128, 128], F32)
    Lt = P.sb("Lt", [128, 128], F32)
    Lexp = [P.sb("Lexp%d" % i, [128, 1024], BF16) for i in range(2)]
    slotc = P.sb("slotc", [128, 3, 128], F32)
    cmpb = P.sb("cmpb", [128, 4, 128], F32)
    negm = P.sb("negm", [128, 4], F32)
    sc = [P.sb("sc%d" % i, [128, 128], F32) for i in range(3)]
    Lselb = P.sb("Lselb", [128, 128], BF16)
    m8 = P.sb("m8", [128, 16], F32)
    rc = P.sb("rc", [128, 16], F32)

    chunk_seq = [(n, j) for n in range(nslot + (1 if do_sample else 0)) for j in (0, 1, 2, 3, 4, 5, 10, 6, 7, 11, 8, 9, 12, 13)]
    st = {"issued": 0, "used": 0, "u": 0}

    def issue_chunks(upto):
        while st["issued"] < min(upto, len(chunk_seq)):
            k = st["issued"]
            P.dma("sp", wch[k % NWB][:], wscr[chunk_seq[k][1]], reads=[(wscr, chunk_seq[k][1])], writes=[wch[k % NWB]],
                  chan="wst%d" % (k % NWB))
            st["issued"] += 1

    def next_chunk(expect_j):
        k = st["used"]
        assert chunk_seq[k][1] == expect_j, (chunk_seq[k], expect_j)
        issue_chunks(k + 1)
        st["used"] += 1
        return wch[k % NWB], k

    def zchunk(j, bank, lh=None, rows=128):
        wb, k = next_chunk(j)
        wv = wb[:].rearrange("p (c n) -> p c n", c=8)
        for kc in range(8):
            mm(bank[0:128, :], hTm[:, kc, :], wv[:, kc, :], kc == 0, kc == 7, [hTm, wb], [bank])
        return wb, wv, k

    def attention(n, g, samp=None):
        gr = slice(g * 64, (g + 1) * 64)
        nkt = 4 * n + 4 if samp is None else 17
        nfull = 4 * n if samp is None else 16
        pm = ps[2] if g == 0 else ps[1]
        sbanks = (ps[3], ps[4]) if g == 0 else (ps[0], ps[1])
        for br in range(4):
            for hh in range(4):
                mm(pm[:, br * 4 + hh:br * 4 + hh + 1], aQT[gr, hh * 128:(hh + 1) * 128], cmb[gr, min(br, 2):min(br, 2) + 1], True, True,
                   [aQT, cmb], [pm])
        for br in range(3):
            P.op("dve", lambda e, br=br: e.max(out=m8[:, 0:8], in_=pm[:, br * 4:br * 4 + 8]), [pm], [m8])
            ts("dve", negm[:, br:br + 1], m8[:, 0:1], -1.0, None, ALU.mult, None, [m8], [negm])
        ts("dve", negm[:, 3:4], negm[:, 1:2], -BIG8, None, ALU.add, None, [negm], [negm])

        pend = []

        def flush():
            while pend:
                pend.pop(0)()

        def unit(Kt, Lap, Lbuf, Vap, Vbuf, Kbuf, obank, first, last, ovl_kt=None):
            u = st["u"]
            st["u"] += 1
            S = sbanks[u % 2]
            PT = PTb[u % 3]
            mm(S[:], Kt, QT[gr, :], True, False, [Kbuf, QT], [S])
            mm(S[:], Lap, I4[:], False, True, [Lbuf, I4], [S])
            act(PT[:], S[:], AF.Exp, [S], [PT], scale=0.125)

            def stage2():
                for hh in range(4):
                    mm(obank[:, hh * 65:(hh + 1) * 65], PT[:, hh * 128:(hh + 1) * 128], Vap, first and hh == 0, last,
                       [PT, Vbuf], [obank], skip_group_check=True)
                if ovl_kt is not None:
                    for hh in range(4):
                        mm(ps[6][:, hh * 128:(hh + 1) * 128], PT[:, hh * 128:(hh + 1) * 128], ovl[:, ovl_kt, :],
                           first and hh == 0, last, [PT, ovl], [ps[6]], skip_group_check=True)
            flush()
            pend.append(stage2)

        def finish_branch(obank, br, first_branch):
            ov = obank[:, 0:260].rearrange("p (h e) -> p h e", e=65)
            ts("dve", rc[:, 0:4], ov[:, :, 64], 1e-30, None, ALU.max, None, [obank], [rc])
            P.op("dve", lambda e: e.reciprocal(out=rc[:, 4:8], in_=rc[:, 0:4]), [rc], [rc])
            gv = gates[:, g * 12:(g + 1) * 12].rearrange("p (h b) -> p h b", b=3)[:, :, br]
            tt("dve", rc[:, 8:12], rc[:, 4:8], gv, ALU.mult, [rc, gates], [rc])
            if samp is not None:
                ts("dve", rc[:, 8:12], rc[:, 8:12], onehot[:, samp:samp + 1], None, ALU.mult, None, [rc, onehot], [rc])
                first_branch = False
            for hh in range(4):
                dst = oat[:, g * 256 + hh * 64:g * 256 + (hh + 1) * 64]
                if first_branch:
                    ts("dve", dst, ov[:, hh, 0:64], rc[:, 8 + hh:9 + hh], None, ALU.mult, None, [obank, rc], [oat])
                else:
                    stt(dst, ov[:, hh, 0:64], rc[:, 8 + hh:9 + hh], dst, ALU.mult, ALU.add, [obank, rc, oat], [oat])

        ktmax = min(3, (32 * n + 30) // 128) if samp is None else 0
        for kt in range(ktmax + 1):
            if samp is None:
                ts("dve", Ls[:, kt, :], cmpb[:, kt, :], negm[:, 0:1], None, ALU.add, None, [cmpb, negm], [(Ls, kt)])
            else:
                ts("dve", Ls[:, kt, :], smask[:, 0, :], negm[:, 0:1], None, ALU.add, None, [smask, negm], [(Ls, kt)])
        for kt in range(ktmax + 1):
            unit(kcT[gr, kt * 128:(kt + 1) * 128], Ls[:, kt, :], (Ls, kt), vca[:, kt, g, :], vca, kcT, ps[5],
                 kt == 0, kt == ktmax, ovl_kt=kt)
        flush()
        ov = ps[5][:, 0:260].rearrange("p (h e) -> p h e", e=65)
        ts("dve", rc[:, 12:16], ov[:, :, 64], 1e-30, None, ALU.max, None, [ps[5]], [rc])
        P.op("dve", lambda e: e.reciprocal(out=rc[:, 12:16], in_=rc[:, 12:16]), [rc], [rc])
        ts("dve", sc[0][:], ps[6][:, 0:128], rc[:, 12:13], None, ALU.mult, None, [ps[6], rc], [sc[0]])
        for hh in range(1, 4):
            stt(sc[0][:], ps[6][:, hh * 128:(hh + 1) * 128], rc[:, 12 + hh:13 + hh], sc[0][:], ALU.mult, ALU.add,
                [ps[6], rc, sc[0]], [sc[0]])
        finish_branch(ps[5], 0, True)
        tt("dve", sc[0][:], sc[0][:], slotc[:, 0, :], ALU.mult, [sc[0], slotc], [sc[0]])
        tt("dve", sc[0][:], sc[0][:], slotc[:, 1, :], ALU.add, [sc[0], slotc], [sc[0]])
        tt("dve", sc[0][:], sc[0][:], slotc[:, 2, :], ALU.max, [sc[0], slotc], [sc[0]])
        ms("dve", sc[0][:, 0:1], 1e6, [sc[0]])
        P.op("dve", lambda e: e.max(out=m8[:, 0:8], in_=sc[0][:]), [sc[0]], [m8])
        P.op("dve", lambda e: e.match_replace(out=sc[1][:], in_to_replace=m8[:, 0:8], in_values=sc[0][:], imm_value=-1e30),
             [sc[0], m8], [sc[1]])
        P.op("dve", lambda e: e.max(out=m8[:, 8:16], in_=sc[1][:]), [sc[1]], [m8])
        ts("dve", sc[2][:], sc[0][:], m8[:, 15:16], BIG8, ALU.is_ge, ALU.mult, [sc[0], m8], [sc[2]])
        ts("dve", Lselb[:], sc[2][:], negm[:, 3:4], None, ALU.add, None, [sc[2], negm], [Lselb])
        wk = [k for k in range(8) if 4 * n - 4 + k >= 0] if samp is None else [0, 1, 2, 3, 4]
        for k in wk:
            if samp is None:
                ts("dve", Ls[:, 4 + k, :], winb[:, k, :], negm[:, 2:3], None, ALU.add, None, [winb, negm], [(Ls, 4 + k)])
            else:
                mi = (2, 3, 3, 3, 1)[k]
                ts("dve", Ls[:, 4 + k, :], smask[:, mi, :], negm[:, 2:3], None, ALU.add, None, [smask, negm], [(Ls, 4 + k)])
        for k in wk:
            kt = 4 * n - 4 + k if samp is None else k
            unit(KwinT[gr, (kt % 8) * 128:(kt % 8 + 1) * 128], Ls[:, 4 + k, :], (Ls, 4 + k), Vwin[:, kt % 8, g, :],
                 (Vwin, kt % 8), (KwinT, kt % 8), ps[7], k == wk[0], k == wk[-1])
        flush()
        finish_branch(ps[7], 2, False)
        for kt in range(nkt):
            first, last = kt == 0, kt == nkt - 1
            if kt < nfull:
                c, o = divmod(kt, 8)
                if o == 0:
                    nb16 = min(16, 2 * nfull - 16 * c)
                    cp("pool", Lexp[c % 2][:, 0:nb16 * 64].rearrange("p (j e) -> p j e", e=64),
                       Lselb[:, 16 * c:16 * c + nb16].unsqueeze(2).to_broadcast([128, nb16, 64]), [Lselb], [Lexp[c % 2]])
                Lap, Lbuf = Lexp[c % 2][:, o * 128:(o + 1) * 128], Lexp[c % 2]
            elif samp is not None:
                ts("dve", Ls[:, 12, :], smask[:, 1, :], negm[:, 1:2], None, ALU.add, None, [smask, negm], [(Ls, 12)])
                Lap, Lbuf = Ls[:, 12, :], (Ls, 12)
            else:
                kr = kt - 4 * n
                cp("dve", Lx[:].rearrange("p (j e) -> p j e", e=64),
                   Lselb[:, 2 * kt:2 * kt + 2].unsqueeze(2).to_broadcast([128, 2, 64]), [Lselb], [Lx])
                stt(Lt[:], selc[:, 1, kr, :], negm[:, 1:2], selc[:, 2, kr, :], ALU.mult, ALU.add, [selc, negm], [Lt])
                tt("dve", Lx[:], Lx[:], selc[:, 0, kr, :], ALU.mult, [Lx, selc], [Lx])
                tt("dve", Ls[:, 12 + kr, :], Lx[:], Lt[:], ALU.add, [Lx, Lt], [(Ls, 12 + kr)])
                Lap, Lbuf = Ls[:, 12 + kr, :], (Ls, 12 + kr)
            unit(KselT[gr, kt * 128:(kt + 1) * 128], Lap, Lbuf, Vsel[:, kt, g, :], (Vsel, kt), (KselT, kt), ps[5],
                 first, last)
        flush()
        finish_branch(ps[5], 1, False)

    def phase_b(n):
        load_x(xm, xmine[n * 128:(n + 1) * 128, :], "xm")
        load_x(xpv, xprev[n * 32:(n + 1) * 32, :], "xpv", rows=32)
        P.dma("sp", slotc[:], slotc_d[n], writes=[slotc], chan="slotc")
        P.dma("sp", cmpb[:], cmpb_d[n], writes=[cmpb], chan="cmpb")
        norm_T(xm, hTm)
        norm_T(xpv, hTp, rows=32)
        kv_proj(hTm, ropeB[:, n, :])
        P.dma("sp", kvp[n * 128:(n + 1) * 128, :], zkv[:, 0:512], reads=[zkv], chan="zkvo")
        if n == nslot - 1:
            P.dma("sp", winp, zkv[:, 512:768], reads=[zkv], chan="zkvo")
        cp("dve", cmb[:], cm[:], [cm], [cmb])
        zchunk(0, ps[0])
        cp("act", qf[:], ps[0][:], [ps[0]], [qf])
        rope_inplace(qf, [qf[:].rearrange("p (h d) -> p h d", d=64)[:, :, 0:16]], ropeB[:, n, :])
        cp("pool", sgc[:].rearrange("p (h g d) -> p h g d", h=4, g=2),
           qf[:].rearrange("p (g h d) -> p h g d", g=2, h=4), [qf], [sgc])
        for jj in range(4):
            tr(ps[2][:, jj * 128:(jj + 1) * 128], sgc[:, jj * 128:(jj + 1) * 128], ident[:], [sgc, ident], [ps[2]])
        cp("act", QT[:], ps[2][:], [ps[2]], [QT])
        act(aQT[:], ps[2][:], AF.Abs, [ps[2]], [aQT])
        for kc in range(8):
            mm(ps[1][:, 0:24], hTm[:, kc, :], wg[:, kc, :], kc == 0, kc == 7, [hTm, wg], [ps[1]])
        act(gates[:], ps[1][:, 0:24], AF.Sigmoid, [ps[1]], [gates])
        zchunk(1, ps[0])
        act(sa[:], ps[0][:], AF.Silu, [ps[0]], [sa])
        zchunk(2, ps[1])
        cp("act", cbf[:], ps[1][:], [ps[1]], [cbf])
        wb, wv, _ = zchunk(3, ps[0])
        for kc in range(8):
            mm(ps[2][0:32, :], hTp[:, kc, :], wv[:, kc, :], kc == 0, kc == 7, [hTp, wb], [ps[2]])
        cp("act", ccf[:], ps[0][:], [ps[0]], [ccf])
        cp("act", ccp[:], ps[2][0:32, :], [ps[2]], [ccp])
        wb, wv, _ = zchunk(4, ps[1])
        for kc in range(8):
            mm(ps[2][0:32, :], hTp[:, kc, :], wv[:, kc, :], kc == 0, kc == 7, [hTp, wb], [ps[2]])
        tt("dve", uu[:], ccf[:], ps[1][:], ALU.mult, [ccf, ps[1]], [uu])
        stt(up[:], ps[2][0:32, :], upsc[:, n:n + 1], ccp[:], ALU.mult, ALU.mult, [ps[2], upsc, ccp], [up])
        if n == nslot - 1:
            P.dma("sp", convp, uu[126:128, :], reads=[uu], chan="uuo")
        for s_ in range(2):
            mm(ps[2 + s_][:], shm[:, s_, :], uu[:], True, False, [shm, uu], [ps[2 + s_]])
            mm(ps[2 + s_][:], shb[:, s_, :], up[:], False, True, [shb, up], [ps[2 + s_]])
        tt("dve", co[:], uu[:], convwb[:, 2, :], ALU.mult, [uu, convwb], [co])
        tt("dve", qf[:], ps[2][:], convwb[:, 1, :], ALU.mult, [ps[2], convwb], [qf])
        tt("dve", co[:], co[:], qf[:], ALU.add, [co, qf], [co])
        tt("dve", qf[:], ps[3][:], convwb[:, 0, :], ALU.mult, [ps[3], convwb], [qf])
        tt("dve", co[:], co[:], qf[:], ALU.add, [co, qf], [co])
        zchunk(5, ps[0])
        act(sgc[:], ps[0][:], AF.Silu, [ps[0]], [sgc])
        tt("dve", co[:], co[:], cbf[:], ALU.mult, [co, cbf], [co])
        tt("dve", co[:], co[:], sgc[:], ALU.mult, [co, sgc], [co])
        for c in range(4):
            tr(ps[1][:, c * 128:(c + 1) * 128], co[:, c * 128:(c + 1) * 128], ident[:], [co, ident], [ps[1]])
        cp("act", obT[:].rearrange("p c n -> p (c n)"), ps[1][:], [ps[1]], [obT])
        for g in range(2):
            attention(n, g)
        tt("dve", oat[:], oat[:], sa[:], ALU.mult, [oat, sa], [oat])
        for c in range(4):
            tr(ps[2][:, c * 128:(c + 1) * 128], oat[:, c * 128:(c + 1) * 128], ident[:], [oat, ident], [ps[2]])
        cp("act", oaT[:].rearrange("p c n -> p (c n)"), ps[2][:], [ps[2]], [oaT])
        for (jw, srcT, jg) in ((10, oaT, (6, 7)), (11, obT, (8, 9))):
            wb, k = next_chunk(jw)
            wv = wb[:].rearrange("p (c n) -> p c n", c=4)
            for half in range(2):
                for kc in range(4):
                    mm(ps[0 + half][:], srcT[:, kc, :], wv[:, kc, half * 512:(half + 1) * 512], kc == 0, kc == 3,
                       [srcT, wb], [ps[half]])
            for half in range(2):
                zchunk(jg[half], ps[2 + half])
                act(sgc[:], ps[2 + half][:], AF.Sigmoid, [ps[2 + half]], [sgc])
                dst = mbuf[:, half * 512:(half + 1) * 512]
                if jw == 10:
                    tt("dve", dst, sgc[:], ps[half][:], ALU.mult, [sgc, ps[half]], [mbuf])
                else:
                    tt("dve", qf[:], sgc[:], ps[half][:], ALU.mult, [sgc, ps[half]], [qf])
                    tt("dve", dst, dst, qf[:], ALU.add, [mbuf, qf], [mbuf])
        for c in range(8):
            tr(ps[4 + c // 4][:, (c % 4) * 128:(c % 4 + 1) * 128], mbuf[:, c * 128:(c + 1) * 128], ident[:],
               [mbuf, ident], [ps[4 + c // 4]])
        cp("act", mT[:, 0:4, :].rearrange("p c n -> p (c n)"), ps[4][:], [ps[4]], [mT])
        cp("dve", mT[:, 4:8, :].rearrange("p c n -> p (c n)"), ps[5][:], [ps[5]], [mT])
        for half in range(2):
            wb, k = next_chunk(12 + half)
            wv = wb[:].rearrange("p (c n) -> p c n", c=8)
            for kc in range(8):
                mm(ps[6 + half][:], mT[:, kc, :], wv[:, kc, :], kc == 0, kc == 7, [mT, wb], [ps[6 + half]])
        issue_chunks(st["used"] + NWB)
        for half in range(2):
            act(qf[:], ps[6 + half][:], AF.Square, [ps[6 + half]], [qf, ss],
                accum_out=ss[:, 4 + half:5 + half])
        tt("dve", ss[:, 6:7], ss[:, 4:5], ss[:, 5:6], ALU.add, [ss], [ss])
        ts("dve", ss[:, 6:7], ss[:, 6:7], 1.0 / D, 1e-6, ALU.mult, ALU.add, [ss], [ss])
        act(ss[:, 7:8], ss[:, 6:7], AF.Sqrt, [ss], [ss])
        P.op("dve", lambda e: e.reciprocal(out=ss[:, 6:7], in_=ss[:, 7:8]), [ss], [ss])
        for half in range(2):
            sl = slice(half * 512, (half + 1) * 512)
            stt(yout[:, sl], ps[6 + half][:], ss[:, 6:7], GP[:, sl], ALU.mult, ALU.mult, [ps[6 + half], ss, GP], [yout])
        tt("dve", yout[:], yout[:], xm[:], ALU.add, [yout, xm], [yout])
        P.dma("sp", yp[n * 128:(n + 1) * 128, :], yout[:], reads=[yout], chan="yout")


    pgb = [P.sb("pgb%d" % i, [128, 512], F32) for i in range(2)]
    idxf = P.sb("idxf", [128, 256], F32)
    idxi = P.sb("idxi", [128, 256], I32)
    pcol = P.sb("pcol", [128, 1], F32)
    smask = P.sb("smask", [128, 4, 128], F32)
    onehot = P.sb("onehot", [128, 16], F32)
    ropeS = P.sb("ropeS", [16, 16], F32)
    qsT = P.sb("qsT", [128, 4, 16], BF16)
    aqsT = P.sb("aqsT", [128, 4, 16], BF16)
    swt = P.sb("swt", [128, 256], F32)

    def phase_s():
        R = 16
        P.dma("sp", slotc[:], slotS_d, writes=[slotc], chan="slotc")
        P.dma("sp", smask[:], smask_d, writes=[smask], chan="c1")
        P.dma("sp", onehot[:], onehot_d, writes=[onehot], chan="c1")
        P.dma("sp", ropeS[:], ropeS_d, writes=[ropeS], chan="c1")
        P.dma("sp", pcol[:], pcol_d, writes=[pcol], chan="c1")
        P.dma("sp", idxi[:], ptab_d.to_broadcast([128, 256]), writes=[idxi], chan="c1")
        cp("dve", idxf[:], idxi[:], [idxi], [idxf])
        ts("dve", idxf[:], idxf[:], 128.0, pcol[:, 0:1], ALU.mult, ALU.add, [idxf, pcol], [idxf])
        cp("dve", idxi[:], idxf[:], [idxf], [idxi])
        load_x(xm, xs_d, "xm", rows=R)
        act(xn[0:R, :], xm[0:R, :], AF.Square, [xm], [xn, ss], accum_out=ss[0:R, 0:1])
        ts("dve", ss[0:R, 1:2], ss[0:R, 0:1], 1.0 / D, 1e-6, ALU.mult, ALU.add, [ss], [ss])
        act(ss[0:R, 2:3], ss[0:R, 1:2], AF.Sqrt, [ss], [ss])
        P.op("dve", lambda e: e.reciprocal(out=ss[0:R, 3:4], in_=ss[0:R, 2:3]), [ss], [ss])
        ts("dve", xn[0:R, :], xm[0:R, :], ss[0:R, 3:4], None, ALU.mult, None, [xm, ss], [xn])
        for c in range(8):
            tr(ps[0][:, c * R:(c + 1) * R], xn[0:R, c * 128:(c + 1) * 128], ident[0:R, 0:R], [xn, ident], [ps[0]])
        tt("dve", bcm[:], ps[0][:, 0:128], G1S[:].rearrange("p c s -> p (c s)"), ALU.mult, [ps[0], G1S], [bcm])
        tt("dve", hTm[:, :, 0:R], bcm[:].rearrange("p (c s) -> p c s", s=R), SHS[:], ALU.add, [bcm, SHS], [hTm])
        kv_proj(hTm, ropeS[:, :], rows=R)
        P.dma("sp", kvs, zkv[0:R, 0:512], reads=[zkv], chan="zkvo")
        P.dma("sp", wins[:, 511, :], zkv[0:R, 512:768], reads=[zkv], chan="zkvo")
        P.dma("sp", convs[:, 0, :], sconv_d[:, 1, :], chan="d2d")
        tr(ps[4][:, 0:R], zkv[0:R, 256:384], ident[0:R, 0:R], [zkv, ident], [ps[4]])
        tr(ps[4][:, R:2 * R], zkv[0:R, 512:640], ident[0:R, 0:R], [zkv, ident], [ps[4]])
        cp("act", KselT[:, 2048:2048 + R], ps[4][:, 0:R], [ps[4]], [(KselT, 16)])
        cp("act", KwinT[:, 512:512 + R], ps[4][:, R:2 * R], [ps[4]], [(KwinT, 4)])
        colmax_update(ps[4], ps[4][:, 0:R], slice(0, 128), 1, R)
        colmax_update(ps[4], ps[4][:, R:2 * R], slice(0, 128), 2, R)
        cp("dve", Vsel[0:R, 16, :, 0:64], zkv[0:R, 384:512].rearrange("p (g d) -> p g d", g=2), [zkv], [(Vsel, 16)])
        cp("dve", Vwin[0:R, 4, :, 0:64], zkv[0:R, 640:768].rearrange("p (g d) -> p g d", g=2), [zkv], [(Vwin, 4)])
        cp("dve", cmb[:], cm[:], [cm], [cmb])
        zchunk(0, ps[0])
        cp("act", qf[0:R, :], ps[0][0:R, :], [ps[0]], [qf])
        rope_inplace(qf, [qf[0:R, :].rearrange("p (h d) -> p h d", d=64)[:, :, 0:16]], ropeS[:, :], rows=R)
        cp("pool", sgc[0:R, :].rearrange("p (h g d) -> p h g d", h=4, g=2),
           qf[0:R, :].rearrange("p (g h d) -> p h g d", g=2, h=4), [qf], [sgc])
        for jj in range(4):
            tr(ps[2][:, jj * R:(jj + 1) * R], sgc[0:R, jj * 128:(jj + 1) * 128], ident[0:R, 0:R], [sgc, ident], [ps[2]])
        cp("act", qsT[:].rearrange("p j s -> p (j s)"), ps[2][:, 0:4 * R], [ps[2]], [qsT])
        act(aqsT[:].rearrange("p j s -> p (j s)"), ps[2][:, 0:4 * R], AF.Abs, [ps[2]], [aqsT])
        for kc in range(8):
            mm(ps[1][:, 0:24], hTm[:, kc, :], wg[:, kc, :], kc == 0, kc == 7, [hTm, wg], [ps[1]])
        act(gates[0:R, :], ps[1][0:R, 0:24], AF.Sigmoid, [ps[1]], [gates])
        zchunk(1, ps[0])
        act(sa[0:R, :], ps[0][0:R, :], AF.Silu, [ps[0]], [sa])
        zchunk(2, ps[1])
        cp("act", cbf[0:R, :], ps[1][0:R, :], [ps[1]], [cbf])
        zchunk(3, ps[0])
        cp("act", ccf[0:R, :], ps[0][0:R, :], [ps[0]], [ccf])
        zchunk(4, ps[1])
        tt("dve", uu[0:R, :], ccf[0:R, :], ps[1][0:R, :], ALU.mult, [ccf, ps[1]], [uu])
        P.dma("sp", convs[:, 1, :], uu[0:R, :], reads=[uu], chan="uuo")
        P.dma("sp", ccp[0:R, :], sconv_d[:, 0, :], writes=[ccp], chan="scv")
        P.dma("sp", up[0:R, :], sconv_d[:, 1, :], writes=[up], chan="scv")
        tt("dve", co[0:R, :], uu[0:R, :], convwb[0:R, 2, :], ALU.mult, [uu, convwb], [co])
        tt("dve", qf[0:R, :], up[0:R, :], convwb[0:R, 1, :], ALU.mult, [up, convwb], [qf])
        tt("dve", co[0:R, :], co[0:R, :], qf[0:R, :], ALU.add, [co, qf], [co])
        tt("dve", qf[0:R, :], ccp[0:R, :], convwb[0:R, 0, :], ALU.mult, [ccp, convwb], [qf])
        tt("dve", co[0:R, :], co[0:R, :], qf[0:R, :], ALU.add, [co, qf], [co])
        zchunk(5, ps[0])
        act(sgc[0:R, :], ps[0][0:R, :], AF.Silu, [ps[0]], [sgc])
        tt("dve", co[0:R, :], co[0:R, :], cbf[0:R, :], ALU.mult, [co, cbf], [co])
        tt("dve", co[0:R, :], co[0:R, :], sgc[0:R, :], ALU.mult, [co, sgc], [co])
        for c in range(4):
            tr(ps[1][:, c * R:(c + 1) * R], co[0:R, c * 128:(c + 1) * 128], ident[0:R, 0:R], [co, ident], [ps[1]])
        cp("act", obT[:, :, 0:R], ps[1][:, 0:4 * R].rearrange("p (c s) -> p c s", s=R), [ps[1]], [obT])
        ms("pool", hsTp[:, 0, 0:2], 0.0, [hsTp, KselT])
        ms("pool", oat[:], 0.0, [oat])
        ms("pool", QT[:], 0.0, [QT])
        ms("pool", aQT[:], 0.0, [aQT])
        k = 0
        for sm in range(R):
            CTb = [KselT[:, 4096 + g_ * 2048:4096 + (g_ + 1) * 2048] for g_ in range(2)]
            for q4 in range(4):
                for pi in range(4):
                    pgi = q4 * 4 + pi
                    pb = pgb[k % 2]
                    col = sm * 16 + pgi
                    P.op("pool", lambda e, pb=pb, col=col: e.indirect_dma_start(
                        out=pb[:], out_offset=None, in_=cache_d,
                        in_offset=bass.IndirectOffsetOnAxis(ap=idxi[:, col:col + 1], axis=0)),
                        reads=[idxi], writes=[pb], chan="pgb%d" % (k % 2))
                    k += 1
                    tr(ps[4][:, pi * 128:(pi + 1) * 128], pb[:, 256:384], ident[:], [pb, ident], [ps[4]])
                    cp("dve", ctsrc[:].rearrange("p (g k d) -> p g k d", g=2, k=2),
                       pb[:, 0:256].rearrange("p (k g d) -> p g k d", k=2, g=2), [pb], [ctsrc])
                    for g_ in range(2):
                        tr(ps[5 + g_][:, pi * 128:(pi + 1) * 128], ctsrc[:, g_ * 128:(g_ + 1) * 128], ident[:],
                           [ctsrc, ident], [ps[5 + g_]])
                    cp("dve", Vsel[:, pgi, :, 0:64], pb[:, 384:512].rearrange("p (g d) -> p g d", g=2), [pb], [(Vsel, pgi)])
                cp("act", KselT[:, q4 * 512:(q4 + 1) * 512], ps[4][:], [ps[4]], [(KselT, 4 * q4 + i_) for i_ in range(4)])
                P.op("dve", lambda e: e.max(out=cmt8[:, 0:8], in_=ps[4][:]), [ps[4]], [cmt8])
                tt("dve", cm[:, 1:2], cm[:, 1:2], cmt8[:, 0:1], ALU.max, [cm, cmt8], [cm])
                ts("dve", sgc[:], ps[4][:], -1.0, None, ALU.mult, None, [ps[4]], [sgc])
                P.op("dve", lambda e: e.max(out=cmt8[:, 8:16], in_=sgc[:]), [sgc], [cmt8])
                tt("dve", cm[:, 1:2], cm[:, 1:2], cmt8[:, 8:9], ALU.max, [cm, cmt8], [cm])
                for g_ in range(2):
                    cp("act", CTb[g_][:, q4 * 512:(q4 + 1) * 512], ps[5 + g_][:], [ps[5 + g_]], [(KselT, "ctb%d" % g_)])
            hsv = Lexp[1][:, 0:512].rearrange("p (k h n) -> p k h n", k=2, h=2)
            for g_ in range(2):
                for kv in range(2):
                    for half in range(2):
                        for r in range(32):
                            rhs = CTb[g_][kv * 64:(kv + 1) * 64, r:r + 16 * 126 + 1:16]
                            mm(ps[5 + 2 * kv][:, half * 128:half * 128 + 127],
                               w1[kv * 64:(kv + 1) * 64, r, half * 128:(half + 1) * 128], rhs, r == 0, r == 31,
                               [w1, (KselT, "ctb%d" % g_)], [ps[5 + 2 * kv]])
                for kv in range(2):
                    for half in range(2):
                        act(hsv[:, kv, half, 0:127], ps[5 + 2 * kv][:, half * 128:half * 128 + 127], AF.Silu,
                            [ps[5 + 2 * kv], bias1], [Lexp[1]], bias=bias1[:, kv, half:half + 1])
                for half in range(2):
                    mm(ps[6][g_ * 64:(g_ + 1) * 64, 0:127], w2[:, 0, half, :], hsv[:, 0, half, 0:127], half == 0, half == 1,
                       [w2, Lexp[1]], [ps[6]])
                cp("act", kcT[g_ * 64:(g_ + 1) * 64, 0:127], ps[6][g_ * 64:(g_ + 1) * 64, 0:127], [ps[6]], [kcT])
                colmax_update(ps[6], ps[6][g_ * 64:(g_ + 1) * 64, 0:127], slice(g_ * 64, (g_ + 1) * 64), 0, 127)
                for half in range(2):
                    mm(ps[4][0:127, 0:64], hsv[:, 1, half, 0:127], w2[:, 1, half, :], half == 0, half == 1,
                       [Lexp[1], w2], [ps[4]])
                cp("dve", vca[0:127, 0, g_, 0:64], ps[4][0:127, 0:64], [ps[4]], [vca])
            for i in range(4):
                P.dma("sp", swt[:], swin_d[sm, i * 128:(i + 1) * 128, :], writes=[swt], chan="swt")
                tr(ps[4][:, 0:128], swt[:, 0:128], ident[:], [swt, ident], [ps[4]])
                cp("act", KwinT[:, i * 128:(i + 1) * 128], ps[4][:, 0:128], [ps[4]], [(KwinT, i)])
                colmax_update(ps[4], ps[4][:, 0:128], slice(0, 128), 2, 128)
                cp("dve", Vwin[:, i, :, 0:64], swt[:, 128:256].rearrange("p (g d) -> p g d", g=2), [swt], [(Vwin, i)])
            P.dma("sp", wins[sm, 0:511, :], swin_d[sm, 1:512, :], chan="d2d")
            cp("dve", cmb[:], cm[:], [cm], [cmb])
            QTv = QT[:].rearrange("p (j q) -> p j q", q=128)
            aQTv = aQT[:].rearrange("p (j q) -> p j q", q=128)
            if sm > 0:
                ms("dve", QTv[:, :, sm - 1], 0.0, [QT])
                ms("dve", aQTv[:, :, sm - 1], 0.0, [aQT])
            cp("dve", QTv[:, :, sm], qsT[:, :, sm], [qsT], [QT])
            cp("dve", aQTv[:, :, sm], aqsT[:, :, sm], [aqsT], [aQT])
            for g in range(2):
                attention(0, g, samp=sm)
        tt("dve", oat[0:R, :], oat[0:R, :], sa[0:R, :], ALU.mult, [oat, sa], [oat])
        for c in range(4):
            tr(ps[2][:, c * R:(c + 1) * R], oat[0:R, c * 128:(c + 1) * 128], ident[0:R, 0:R], [oat, ident], [ps[2]])
        cp("act", oaT[:, :, 0:R], ps[2][:, 0:4 * R].rearrange("p (c s) -> p c s", s=R), [ps[2]], [oaT])
        for (jw, srcT, jg) in ((10, oaT, (6, 7)), (11, obT, (8, 9))):
            wb, kk = next_chunk(jw)
            wv = wb[:].rearrange("p (c n) -> p c n", c=4)
            for half in range(2):
                for kc in range(4):
                    mm(ps[0 + half][:], srcT[:, kc, :], wv[:, kc, half * 512:(half + 1) * 512], kc == 0, kc == 3,
                       [srcT, wb], [ps[half]])
            for half in range(2):
                zchunk(jg[half], ps[2 + half])
                act(sgc[0:R, :], ps[2 + half][0:R, :], AF.Sigmoid, [ps[2 + half]], [sgc])
                dst = mbuf[0:R, half * 512:(half + 1) * 512]
                if jw == 10:
                    tt("dve", dst, sgc[0:R, :], ps[half][0:R, :], ALU.mult, [sgc, ps[half]], [mbuf])
                else:
                    tt("dve", qf[0:R, :], sgc[0:R, :], ps[half][0:R, :], ALU.mult, [sgc, ps[half]], [qf])
                    tt("dve", dst, dst, qf[0:R, :], ALU.add, [mbuf, qf], [mbuf])
        for c in range(8):
            tr(ps[4 + c // 4][:, (c % 4) * R:(c % 4 + 1) * R], mbuf[0:R, c * 128:(c + 1) * 128], ident[0:R, 0:R],
               [mbuf, ident], [ps[4 + c // 4]])
        cp("act", mT[:, 0:4, 0:R], ps[4][:, 0:4 * R].rearrange("p (c s) -> p c s", s=R), [ps[4]], [mT])
        cp("dve", mT[:, 4:8, 0:R], ps[5][:, 0:4 * R].rearrange("p (c s) -> p c s", s=R), [ps[5]], [mT])
        for half in range(2):
            wb, kk = next_chunk(12 + half)
            wv = wb[:].rearrange("p (c n) -> p c n", c=8)
            for kc in range(8):
                mm(ps[6 + half][:], mT[:, kc, :], wv[:, kc, :], kc == 0, kc == 7, [mT, wb], [ps[6 + half]])
        for c in range(8):
            tr(ps[2 + c // 4][0:R, (c % 4) * 128:(c % 4 + 1) * 128], GST[:, c, :], ident[:], [GST, ident], [ps[2 + c // 4]])
        cp("dve", GP[0:R, 0:512], ps[2][0:R, :], [ps[2]], [GP])
        cp("dve", GP[0:R, 512:1024], ps[3][0:R, :], [ps[3]], [GP])
        for half in range(2):
            act(qf[0:R, :], ps[6 + half][0:R, :], AF.Square, [ps[6 + half]], [qf, ss], accum_out=ss[0:R, 4 + half:5 + half])
        tt("dve", ss[0:R, 6:7], ss[0:R, 4:5], ss[0:R, 5:6], ALU.add, [ss], [ss])
        ts("dve", ss[0:R, 6:7], ss[0:R, 6:7], 1.0 / D, 1e-6, ALU.mult, ALU.add, [ss], [ss])
        act(ss[0:R, 7:8], ss[0:R, 6:7], AF.Sqrt, [ss], [ss])
        P.op("dve", lambda e: e.reciprocal(out=ss[0:R, 6:7], in_=ss[0:R, 7:8]), [ss], [ss])
        for half in range(2):
            sl = slice(half * 512, (half + 1) * 512)
            stt(mbuf[0:R, sl], ps[6 + half][0:R, :], ss[0:R, 6:7], GP[0:R, sl], ALU.mult, ALU.mult, [ps[6 + half], ss, GP], [mbuf])
        tt("dve", mbuf[0:R, :], mbuf[0:R, :], xm[0:R, :], ALU.add, [mbuf, xm], [mbuf])
        P.dma("sp", ys, mbuf[0:R, :], reads=[mbuf], chan="yout")

    stop = ""
    if stop != "p0":
        load_x(xt[0], xp[0:128, :], "xt0")
        for t in range(nt):
            phase_a(t)
            if stop.startswith("a"):
                continue
            if t % 4 == 3 and t // 4 < nslot:
                phase_b(t // 4)
    if do_sample:
        phase_s()

    P.emit()
    return nc, P


def make_consts(r):
    f = np.float32
    c = {}
    c["ident"] = np.eye(128, dtype=f)
    inv = (np.float32(500000.0) ** (-(np.arange(8, dtype=f)) / np.float32(8))).astype(f)
    p = np.arange(128)

    def rope_tab(pos):
        ang = (pos.astype(f)[..., None] * inv).astype(f)
        return np.concatenate([np.cos(ang.astype(np.float64)), np.sin(ang.astype(np.float64))], -1).astype(f)

    c["ropeA"] = rope_tab(np.arange(NT)[None, :] * 128 + p[:, None])
    tn = 4 * np.arange(NSLOT) + r
    c["ropeB"] = rope_tab(tn[None, :] * 128 + p[:, None])
    j = np.arange(128)
    slotc = np.zeros((NSLOT, 128, 3, 128), f)
    cmpb = np.zeros((NSLOT, 128, 4, 128), f)
    cidx = (np.arange(4)[:, None] * 128 + np.arange(128)[None, :])
    for n in range(NSLOT):
        b = 4 * n + r
        cur = 2 * b + (p >= 64)
        A = (j[None, :] <= cur[:, None]).astype(f)
        slotc[n, :, 0] = A
        slotc[n, :, 1] = A - 1
        slotc[n, :, 2] = np.where((j[None, :] == cur[:, None]) | (j[None, :] == cur[:, None] - 1), 1e6, -2.0)
        valid = (16 * cidx[None] + 31 <= (128 * b + p)[:, None, None]) & (cidx[None] < 511)
        cmpb[n] = np.where(valid, 0.0, -BIG8)
    c["slotc"] = slotc
    c["cmpb"] = cmpb
    q = p[:, None]
    k = p[None, :]
    winb = np.zeros((128, 8, 128), f)
    for kr in range(8):
        dt = 128 * (r + 4 - kr) + q - k
        winb[:, kr] = np.where((dt >= 0) & (dt < 512), 0.0, -BIG8)
    c["winb"] = winb
    selc = np.zeros((128, 3, 4, 128), f)
    for kr in range(4):
        if kr < r:
            selc[:, 0, kr] = 1.0
        elif kr == r:
            selc[:, 1, kr] = 1.0
            selc[:, 2, kr] = np.where(k <= q, 0.0, -BIG8)
        else:
            selc[:, 1, kr] = 1.0
            selc[:, 2, kr] = -BIG8
    c["selc"] = selc
    cs = 16 * cidx
    ov = (cs[:, :, None] < 64 * j[None, None, :] + 64) & (cs[:, :, None] + 32 > 64 * j[None, None, :]) & (cidx[:, :, None] < 511)
    c["ovl"] = np.ascontiguousarray(ov.transpose(1, 0, 2)).astype(f)
    shm = np.zeros((128, 2, 128), f)
    shb = np.zeros((32, 2, 128), f)
    for s in (1, 2):
        for m in range(128):
            kk = m - s
            if kk >= 0:
                shm[kk, s - 1, m] = 1.0
            else:
                shb[32 + kk, s - 1, m] = 1.0
    c["shm"] = shm
    c["shb"] = shb
    ups = np.ones((32, NSLOT), f)
    if r == 0:
        ups[:, 0] = 0.0
    c["upsc"] = ups
    c["ropeS"] = np.repeat(rope_tab(np.array([2048])), 16, axis=0)
    slotS = np.zeros((128, 3, 128), f)
    A = (j <= 32).astype(f)
    slotS[:, 0] = A[None, :]
    slotS[:, 1] = A[None, :] - 1
    slotS[:, 2] = np.where((j == 31) | (j == 32), 1e6, -2.0)[None, :]
    c["slotS"] = slotS
    smask = np.zeros((128, 4, 128), f)
    smask[:, 0] = np.where(k <= 126, 0.0, -BIG8)
    smask[:, 1] = np.where(k == q, 0.0, -BIG8)
    smask[:, 2] = np.where(k >= 1, 0.0, -BIG8)
    c["smask"] = smask
    oh = np.zeros((128, 16), f)
    oh[np.arange(16), np.arange(16)] = 1.0
    c["onehot"] = oh
    c["pcol"] = np.arange(128, dtype=f).reshape(128, 1)
    return c


_CACHE = {}


def kernel(x_prompt, x_sample, cache_kv_pages, state_win_kv, state_conv, page_table, c_prompt, c_sample,
           w_ada, b_ada, g_pre, g_post, w_in, pe_cmp, w_cmp1, w_cmp2, conv_w, w_br_a, w_br_b, w_out,
           _nt=NT, _nslot=NSLOT):
    f = np.float32
    key = (_nt, _nslot)
    if key not in _CACHE:
        _CACHE[key] = build(_nt, _nslot)
    nc, P = _CACHE[key]
    in_maps = []
    cache_flat = np.ascontiguousarray(cache_kv_pages[0], dtype=f).reshape(-1, 512)
    for c in range(8):
        bi, r = c // 4, c % 4
        xb = np.ascontiguousarray(x_prompt[bi], dtype=f)
        tiles = xb.reshape(NT, 128, D)
        tn = 4 * np.arange(NSLOT) + r
        xmine = np.ascontiguousarray(tiles[tn]).reshape(NSLOT * 128, D)
        xprev = np.zeros((NSLOT, 32, D), f)
        for n in range(NSLOT):
            if tn[n] > 0:
                xprev[n] = xb[tn[n] * 128 - 32:tn[n] * 128]
        m = {
            "xp": xb, "xmine": xmine, "xprev": xprev.reshape(NSLOT * 32, D),
            "cpr": np.ascontiguousarray(c_prompt[bi:bi + 1], dtype=f),
            "w_ada": np.ascontiguousarray(w_ada[0]), "b_ada": np.ascontiguousarray(b_ada), "g_pre": np.ascontiguousarray(g_pre),
            "g_post": np.ascontiguousarray(g_post), "w_in": np.ascontiguousarray(w_in[0]), "pe_cmp": np.ascontiguousarray(pe_cmp[0]),
            "w_cmp1": np.ascontiguousarray(w_cmp1[0]), "w_cmp2": np.ascontiguousarray(w_cmp2[0]),
            "conv_w": np.ascontiguousarray(conv_w[0]), "w_br_a": np.ascontiguousarray(w_br_a[0]),
            "w_br_b": np.ascontiguousarray(w_br_b[0]), "w_out": np.ascontiguousarray(w_out[0]),
        }
        m.update(make_consts(r))
        sl = slice(16 * c, 16 * c + 16)
        m["xs"] = np.ascontiguousarray(x_sample[sl, 0, :], dtype=f)
        m["cs"] = np.ascontiguousarray(c_sample[sl], dtype=f)
        m["ptab"] = np.ascontiguousarray(page_table[sl], dtype=np.int32).reshape(1, 256)
        m["cache"] = cache_flat
        m["swin"] = np.ascontiguousarray(state_win_kv[0, sl], dtype=f).reshape(16, 512, 256)
        m["sconv"] = np.ascontiguousarray(state_conv[0, sl], dtype=f)
        in_maps.append(m)
    res = run_bass_kernel_spmd(nc, in_maps, core_ids=list(range(8)))
    B, T = x_prompt.shape[0], x_prompt.shape[1]
    y_prompt = np.zeros((B, T, D), f)
    kv_rows_prompt = np.zeros((1, B, T, 4, 2, 64), f)
    win_kv_prompt = np.zeros((1, B, 512, 2, 2, 64), f)
    conv_state_prompt = np.zeros((1, B, 2, 512), f)
    for c in range(8):
        bi, r = c // 4, c % 4
        o = res.results[c]
        for n in range(NSLOT):
            t = 4 * n + r
            y_prompt[bi, t * 128:(t + 1) * 128] = o["yp"][n * 128:(n + 1) * 128]
            kv_rows_prompt[0, bi, t * 128:(t + 1) * 128] = o["kvp"][n * 128:(n + 1) * 128].reshape(128, 4, 2, 64)
        win_kv_prompt[0, bi, r * 128:(r + 1) * 128] = o["winp"].reshape(128, 2, 2, 64)
        if r == 3:
            conv_state_prompt[0, bi] = o["convp"]
    nS = x_sample.shape[0]
    y_sample = np.zeros((nS, 1, D), f)
    kv_rows_sample = np.zeros((1, nS, 1, 4, 2, 64), f)
    win_kv_sample = np.zeros((1, nS, 512, 2, 2, 64), f)
    conv_state_sample = np.zeros((1, nS, 2, 512), f)
    for c in range(8):
        o = res.results[c]
        sl = slice(16 * c, 16 * c + 16)
        y_sample[sl, 0] = o["ys"]
        kv_rows_sample[0, sl, 0] = o["kvs"].reshape(16, 4, 2, 64)
        win_kv_sample[0, sl] = o["wins"].reshape(16, 512, 2, 2, 64)
        conv_state_sample[0, sl] = o["convs"]
    return (y_prompt, y_sample, kv_rows_prompt, win_kv_prompt, conv_state_prompt, kv_rows_sample, win_kv_sample,
            conv_state_sample)
```

```python
from concourse.bass_utils import run_bass_kernel_spmd

from contextlib import ExitStack
import numpy as np
import concourse.bass as bass
import concourse.mybir as mybir

F32 = mybir.dt.float32
BF16 = mybir.dt.bfloat16
I32 = mybir.dt.int32
U32 = mybir.dt.uint32
AF = mybir.ActivationFunctionType
ALU = mybir.AluOpType
AX = mybir.AxisListType

SAME_ENG_SYNC = True
COMPUTE = ("pe", "dve", "act", "pool")


class Buf:
    def __init__(self, prog, name, t, is_dram=False):
        self.prog = prog
        self.name = name
        self.t = t
        self.is_dram = is_dram
        self.st = {}
        self.whole = [None, []]
        self.excl = False

    def __getitem__(self, idx):
        return self.t[idx]


class Op:
    __slots__ = ("eng", "fn", "deps", "signal", "val", "sem", "chan", "idx", "is_dma")

    def __init__(self, eng, fn):
        self.eng = eng
        self.fn = fn
        self.deps = {}
        self.signal = False
        self.val = None
        self.sem = None
        self.chan = None
        self.is_dma = False


class Prog:
    def __init__(self, nc):
        self.nc = nc
        self.ops = []
        self.stack = ExitStack()
        self.chan_count = {}
        self.nbuf = 0

    def sb(self, name, shape, dtype):
        t = self.stack.enter_context(self.nc.sbuf_tensor("sb_" + name, list(shape), dtype))
        return Buf(self, name, t)

    def ps(self, name, shape, dtype):
        t = self.stack.enter_context(self.nc.psum_tensor("ps_" + name, list(shape), dtype))
        b = Buf(self, name, t)
        b.excl = True
        return b

    def dram(self, name, shape, dtype, kind="Internal"):
        t = self.nc.dram_tensor(name, list(shape), dtype, kind=kind)
        return Buf(self, name, t.ap(), is_dram=True)

    def _norm(self, lst):
        out = []
        for x in lst:
            if x is None:
                continue
            if isinstance(x, Buf):
                out.append((x, None))
            else:
                out.append(x)
        return out

    def op(self, eng, fn, reads=(), writes=(), chan=None):
        o = Op(eng, fn)
        o.idx = len(self.ops)
        if chan is not None:
            o.is_dma = True
            o.chan = chan
            self.chan_count[chan] = self.chan_count.get(chan, 0) + 1
            o.val = 16 * self.chan_count[chan]
            o.signal = True
        reads = self._norm(reads)
        writes = self._norm(writes)
        writes = writes + [(b_, k_) for (b_, k_) in reads if b_.excl]
        reads = [(b_, k_) for (b_, k_) in reads if not b_.excl]

        def add_dep(d):
            if d is None or d is o:
                return
            if d.is_dma:
                v = 16 * self.chan_count[d.chan]
                if d is not o and d.chan == o.chan:
                    v = d.val
                o.deps[d] = max(o.deps.get(d, 0), v)
            else:
                if d.eng == o.eng and not o.is_dma:
                    if d.eng == "pe" or not SAME_ENG_SYNC:
                        return
                o.deps[d] = 0

        for b, k in reads:
            add_dep(b.whole[0])
            if k is None:
                for st in b.st.values():
                    add_dep(st[0])
            elif k in b.st:
                add_dep(b.st[k][0])
        for b, k in writes:
            add_dep(b.whole[0])
            for r in b.whole[1]:
                add_dep(r)
            if k is None:
                for st in b.st.values():
                    add_dep(st[0])
                    for r in st[1]:
                        add_dep(r)
            elif k in b.st:
                add_dep(b.st[k][0])
                for r in b.st[k][1]:
                    add_dep(r)
        for b, k in reads:
            if k is None:
                b.whole[1].append(o)
            else:
                b.st.setdefault(k, [None, []])[1].append(o)
        for b, k in writes:
            if k is None:
                b.whole = [o, []]
                b.st = {}
            else:
                b.st[k] = [o, []]
        self.ops.append(o)
        return o

    def dma(self, eng, out, in_, reads=(), writes=(), chan=None, **kw):
        assert chan is not None
        return self.op(eng, lambda e: e.dma_start(out=out, in_=in_, **kw), reads, writes, chan=chan)

    def emit(self):
        nc = self.nc
        for o in self.ops:
            for d in o.deps:
                d.signal = True
        sems = {}
        for e in COMPUTE:
            sems[e] = self.stack.enter_context(nc.semaphore("s_" + e))
        for c in self.chan_count:
            sems["c_" + c] = self.stack.enter_context(nc.semaphore("c_" + c))
        cnt = {e: 0 for e in COMPUTE}
        for o in self.ops:
            if o.is_dma:
                o.sem = sems["c_" + o.chan]
            else:
                o.sem = sems[o.eng]
                if o.signal:
                    cnt[o.eng] += 1
                    o.val = cnt[o.eng]
        by_eng = {}
        for o in self.ops:
            by_eng.setdefault(o.eng, []).append(o)
        last_chan_eng = {}
        for o in self.ops:
            if o.is_dma:
                last_chan_eng[o.chan] = o.eng
        self.n_inst = {e: len(v) for e, v in by_eng.items()}

        def make_section(ename, ops):
            def section(eng):
                waited = {}
                for o in ops:
                    for d, v in o.deps.items():
                        val = v if d.is_dma else d.val
                        key = id(d.sem)
                        if waited.get(key, 0) >= val:
                            continue
                        eng.wait_ge(d.sem, val)
                        waited[key] = val
                    inst = o.fn(eng)
                    if o.signal:
                        inst.then_inc(o.sem, 16 if o.is_dma else 1)
                for c, e in last_chan_eng.items():
                    if e == ename:
                        eng.wait_ge(sems["c_" + c], 16 * self.chan_count[c])
            return section

        with nc.Block() as block:
            dec = {"pe": block.tensor, "dve": block.vector, "act": block.scalar,
                   "pool": block.gpsimd, "sp": block.sync}
            for ename, ops in by_eng.items():
                dec[ename](make_section(ename, ops))
        self.stack.close()

D = 1024
NT = 64
NSLOT = 16
BIG8 = 240000.0
IN_W = 5912
KV0 = 512
G0 = 1280
CH_COLS = [0, 1304, 1816, 2328, 2840, 3352, 3864, 4376, 4888, 5400]
NWB = 2
NCH = 14


def _mk(P):
    class H:
        pass
    h = H()

    def mm(out, lhsT, rhs, start, stop, reads, writes, **kw):
        return P.op("pe", lambda e: e.matmul(out=out, lhsT=lhsT, rhs=rhs, start=start, stop=stop, **kw), reads, writes)

    def tr(out, in_, ident, reads, writes):
        return P.op("pe", lambda e: e.transpose(out=out, in_=in_, identity=ident), reads, writes)

    def act(out, in_, func, reads, writes, **kw):
        return P.op("act", lambda e: e.activation(out=out, in_=in_, func=func, **kw), reads, writes)

    def tt(eng, out, in0, in1, op, reads, writes):
        return P.op(eng, lambda e: e.tensor_tensor(out=out, in0=in0, in1=in1, op=op), reads, writes)

    def ts(eng, out, in0, s1, s2, op0, op1, reads, writes, **kw):
        if op1 is None:
            return P.op(eng, lambda e: e.tensor_scalar(out=out, in0=in0, scalar1=s1, scalar2=None, op0=op0, **kw), reads, writes)
        return P.op(eng, lambda e: e.tensor_scalar(out=out, in0=in0, scalar1=s1, scalar2=s2, op0=op0, op1=op1, **kw), reads, writes)

    def stt(out, in0, scalar, in1, op0, op1, reads, writes):
        return P.op("dve", lambda e: e.scalar_tensor_tensor(out=out, in0=in0, scalar=scalar, in1=in1, op0=op0, op1=op1), reads, writes)

    def cp(eng, out, in_, reads, writes):
        if eng == "act":
            return P.op("act", lambda e: e.copy(out=out, in_=in_), reads, writes)
        return P.op(eng, lambda e: e.tensor_copy(out=out, in_=in_), reads, writes)

    def ms(eng, ap, val, writes):
        return P.op(eng, lambda e: e.memset(ap, val), (), writes)

    h.mm, h.tr, h.act, h.tt, h.ts, h.stt, h.cp, h.ms = mm, tr, act, tt, ts, stt, cp, ms
    return h


def build(nt=NT, nslot=NSLOT, do_sample=True, dbg=False):
    nc = bass.Bass("TRN2", target_bir_lowering=False)
    P = Prog(nc)
    h = _mk(P)
    mm, tr, act, tt, ts, stt, cp, ms = h.mm, h.tr, h.act, h.tt, h.ts, h.stt, h.cp, h.ms

    def din(name, shape, dt=F32):
        return nc.dram_tensor(name, list(shape), dt, kind="ExternalInput").ap()

    def dout(name, shape, dt=F32):
        return nc.dram_tensor(name, list(shape), dt, kind="ExternalOutput").ap()

    xp = din("xp", [NT * 128, D])
    xmine = din("xmine", [NSLOT * 128, D])
    xprev = din("xprev", [NSLOT * 32, D])
    cp_d = din("cpr", [1, D])
    w_ada = din("w_ada", [D, 3 * D])
    b_ada = din("b_ada", [1, 3 * D])
    g_pre = din("g_pre", [1, D])
    g_post = din("g_post", [1, D])
    w_in = din("w_in", [D, IN_W])
    pe_cmp = din("pe_cmp", [2, 32, 64])
    w_cmp1 = din("w_cmp1", [2, 2048, 256])
    w_cmp2 = din("w_cmp2", [2, 256, 64])
    conv_w = din("conv_w", [3, 512])
    w_br_a = din("w_br_a", [512, D])
    w_br_b = din("w_br_b", [512, D])
    w_out = din("w_out", [D, D])
    ident_d = din("ident", [128, 128])
    ropeA_d = din("ropeA", [128, NT, 16])
    ropeB_d = din("ropeB", [128, NSLOT, 16])
    slotc_d = din("slotc", [NSLOT, 128, 3, 128])
    cmpb_d = din("cmpb", [NSLOT, 128, 4, 128])
    winb_d = din("winb", [128, 8, 128])
    selc_d = din("selc", [128, 3, 4, 128])
    ovl_d = din("ovl", [128, 4, 128])
    shm_d = din("shm", [128, 2, 128])
    shb_d = din("shb", [32, 2, 128])
    upsc_d = din("upsc", [32, NSLOT])

    xs_d = din("xs", [16, D])
    cs_d = din("cs", [16, D])
    ptab_d = din("ptab", [1, 256], I32)
    cache_d = din("cache", [2560 * 128, 512])
    swin_d = din("swin", [16, 512, 256])
    sconv_d = din("sconv", [16, 2, 512])
    ropeS_d = din("ropeS", [16, 16])
    slotS_d = din("slotS", [128, 3, 128])
    smask_d = din("smask", [128, 4, 128])
    onehot_d = din("onehot", [128, 16])
    pcol_d = din("pcol", [128, 1])
    ys = dout("ys", [16, D])
    kvs = dout("kvs", [16, 512])
    wins = dout("wins", [16, 512, 256])
    convs = dout("convs", [16, 2, 512])
    yp = dout("yp", [NSLOT * 128, D])
    kvp = dout("kvp", [NSLOT * 128, 512])
    winp = dout("winp", [128, 256])
    convp = dout("convp", [2, 512])
    dbg_outs = {}

    def dbg_out(name, buf, ap, shape, dt=F32):
        if not dbg:
            return
        d = dout("dbg_" + name, shape, dt)
        P.dma("sp", d, ap, reads=[buf], chan="dbg_" + name)

    wscr = P.dram("wscr", [NCH, 128, 4096], BF16)

    ident = P.sb("ident", [128, 128], F32)
    identb = P.sb("identb", [128, 128], BF16)
    I4 = P.sb("I4", [128, 512], BF16)
    ones1 = P.sb("ones1", [1, 128], F32)
    wkv = P.sb("wkv", [128, 8, 768], BF16)
    wg = P.sb("wg", [128, 8, 24], BF16)
    w1 = P.sb("w1", [128, 32, 256], BF16)
    w2 = P.sb("w2", [128, 2, 2, 64], BF16)
    wch = [P.sb("wch%d" % i, [128, 4096], BF16) for i in range(NWB)]
    KselT = P.sb("KselT", [128, NT * 128], BF16)
    Vsel = P.sb("Vsel", [128, NT, 2, 65], BF16)
    KwinT = P.sb("KwinT", [128, 8 * 128], BF16)
    Vwin = P.sb("Vwin", [128, 8, 2, 65], BF16)
    kcT = P.sb("kcT", [128, 512], BF16)
    vca = P.sb("vca", [128, 4, 2, 65], BF16)
    vcf = P.sb("vcf", [128, 4, 2, 64], F32)
    CT = [P.sb("CT%d" % g, [128, 528], BF16) for g in range(2)]
    xt = [P.sb("xt0", [128, D], F32)] * 2
    xn = P.sb("xn", [128, D], F32)
    hT = [P.sb("hT0", [128, 8, 128], BF16)] * 2
    zkv = P.sb("zkv", [128, 768], F32)
    ctsrc = P.sb("ctsrc", [128, 256], F32)
    ropeA = P.sb("ropeA", [128, NT, 16], F32)
    ropeB = P.sb("ropeB", [128, NSLOT, 16], F32)
    rt = P.sb("rt", [128, 4, 8 * 8], F32)
    ss = P.sb("ss", [128, 8], F32)
    G1col = P.sb("G1col", [128, 8], F32)
    SHcol = P.sb("SHcol", [128, 8], F32)
    GP = P.sb("GP", [128, D], F32)
    cm = P.sb("cm", [128, 4], F32)
    cmb = P.sb("cmb", [128, 4], BF16)
    cmt8 = P.sb("cmt8", [128, 16], F32)
    cmneg = P.sb("cmneg", [128, 128], F32)
    bias1 = P.sb("bias1", [128, 2, 2], F32)
    hsT = P.sb("hsT", [128, 2, 2, 32], BF16)
    hsTp = P.sb("hsTp", [128, 2, 256], BF16)
    sT = P.sb("sT", [128, 8, 17], F32)
    modT = P.sb("modT", [128, 24, 17], F32)
    coltmp = P.sb("coltmp", [128, 32], F32)
    G1S = P.sb("G1S", [128, 8, 16], F32)
    SHS = P.sb("SHS", [128, 8, 16], F32)
    GST = P.sb("GST", [128, 8, 16], F32)
    bcm = P.sb("bcm", [128, 128], F32)
    convwb = P.sb("convwb", [128, 3, 512], F32)
    winb = P.sb("winb", [128, 8, 128], F32)
    selc = P.sb("selc", [128, 3, 4, 128], F32)
    ovl = P.sb("ovl", [128, 4, 128], BF16)
    ovlf = P.sb("ovlf", [128, 4, 128], F32)
    shm = P.sb("shm", [128, 2, 128], F32)
    shb = P.sb("shb", [32, 2, 128], F32)
    upsc = P.sb("upsc", [32, NSLOT], F32)
    pet = P.sb("pet", [32, 2, 64], F32)
    peT = P.sb("peT", [128, 32], BF16)

    xm = P.sb("xm", [128, D], F32)
    ps = [P.ps("b%d" % i, [128, 512], F32) for i in range(8)]

    P.dma("sp", ident[:], ident_d, writes=[ident], chan="c0")
    P.dma("sp", ropeA[:], ropeA_d, writes=[ropeA], chan="c0")
    P.dma("sp", ropeB[:], ropeB_d, writes=[ropeB], chan="c0")
    P.dma("sp", winb[:], winb_d, writes=[winb], chan="c0")
    P.dma("sp", selc[:], selc_d, writes=[selc], chan="c0")
    P.dma("sp", ovlf[:], ovl_d, writes=[ovlf], chan="c0")
    P.dma("sp", shm[:], shm_d, writes=[shm], chan="c0")
    P.dma("sp", shb[:], shb_d, writes=[shb], chan="c0")
    P.dma("sp", upsc[:], upsc_d, writes=[upsc], chan="c0")
    P.dma("sp", pet[:], pe_cmp.rearrange("k r d -> r k d"), writes=[pet], chan="c0")
    cp("dve", identb[:], ident[:], [ident], [identb])
    cp("dve", ovl[:], ovlf[:], [ovlf], [ovl])
    for i in range(4):
        cp("dve", I4[:, i * 128:(i + 1) * 128], ident[:], [ident], [I4])
    ms("dve", ones1[:], 1.0, [ones1])
    ms("pool", Vsel[:], 1.0, [Vsel])
    ms("pool", Vwin[:], 1.0, [Vwin])
    ms("pool", vca[:], 1.0, [vca])
    ms("pool", vcf[:], 0.0, [vcf])
    ms("pool", kcT[:], 0.0, [kcT])
    ms("pool", cm[:], 0.0, [cm])
    for g in range(2):
        ms("pool", CT[g][:], 0.0, [CT[g]])
    ms("pool", KwinT[:], 0.0, [KwinT])
    ms("pool", KselT[:], 0.0, [KselT])

    P.dma("pool", wkv[:], w_in[:, KV0:KV0 + 768].rearrange("(c p) n -> p c n", p=128), writes=[wkv], chan="wk")
    P.dma("pool", wg[:], w_in[:, G0:G0 + 24].rearrange("(c p) n -> p c n", p=128), writes=[wg], chan="wk")
    for kv in range(2):
        P.dma("pool", w1[kv * 64:(kv + 1) * 64, :, :], w_cmp1[kv].rearrange("(r d) n -> d r n", d=64),
              writes=[(w1, kv)], chan="wk")
        P.dma("pool", w2[:, kv, :, :], w_cmp2[kv].rearrange("(h p) n -> p h n", p=128), writes=[(w2, kv)], chan="wk")
    for j in range(NCH):
        wb = wch[j % NWB]
        if j < 10:
            src = w_in[:, CH_COLS[j]:CH_COLS[j] + 512].rearrange("(c p) n -> p c n", p=128)
            dst = wb[:].rearrange("p (c n) -> p c n", c=8)
        elif j == 10:
            src = w_br_a.rearrange("(c p) n -> p c n", p=128)
            dst = wb[:].rearrange("p (c n) -> p c n", c=4)
        elif j == 11:
            src = w_br_b.rearrange("(c p) n -> p c n", p=128)
            dst = wb[:].rearrange("p (c n) -> p c n", c=4)
        else:
            o = (j - 12) * 512
            src = w_out[:, o:o + 512].rearrange("(c p) n -> p c n", p=128)
            dst = wb[:].rearrange("p (c n) -> p c n", c=8)
        P.dma("pool", dst, src, writes=[wb], chan="wst%d" % (j % NWB))
        P.dma("sp", wscr[j], wb[:], reads=[wb], writes=[(wscr, j)], chan="wso%d" % (j % NWB))

    P.dma("sp", xn[0:1, :], cp_d, writes=[xn], chan="xnl")
    P.dma("sp", xn[1:17, :], cs_d, writes=[xn], chan="xnl")
    act(xn[0:17, :], xn[0:17, :], AF.Silu, [xn], [xn])
    for k in range(8):
        tr(ps[0][:, k * 17:(k + 1) * 17], xn[0:17, k * 128:(k + 1) * 128], ident[0:17, 0:17], [xn, ident], [ps[0]])
    cp("dve", sT[:].rearrange("p k s -> p (k s)"), ps[0][:, 0:136], [ps[0]], [sT])

    def row_to_cols(row_ap, dst_buf, dst_ap_fn, n):
        for j in range(n):
            tr(ps[0][:, j:j + 1], row_ap[:, j * 128:(j + 1) * 128], ident[0:1, 0:1], [xm, ident], [ps[0]])
        cp("dve", dst_ap_fn, ps[0][:, 0:n], [ps[0]], [dst_buf])

    for j in range(24):
        wa = xt[j % 2]
        P.dma("sp", wa[:].rearrange("p (c n) -> p c n", c=8),
              w_ada[:, j * 128:(j + 1) * 128].rearrange("(c p) n -> p c n", p=128), writes=[wa], chan="xt0")
        for k in range(8):
            mm(ps[1][:, j * 17:(j + 1) * 17], wa[:, k * 128:(k + 1) * 128], sT[:, k, :], k == 0, k == 7, [wa, sT], [ps[1]])
    cp("dve", modT[:].rearrange("p j s -> p (j s)"), ps[1][:, 0:408], [ps[1]], [modT])
    for i3 in range(3):
        P.dma("sp", xm[0:1, :], b_ada[:, i3 * D:(i3 + 1) * D], writes=[xm], chan="xm")
        row_to_cols(xm[0:1, :], coltmp, coltmp[:, i3 * 8:(i3 + 1) * 8], 8)
    tt("dve", modT[:], modT[:], coltmp[:, 0:24].unsqueeze(2).to_broadcast([128, 24, 17]), ALU.add, [modT, coltmp], [modT])
    P.dma("sp", xm[0:1, :], g_pre, writes=[xm], chan="xm")
    row_to_cols(xm[0:1, :], coltmp, coltmp[:, 24:32], 8)
    stt(G1col[:], modT[:, 8:16, 0], 1.0, coltmp[:, 24:32], ALU.add, ALU.mult, [modT, coltmp], [G1col])
    cp("dve", SHcol[:], modT[:, 0:8, 0], [modT], [SHcol])
    ts("dve", G1S[:], modT[:, 8:16, 1:17], 1.0, None, ALU.add, None, [modT], [G1S])
    tt("dve", G1S[:], G1S[:], coltmp[:, 24:32].unsqueeze(2).to_broadcast([128, 8, 16]), ALU.mult, [G1S, coltmp], [G1S])
    cp("dve", SHS[:], modT[:, 0:8, 1:17], [modT], [SHS])
    P.dma("sp", xm[0:1, :], g_post, writes=[xm], chan="xm")
    row_to_cols(xm[0:1, :], coltmp, coltmp[:, 0:8], 8)
    tt("dve", coltmp[:, 8:16], coltmp[:, 0:8], modT[:, 16:24, 0], ALU.mult, [coltmp, modT], [coltmp])
    tt("dve", GST[:], modT[:, 16:24, 1:17], coltmp[:, 0:8].unsqueeze(2).to_broadcast([128, 8, 16]), ALU.mult,
       [modT, coltmp], [GST])
    for c in range(8):
        cp("dve", bcm[:], coltmp[:, 8 + c:9 + c].to_broadcast([128, 128]), [coltmp], [bcm])
        tr(ps[2 + c // 4][:, (c % 4) * 128:(c % 4 + 1) * 128], bcm[:], ident[:], [bcm, ident], [ps[2 + c // 4]])
    cp("dve", GP[:, 0:512], ps[2][:], [ps[2]], [GP])
    cp("dve", GP[:, 512:1024], ps[3][:], [ps[3]], [GP])
    for k in range(3):
        P.dma("sp", xm[0:1, 0:512], conv_w[k:k + 1, :], writes=[xm], chan="xm")
        mm(ps[4][:], ones1[0:1, :], xm[0:1, 0:512], True, True, [ones1, xm], [ps[4]])
        cp("dve", convwb[:, k, :], ps[4][:], [ps[4]], [convwb])
    tr(ps[5][:, 0:32], pet[:].rearrange("r k d -> r (k d)"), ident[0:32, 0:32], [pet, ident], [ps[5]])
    cp("dve", peT[:], ps[5][:, 0:32], [ps[5]], [peT])
    for kv in range(2):
        for half in range(2):
            for r in range(32):
                mm(ps[6 + kv][:, half:half + 1], w1[kv * 64:(kv + 1) * 64, r, half * 128:(half + 1) * 128],
                   peT[kv * 64:(kv + 1) * 64, r:r + 1], r == 0, r == 31, [w1, peT], [ps[6 + kv]])
    for kv in range(2):
        cp("dve", bias1[:, kv, :], ps[6 + kv][:, 0:2], [ps[6 + kv]], [bias1])

    def load_x(buf, src_ap, chan, rows=128):
        P.dma("sp", buf[0:rows, :], src_ap, writes=[buf], chan=chan)

    def norm_T(xb, hTb, rows=128):
        act(xn[0:rows, :], xb[0:rows, :], AF.Square, [xb], [xn, ss], accum_out=ss[0:rows, 0:1])
        ts("dve", ss[0:rows, 1:2], ss[0:rows, 0:1], 1.0 / D, 1e-6, ALU.mult, ALU.add, [ss], [ss])
        act(ss[0:rows, 2:3], ss[0:rows, 1:2], AF.Sqrt, [ss], [ss])
        P.op("dve", lambda e: e.reciprocal(out=ss[0:rows, 3:4], in_=ss[0:rows, 2:3]), [ss], [ss])
        ts("dve", xn[0:rows, :], xb[0:rows, :], ss[0:rows, 3:4], None, ALU.mult, None, [xb, ss], [xn])
        for c in range(8):
            tr(ps[c // 4][:, (c % 4) * 128:(c % 4) * 128 + rows], xn[0:rows, c * 128:(c + 1) * 128],
               ident[0:rows, 0:rows], [xn, ident], [ps[c // 4]])
        for c in range(8):
            act(hTb[:, c, 0:rows], ps[c // 4][:, (c % 4) * 128:(c % 4) * 128 + rows], AF.Identity,
                [ps[c // 4], G1col, SHcol], [hTb], scale=G1col[:, c:c + 1], bias=SHcol[:, c:c + 1])

    def rope_inplace(zb, base_views, tab_ap, rows=128):
        for v in base_views:
            n = v.shape[1]
            x1 = v[:, :, 0:8]
            x2 = v[:, :, 8:16]
            cs = tab_ap[:, 0:8].unsqueeze(1).to_broadcast([rows, n, 8])
            sn = tab_ap[:, 8:16].unsqueeze(1).to_broadcast([rows, n, 8])
            t = [rt[0:rows, i, 0:n * 8].rearrange("p (n e) -> p n e", e=8) for i in range(4)]
            tt("dve", t[0], x1, cs, ALU.mult, [zb], [rt])
            tt("dve", t[1], x2, sn, ALU.mult, [zb], [rt])
            tt("dve", t[2], x1, sn, ALU.mult, [zb], [rt])
            tt("dve", t[3], x2, cs, ALU.mult, [zb], [rt])
            tt("dve", x1, t[0], t[1], ALU.subtract, [rt], [zb])
            tt("dve", x2, t[2], t[3], ALU.add, [rt], [zb])

    def kv_proj(hTb, tab_ap, rows=128):
        for half in range(2):
            for k in range(8):
                mm(ps[2 + half][0:rows, 0:384], hTb[:, k, 0:rows], wkv[:, k, half * 384:(half + 1) * 384], k == 0, k == 7,
                   [hTb, wkv], [ps[2 + half]])
        cp("act", zkv[0:rows, 0:384], ps[2][0:rows, 0:384], [ps[2]], [zkv])
        cp("dve", zkv[0:rows, 384:768], ps[3][0:rows, 0:384], [ps[3]], [zkv])
        views = [zkv[0:rows, o:o + 128].rearrange("p (g d) -> p g d", g=2)[:, :, 0:16] for o in (0, 256, 512)]
        rope_inplace(zkv, views, tab_ap, rows=rows)

    def colmax_update(buf, ap, prt, bi, n):
        P.op("dve", lambda e: e.max(out=cmt8[prt, 0:8], in_=ap), [buf], [cmt8])
        tt("dve", cm[prt, bi:bi + 1], cm[prt, bi:bi + 1], cmt8[prt, 0:1], ALU.max, [cm, cmt8], [cm])
        ts("dve", cmneg[prt, 0:n], ap, -1.0, None, ALU.mult, None, [buf], [cmneg])
        P.op("dve", lambda e: e.max(out=cmt8[prt, 8:16], in_=cmneg[prt, 0:n]), [cmneg], [cmt8])
        tt("dve", cm[prt, bi:bi + 1], cm[prt, bi:bi + 1], cmt8[prt, 8:9], ALU.max, [cm, cmt8], [cm])

    KST = ""

    def phase_a(t):
        xb = xt[t % 2]
        hTb = hT[t % 2]
        norm_T(xb, hTb)
        if t + 1 < nt:
            load_x(xt[(t + 1) % 2], xp[(t + 1) * 128:(t + 2) * 128, :], "xt0")
        kv_proj(hTb, ropeA[:, t, :])
        ingest(t, zkv, True)

    def ingest(t, zb, has_win):
        tr(ps[4][:, 0:128], zb[:, 256:384], ident[:], [zb, ident], [ps[4]])
        if has_win:
            tr(ps[4][:, 128:256], zb[:, 512:640], ident[:], [zb, ident], [ps[4]])
        cp("dve", ctsrc[:].rearrange("p (g k d) -> p g k d", g=2, k=2),
           zb[:, 0:256].rearrange("p (k g d) -> p g k d", k=2, g=2), [zb], [ctsrc])
        for g in range(2):
            tr(ps[4][:, 256 + g * 128:384 + g * 128], ctsrc[:, g * 128:(g + 1) * 128], ident[:], [ctsrc, ident], [ps[4]])
        cp("act", KselT[:, t * 128:(t + 1) * 128], ps[4][:, 0:128], [ps[4]], [(KselT, t)])
        if has_win:
            cp("act", KwinT[:, (t % 8) * 128:(t % 8 + 1) * 128], ps[4][:, 128:256], [ps[4]], [(KwinT, t % 8)])
        tq = t % 4
        for g in range(2):
            cp("act", CT[g][:, 16 + tq * 128:144 + tq * 128], ps[4][:, 256 + g * 128:384 + g * 128], [ps[4]], [CT[g]])
        for bi, o in (((1, 0), (2, 128)) if has_win else ((1, 0),)):
            colmax_update(ps[4], ps[4][:, o:o + 128], slice(0, 128), bi, 128)
        cp("dve", Vsel[:, t, :, 0:64], zb[:, 384:512].rearrange("p (g d) -> p g d", g=2), [zb], [(Vsel, t)])
        if has_win:
            cp("dve", Vwin[:, t % 8, :, 0:64], zb[:, 640:768].rearrange("p (g d) -> p g d", g=2), [zb], [(Vwin, t % 8)])
        if tq != 3:
            return
        t0 = t - 3
        m0 = 1 if t0 == 0 else 0
        nb = 32 - m0
        i0 = 8 * t0 - 1 + m0
        kt0 = i0 // 128
        c0 = i0 - kt0 * 128
        for g in range(2):
            for kv in range(2):
                for half in range(2):
                    col = (g * 2 + half) * 32
                    for r in range(32):
                        rhs = CT[g][kv * 64:(kv + 1) * 64, r:r + 16 * 31 + 1:16]
                        mm(ps[5 + 2 * kv][:, col:col + 32], w1[kv * 64:(kv + 1) * 64, r, half * 128:(half + 1) * 128], rhs,
                           r == 0, r == 31, [w1, CT[g]], [ps[5 + 2 * kv]])
            ms("pool", hsTp[:], 0.0, [hsTp])
            for kv in range(2):
                for half in range(2):
                    col = (g * 2 + half) * 32
                    act(hsT[:, kv, half, :], ps[5 + 2 * kv][:, col:col + 32], AF.Silu, [ps[5 + 2 * kv], bias1], [hsT],
                        bias=bias1[:, kv, half:half + 1])
            cp("dve", hsTp[:, :, c0:c0 + nb], hsT[:, 1, :, m0:32], [hsT], [hsTp])
            for half in range(2):
                mm(ps[6][g * 64:(g + 1) * 64, 0:32], w2[:, 0, half, :], hsT[:, 0, half, :], half == 0, half == 1,
                   [w2, hsT], [ps[6]])
            cp("act", kcT[g * 64:(g + 1) * 64, i0:i0 + nb], ps[6][g * 64:(g + 1) * 64, m0:32], [ps[6]], [kcT])
            colmax_update(ps[6], ps[6][g * 64:(g + 1) * 64, 0:32], slice(g * 64, (g + 1) * 64), 0, 32)
            for w in range(2):
                if c0 + nb <= w * 128 or c0 >= (w + 1) * 128 or kt0 + w > 3:
                    continue
                for half in range(2):
                    mm(ps[7][:, 0:64], hsTp[:, half, w * 128:(w + 1) * 128], w2[:, 1, half, :], half == 0, half == 1,
                       [hsTp, w2], [ps[7]])
                tt("dve", vcf[:, kt0 + w, g, :], vcf[:, kt0 + w, g, :], ps[7][:, 0:64], ALU.add, [vcf, ps[7]], [vcf])
                cp("dve", vca[:, kt0 + w, g, 0:64], vcf[:, kt0 + w, g, :], [vcf], [vca])
        for g in range(2):
            cp("dve", CT[g][:, 0:16], CT[g][:, 512:528], [CT[g]], [CT[g]])

    xpv = P.sb("xpv", [32, D], F32)
    hTm = P.sb("hTm", [128, 8, 128], BF16)
    hTp = P.sb("hTp", [128, 8, 32], BF16)
    qf = P.sb("qf", [128, 512], F32)
    QT = P.sb("QT", [128, 512], BF16)
    aQT = P.sb("aQT", [128, 512], BF16)
    gates = P.sb("gates", [128, 24], F32)
    sa = P.sb("sa", [128, 512], F32)
    cbf = P.sb("cbf", [128, 512], F32)
    ccf = P.sb("ccf", [128, 512], F32)
    uu = P.sb("uu", [128, 512], F32)
    ccp = P.sb("ccp", [32, 512], F32)
    up = P.sb("up", [32, 512], F32)
    co = P.sb("co", [128, 512], F32)
    sgc = P.sb("sgc", [128, 512], F32)
    obT = P.sb("obT", [128, 4, 128], BF16)
    oat = P.sb("oat", [128, 512], F32)
    oaT = P.sb("oaT", [128, 4, 128], BF16)
    mbuf = P.sb("mbuf", [128, D], F32)
    mT = P.sb("mT", [128, 8, 128], BF16)
    yout = mbuf
    PTb = [P.sb("PT%d" % i, [128, 512], BF16) for i in range(3)]
    Ls = P.sb("Ls", [128, 16, 128], BF16)
    Lx = P.sb("Lx", [128, 128], F32)
    Lt = P.sb("Lt", [128, 128], F32)
    Lexp = [P.sb("Lexp%d" % i, [128, 1024], BF16) for i in range(2)]
    slotc = P.sb("slotc", [128, 3, 128], F32)
    cmpb = P.sb("cmpb", [128, 4, 128], F32)
    negm = P.sb("negm", [128, 4], F32)
    sc = [P.sb("sc%d" % i, [128, 128], F32) for i in range(3)]
    Lselb = P.sb("Lselb", [128, 128], BF16)
    m8 = P.sb("m8", [128, 16], F32)
    rc = P.sb("rc", [128, 16], F32)

    chunk_seq = [(n, j) for n in range(nslot + (1 if do_sample else 0)) for j in (0, 1, 2, 3, 4, 5, 10, 6, 7, 11, 8, 9, 12, 13)]
    st = {"issued": 0, "used": 0, "u": 0}

    def issue_chunks(upto):
        while st["issued"] < min(upto, len(chunk_seq)):
            k = st["issued"]
            P.dma("sp", wch[k % NWB][:], wscr[chunk_seq[k][1]], reads=[(wscr, chunk_seq[k][1])], writes=[wch[k % NWB]],
                  chan="wst%d" % (k % NWB))
            st["issued"] += 1

    def next_chunk(expect_j):
        k = st["used"]
        assert chunk_seq[k][1] == expect_j, (chunk_seq[k], expect_j)
        issue_chunks(k + 1)
        st["used"] += 1
        return wch[k % NWB], k

    def zchunk(j, bank, lh=None, rows=128):
        wb, k = next_chunk(j)
        wv = wb[:].rearrange("p (c n) -> p c n", c=8)
        for kc in range(8):
            mm(bank[0:128, :], hTm[:, kc, :], wv[:, kc, :], kc == 0, kc == 7, [hTm, wb], [bank])
        return wb, wv, k

    def attention(n, g, samp=None):
        gr = slice(g * 64, (g + 1) * 64)
        nkt = 4 * n + 4 if samp is None else 17
        nfull = 4 * n if samp is None else 16
        pm = ps[2] if g == 0 else ps[1]
        sbanks = (ps[3], ps[4]) if g == 0 else (ps[0], ps[1])
        for br in range(4):
            for hh in range(4):
                mm(pm[:, br * 4 + hh:br * 4 + hh + 1], aQT[gr, hh * 128:(hh + 1) * 128], cmb[gr, min(br, 2):min(br, 2) + 1], True, True,
                   [aQT, cmb], [pm])
        for br in range(3):
            P.op("dve", lambda e, br=br: e.max(out=m8[:, 0:8], in_=pm[:, br * 4:br * 4 + 8]), [pm], [m8])
            ts("dve", negm[:, br:br + 1], m8[:, 0:1], -1.0, None, ALU.mult, None, [m8], [negm])
        ts("dve", negm[:, 3:4], negm[:, 1:2], -BIG8, None, ALU.add, None, [negm], [negm])

        pend = []

        def flush():
            while pend:
                pend.pop(0)()

        def unit(Kt, Lap, Lbuf, Vap, Vbuf, Kbuf, obank, first, last, ovl_kt=None):
            u = st["u"]
            st["u"] += 1
            S = sbanks[u % 2]
            PT = PTb[u % 3]
            mm(S[:], Kt, QT[gr, :], True, False, [Kbuf, QT], [S])
            mm(S[:], Lap, I4[:], False, True, [Lbuf, I4], [S])
            act(PT[:], S[:], AF.Exp, [S], [PT], scale=0.125)

            def stage2():
                for hh in range(4):
                    mm(obank[:, hh * 65:(hh + 1) * 65], PT[:, hh * 128:(hh + 1) * 128], Vap, first and hh == 0, last,
                       [PT, Vbuf], [obank], skip_group_check=True)
                if ovl_kt is not None:
                    for hh in range(4):
                        mm(ps[6][:, hh * 128:(hh + 1) * 128], PT[:, hh * 128:(hh + 1) * 128], ovl[:, ovl_kt, :],
                           first and hh == 0, last, [PT, ovl], [ps[6]], skip_group_check=True)
            flush()
            pend.append(stage2)

        def finish_branch(obank, br, first_branch):
            ov = obank[:, 0:260].rearrange("p (h e) -> p h e", e=65)
            ts("dve", rc[:, 0:4], ov[:, :, 64], 1e-30, None, ALU.max, None, [obank], [rc])
            P.op("dve", lambda e: e.reciprocal(out=rc[:, 4:8], in_=rc[:, 0:4]), [rc], [rc])
            gv = gates[:, g * 12:(g + 1) * 12].rearrange("p (h b) -> p h b", b=3)[:, :, br]
            tt("dve", rc[:, 8:12], rc[:, 4:8], gv, ALU.mult, [rc, gates], [rc])
            if samp is not None:
                ts("dve", rc[:, 8:12], rc[:, 8:12], onehot[:, samp:samp + 1], None, ALU.mult, None, [rc, onehot], [rc])
                first_branch = False
            for hh in range(4):
                dst = oat[:, g * 256 + hh * 64:g * 256 + (hh + 1) * 64]
                if first_branch:
                    ts("dve", dst, ov[:, hh, 0:64], rc[:, 8 + hh:9 + hh], None, ALU.mult, None, [obank, rc], [oat])
                else:
                    stt(dst, ov[:, hh, 0:64], rc[:, 8 + hh:9 + hh], dst, ALU.mult, ALU.add, [obank, rc, oat], [oat])

        ktmax = min(3, (32 * n + 30) // 128) if samp is None else 0
        for kt in range(ktmax + 1):
            if samp is None:
                ts("dve", Ls[:, kt, :], cmpb[:, kt, :], negm[:, 0:1], None, ALU.add, None, [cmpb, negm], [(Ls, kt)])
            else:
                ts("dve", Ls[:, kt, :], smask[:, 0, :], negm[:, 0:1], None, ALU.add, None, [smask, negm], [(Ls, kt)])
        for kt in range(ktmax + 1):
            unit(kcT[gr, kt * 128:(kt + 1) * 128], Ls[:, kt, :], (Ls, kt), vca[:, kt, g, :], vca, kcT, ps[5],
                 kt == 0, kt == ktmax, ovl_kt=kt)
        flush()
        ov = ps[5][:, 0:260].rearrange("p (h e) -> p h e", e=65)
        ts("dve", rc[:, 12:16], ov[:, :, 64], 1e-30, None, ALU.max, None, [ps[5]], [rc])
        P.op("dve", lambda e: e.reciprocal(out=rc[:, 12:16], in_=rc[:, 12:16]), [rc], [rc])
        ts("dve", sc[0][:], ps[6][:, 0:128], rc[:, 12:13], None, ALU.mult, None, [ps[6], rc], [sc[0]])
        for hh in range(1, 4):
            stt(sc[0][:], ps[6][:, hh * 128:(hh + 1) * 128], rc[:, 12 + hh:13 + hh], sc[0][:], ALU.mult, ALU.add,
                [ps[6], rc, sc[0]], [sc[0]])
        finish_branch(ps[5], 0, True)
        tt("dve", sc[0][:], sc[0][:], slotc[:, 0, :], ALU.mult, [sc[0], slotc], [sc[0]])
        tt("dve", sc[0][:], sc[0][:], slotc[:, 1, :], ALU.add, [sc[0], slotc], [sc[0]])
        tt("dve", sc[0][:], sc[0][:], slotc[:, 2, :], ALU.max, [sc[0], slotc], [sc[0]])
        ms("dve", sc[0][:, 0:1], 1e6, [sc[0]])
        P.op("dve", lambda e: e.max(out=m8[:, 0:8], in_=sc[0][:]), [sc[0]], [m8])
        P.op("dve", lambda e: e.match_replace(out=sc[1][:], in_to_replace=m8[:, 0:8], in_values=sc[0][:], imm_value=-1e30),
             [sc[0], m8], [sc[1]])
        P.op("dve", lambda e: e.max(out=m8[:, 8:16], in_=sc[1][:]), [sc[1]], [m8])
        ts("dve", sc[2][:], sc[0][:], m8[:, 15:16], BIG8, ALU.is_ge, ALU.mult, [sc[0], m8], [sc[2]])
        ts("dve", Lselb[:], sc[2][:], negm[:, 3:4], None, ALU.add, None, [sc[2], negm], [Lselb])
        wk = [k for k in range(8) if 4 * n - 4 + k >= 0] if samp is None else [0, 1, 2, 3, 4]
        for k in wk:
            if samp is None:
                ts("dve", Ls[:, 4 + k, :], winb[:, k, :], negm[:, 2:3], None, ALU.add, None, [winb, negm], [(Ls, 4 + k)])
            else:
                mi = (2, 3, 3, 3, 1)[k]
                ts("dve", Ls[:, 4 + k, :], smask[:, mi, :], negm[:, 2:3], None, ALU.add, None, [smask, negm], [(Ls, 4 + k)])
        for k in wk:
            kt = 4 * n - 4 + k if samp is None else k
            unit(KwinT[gr, (kt % 8) * 128:(kt % 8 + 1) * 128], Ls[:, 4 + k, :], (Ls, 4 + k), Vwin[:, kt % 8, g, :],
                 (Vwin, kt % 8), (KwinT, kt % 8), ps[7], k == wk[0], k == wk[-1])
        flush()
        finish_branch(ps[7], 2, False)
        for kt in range(nkt):
            first, last = kt == 0, kt == nkt - 1
            if kt < nfull:
                c, o = divmod(kt, 8)
                if o == 0:
                    nb16 = min(16, 2 * nfull - 16 * c)
                    cp("pool", Lexp[c % 2][:, 0:nb16 * 64].rearrange("p (j e) -> p j e", e=64),
                       Lselb[:, 16 * c:16 * c + nb16].unsqueeze(2).to_broadcast([128, nb16, 64]), [Lselb], [Lexp[c % 2]])
                Lap, Lbuf = Lexp[c % 2][:, o * 128:(o + 1) * 128], Lexp[c % 2]
            elif samp is not None:
                ts("dve", Ls[:, 12, :], smask[:, 1, :], negm[:, 1:2], None, ALU.add, None, [smask, negm], [(Ls, 12)])
                Lap, Lbuf = Ls[:, 12, :], (Ls, 12)
            else:
                kr = kt - 4 * n
                cp("dve", Lx[:].rearrange("p (j e) -> p j e", e=64),
                   Lselb[:, 2 * kt:2 * kt + 2].unsqueeze(2).to_broadcast([128, 2, 64]), [Lselb], [Lx])
                stt(Lt[:], selc[:, 1, kr, :], negm[:, 1:2], selc[:, 2, kr, :], ALU.mult, ALU.add, [selc, negm], [Lt])
                tt("dve", Lx[:], Lx[:], selc[:, 0, kr, :], ALU.mult, [Lx, selc], [Lx])
                tt("dve", Ls[:, 12 + kr, :], Lx[:], Lt[:], ALU.add, [Lx, Lt], [(Ls, 12 + kr)])
                Lap, Lbuf = Ls[:, 12 + kr, :], (Ls, 12 + kr)
            unit(KselT[gr, kt * 128:(kt + 1) * 128], Lap, Lbuf, Vsel[:, kt, g, :], (Vsel, kt), (KselT, kt), ps[5],
                 first, last)
        flush()
        finish_branch(ps[5], 1, False)

    def phase_b(n):
        load_x(xm, xmine[n * 128:(n + 1) * 128, :], "xm")
        load_x(xpv, xprev[n * 32:(n + 1) * 32, :], "xpv", rows=32)
        P.dma("sp", slotc[:], slotc_d[n], writes=[slotc], chan="slotc")
        P.dma("sp", cmpb[:], cmpb_d[n], writes=[cmpb], chan="cmpb")
        norm_T(xm, hTm)
        norm_T(xpv, hTp, rows=32)
        kv_proj(hTm, ropeB[:, n, :])
        P.dma("sp", kvp[n * 128:(n + 1) * 128, :], zkv[:, 0:512], reads=[zkv], chan="zkvo")
        if n == nslot - 1:
            P.dma("sp", winp, zkv[:, 512:768], reads=[zkv], chan="zkvo")
        cp("dve", cmb[:], cm[:], [cm], [cmb])
        zchunk(0, ps[0])
        cp("act", qf[:], ps[0][:], [ps[0]], [qf])
        rope_inplace(qf, [qf[:].rearrange("p (h d) -> p h d", d=64)[:, :, 0:16]], ropeB[:, n, :])
        cp("pool", sgc[:].rearrange("p (h g d) -> p h g d", h=4, g=2),
           qf[:].rearrange("p (g h d) -> p h g d", g=2, h=4), [qf], [sgc])
        for jj in range(4):
            tr(ps[2][:, jj * 128:(jj + 1) * 128], sgc[:, jj * 128:(jj + 1) * 128], ident[:], [sgc, ident], [ps[2]])
        cp("act", QT[:], ps[2][:], [ps[2]], [QT])
        act(aQT[:], ps[2][:], AF.Abs, [ps[2]], [aQT])
        for kc in range(8):
            mm(ps[1][:, 0:24], hTm[:, kc, :], wg[:, kc, :], kc == 0, kc == 7, [hTm, wg], [ps[1]])
        act(gates[:], ps[1][:, 0:24], AF.Sigmoid, [ps[1]], [gates])
        zchunk(1, ps[0])
        act(sa[:], ps[0][:], AF.Silu, [ps[0]], [sa])
        zchunk(2, ps[1])
        cp("act", cbf[:], ps[1][:], [ps[1]], [cbf])
        wb, wv, _ = zchunk(3, ps[0])
        for kc in range(8):
            mm(ps[2][0:32, :], hTp[:, kc, :], wv[:, kc, :], kc == 0, kc == 7, [hTp, wb], [ps[2]])
        cp("act", ccf[:], ps[0][:], [ps[0]], [ccf])
        cp("act", ccp[:], ps[2][0:32, :], [ps[2]], [ccp])
        wb, wv, _ = zchunk(4, ps[1])
        for kc in range(8):
            mm(ps[2][0:32, :], hTp[:, kc, :], wv[:, kc, :], kc == 0, kc == 7, [hTp, wb], [ps[2]])
        tt("dve", uu[:], ccf[:], ps[1][:], ALU.mult, [ccf, ps[1]], [uu])
        stt(up[:], ps[2][0:32, :], upsc[:, n:n + 1], ccp[:], ALU.mult, ALU.mult, [ps[2], upsc, ccp], [up])
        if n == nslot - 1:
            P.dma("sp", convp, uu[126:128, :], reads=[uu], chan="uuo")
        for s_ in range(2):
            mm(ps[2 + s_][:], shm[:, s_, :], uu[:], True, False, [shm, uu], [ps[2 + s_]])
            mm(ps[2 + s_][:], shb[:, s_, :], up[:], False, True, [shb, up], [ps[2 + s_]])
        tt("dve", co[:], uu[:], convwb[:, 2, :], ALU.mult, [uu, convwb], [co])
        tt("dve", qf[:], ps[2][:], convwb[:, 1, :], ALU.mult, [ps[2], convwb], [qf])
        tt("dve", co[:], co[:], qf[:], ALU.add, [co, qf], [co])
        tt("dve", qf[:], ps[3][:], convwb[:, 0, :], ALU.mult, [ps[3], convwb], [qf])
        tt("dve", co[:], co[:], qf[:], ALU.add, [co, qf], [co])
        zchunk(5, ps[0])
        act(sgc[:], ps[0][:], AF.Silu, [ps[0]], [sgc])
        tt("dve", co[:], co[:], cbf[:], ALU.mult, [co, cbf], [co])
        tt("dve", co[:], co[:], sgc[:], ALU.mult, [co, sgc], [co])
        for c in range(4):
            tr(ps[1][:, c * 128:(c + 1) * 128], co[:, c * 128:(c + 1) * 128], ident[:], [co, ident], [ps[1]])
        cp("act", obT[:].rearrange("p c n -> p (c n)"), ps[1][:], [ps[1]], [obT])
        for g in range(2):
            attention(n, g)
        tt("dve", oat[:], oat[:], sa[:], ALU.mult, [oat, sa], [oat])
        for c in range(4):
            tr(ps[2][:, c * 128:(c + 1) * 128], oat[:, c * 128:(c + 1) * 128], ident[:], [oat, ident], [ps[2]])
        cp("act", oaT[:].rearrange("p c n -> p (c n)"), ps[2][:], [ps[2]], [oaT])
        for (jw, srcT, jg) in ((10, oaT, (6, 7)), (11, obT, (8, 9))):
            wb, k = next_chunk(jw)
            wv = wb[:].rearrange("p (c n) -> p c n", c=4)
            for half in range(2):
                for kc in range(4):
                    mm(ps[0 + half][:], srcT[:, kc, :], wv[:, kc, half * 512:(half + 1) * 512], kc == 0, kc == 3,
                       [srcT, wb], [ps[half]])
            for half in range(2):
                zchunk(jg[half], ps[2 + half])
                act(sgc[:], ps[2 + half][:], AF.Sigmoid, [ps[2 + half]], [sgc])
                dst = mbuf[:, half * 512:(half + 1) * 512]
                if jw == 10:
                    tt("dve", dst, sgc[:], ps[half][:], ALU.mult, [sgc, ps[half]], [mbuf])
                else:
                    tt("dve", qf[:], sgc[:], ps[half][:], ALU.mult, [sgc, ps[half]], [qf])
                    tt("dve", dst, dst, qf[:], ALU.add, [mbuf, qf], [mbuf])
        for c in range(8):
            tr(ps[4 + c // 4][:, (c % 4) * 128:(c % 4 + 1) * 128], mbuf[:, c * 128:(c + 1) * 128], ident[:],
               [mbuf, ident], [ps[4 + c // 4]])
        cp("act", mT[:, 0:4, :].rearrange("p c n -> p (c n)"), ps[4][:], [ps[4]], [mT])
        cp("dve", mT[:, 4:8, :].rearrange("p c n -> p (c n)"), ps[5][:], [ps[5]], [mT])
        for half in range(2):
            wb, k = next_chunk(12 + half)
            wv = wb[:].rearrange("p (c n) -> p c n", c=8)
            for kc in range(8):
                mm(ps[6 + half][:], mT[:, kc, :], wv[:, kc, :], kc == 0, kc == 7, [mT, wb], [ps[6 + half]])
        issue_chunks(st["used"] + NWB)
        for half in range(2):
            act(qf[:], ps[6 + half][:], AF.Square, [ps[6 + half]], [qf, ss],
                accum_out=ss[:, 4 + half:5 + half])
        tt("dve", ss[:, 6:7], ss[:, 4:5], ss[:, 5:6], ALU.add, [ss], [ss])
        ts("dve", ss[:, 6:7], ss[:, 6:7], 1.0 / D, 1e-6, ALU.mult, ALU.add, [ss], [ss])
        act(ss[:, 7:8], ss[:, 6:7], AF.Sqrt, [ss], [ss])
        P.op("dve", lambda e: e.reciprocal(out=ss[:, 6:7], in_=ss[:, 7:8]), [ss], [ss])
        for half in range(2):
            sl = slice(half * 512, (half + 1) * 512)
            stt(yout[:, sl], ps[6 + half][:], ss[:, 6:7], GP[:, sl], ALU.mult, ALU.mult, [ps[6 + half], ss, GP], [yout])
        tt("dve", yout[:], yout[:], xm[:], ALU.add, [yout, xm], [yout])
        P.dma("sp", yp[n * 128:(n + 1) * 128, :], yout[:], reads=[yout], chan="yout")


    pgb = [P.sb("pgb%d" % i, [128, 512], F32) for i in range(2)]
    idxf = P.sb("idxf", [128, 256], F32)
    idxi = P.sb("idxi", [128, 256], I32)
    pcol = P.sb("pcol", [128, 1], F32)
    smask = P.sb("smask", [128, 4, 128], F32)
    onehot = P.sb("onehot", [128, 16], F32)
    ropeS = P.sb("ropeS", [16, 16], F32)
    qsT = P.sb("qsT", [128, 4, 16], BF16)
    aqsT = P.sb("aqsT", [128, 4, 16], BF16)
    swt = ccf

    def phase_s():
        R = 16
        P.dma("sp", slotc[:], slotS_d, writes=[slotc], chan="slotc")
        P.dma("sp", smask[:], smask_d, writes=[smask], chan="c1")
        P.dma("sp", onehot[:], onehot_d, writes=[onehot], chan="c1")
        P.dma("sp", ropeS[:], ropeS_d, writes=[ropeS], chan="c1")
        P.dma("sp", pcol[:], pcol_d, writes=[pcol], chan="c1")
        P.dma("sp", idxi[:], ptab_d.to_broadcast([128, 256]), writes=[idxi], chan="c1")
        cp("dve", idxf[:], idxi[:], [idxi], [idxf])
        ts("dve", idxf[:], idxf[:], 128.0, pcol[:, 0:1], ALU.mult, ALU.add, [idxf, pcol], [idxf])
        cp("dve", idxi[:], idxf[:], [idxf], [idxi])
        load_x(xm, xs_d, "xm", rows=R)
        act(xn[0:R, :], xm[0:R, :], AF.Square, [xm], [xn, ss], accum_out=ss[0:R, 0:1])
        ts("dve", ss[0:R, 1:2], ss[0:R, 0:1], 1.0 / D, 1e-6, ALU.mult, ALU.add, [ss], [ss])
        act(ss[0:R, 2:3], ss[0:R, 1:2], AF.Sqrt, [ss], [ss])
        P.op("dve", lambda e: e.reciprocal(out=ss[0:R, 3:4], in_=ss[0:R, 2:3]), [ss], [ss])
        ts("dve", xn[0:R, :], xm[0:R, :], ss[0:R, 3:4], None, ALU.mult, None, [xm, ss], [xn])
        for c in range(8):
            tr(ps[0][:, c * R:(c + 1) * R], xn[0:R, c * 128:(c + 1) * 128], ident[0:R, 0:R], [xn, ident], [ps[0]])
        tt("dve", bcm[:], ps[0][:, 0:128], G1S[:].rearrange("p c s -> p (c s)"), ALU.mult, [ps[0], G1S], [bcm])
        tt("dve", hTm[:, :, 0:R], bcm[:].rearrange("p (c s) -> p c s", s=R), SHS[:], ALU.add, [bcm, SHS], [hTm])
        kv_proj(hTm, ropeS[:, :], rows=R)
        P.dma("sp", kvs, zkv[0:R, 0:512], reads=[zkv], chan="zkvo")
        P.dma("sp", wins[:, 511, :], zkv[0:R, 512:768], reads=[zkv], chan="zkvo")
        P.dma("sp", convs[:, 0, :], sconv_d[:, 1, :], chan="d2d")
        tr(ps[4][:, 0:R], zkv[0:R, 256:384], ident[0:R, 0:R], [zkv, ident], [ps[4]])
        tr(ps[4][:, R:2 * R], zkv[0:R, 512:640], ident[0:R, 0:R], [zkv, ident], [ps[4]])
        cp("act", KselT[:, 2048:2048 + R], ps[4][:, 0:R], [ps[4]], [(KselT, 16)])
        cp("act", KwinT[:, 512:512 + R], ps[4][:, R:2 * R], [ps[4]], [(KwinT, 4)])
        colmax_update(ps[4], ps[4][:, 0:R], slice(0, 128), 1, R)
        colmax_update(ps[4], ps[4][:, R:2 * R], slice(0, 128), 2, R)
        cp("dve", Vsel[0:R, 16, :, 0:64], zkv[0:R, 384:512].rearrange("p (g d) -> p g d", g=2), [zkv], [(Vsel, 16)])
        cp("dve", Vwin[0:R, 4, :, 0:64], zkv[0:R, 640:768].rearrange("p (g d) -> p g d", g=2), [zkv], [(Vwin, 4)])
        cp("dve", cmb[:], cm[:], [cm], [cmb])
        zchunk(0, ps[0])
        cp("act", qf[0:R, :], ps[0][0:R, :], [ps[0]], [qf])
        rope_inplace(qf, [qf[0:R, :].rearrange("p (h d) -> p h d", d=64)[:, :, 0:16]], ropeS[:, :], rows=R)
        cp("pool", sgc[0:R, :].rearrange("p (h g d) -> p h g d", h=4, g=2),
           qf[0:R, :].rearrange("p (g h d) -> p h g d", g=2, h=4), [qf], [sgc])
        for jj in range(4):
            tr(ps[2][:, jj * R:(jj + 1) * R], sgc[0:R, jj * 128:(jj + 1) * 128], ident[0:R, 0:R], [sgc, ident], [ps[2]])
        cp("act", qsT[:].rearrange("p j s -> p (j s)"), ps[2][:, 0:4 * R], [ps[2]], [qsT])
        act(aqsT[:].rearrange("p j s -> p (j s)"), ps[2][:, 0:4 * R], AF.Abs, [ps[2]], [aqsT])
        for kc in range(8):
            mm(ps[1][:, 0:24], hTm[:, kc, :], wg[:, kc, :], kc == 0, kc == 7, [hTm, wg], [ps[1]])
        act(gates[0:R, :], ps[1][0:R, 0:24], AF.Sigmoid, [ps[1]], [gates])
        zchunk(1, ps[0])
        act(sa[0:R, :], ps[0][0:R, :], AF.Silu, [ps[0]], [sa])
        zchunk(2, ps[1])
        cp("act", cbf[0:R, :], ps[1][0:R, :], [ps[1]], [cbf])
        zchunk(3, ps[0])
        cp("act", ccf[0:R, :], ps[0][0:R, :], [ps[0]], [ccf])
        zchunk(4, ps[1])
        tt("dve", uu[0:R, :], ccf[0:R, :], ps[1][0:R, :], ALU.mult, [ccf, ps[1]], [uu])
        P.dma("sp", convs[:, 1, :], uu[0:R, :], reads=[uu], chan="uuo")
        P.dma("sp", ccp[0:R, :], sconv_d[:, 0, :], writes=[ccp], chan="scv")
        P.dma("sp", up[0:R, :], sconv_d[:, 1, :], writes=[up], chan="scv")
        tt("dve", co[0:R, :], uu[0:R, :], convwb[0:R, 2, :], ALU.mult, [uu, convwb], [co])
        tt("dve", qf[0:R, :], up[0:R, :], convwb[0:R, 1, :], ALU.mult, [up, convwb], [qf])
        tt("dve", co[0:R, :], co[0:R, :], qf[0:R, :], ALU.add, [co, qf], [co])
        tt("dve", qf[0:R, :], ccp[0:R, :], convwb[0:R, 0, :], ALU.mult, [ccp, convwb], [qf])
        tt("dve", co[0:R, :], co[0:R, :], qf[0:R, :], ALU.add, [co, qf], [co])
        zchunk(5, ps[0])
        act(sgc[0:R, :], ps[0][0:R, :], AF.Silu, [ps[0]], [sgc])
        tt("dve", co[0:R, :], co[0:R, :], cbf[0:R, :], ALU.mult, [co, cbf], [co])
        tt("dve", co[0:R, :], co[0:R, :], sgc[0:R, :], ALU.mult, [co, sgc], [co])
        for c in range(4):
            tr(ps[1][:, c * R:(c + 1) * R], co[0:R, c * 128:(c + 1) * 128], ident[0:R, 0:R], [co, ident], [ps[1]])
        cp("act", obT[:, :, 0:R], ps[1][:, 0:4 * R].rearrange("p (c s) -> p c s", s=R), [ps[1]], [obT])
        ms("pool", hsTp[:, 0, 0:2], 0.0, [hsTp, KselT])
        ms("pool", oat[:], 0.0, [oat])
        ms("pool", QT[:], 0.0, [QT])
        ms("pool", aQT[:], 0.0, [aQT])
        k = 0
        for sm in range(R):
            CTb = [KselT[:, 4096 + g_ * 2048:4096 + (g_ + 1) * 2048] for g_ in range(2)]
            for q4 in range(4):
                for pi in range(4):
                    pgi = q4 * 4 + pi
                    pb = pgb[k % 2]
                    col = sm * 16 + pgi
                    P.op("pool", lambda e, pb=pb, col=col: e.indirect_dma_start(
                        out=pb[:], out_offset=None, in_=cache_d,
                        in_offset=bass.IndirectOffsetOnAxis(ap=idxi[:, col:col + 1], axis=0)),
                        reads=[idxi], writes=[pb], chan="pgb%d" % (k % 2))
                    k += 1
                    tr(ps[4][:, pi * 128:(pi + 1) * 128], pb[:, 256:384], ident[:], [pb, ident], [ps[4]])
                    cp("dve", ctsrc[:].rearrange("p (g k d) -> p g k d", g=2, k=2),
                       pb[:, 0:256].rearrange("p (k g d) -> p g k d", k=2, g=2), [pb], [ctsrc])
                    for g_ in range(2):
                        tr(ps[5 + g_][:, pi * 128:(pi + 1) * 128], ctsrc[:, g_ * 128:(g_ + 1) * 128], ident[:],
                           [ctsrc, ident], [ps[5 + g_]])
                    cp("dve", Vsel[:, pgi, :, 0:64], pb[:, 384:512].rearrange("p (g d) -> p g d", g=2), [pb], [(Vsel, pgi)])
                cp("act", KselT[:, q4 * 512:(q4 + 1) * 512], ps[4][:], [ps[4]], [(KselT, 4 * q4 + i_) for i_ in range(4)])
                P.op("dve", lambda e: e.max(out=cmt8[:, 0:8], in_=ps[4][:]), [ps[4]], [cmt8])
                tt("dve", cm[:, 1:2], cm[:, 1:2], cmt8[:, 0:1], ALU.max, [cm, cmt8], [cm])
                ts("dve", sgc[:], ps[4][:], -1.0, None, ALU.mult, None, [ps[4]], [sgc])
                P.op("dve", lambda e: e.max(out=cmt8[:, 8:16], in_=sgc[:]), [sgc], [cmt8])
                tt("dve", cm[:, 1:2], cm[:, 1:2], cmt8[:, 8:9], ALU.max, [cm, cmt8], [cm])
                for g_ in range(2):
                    cp("act", CTb[g_][:, q4 * 512:(q4 + 1) * 512], ps[5 + g_][:], [ps[5 + g_]], [(KselT, "ctb%d" % g_)])
            hsv = Lexp[1][:, 0:512].rearrange("p (k h n) -> p k h n", k=2, h=2)
            for g_ in range(2):
                for kv in range(2):
                    for half in range(2):
                        for r in range(32):
                            rhs = CTb[g_][kv * 64:(kv + 1) * 64, r:r + 16 * 126 + 1:16]
                            mm(ps[5 + 2 * kv][:, half * 128:half * 128 + 127],
                               w1[kv * 64:(kv + 1) * 64, r, half * 128:(half + 1) * 128], rhs, r == 0, r == 31,
                               [w1, (KselT, "ctb%d" % g_)], [ps[5 + 2 * kv]])
                for kv in range(2):
                    for half in range(2):
                        act(hsv[:, kv, half, 0:127], ps[5 + 2 * kv][:, half * 128:half * 128 + 127], AF.Silu,
                            [ps[5 + 2 * kv], bias1], [Lexp[1]], bias=bias1[:, kv, half:half + 1])
                for half in range(2):
                    mm(ps[6][g_ * 64:(g_ + 1) * 64, 0:127], w2[:, 0, half, :], hsv[:, 0, half, 0:127], half == 0, half == 1,
                       [w2, Lexp[1]], [ps[6]])
                cp("act", kcT[g_ * 64:(g_ + 1) * 64, 0:127], ps[6][g_ * 64:(g_ + 1) * 64, 0:127], [ps[6]], [kcT])
                colmax_update(ps[6], ps[6][g_ * 64:(g_ + 1) * 64, 0:127], slice(g_ * 64, (g_ + 1) * 64), 0, 127)
                for half in range(2):
                    mm(ps[4][0:127, 0:64], hsv[:, 1, half, 0:127], w2[:, 1, half, :], half == 0, half == 1,
                       [Lexp[1], w2], [ps[4]])
                cp("dve", vca[0:127, 0, g_, 0:64], ps[4][0:127, 0:64], [ps[4]], [vca])
            for i in range(4):
                P.dma("sp", swt[:, 0:256], swin_d[sm, i * 128:(i + 1) * 128, :], writes=[swt], chan="swt")
                tr(ps[4][:, 0:128], swt[:, 0:128], ident[:], [swt, ident], [ps[4]])
                cp("act", KwinT[:, i * 128:(i + 1) * 128], ps[4][:, 0:128], [ps[4]], [(KwinT, i)])
                colmax_update(ps[4], ps[4][:, 0:128], slice(0, 128), 2, 128)
                cp("dve", Vwin[:, i, :, 0:64], swt[:, 128:256].rearrange("p (g d) -> p g d", g=2), [swt], [(Vwin, i)])
            P.dma("sp", wins[sm, 0:511, :], swin_d[sm, 1:512, :], chan="d2d")
            cp("dve", cmb[:], cm[:], [cm], [cmb])
            QTv = QT[:].rearrange("p (j q) -> p j q", q=128)
            aQTv = aQT[:].rearrange("p (j q) -> p j q", q=128)
            if sm > 0:
                ms("dve", QTv[:, :, sm - 1], 0.0, [QT])
                ms("dve", aQTv[:, :, sm - 1], 0.0, [aQT])
            cp("dve", QTv[:, :, sm], qsT[:, :, sm], [qsT], [QT])
            cp("dve", aQTv[:, :, sm], aqsT[:, :, sm], [aqsT], [aQT])
            for g in range(2):
                attention(0, g, samp=sm)
        tt("dve", oat[0:R, :], oat[0:R, :], sa[0:R, :], ALU.mult, [oat, sa], [oat])
        for c in range(4):
            tr(ps[2][:, c * R:(c + 1) * R], oat[0:R, c * 128:(c + 1) * 128], ident[0:R, 0:R], [oat, ident], [ps[2]])
        cp("act", oaT[:, :, 0:R], ps[2][:, 0:4 * R].rearrange("p (c s) -> p c s", s=R), [ps[2]], [oaT])
        for (jw, srcT, jg) in ((10, oaT, (6, 7)), (11, obT, (8, 9))):
            wb, kk = next_chunk(jw)
            wv = wb[:].rearrange("p (c n) -> p c n", c=4)
            for half in range(2):
                for kc in range(4):
                    mm(ps[0 + half][:], srcT[:, kc, :], wv[:, kc, half * 512:(half + 1) * 512], kc == 0, kc == 3,
                       [srcT, wb], [ps[half]])
            for half in range(2):
                zchunk(jg[half], ps[2 + half])
                act(sgc[0:R, :], ps[2 + half][0:R, :], AF.Sigmoid, [ps[2 + half]], [sgc])
                dst = mbuf[0:R, half * 512:(half + 1) * 512]
                if jw == 10:
                    tt("dve", dst, sgc[0:R, :], ps[half][0:R, :], ALU.mult, [sgc, ps[half]], [mbuf])
                else:
                    tt("dve", qf[0:R, :], sgc[0:R, :], ps[half][0:R, :], ALU.mult, [sgc, ps[half]], [qf])
                    tt("dve", dst, dst, qf[0:R, :], ALU.add, [mbuf, qf], [mbuf])
        for c in range(8):
            tr(ps[4 + c // 4][:, (c % 4) * R:(c % 4 + 1) * R], mbuf[0:R, c * 128:(c + 1) * 128], ident[0:R, 0:R],
               [mbuf, ident], [ps[4 + c // 4]])
        cp("act", mT[:, 0:4, 0:R], ps[4][:, 0:4 * R].rearrange("p (c s) -> p c s", s=R), [ps[4]], [mT])
        cp("dve", mT[:, 4:8, 0:R], ps[5][:, 0:4 * R].rearrange("p (c s) -> p c s", s=R), [ps[5]], [mT])
        for half in range(2):
            wb, kk = next_chunk(12 + half)
            wv = wb[:].rearrange("p (c n) -> p c n", c=8)
            for kc in range(8):
                mm(ps[6 + half][:], mT[:, kc, :], wv[:, kc, :], kc == 0, kc == 7, [mT, wb], [ps[6 + half]])
        for c in range(8):
            tr(ps[2 + c // 4][0:R, (c % 4) * 128:(c % 4 + 1) * 128], GST[:, c, :], ident[:], [GST, ident], [ps[2 + c // 4]])
        cp("dve", GP[0:R, 0:512], ps[2][0:R, :], [ps[2]], [GP])
        cp("dve", GP[0:R, 512:1024], ps[3][0:R, :], [ps[3]], [GP])
        for half in range(2):
            act(qf[0:R, :], ps[6 + half][0:R, :], AF.Square, [ps[6 + half]], [qf, ss], accum_out=ss[0:R, 4 + half:5 + half])
        tt("dve", ss[0:R, 6:7], ss[0:R, 4:5], ss[0:R, 5:6], ALU.add, [ss], [ss])
        ts("dve", ss[0:R, 6:7], ss[0:R, 6:7], 1.0 / D, 1e-6, ALU.mult, ALU.add, [ss], [ss])
        act(ss[0:R, 7:8], ss[0:R, 6:7], AF.Sqrt, [ss], [ss])
        P.op("dve", lambda e: e.reciprocal(out=ss[0:R, 6:7], in_=ss[0:R, 7:8]), [ss], [ss])
        for half in range(2):
            sl = slice(half * 512, (half + 1) * 512)
            stt(mbuf[0:R, sl], ps[6 + half][0:R, :], ss[0:R, 6:7], GP[0:R, sl], ALU.mult, ALU.mult, [ps[6 + half], ss, GP], [mbuf])
        tt("dve", mbuf[0:R, :], mbuf[0:R, :], xm[0:R, :], ALU.add, [mbuf, xm], [mbuf])
        P.dma("sp", ys, mbuf[0:R, :], reads=[mbuf], chan="yout")

    stop = ""
    if stop != "p0":
        load_x(xt[0], xp[0:128, :], "xt0")
        for t in range(nt):
            phase_a(t)
            if stop.startswith("a"):
                continue
            if t % 4 == 3 and t // 4 < nslot:
                phase_b(t // 4)
    if do_sample:
        phase_s()

    P.emit()
    return nc, P


def make_consts(r):
    f = np.float32
    c = {}
    c["ident"] = np.eye(128, dtype=f)
    inv = (np.float32(500000.0) ** (-(np.arange(8, dtype=f)) / np.float32(8))).astype(f)
    p = np.arange(128)

    def rope_tab(pos):
        ang = (pos.astype(f)[..., None] * inv).astype(f)
        return np.concatenate([np.cos(ang.astype(np.float64)), np.sin(ang.astype(np.float64))], -1).astype(f)

    c["ropeA"] = rope_tab(np.arange(NT)[None, :] * 128 + p[:, None])
    tn = 4 * np.arange(NSLOT) + r
    c["ropeB"] = rope_tab(tn[None, :] * 128 + p[:, None])
    j = np.arange(128)
    slotc = np.zeros((NSLOT, 128, 3, 128), f)
    cmpb = np.zeros((NSLOT, 128, 4, 128), f)
    cidx = (np.arange(4)[:, None] * 128 + np.arange(128)[None, :])
    for n in range(NSLOT):
        b = 4 * n + r
        cur = 2 * b + (p >= 64)
        A = (j[None, :] <= cur[:, None]).astype(f)
        slotc[n, :, 0] = A
        slotc[n, :, 1] = A - 1
        slotc[n, :, 2] = np.where((j[None, :] == cur[:, None]) | (j[None, :] == cur[:, None] - 1), 1e6, -2.0)
        valid = (16 * cidx[None] + 31 <= (128 * b + p)[:, None, None]) & (cidx[None] < 511)
        cmpb[n] = np.where(valid, 0.0, -BIG8)
    c["slotc"] = slotc
    c["cmpb"] = cmpb
    q = p[:, None]
    k = p[None, :]
    winb = np.zeros((128, 8, 128), f)
    for kr in range(8):
        dt = 128 * (r + 4 - kr) + q - k
        winb[:, kr] = np.where((dt >= 0) & (dt < 512), 0.0, -BIG8)
    c["winb"] = winb
    selc = np.zeros((128, 3, 4, 128), f)
    for kr in range(4):
        if kr < r:
            selc[:, 0, kr] = 1.0
        elif kr == r:
            selc[:, 1, kr] = 1.0
            selc[:, 2, kr] = np.where(k <= q, 0.0, -BIG8)
        else:
            selc[:, 1, kr] = 1.0
            selc[:, 2, kr] = -BIG8
    c["selc"] = selc
    cs = 16 * cidx
    ov = (cs[:, :, None] < 64 * j[None, None, :] + 64) & (cs[:, :, None] + 32 > 64 * j[None, None, :]) & (cidx[:, :, None] < 511)
    c["ovl"] = np.ascontiguousarray(ov.transpose(1, 0, 2)).astype(f)
    shm = np.zeros((128, 2, 128), f)
    shb = np.zeros((32, 2, 128), f)
    for s in (1, 2):
        for m in range(128):
            kk = m - s
            if kk >= 0:
                shm[kk, s - 1, m] = 1.0
            else:
                shb[32 + kk, s - 1, m] = 1.0
    c["shm"] = shm
    c["shb"] = shb
    ups = np.ones((32, NSLOT), f)
    if r == 0:
        ups[:, 0] = 0.0
    c["upsc"] = ups
    c["ropeS"] = np.repeat(rope_tab(np.array([2048])), 16, axis=0)
    slotS = np.zeros((128, 3, 128), f)
    A = (j <= 32).astype(f)
    slotS[:, 0] = A[None, :]
    slotS[:, 1] = A[None, :] - 1
    slotS[:, 2] = np.where((j == 31) | (j == 32), 1e6, -2.0)[None, :]
    c["slotS"] = slotS
    smask = np.zeros((128, 4, 128), f)
    smask[:, 0] = np.where(k <= 126, 0.0, -BIG8)
    smask[:, 1] = np.where(k == q, 0.0, -BIG8)
    smask[:, 2] = np.where(k >= 1, 0.0, -BIG8)
    c["smask"] = smask
    oh = np.zeros((128, 16), f)
    oh[np.arange(16), np.arange(16)] = 1.0
    c["onehot"] = oh
    c["pcol"] = np.arange(128, dtype=f).reshape(128, 1)
    return c


_CACHE = {}


def kernel(x_prompt, x_sample, cache_kv_pages, state_win_kv, state_conv, page_table, c_prompt, c_sample,
           w_ada, b_ada, g_pre, g_post, w_in, pe_cmp, w_cmp1, w_cmp2, conv_w, w_br_a, w_br_b, w_out,
           _nt=NT, _nslot=NSLOT):
    f = np.float32
    key = (_nt, _nslot)
    if key not in _CACHE:
        _CACHE[key] = build(_nt, _nslot)
    nc, P = _CACHE[key]
    in_maps = []
    cache_flat = np.ascontiguousarray(cache_kv_pages[0], dtype=f).reshape(-1, 512)
    for c in range(8):
        bi, r = c // 4, c % 4
        xb = np.ascontiguousarray(x_prompt[bi], dtype=f)
        tiles = xb.reshape(NT, 128, D)
        tn = 4 * np.arange(NSLOT) + r
        xmine = np.ascontiguousarray(tiles[tn]).reshape(NSLOT * 128, D)
        xprev = np.zeros((NSLOT, 32, D), f)
        for n in range(NSLOT):
            if tn[n] > 0:
                xprev[n] = xb[tn[n] * 128 - 32:tn[n] * 128]
        m = {
            "xp": xb, "xmine": xmine, "xprev": xprev.reshape(NSLOT * 32, D),
            "cpr": np.ascontiguousarray(c_prompt[bi:bi + 1], dtype=f),
            "w_ada": np.ascontiguousarray(w_ada[0]), "b_ada": np.ascontiguousarray(b_ada), "g_pre": np.ascontiguousarray(g_pre),
            "g_post": np.ascontiguousarray(g_post), "w_in": np.ascontiguousarray(w_in[0]), "pe_cmp": np.ascontiguousarray(pe_cmp[0]),
            "w_cmp1": np.ascontiguousarray(w_cmp1[0]), "w_cmp2": np.ascontiguousarray(w_cmp2[0]),
            "conv_w": np.ascontiguousarray(conv_w[0]), "w_br_a": np.ascontiguousarray(w_br_a[0]),
            "w_br_b": np.ascontiguousarray(w_br_b[0]), "w_out": np.ascontiguousarray(w_out[0]),
        }
        m.update(make_consts(r))
        sl = slice(16 * c, 16 * c + 16)
        m["xs"] = np.ascontiguousarray(x_sample[sl, 0, :], dtype=f)
        m["cs"] = np.ascontiguousarray(c_sample[sl], dtype=f)
        m["ptab"] = np.ascontiguousarray(page_table[sl], dtype=np.int32).reshape(1, 256)
        m["cache"] = cache_flat
        m["swin"] = np.ascontiguousarray(state_win_kv[0, sl], dtype=f).reshape(16, 512, 256)
        m["sconv"] = np.ascontiguousarray(state_conv[0, sl], dtype=f)
        in_maps.append(m)
    res = run_bass_kernel_spmd(nc, in_maps, core_ids=list(range(8)))
    B, T = x_prompt.shape[0], x_prompt.shape[1]
    y_prompt = np.zeros((B, T, D), f)
    kv_rows_prompt = np.zeros((1, B, T, 4, 2, 64), f)
    win_kv_prompt = np.zeros((1, B, 512, 2, 2, 64), f)
    conv_state_prompt = np.zeros((1, B, 2, 512), f)
    for c in range(8):
        bi, r = c // 4, c % 4
        o = res.results[c]
        for n in range(NSLOT):
            t = 4 * n + r
            y_prompt[bi, t * 128:(t + 1) * 128] = o["yp"][n * 128:(n + 1) * 128]
            kv_rows_prompt[0, bi, t * 128:(t + 1) * 128] = o["kvp"][n * 128:(n + 1) * 128].reshape(128, 4, 2, 64)
        win_kv_prompt[0, bi, r * 128:(r + 1) * 128] = o["winp"].reshape(128, 2, 2, 64)
        if r == 3:
            conv_state_prompt[0, bi] = o["convp"]
    nS = x_sample.shape[0]
    y_sample = np.zeros((nS, 1, D), f)
    kv_rows_sample = np.zeros((1, nS, 1, 4, 2, 64), f)
    win_kv_sample = np.zeros((1, nS, 512, 2, 2, 64), f)
    conv_state_sample = np.zeros((1, nS, 2, 512), f)
    for c in range(8):
        o = res.results[c]
        sl = slice(16 * c, 16 * c + 16)
        y_sample[sl, 0] = o["ys"]
        kv_rows_sample[0, sl, 0] = o["kvs"].reshape(16, 4, 2, 64)
        win_kv_sample[0, sl] = o["wins"].reshape(16, 512, 2, 2, 64)
        conv_state_sample[0, sl] = o["convs"]
    return (y_prompt, y_sample, kv_rows_prompt, win_kv_prompt, conv_state_prompt, kv_rows_sample, win_kv_sample,
            conv_state_sample)
```

```python
from concourse.bass_utils import run_bass_kernel_spmd

from contextlib import ExitStack
import numpy as np
import concourse.bass as bass
import concourse.mybir as mybir

F32 = mybir.dt.float32
BF16 = mybir.dt.bfloat16
I32 = mybir.dt.int32
U32 = mybir.dt.uint32
AF = mybir.ActivationFunctionType
ALU = mybir.AluOpType
AX = mybir.AxisListType

SAME_ENG_SYNC = True
COMPUTE = ("pe", "dve", "act", "pool")


class Buf:
    def __init__(self, prog, name, t, is_dram=False):
        self.prog = prog
        self.name = name
        self.t = t
        self.is_dram = is_dram
        self.st = {}
        self.whole = [None, []]
        self.excl = False

    def __getitem__(self, idx):
        return self.t[idx]


class Op:
    __slots__ = ("eng", "fn", "deps", "signal", "val", "sem", "chan", "idx", "is_dma")

    def __init__(self, eng, fn):
        self.eng = eng
        self.fn = fn
        self.deps = {}
        self.signal = False
        self.val = None
        self.sem = None
        self.chan = None
        self.is_dma = False


class Prog:
    def __init__(self, nc):
        self.nc = nc
        self.ops = []
        self.stack = ExitStack()
        self.chan_count = {}
        self.nbuf = 0

    def sb(self, name, shape, dtype):
        t = self.stack.enter_context(self.nc.sbuf_tensor("sb_" + name, list(shape), dtype))
        return Buf(self, name, t)

    def ps(self, name, shape, dtype):
        t = self.stack.enter_context(self.nc.psum_tensor("ps_" + name, list(shape), dtype))
        b = Buf(self, name, t)
        b.excl = True
        return b

    def dram(self, name, shape, dtype, kind="Internal"):
        t = self.nc.dram_tensor(name, list(shape), dtype, kind=kind)
        return Buf(self, name, t.ap(), is_dram=True)

    def _norm(self, lst):
        out = []
        for x in lst:
            if x is None:
                continue
            if isinstance(x, Buf):
                out.append((x, None))
            else:
                out.append(x)
        return out

    def op(self, eng, fn, reads=(), writes=(), chan=None):
        o = Op(eng, fn)
        o.idx = len(self.ops)
        if chan is not None:
            o.is_dma = True
            o.chan = chan
            self.chan_count[chan] = self.chan_count.get(chan, 0) + 1
            o.val = 16 * self.chan_count[chan]
            o.signal = True
        reads = self._norm(reads)
        writes = self._norm(writes)
        writes = writes + [(b_, k_) for (b_, k_) in reads if b_.excl]
        reads = [(b_, k_) for (b_, k_) in reads if not b_.excl]

        def add_dep(d):
            if d is None or d is o:
                return
            if d.is_dma:
                v = 16 * self.chan_count[d.chan]
                if d is not o and d.chan == o.chan:
                    v = d.val
                o.deps[d] = max(o.deps.get(d, 0), v)
            else:
                if d.eng == o.eng and not o.is_dma:
                    if d.eng == "pe" or not SAME_ENG_SYNC:
                        return
                o.deps[d] = 0

        for b, k in reads:
            add_dep(b.whole[0])
            if k is None:
                for st in b.st.values():
                    add_dep(st[0])
            elif k in b.st:
                add_dep(b.st[k][0])
        for b, k in writes:
            add_dep(b.whole[0])
            for r in b.whole[1]:
                add_dep(r)
            if k is None:
                for st in b.st.values():
                    add_dep(st[0])
                    for r in st[1]:
                        add_dep(r)
            elif k in b.st:
                add_dep(b.st[k][0])
                for r in b.st[k][1]:
                    add_dep(r)
        for b, k in reads:
            if k is None:
                b.whole[1].append(o)
            else:
                b.st.setdefault(k, [None, []])[1].append(o)
        for b, k in writes:
            if k is None:
                b.whole = [o, []]
                b.st = {}
            else:
                b.st[k] = [o, []]
        self.ops.append(o)
        return o

    def dma(self, eng, out, in_, reads=(), writes=(), chan=None, **kw):
        assert chan is not None
        return self.op(eng, lambda e: e.dma_start(out=out, in_=in_, **kw), reads, writes, chan=chan)

    def emit(self):
        nc = self.nc
        for o in self.ops:
            for d in o.deps:
                d.signal = True
        sems = {}
        for e in COMPUTE:
            sems[e] = self.stack.enter_context(nc.semaphore("s_" + e))
        for c in self.chan_count:
            sems["c_" + c] = self.stack.enter_context(nc.semaphore("c_" + c))
        cnt = {e: 0 for e in COMPUTE}
        for o in self.ops:
            if o.is_dma:
                o.sem = sems["c_" + o.chan]
            else:
                o.sem = sems[o.eng]
                if o.signal:
                    cnt[o.eng] += 1
                    o.val = cnt[o.eng]
        by_eng = {}
        for o in self.ops:
            by_eng.setdefault(o.eng, []).append(o)
        last_chan_eng = {}
        for o in self.ops:
            if o.is_dma:
                last_chan_eng[o.chan] = o.eng
        self.n_inst = {e: len(v) for e, v in by_eng.items()}

        def make_section(ename, ops):
            def section(eng):
                waited = {}
                for o in ops:
                    for d, v in o.deps.items():
                        val = v if d.is_dma else d.val
                        key = id(d.sem)
                        if waited.get(key, 0) >= val:
                            continue
                        eng.wait_ge(d.sem, val)
                        waited[key] = val
                    inst = o.fn(eng)
                    if o.signal:
                        inst.then_inc(o.sem, 16 if o.is_dma else 1)
                for c, e in last_chan_eng.items():
                    if e == ename:
                        eng.wait_ge(sems["c_" + c], 16 * self.chan_count[c])
            return section

        with nc.Block() as block:
            dec = {"pe": block.tensor, "dve": block.vector, "act": block.scalar,
                   "pool": block.gpsimd, "sp": block.sync}
            for ename, ops in by_eng.items():
                dec[ename](make_section(ename, ops))
        self.stack.close()

D = 1024
NT = 64
NSLOT = 16
BIG8 = 240000.0
IN_W = 5912
KV0 = 512
G0 = 1280
CH_COLS = [0, 1304, 1816, 2328, 2840, 3352, 3864, 4376, 4888, 5400]
NWB = 2
NCH = 14


def _mk(P):
    class H:
        pass
    h = H()

    def mm(out, lhsT, rhs, start, stop, reads, writes, **kw):
        return P.op("pe", lambda e: e.matmul(out=out, lhsT=lhsT, rhs=rhs, start=start, stop=stop, **kw), reads, writes)

    def tr(out, in_, ident, reads, writes):
        return P.op("pe", lambda e: e.transpose(out=out, in_=in_, identity=ident), reads, writes)

    def act(out, in_, func, reads, writes, **kw):
        return P.op("act", lambda e: e.activation(out=out, in_=in_, func=func, **kw), reads, writes)

    def tt(eng, out, in0, in1, op, reads, writes):
        return P.op(eng, lambda e: e.tensor_tensor(out=out, in0=in0, in1=in1, op=op), reads, writes)

    def ts(eng, out, in0, s1, s2, op0, op1, reads, writes, **kw):
        if op1 is None:
            return P.op(eng, lambda e: e.tensor_scalar(out=out, in0=in0, scalar1=s1, scalar2=None, op0=op0, **kw), reads, writes)
        return P.op(eng, lambda e: e.tensor_scalar(out=out, in0=in0, scalar1=s1, scalar2=s2, op0=op0, op1=op1, **kw), reads, writes)

    def stt(out, in0, scalar, in1, op0, op1, reads, writes):
        return P.op("dve", lambda e: e.scalar_tensor_tensor(out=out, in0=in0, scalar=scalar, in1=in1, op0=op0, op1=op1), reads, writes)

    def cp(eng, out, in_, reads, writes):
        if eng == "act":
            return P.op("act", lambda e: e.copy(out=out, in_=in_), reads, writes)
        return P.op(eng, lambda e: e.tensor_copy(out=out, in_=in_), reads, writes)

    def ms(eng, ap, val, writes):
        return P.op(eng, lambda e: e.memset(ap, val), (), writes)

    h.mm, h.tr, h.act, h.tt, h.ts, h.stt, h.cp, h.ms = mm, tr, act, tt, ts, stt, cp, ms
    return h


def build(nt=NT, nslot=NSLOT, do_sample=True, dbg=False):
    nc = bass.Bass("TRN2", target_bir_lowering=False)
    P = Prog(nc)
    h = _mk(P)
    mm, tr, act, tt, ts, stt, cp, ms = h.mm, h.tr, h.act, h.tt, h.ts, h.stt, h.cp, h.ms

    def din(name, shape, dt=F32):
        return nc.dram_tensor(name, list(shape), dt, kind="ExternalInput").ap()

    def dout(name, shape, dt=F32):
        return nc.dram_tensor(name, list(shape), dt, kind="ExternalOutput").ap()

    xp = din("xp", [NT * 128, D])
    xmine = din("xmine", [NSLOT * 128, D])
    xprev = din("xprev", [NSLOT * 32, D])
    cp_d = din("cpr", [1, D])
    w_ada = din("w_ada", [D, 3 * D])
    b_ada = din("b_ada", [1, 3 * D])
    g_pre = din("g_pre", [1, D])
    g_post = din("g_post", [1, D])
    w_in = din("w_in", [D, IN_W])
    pe_cmp = din("pe_cmp", [2, 32, 64])
    w_cmp1 = din("w_cmp1", [2, 2048, 256])
    w_cmp2 = din("w_cmp2", [2, 256, 64])
    conv_w = din("conv_w", [3, 512])
    w_br_a = din("w_br_a", [512, D])
    w_br_b = din("w_br_b", [512, D])
    w_out = din("w_out", [D, D])
    ident_d = din("ident", [128, 128])
    ropeA_d = din("ropeA", [128, NT, 16])
    ropeB_d = din("ropeB", [128, NSLOT, 16])
    slotc_d = din("slotc", [NSLOT, 128, 3, 128])
    cmpb_d = din("cmpb", [NSLOT, 128, 4, 128])
    winb_d = din("winb", [128, 8, 128])
    selc_d = din("selc", [128, 3, 4, 128])
    ovl_d = din("ovl", [128, 4, 128])
    shm_d = din("shm", [128, 2, 128])
    shb_d = din("shb", [32, 2, 128])
    upsc_d = din("upsc", [32, NSLOT])

    xs_d = din("xs", [16, D])
    cs_d = din("cs", [16, D])
    ptab_d = din("ptab", [1, 256], I32)
    cache_d = din("cache", [2560 * 128, 512])
    swin_d = din("swin", [16, 512, 256])
    sconv_d = din("sconv", [16, 2, 512])
    ropeS_d = din("ropeS", [16, 16])
    slotS_d = din("slotS", [128, 3, 128])
    smask_d = din("smask", [128, 4, 128])
    onehot_d = din("onehot", [128, 16])
    pcol_d = din("pcol", [128, 1])
    ys = dout("ys", [16, D])
    kvs = dout("kvs", [16, 512])
    wins = dout("wins", [16, 512, 256])
    convs = dout("convs", [16, 2, 512])
    yp = dout("yp", [NSLOT * 128, D])
    kvp = dout("kvp", [NSLOT * 128, 512])
    winp = dout("winp", [128, 256])
    convp = dout("convp", [2, 512])
    dbg_outs = {}

    def dbg_out(name, buf, ap, shape, dt=F32):
        if not dbg:
            return
        d = dout("dbg_" + name, shape, dt)
        P.dma("sp", d, ap, reads=[buf], chan="dbg_" + name)

    wscr = P.dram("wscr", [NCH, 128, 4096], BF16)

    ident = P.sb("ident", [128, 128], F32)
    identb = P.sb("identb", [128, 128], BF16)
    I4 = P.sb("I4", [128, 512], BF16)
    ones1 = P.sb("ones1", [1, 128], F32)
    wkv = P.sb("wkv", [128, 8, 768], BF16)
    wg = P.sb("wg", [128, 8, 24], BF16)
    w1 = P.sb("w1", [128, 32, 256], BF16)
    w2 = P.sb("w2", [128, 2, 2, 64], BF16)
    wch = [P.sb("wch%d" % i, [128, 4096], BF16) for i in range(NWB)]
    KselT = P.sb("KselT", [128, NT * 128], BF16)
    Vsel = P.sb("Vsel", [128, NT, 2, 65], BF16)
    KwinT = P.sb("KwinT", [128, 8 * 128], BF16)
    Vwin = P.sb("Vwin", [128, 8, 2, 65], BF16)
    kcT = P.sb("kcT", [128, 512], BF16)
    vca = P.sb("vca", [128, 4, 2, 65], BF16)
    vcf = P.sb("vcf", [128, 4, 2, 64], F32)
    CT = [P.sb("CT%d" % g, [128, 528], BF16) for g in range(2)]
    xt = [P.sb("xt0", [128, D], F32)] * 2
    xn = P.sb("xn", [128, D], F32)
    hT = [P.sb("hT0", [128, 8, 128], BF16)] * 2
    zkv = P.sb("zkv", [128, 768], F32)
    ctsrc = P.sb("ctsrc", [128, 256], F32)
    ropeA = P.sb("ropeA", [128, NT, 16], F32)
    ropeB = P.sb("ropeB", [128, NSLOT, 16], F32)
    rt = P.sb("rt", [128, 4, 8 * 8], F32)
    ss = P.sb("ss", [128, 8], F32)
    G1col = P.sb("G1col", [128, 8], F32)
    SHcol = P.sb("SHcol", [128, 8], F32)
    GP = P.sb("GP", [128, D], F32)
    cm = P.sb("cm", [128, 4], F32)
    cmb = P.sb("cmb", [128, 4], BF16)
    cmt8 = P.sb("cmt8", [128, 16], F32)
    cmneg = P.sb("cmneg", [128, 128], F32)
    bias1 = P.sb("bias1", [128, 2, 2], F32)
    hsT = P.sb("hsT", [128, 2, 2, 32], BF16)
    hsTp = P.sb("hsTp", [128, 2, 256], BF16)
    sT = P.sb("sT", [128, 8, 17], F32)
    modT = P.sb("modT", [128, 24, 17], F32)
    coltmp = P.sb("coltmp", [128, 32], F32)
    G1S = P.sb("G1S", [128, 8, 16], F32)
    SHS = P.sb("SHS", [128, 8, 16], F32)
    GST = P.sb("GST", [128, 8, 16], F32)
    bcm = P.sb("bcm", [128, 128], F32)
    convwb = P.sb("convwb", [128, 3, 512], F32)
    winb = P.sb("winb", [128, 8, 128], F32)
    selc = P.sb("selc", [128, 3, 4, 128], F32)
    ovl = P.sb("ovl", [128, 4, 128], BF16)
    ovlf = P.sb("ovlf", [128, 4, 128], F32)
    shm = P.sb("shm", [128, 2, 128], F32)
    shb = P.sb("shb", [32, 2, 128], F32)
    upsc = P.sb("upsc", [32, NSLOT], F32)
    pet = P.sb("pet", [32, 2, 64], F32)
    peT = P.sb("peT", [128, 32], BF16)

    xm = P.sb("xm", [128, D], F32)
    ps = [P.ps("b%d" % i, [128, 512], F32) for i in range(8)]

    P.dma("sp", ident[:], ident_d, writes=[ident], chan="c0")
    P.dma("sp", ropeA[:], ropeA_d, writes=[ropeA], chan="c0")
    P.dma("sp", ropeB[:], ropeB_d, writes=[ropeB], chan="c0")
    P.dma("sp", winb[:], winb_d, writes=[winb], chan="c0")
    P.dma("sp", selc[:], selc_d, writes=[selc], chan="c0")
    P.dma("sp", ovlf[:], ovl_d, writes=[ovlf], chan="c0")
    P.dma("sp", shm[:], shm_d, writes=[shm], chan="c0")
    P.dma("sp", shb[:], shb_d, writes=[shb], chan="c0")
    P.dma("sp", upsc[:], upsc_d, writes=[upsc], chan="c0")
    P.dma("sp", pet[:], pe_cmp.rearrange("k r d -> r k d"), writes=[pet], chan="c0")
    cp("dve", identb[:], ident[:], [ident], [identb])
    cp("dve", ovl[:], ovlf[:], [ovlf], [ovl])
    for i in range(4):
        cp("dve", I4[:, i * 128:(i + 1) * 128], ident[:], [ident], [I4])
    ms("dve", ones1[:], 1.0, [ones1])
    ms("pool", Vsel[:], 1.0, [Vsel])
    ms("pool", Vwin[:], 1.0, [Vwin])
    ms("pool", vca[:], 1.0, [vca])
    ms("pool", vcf[:], 0.0, [vcf])
    ms("pool", kcT[:], 0.0, [kcT])
    ms("pool", cm[:], 0.0, [cm])
    for g in range(2):
        ms("pool", CT[g][:], 0.0, [CT[g]])
    ms("pool", KwinT[:], 0.0, [KwinT])
    ms("pool", KselT[:], 0.0, [KselT])

    P.dma("pool", wkv[:], w_in[:, KV0:KV0 + 768].rearrange("(c p) n -> p c n", p=128), writes=[wkv], chan="wk")
    P.dma("pool", wg[:], w_in[:, G0:G0 + 24].rearrange("(c p) n -> p c n", p=128), writes=[wg], chan="wk")
    for kv in range(2):
        P.dma("pool", w1[kv * 64:(kv + 1) * 64, :, :], w_cmp1[kv].rearrange("(r d) n -> d r n", d=64),
              writes=[(w1, kv)], chan="wk")
        P.dma("pool", w2[:, kv, :, :], w_cmp2[kv].rearrange("(h p) n -> p h n", p=128), writes=[(w2, kv)], chan="wk")
    for j in range(NCH):
        wb = wch[j % NWB]
        if j < 10:
            src = w_in[:, CH_COLS[j]:CH_COLS[j] + 512].rearrange("(c p) n -> p c n", p=128)
            dst = wb[:].rearrange("p (c n) -> p c n", c=8)
        elif j == 10:
            src = w_br_a.rearrange("(c p) n -> p c n", p=128)
            dst = wb[:].rearrange("p (c n) -> p c n", c=4)
        elif j == 11:
            src = w_br_b.rearrange("(c p) n -> p c n", p=128)
            dst = wb[:].rearrange("p (c n) -> p c n", c=4)
        else:
            o = (j - 12) * 512
            src = w_out[:, o:o + 512].rearrange("(c p) n -> p c n", p=128)
            dst = wb[:].rearrange("p (c n) -> p c n", c=8)
        P.dma("pool", dst, src, writes=[wb], chan="wsp%d" % (j % NWB))
        P.dma("sp", wscr[j], wb[:], reads=[wb], writes=[(wscr, j)], chan="wso%d" % (j % NWB))

    P.dma("sp", xn[0:1, :], cp_d, writes=[xn], chan="xnl")
    P.dma("sp", xn[1:17, :], cs_d, writes=[xn], chan="xnl")
    act(xn[0:17, :], xn[0:17, :], AF.Silu, [xn], [xn])
    for k in range(8):
        tr(ps[0][:, k * 17:(k + 1) * 17], xn[0:17, k * 128:(k + 1) * 128], ident[0:17, 0:17], [xn, ident], [ps[0]])
    cp("dve", sT[:].rearrange("p k s -> p (k s)"), ps[0][:, 0:136], [ps[0]], [sT])

    def row_to_cols(row_ap, dst_buf, dst_ap_fn, n):
        for j in range(n):
            tr(ps[0][:, j:j + 1], row_ap[:, j * 128:(j + 1) * 128], ident[0:1, 0:1], [xm, ident], [ps[0]])
        cp("dve", dst_ap_fn, ps[0][:, 0:n], [ps[0]], [dst_buf])

    for j in range(24):
        wa = xt[j % 2]
        P.dma("sp", wa[:].rearrange("p (c n) -> p c n", c=8),
              w_ada[:, j * 128:(j + 1) * 128].rearrange("(c p) n -> p c n", p=128), writes=[wa], chan="xt0")
        for k in range(8):
            mm(ps[1][:, j * 17:(j + 1) * 17], wa[:, k * 128:(k + 1) * 128], sT[:, k, :], k == 0, k == 7, [wa, sT], [ps[1]])
    cp("dve", modT[:].rearrange("p j s -> p (j s)"), ps[1][:, 0:408], [ps[1]], [modT])
    for i3 in range(3):
        P.dma("sp", xm[0:1, :], b_ada[:, i3 * D:(i3 + 1) * D], writes=[xm], chan="xm")
        row_to_cols(xm[0:1, :], coltmp, coltmp[:, i3 * 8:(i3 + 1) * 8], 8)
    tt("dve", modT[:], modT[:], coltmp[:, 0:24].unsqueeze(2).to_broadcast([128, 24, 17]), ALU.add, [modT, coltmp], [modT])
    P.dma("sp", xm[0:1, :], g_pre, writes=[xm], chan="xm")
    row_to_cols(xm[0:1, :], coltmp, coltmp[:, 24:32], 8)
    stt(G1col[:], modT[:, 8:16, 0], 1.0, coltmp[:, 24:32], ALU.add, ALU.mult, [modT, coltmp], [G1col])
    cp("dve", SHcol[:], modT[:, 0:8, 0], [modT], [SHcol])
    ts("dve", G1S[:], modT[:, 8:16, 1:17], 1.0, None, ALU.add, None, [modT], [G1S])
    tt("dve", G1S[:], G1S[:], coltmp[:, 24:32].unsqueeze(2).to_broadcast([128, 8, 16]), ALU.mult, [G1S, coltmp], [G1S])
    cp("dve", SHS[:], modT[:, 0:8, 1:17], [modT], [SHS])
    P.dma("sp", xm[0:1, :], g_post, writes=[xm], chan="xm")
    row_to_cols(xm[0:1, :], coltmp, coltmp[:, 0:8], 8)
    tt("dve", coltmp[:, 8:16], coltmp[:, 0:8], modT[:, 16:24, 0], ALU.mult, [coltmp, modT], [coltmp])
    tt("dve", GST[:], modT[:, 16:24, 1:17], coltmp[:, 0:8].unsqueeze(2).to_broadcast([128, 8, 16]), ALU.mult,
       [modT, coltmp], [GST])
    for c in range(8):
        cp("dve", bcm[:], coltmp[:, 8 + c:9 + c].to_broadcast([128, 128]), [coltmp], [bcm])
        tr(ps[2 + c // 4][:, (c % 4) * 128:(c % 4 + 1) * 128], bcm[:], ident[:], [bcm, ident], [ps[2 + c // 4]])
    cp("dve", GP[:, 0:512], ps[2][:], [ps[2]], [GP])
    cp("dve", GP[:, 512:1024], ps[3][:], [ps[3]], [GP])
    for k in range(3):
        P.dma("sp", xm[0:1, 0:512], conv_w[k:k + 1, :], writes=[xm], chan="xm")
        mm(ps[4][:], ones1[0:1, :], xm[0:1, 0:512], True, True, [ones1, xm], [ps[4]])
        cp("dve", convwb[:, k, :], ps[4][:], [ps[4]], [convwb])
    tr(ps[5][:, 0:32], pet[:].rearrange("r k d -> r (k d)"), ident[0:32, 0:32], [pet, ident], [ps[5]])
    cp("dve", peT[:], ps[5][:, 0:32], [ps[5]], [peT])
    for kv in range(2):
        for half in range(2):
            for r in range(32):
                mm(ps[6 + kv][:, half:half + 1], w1[kv * 64:(kv + 1) * 64, r, half * 128:(half + 1) * 128],
                   peT[kv * 64:(kv + 1) * 64, r:r + 1], r == 0, r == 31, [w1, peT], [ps[6 + kv]])
    for kv in range(2):
        cp("dve", bias1[:, kv, :], ps[6 + kv][:, 0:2], [ps[6 + kv]], [bias1])

    def load_x(buf, src_ap, chan, rows=128):
        P.dma("sp", buf[0:rows, :], src_ap, writes=[buf], chan=chan)

    def norm_T(xb, hTb, rows=128):
        act(xn[0:rows, :], xb[0:rows, :], AF.Square, [xb], [xn, ss], accum_out=ss[0:rows, 0:1])
        ts("dve", ss[0:rows, 1:2], ss[0:rows, 0:1], 1.0 / D, 1e-6, ALU.mult, ALU.add, [ss], [ss])
        act(ss[0:rows, 2:3], ss[0:rows, 1:2], AF.Sqrt, [ss], [ss])
        P.op("dve", lambda e: e.reciprocal(out=ss[0:rows, 3:4], in_=ss[0:rows, 2:3]), [ss], [ss])
        ts("dve", xn[0:rows, :], xb[0:rows, :], ss[0:rows, 3:4], None, ALU.mult, None, [xb, ss], [xn])
        for c in range(8):
            tr(ps[c // 4][:, (c % 4) * 128:(c % 4) * 128 + rows], xn[0:rows, c * 128:(c + 1) * 128],
               ident[0:rows, 0:rows], [xn, ident], [ps[c // 4]])
        for c in range(8):
            act(hTb[:, c, 0:rows], ps[c // 4][:, (c % 4) * 128:(c % 4) * 128 + rows], AF.Identity,
                [ps[c // 4], G1col, SHcol], [hTb], scale=G1col[:, c:c + 1], bias=SHcol[:, c:c + 1])

    def rope_inplace(zb, base_views, tab_ap, rows=128):
        for v in base_views:
            n = v.shape[1]
            x1 = v[:, :, 0:8]
            x2 = v[:, :, 8:16]
            cs = tab_ap[:, 0:8].unsqueeze(1).to_broadcast([rows, n, 8])
            sn = tab_ap[:, 8:16].unsqueeze(1).to_broadcast([rows, n, 8])
            t = [rt[0:rows, i, 0:n * 8].rearrange("p (n e) -> p n e", e=8) for i in range(4)]
            tt("dve", t[0], x1, cs, ALU.mult, [zb], [rt])
            tt("dve", t[1], x2, sn, ALU.mult, [zb], [rt])
            tt("dve", t[2], x1, sn, ALU.mult, [zb], [rt])
            tt("dve", t[3], x2, cs, ALU.mult, [zb], [rt])
            tt("dve", x1, t[0], t[1], ALU.subtract, [rt], [zb])
            tt("dve", x2, t[2], t[3], ALU.add, [rt], [zb])

    def kv_proj(hTb, tab_ap, rows=128):
        for half in range(2):
            for k in range(8):
                mm(ps[2 + half][0:rows, 0:384], hTb[:, k, 0:rows], wkv[:, k, half * 384:(half + 1) * 384], k == 0, k == 7,
                   [hTb, wkv], [ps[2 + half]])
        cp("act", zkv[0:rows, 0:384], ps[2][0:rows, 0:384], [ps[2]], [zkv])
        cp("dve", zkv[0:rows, 384:768], ps[3][0:rows, 0:384], [ps[3]], [zkv])
        views = [zkv[0:rows, o:o + 128].rearrange("p (g d) -> p g d", g=2)[:, :, 0:16] for o in (0, 256, 512)]
        rope_inplace(zkv, views, tab_ap, rows=rows)

    def colmax_update(buf, ap, prt, bi, n):
        P.op("dve", lambda e: e.max(out=cmt8[prt, 0:8], in_=ap), [buf], [cmt8])
        tt("dve", cm[prt, bi:bi + 1], cm[prt, bi:bi + 1], cmt8[prt, 0:1], ALU.max, [cm, cmt8], [cm])
        ts("dve", cmneg[prt, 0:n], ap, -1.0, None, ALU.mult, None, [buf], [cmneg])
        P.op("dve", lambda e: e.max(out=cmt8[prt, 8:16], in_=cmneg[prt, 0:n]), [cmneg], [cmt8])
        tt("dve", cm[prt, bi:bi + 1], cm[prt, bi:bi + 1], cmt8[prt, 8:9], ALU.max, [cm, cmt8], [cm])

    KST = ""

    def phase_a(t):
        xb = xt[t % 2]
        hTb = hT[t % 2]
        norm_T(xb, hTb)
        if t + 1 < nt:
            load_x(xt[(t + 1) % 2], xp[(t + 1) * 128:(t + 2) * 128, :], "xt0")
        kv_proj(hTb, ropeA[:, t, :])
        ingest(t, zkv, True)

    def ingest(t, zb, has_win):
        tr(ps[4][:, 0:128], zb[:, 256:384], ident[:], [zb, ident], [ps[4]])
        if has_win:
            tr(ps[4][:, 128:256], zb[:, 512:640], ident[:], [zb, ident], [ps[4]])
        cp("dve", ctsrc[:].rearrange("p (g k d) -> p g k d", g=2, k=2),
           zb[:, 0:256].rearrange("p (k g d) -> p g k d", k=2, g=2), [zb], [ctsrc])
        for g in range(2):
            tr(ps[4][:, 256 + g * 128:384 + g * 128], ctsrc[:, g * 128:(g + 1) * 128], ident[:], [ctsrc, ident], [ps[4]])
        cp("act", KselT[:, t * 128:(t + 1) * 128], ps[4][:, 0:128], [ps[4]], [(KselT, t)])
        if has_win:
            cp("act", KwinT[:, (t % 8) * 128:(t % 8 + 1) * 128], ps[4][:, 128:256], [ps[4]], [(KwinT, t % 8)])
        tq = t % 4
        for g in range(2):
            cp("act", CT[g][:, 16 + tq * 128:144 + tq * 128], ps[4][:, 256 + g * 128:384 + g * 128], [ps[4]], [CT[g]])
        for bi, o in (((1, 0), (2, 128)) if has_win else ((1, 0),)):
            colmax_update(ps[4], ps[4][:, o:o + 128], slice(0, 128), bi, 128)
        cp("dve", Vsel[:, t, :, 0:64], zb[:, 384:512].rearrange("p (g d) -> p g d", g=2), [zb], [(Vsel, t)])
        if has_win:
            cp("dve", Vwin[:, t % 8, :, 0:64], zb[:, 640:768].rearrange("p (g d) -> p g d", g=2), [zb], [(Vwin, t % 8)])
        if tq != 3:
            return
        t0 = t - 3
        m0 = 1 if t0 == 0 else 0
        nb = 32 - m0
        i0 = 8 * t0 - 1 + m0
        kt0 = i0 // 128
        c0 = i0 - kt0 * 128
        for g in range(2):
            for half in range(2):
                col = (g * 2 + half) * 32
                for r in range(32):
                    for kv in range(2):
                        rhs = CT[g][kv * 64:(kv + 1) * 64, r:r + 16 * 31 + 1:16]
                        mm(ps[5 + 2 * kv][:, col:col + 32], w1[kv * 64:(kv + 1) * 64, r, half * 128:(half + 1) * 128], rhs,
                           r == 0, r == 31, [w1, CT[g]], [ps[5 + 2 * kv]])
            ms("pool", hsTp[:], 0.0, [hsTp])
            for kv in range(2):
                for half in range(2):
                    col = (g * 2 + half) * 32
                    act(hsT[:, kv, half, :], ps[5 + 2 * kv][:, col:col + 32], AF.Silu, [ps[5 + 2 * kv], bias1], [hsT],
                        bias=bias1[:, kv, half:half + 1])
            cp("dve", hsTp[:, :, c0:c0 + nb], hsT[:, 1, :, m0:32], [hsT], [hsTp])
            for half in range(2):
                mm(ps[6][g * 64:(g + 1) * 64, 0:32], w2[:, 0, half, :], hsT[:, 0, half, :], half == 0, half == 1,
                   [w2, hsT], [ps[6]])
            cp("act", kcT[g * 64:(g + 1) * 64, i0:i0 + nb], ps[6][g * 64:(g + 1) * 64, m0:32], [ps[6]], [kcT])
            colmax_update(ps[6], ps[6][g * 64:(g + 1) * 64, 0:32], slice(g * 64, (g + 1) * 64), 0, 32)
            for w in range(2):
                if c0 + nb <= w * 128 or c0 >= (w + 1) * 128 or kt0 + w > 3:
                    continue
                for half in range(2):
                    mm(ps[7][:, 0:64], hsTp[:, half, w * 128:(w + 1) * 128], w2[:, 1, half, :], half == 0, half == 1,
                       [hsTp, w2], [ps[7]])
                tt("dve", vcf[:, kt0 + w, g, :], vcf[:, kt0 + w, g, :], ps[7][:, 0:64], ALU.add, [vcf, ps[7]], [vcf])
                cp("dve", vca[:, kt0 + w, g, 0:64], vcf[:, kt0 + w, g, :], [vcf], [vca])
        for g in range(2):
            cp("dve", CT[g][:, 0:16], CT[g][:, 512:528], [CT[g]], [CT[g]])

    xpv = P.sb("xpv", [32, D], F32)
    hTm = P.sb("hTm", [128, 8, 128], BF16)
    hTp = P.sb("hTp", [128, 8, 32], BF16)
    qf = P.sb("qf", [128, 512], F32)
    QT = P.sb("QT", [128, 512], BF16)
    aQT = P.sb("aQT", [128, 512], BF16)
    gates = P.sb("gates", [128, 24], F32)
    sa = P.sb("sa", [128, 512], F32)
    cbf = P.sb("cbf", [128, 512], F32)
    ccf = P.sb("ccf", [128, 512], F32)
    uu = P.sb("uu", [128, 512], F32)
    ccp = P.sb("ccp", [32, 512], F32)
    up = P.sb("up", [32, 512], F32)
    co = P.sb("co", [128, 512], F32)
    sgc = P.sb("sgc", [128, 512], F32)
    obT = P.sb("obT", [128, 4, 128], BF16)
    oat = P.sb("oat", [128, 512], F32)
    oaT = P.sb("oaT", [128, 4, 128], BF16)
    mbuf = P.sb("mbuf", [128, D], F32)
    mT = P.sb("mT", [128, 8, 128], BF16)
    yout = mbuf
    PTb = [P.sb("PT%d" % i, [128, 512], BF16) for i in range(3)]
    Ls = P.sb("Ls", [128, 16, 128], BF16)
    Lx = P.sb("Lx", [128, 128], F32)
    Lt = P.sb("Lt", [128, 128], F32)
    Lexp = [P.sb("Lexp%d" % i, [128, 1024], BF16) for i in range(2)]
    slotc = P.sb("slotc", [128, 3, 128], F32)
    cmpb = P.sb("cmpb", [128, 4, 128], F32)
    negm = P.sb("negm", [128, 4], F32)
    sc = [P.sb("sc%d" % i, [128, 128], F32) for i in range(3)]
    Lselb = P.sb("Lselb", [128, 128], BF16)
    m8 = P.sb("m8", [128, 16], F32)
    rc = P.sb("rc", [128, 16], F32)

    chunk_seq = [(n, j) for n in range(nslot + (1 if do_sample else 0)) for j in (0, 1, 2, 3, 4, 5, 10, 6, 7, 11, 8, 9, 12, 13)]
    st = {"issued": 0, "used": 0, "u": 0}

    def issue_chunks(upto):
        while st["issued"] < min(upto, len(chunk_seq)):
            k = st["issued"]
            P.dma("sp", wch[k % NWB][:], wscr[chunk_seq[k][1]], reads=[(wscr, chunk_seq[k][1])], writes=[wch[k % NWB]],
                  chan="wst%d" % (k % NWB))
            st["issued"] += 1

    def next_chunk(expect_j):
        k = st["used"]
        assert chunk_seq[k][1] == expect_j, (chunk_seq[k], expect_j)
        issue_chunks(k + 1)
        st["used"] += 1
        return wch[k % NWB], k

    def zchunk(j, bank, lh=None, rows=128):
        wb, k = next_chunk(j)
        wv = wb[:].rearrange("p (c n) -> p c n", c=8)
        for kc in range(8):
            mm(bank[0:128, :], hTm[:, kc, :], wv[:, kc, :], kc == 0, kc == 7, [hTm, wb], [bank])
        return wb, wv, k

    def attention(n, g, samp=None):
        gr = slice(g * 64, (g + 1) * 64)
        nkt = 4 * n + 4 if samp is None else 17
        nfull = 4 * n if samp is None else 16
        pm = ps[2] if g == 0 else ps[1]
        sbanks = (ps[3], ps[4]) if g == 0 else (ps[0], ps[1])
        for br in range(4):
            for hh in range(4):
                mm(pm[:, br * 4 + hh:br * 4 + hh + 1], aQT[gr, hh * 128:(hh + 1) * 128], cmb[gr, min(br, 2):min(br, 2) + 1], True, True,
                   [aQT, cmb], [pm])
        for br in range(3):
            P.op("dve", lambda e, br=br: e.max(out=m8[:, 0:8], in_=pm[:, br * 4:br * 4 + 8]), [pm], [m8])
            ts("dve", negm[:, br:br + 1], m8[:, 0:1], -1.0, None, ALU.mult, None, [m8], [negm])
        ts("dve", negm[:, 3:4], negm[:, 1:2], -BIG8, None, ALU.add, None, [negm], [negm])

        pend = []

        def flush():
            while pend:
                pend.pop(0)()

        def unit(Kt, Lap, Lbuf, Vap, Vbuf, Kbuf, obank, first, last, ovl_kt=None):
            u = st["u"]
            st["u"] += 1
            S = sbanks[u % 2]
            PT = PTb[u % 3]
            mm(S[:], Kt, QT[gr, :], True, False, [Kbuf, QT], [S])
            mm(S[:], Lap, I4[:], False, True, [Lbuf, I4], [S])
            act(PT[:], S[:], AF.Exp, [S], [PT], scale=0.125)

            def stage2():
                for hh in range(4):
                    mm(obank[:, hh * 65:(hh + 1) * 65], PT[:, hh * 128:(hh + 1) * 128], Vap, first and hh == 0, last,
                       [PT, Vbuf], [obank], skip_group_check=True)
                if ovl_kt is not None:
                    for hh in range(4):
                        mm(ps[6][:, hh * 128:(hh + 1) * 128], PT[:, hh * 128:(hh + 1) * 128], ovl[:, ovl_kt, :],
                           first and hh == 0, last, [PT, ovl], [ps[6]], skip_group_check=True)
            flush()
            pend.append(stage2)

        def finish_branch(obank, br, first_branch):
            ov = obank[:, 0:260].rearrange("p (h e) -> p h e", e=65)
            ts("dve", rc[:, 0:4], ov[:, :, 64], 1e-30, None, ALU.max, None, [obank], [rc])
            P.op("dve", lambda e: e.reciprocal(out=rc[:, 4:8], in_=rc[:, 0:4]), [rc], [rc])
            gv = gates[:, g * 12:(g + 1) * 12].rearrange("p (h b) -> p h b", b=3)[:, :, br]
            tt("dve", rc[:, 8:12], rc[:, 4:8], gv, ALU.mult, [rc, gates], [rc])
            if samp is not None:
                ts("dve", rc[:, 8:12], rc[:, 8:12], onehot[:, samp:samp + 1], None, ALU.mult, None, [rc, onehot], [rc])
                first_branch = False
            for hh in range(4):
                dst = oat[:, g * 256 + hh * 64:g * 256 + (hh + 1) * 64]
                if first_branch:
                    ts("dve", dst, ov[:, hh, 0:64], rc[:, 8 + hh:9 + hh], None, ALU.mult, None, [obank, rc], [oat])
                else:
                    stt(dst, ov[:, hh, 0:64], rc[:, 8 + hh:9 + hh], dst, ALU.mult, ALU.add, [obank, rc, oat], [oat])

        ktmax = min(3, (32 * n + 30) // 128) if samp is None else 0
        for kt in range(ktmax + 1):
            if samp is None:
                ts("dve", Ls[:, kt, :], cmpb[:, kt, :], negm[:, 0:1], None, ALU.add, None, [cmpb, negm], [(Ls, kt)])
            else:
                ts("dve", Ls[:, kt, :], smask[:, 0, :], negm[:, 0:1], None, ALU.add, None, [smask, negm], [(Ls, kt)])
        for kt in range(ktmax + 1):
            unit(kcT[gr, kt * 128:(kt + 1) * 128], Ls[:, kt, :], (Ls, kt), vca[:, kt, g, :], vca, kcT, ps[5],
                 kt == 0, kt == ktmax, ovl_kt=kt)
        flush()
        ov = ps[5][:, 0:260].rearrange("p (h e) -> p h e", e=65)
        ts("dve", rc[:, 12:16], ov[:, :, 64], 1e-30, None, ALU.max, None, [ps[5]], [rc])
        P.op("dve", lambda e: e.reciprocal(out=rc[:, 12:16], in_=rc[:, 12:16]), [rc], [rc])
        ts("dve", sc[0][:], ps[6][:, 0:128], rc[:, 12:13], None, ALU.mult, None, [ps[6], rc], [sc[0]])
        for hh in range(1, 4):
            stt(sc[0][:], ps[6][:, hh * 128:(hh + 1) * 128], rc[:, 12 + hh:13 + hh], sc[0][:], ALU.mult, ALU.add,
                [ps[6], rc, sc[0]], [sc[0]])
        finish_branch(ps[5], 0, True)
        tt("dve", sc[0][:], sc[0][:], slotc[:, 0, :], ALU.mult, [sc[0], slotc], [sc[0]])
        tt("dve", sc[0][:], sc[0][:], slotc[:, 1, :], ALU.add, [sc[0], slotc], [sc[0]])
        tt("dve", sc[0][:], sc[0][:], slotc[:, 2, :], ALU.max, [sc[0], slotc], [sc[0]])
        ms("dve", sc[0][:, 0:1], 1e6, [sc[0]])
        P.op("dve", lambda e: e.max(out=m8[:, 0:8], in_=sc[0][:]), [sc[0]], [m8])
        P.op("dve", lambda e: e.match_replace(out=sc[1][:], in_to_replace=m8[:, 0:8], in_values=sc[0][:], imm_value=-1e30),
             [sc[0], m8], [sc[1]])
        P.op("dve", lambda e: e.max(out=m8[:, 8:16], in_=sc[1][:]), [sc[1]], [m8])
        ts("dve", sc[2][:], sc[0][:], m8[:, 15:16], BIG8, ALU.is_ge, ALU.mult, [sc[0], m8], [sc[2]])
        ts("dve", Lselb[:], sc[2][:], negm[:, 3:4], None, ALU.add, None, [sc[2], negm], [Lselb])
        wk = [k for k in range(8) if 4 * n - 4 + k >= 0] if samp is None else [0, 1, 2, 3, 4]
        for k in wk:
            if samp is None:
                ts("dve", Ls[:, 4 + k, :], winb[:, k, :], negm[:, 2:3], None, ALU.add, None, [winb, negm], [(Ls, 4 + k)])
            else:
                mi = (2, 3, 3, 3, 1)[k]
                ts("dve", Ls[:, 4 + k, :], smask[:, mi, :], negm[:, 2:3], None, ALU.add, None, [smask, negm], [(Ls, 4 + k)])
        for k in wk:
            kt = 4 * n - 4 + k if samp is None else k
            unit(KwinT[gr, (kt % 8) * 128:(kt % 8 + 1) * 128], Ls[:, 4 + k, :], (Ls, 4 + k), Vwin[:, kt % 8, g, :],
                 (Vwin, kt % 8), (KwinT, kt % 8), ps[7], k == wk[0], k == wk[-1])
        flush()
        finish_branch(ps[7], 2, False)
        for kt in range(nkt):
            first, last = kt == 0, kt == nkt - 1
            if kt < nfull:
                c, o = divmod(kt, 8)
                if o == 0:
                    nb16 = min(16, 2 * nfull - 16 * c)
                    cp("pool", Lexp[c % 2][:, 0:nb16 * 64].rearrange("p (j e) -> p j e", e=64),
                       Lselb[:, 16 * c:16 * c + nb16].unsqueeze(2).to_broadcast([128, nb16, 64]), [Lselb], [Lexp[c % 2]])
                Lap, Lbuf = Lexp[c % 2][:, o * 128:(o + 1) * 128], Lexp[c % 2]
            elif samp is not None:
                ts("dve", Ls[:, 12, :], smask[:, 1, :], negm[:, 1:2], None, ALU.add, None, [smask, negm], [(Ls, 12)])
                Lap, Lbuf = Ls[:, 12, :], (Ls, 12)
            else:
                kr = kt - 4 * n
                cp("dve", Lx[:].rearrange("p (j e) -> p j e", e=64),
                   Lselb[:, 2 * kt:2 * kt + 2].unsqueeze(2).to_broadcast([128, 2, 64]), [Lselb], [Lx])
                stt(Lt[:], selc[:, 1, kr, :], negm[:, 1:2], selc[:, 2, kr, :], ALU.mult, ALU.add, [selc, negm], [Lt])
                tt("dve", Lx[:], Lx[:], selc[:, 0, kr, :], ALU.mult, [Lx, selc], [Lx])
                tt("dve", Ls[:, 12 + kr, :], Lx[:], Lt[:], ALU.add, [Lx, Lt], [(Ls, 12 + kr)])
                Lap, Lbuf = Ls[:, 12 + kr, :], (Ls, 12 + kr)
            unit(KselT[gr, kt * 128:(kt + 1) * 128], Lap, Lbuf, Vsel[:, kt, g, :], (Vsel, kt), (KselT, kt), ps[5],
                 first, last)
        flush()
        finish_branch(ps[5], 1, False)

    def phase_b(n):
        load_x(xm, xmine[n * 128:(n + 1) * 128, :], "xm")
        load_x(xpv, xprev[n * 32:(n + 1) * 32, :], "xpv", rows=32)
        P.dma("sp", slotc[:], slotc_d[n], writes=[slotc], chan="slotc")
        P.dma("sp", cmpb[:], cmpb_d[n], writes=[cmpb], chan="cmpb")
        norm_T(xm, hTm)
        norm_T(xpv, hTp, rows=32)
        kv_proj(hTm, ropeB[:, n, :])
        P.dma("pool", kvp[n * 128:(n + 1) * 128, :], zkv[:, 0:512], reads=[zkv], chan="zkvo")
        if n == nslot - 1:
            P.dma("pool", winp, zkv[:, 512:768], reads=[zkv], chan="zkvo")
        cp("dve", cmb[:], cm[:], [cm], [cmb])
        zchunk(0, ps[0])
        cp("act", qf[:], ps[0][:], [ps[0]], [qf])
        rope_inplace(qf, [qf[:].rearrange("p (h d) -> p h d", d=64)[:, :, 0:16]], ropeB[:, n, :])
        cp("pool", sgc[:].rearrange("p (h g d) -> p h g d", h=4, g=2),
           qf[:].rearrange("p (g h d) -> p h g d", g=2, h=4), [qf], [sgc])
        for jj in range(4):
            tr(ps[2][:, jj * 128:(jj + 1) * 128], sgc[:, jj * 128:(jj + 1) * 128], ident[:], [sgc, ident], [ps[2]])
        cp("act", QT[:], ps[2][:], [ps[2]], [QT])
        act(aQT[:], ps[2][:], AF.Abs, [ps[2]], [aQT])
        for kc in range(8):
            mm(ps[1][:, 0:24], hTm[:, kc, :], wg[:, kc, :], kc == 0, kc == 7, [hTm, wg], [ps[1]])
        act(gates[:], ps[1][:, 0:24], AF.Sigmoid, [ps[1]], [gates])
        zchunk(1, ps[0])
        act(sa[:], ps[0][:], AF.Silu, [ps[0]], [sa])
        zchunk(2, ps[1])
        cp("act", cbf[:], ps[1][:], [ps[1]], [cbf])
        wb, wv, _ = zchunk(3, ps[0])
        for kc in range(8):
            mm(ps[2][0:32, :], hTp[:, kc, :], wv[:, kc, :], kc == 0, kc == 7, [hTp, wb], [ps[2]])
        cp("act", ccf[:], ps[0][:], [ps[0]], [ccf])
        cp("act", ccp[:], ps[2][0:32, :], [ps[2]], [ccp])
        wb, wv, _ = zchunk(4, ps[1])
        for kc in range(8):
            mm(ps[2][0:32, :], hTp[:, kc, :], wv[:, kc, :], kc == 0, kc == 7, [hTp, wb], [ps[2]])
        tt("dve", uu[:], ccf[:], ps[1][:], ALU.mult, [ccf, ps[1]], [uu])
        stt(up[:], ps[2][0:32, :], upsc[:, n:n + 1], ccp[:], ALU.mult, ALU.mult, [ps[2], upsc, ccp], [up])
        if n == nslot - 1:
            P.dma("sp", convp, uu[126:128, :], reads=[uu], chan="uuo")
        for s_ in range(2):
            mm(ps[2 + s_][:], shm[:, s_, :], uu[:], True, False, [shm, uu], [ps[2 + s_]])
            mm(ps[2 + s_][:], shb[:, s_, :], up[:], False, True, [shb, up], [ps[2 + s_]])
        tt("dve", co[:], uu[:], convwb[:, 2, :], ALU.mult, [uu, convwb], [co])
        tt("dve", qf[:], ps[2][:], convwb[:, 1, :], ALU.mult, [ps[2], convwb], [qf])
        tt("dve", co[:], co[:], qf[:], ALU.add, [co, qf], [co])
        tt("dve", qf[:], ps[3][:], convwb[:, 0, :], ALU.mult, [ps[3], convwb], [qf])
        tt("dve", co[:], co[:], qf[:], ALU.add, [co, qf], [co])
        zchunk(5, ps[0])
        act(sgc[:], ps[0][:], AF.Silu, [ps[0]], [sgc])
        tt("dve", co[:], co[:], cbf[:], ALU.mult, [co, cbf], [co])
        tt("dve", co[:], co[:], sgc[:], ALU.mult, [co, sgc], [co])
        for c in range(4):
            tr(ps[1][:, c * 128:(c + 1) * 128], co[:, c * 128:(c + 1) * 128], ident[:], [co, ident], [ps[1]])
        cp("act", obT[:].rearrange("p c n -> p (c n)"), ps[1][:], [ps[1]], [obT])
        for g in range(2):
            attention(n, g)
        tt("dve", oat[:], oat[:], sa[:], ALU.mult, [oat, sa], [oat])
        for c in range(4):
            tr(ps[2][:, c * 128:(c + 1) * 128], oat[:, c * 128:(c + 1) * 128], ident[:], [oat, ident], [ps[2]])
        cp("act", oaT[:].rearrange("p c n -> p (c n)"), ps[2][:], [ps[2]], [oaT])
        for (jw, srcT, jg) in ((10, oaT, (6, 7)), (11, obT, (8, 9))):
            wb, k = next_chunk(jw)
            wv = wb[:].rearrange("p (c n) -> p c n", c=4)
            for half in range(2):
                for kc in range(4):
                    mm(ps[0 + half][:], srcT[:, kc, :], wv[:, kc, half * 512:(half + 1) * 512], kc == 0, kc == 3,
                       [srcT, wb], [ps[half]])
            for half in range(2):
                zchunk(jg[half], ps[2 + half])
                act(sgc[:], ps[2 + half][:], AF.Sigmoid, [ps[2 + half]], [sgc])
                dst = mbuf[:, half * 512:(half + 1) * 512]
                if jw == 10:
                    tt("dve", dst, sgc[:], ps[half][:], ALU.mult, [sgc, ps[half]], [mbuf])
                else:
                    tt("dve", qf[:], sgc[:], ps[half][:], ALU.mult, [sgc, ps[half]], [qf])
                    tt("dve", dst, dst, qf[:], ALU.add, [mbuf, qf], [mbuf])
        for c in range(8):
            tr(ps[4 + c // 4][:, (c % 4) * 128:(c % 4 + 1) * 128], mbuf[:, c * 128:(c + 1) * 128], ident[:],
               [mbuf, ident], [ps[4 + c // 4]])
        cp("act", mT[:, 0:4, :].rearrange("p c n -> p (c n)"), ps[4][:], [ps[4]], [mT])
        cp("dve", mT[:, 4:8, :].rearrange("p c n -> p (c n)"), ps[5][:], [ps[5]], [mT])
        for half in range(2):
            wb, k = next_chunk(12 + half)
            wv = wb[:].rearrange("p (c n) -> p c n", c=8)
            for kc in range(8):
                mm(ps[6 + half][:], mT[:, kc, :], wv[:, kc, :], kc == 0, kc == 7, [mT, wb], [ps[6 + half]])
        issue_chunks(st["used"] + NWB)
        for half in range(2):
            act(qf[:], ps[6 + half][:], AF.Square, [ps[6 + half]], [qf, ss],
                accum_out=ss[:, 4 + half:5 + half])
        tt("dve", ss[:, 6:7], ss[:, 4:5], ss[:, 5:6], ALU.add, [ss], [ss])
        ts("dve", ss[:, 6:7], ss[:, 6:7], 1.0 / D, 1e-6, ALU.mult, ALU.add, [ss], [ss])
        act(ss[:, 7:8], ss[:, 6:7], AF.Sqrt, [ss], [ss])
        P.op("dve", lambda e: e.reciprocal(out=ss[:, 6:7], in_=ss[:, 7:8]), [ss], [ss])
        for half in range(2):
            sl = slice(half * 512, (half + 1) * 512)
            stt(yout[:, sl], ps[6 + half][:], ss[:, 6:7], GP[:, sl], ALU.mult, ALU.mult, [ps[6 + half], ss, GP], [yout])
        tt("dve", yout[:], yout[:], xm[:], ALU.add, [yout, xm], [yout])
        P.dma("pool", yp[n * 128:(n + 1) * 128, :], yout[:], reads=[yout], chan="yout")


    pgb = [P.sb("pgb%d" % i, [128, 512], F32) for i in range(2)]
    idxf = P.sb("idxf", [128, 256], F32)
    idxi = P.sb("idxi", [128, 256], I32)
    pcol = P.sb("pcol", [128, 1], F32)
    smask = P.sb("smask", [128, 4, 128], F32)
    onehot = P.sb("onehot", [128, 16], F32)
    ropeS = P.sb("ropeS", [16, 16], F32)
    qsT = P.sb("qsT", [128, 4, 16], BF16)
    aqsT = P.sb("aqsT", [128, 4, 16], BF16)
    swt = ccf

    def phase_s():
        R = 16
        P.dma("sp", slotc[:], slotS_d, writes=[slotc], chan="slotc")
        P.dma("sp", smask[:], smask_d, writes=[smask], chan="c1")
        P.dma("sp", onehot[:], onehot_d, writes=[onehot], chan="c1")
        P.dma("sp", ropeS[:], ropeS_d, writes=[ropeS], chan="c1")
        P.dma("sp", pcol[:], pcol_d, writes=[pcol], chan="c1")
        P.dma("sp", idxi[:], ptab_d.to_broadcast([128, 256]), writes=[idxi], chan="c1")
        cp("dve", idxf[:], idxi[:], [idxi], [idxf])
        ts("dve", idxf[:], idxf[:], 128.0, pcol[:, 0:1], ALU.mult, ALU.add, [idxf, pcol], [idxf])
        cp("dve", idxi[:], idxf[:], [idxf], [idxi])
        load_x(xm, xs_d, "xm", rows=R)
        act(xn[0:R, :], xm[0:R, :], AF.Square, [xm], [xn, ss], accum_out=ss[0:R, 0:1])
        ts("dve", ss[0:R, 1:2], ss[0:R, 0:1], 1.0 / D, 1e-6, ALU.mult, ALU.add, [ss], [ss])
        act(ss[0:R, 2:3], ss[0:R, 1:2], AF.Sqrt, [ss], [ss])
        P.op("dve", lambda e: e.reciprocal(out=ss[0:R, 3:4], in_=ss[0:R, 2:3]), [ss], [ss])
        ts("dve", xn[0:R, :], xm[0:R, :], ss[0:R, 3:4], None, ALU.mult, None, [xm, ss], [xn])
        for c in range(8):
            tr(ps[0][:, c * R:(c + 1) * R], xn[0:R, c * 128:(c + 1) * 128], ident[0:R, 0:R], [xn, ident], [ps[0]])
        tt("dve", bcm[:], ps[0][:, 0:128], G1S[:].rearrange("p c s -> p (c s)"), ALU.mult, [ps[0], G1S], [bcm])
        tt("dve", hTm[:, :, 0:R], bcm[:].rearrange("p (c s) -> p c s", s=R), SHS[:], ALU.add, [bcm, SHS], [hTm])
        kv_proj(hTm, ropeS[:, :], rows=R)
        P.dma("pool", kvs, zkv[0:R, 0:512], reads=[zkv], chan="zkvo")
        P.dma("pool", wins[:, 511, :], zkv[0:R, 512:768], reads=[zkv], chan="zkvo")
        P.dma("sp", convs[:, 0, :], sconv_d[:, 1, :], chan="d2d")
        tr(ps[4][:, 0:R], zkv[0:R, 256:384], ident[0:R, 0:R], [zkv, ident], [ps[4]])
        tr(ps[4][:, R:2 * R], zkv[0:R, 512:640], ident[0:R, 0:R], [zkv, ident], [ps[4]])
        cp("act", KselT[:, 2048:2048 + R], ps[4][:, 0:R], [ps[4]], [(KselT, 16)])
        cp("act", KwinT[:, 512:512 + R], ps[4][:, R:2 * R], [ps[4]], [(KwinT, 4)])
        colmax_update(ps[4], ps[4][:, 0:R], slice(0, 128), 1, R)
        colmax_update(ps[4], ps[4][:, R:2 * R], slice(0, 128), 2, R)
        cp("dve", Vsel[0:R, 16, :, 0:64], zkv[0:R, 384:512].rearrange("p (g d) -> p g d", g=2), [zkv], [(Vsel, 16)])
        cp("dve", Vwin[0:R, 4, :, 0:64], zkv[0:R, 640:768].rearrange("p (g d) -> p g d", g=2), [zkv], [(Vwin, 4)])
        cp("dve", cmb[:], cm[:], [cm], [cmb])
        zchunk(0, ps[0])
        cp("act", qf[0:R, :], ps[0][0:R, :], [ps[0]], [qf])
        rope_inplace(qf, [qf[0:R, :].rearrange("p (h d) -> p h d", d=64)[:, :, 0:16]], ropeS[:, :], rows=R)
        cp("pool", sgc[0:R, :].rearrange("p (h g d) -> p h g d", h=4, g=2),
           qf[0:R, :].rearrange("p (g h d) -> p h g d", g=2, h=4), [qf], [sgc])
        for jj in range(4):
            tr(ps[2][:, jj * R:(jj + 1) * R], sgc[0:R, jj * 128:(jj + 1) * 128], ident[0:R, 0:R], [sgc, ident], [ps[2]])
        cp("act", qsT[:].rearrange("p j s -> p (j s)"), ps[2][:, 0:4 * R], [ps[2]], [qsT])
        act(aqsT[:].rearrange("p j s -> p (j s)"), ps[2][:, 0:4 * R], AF.Abs, [ps[2]], [aqsT])
        for kc in range(8):
            mm(ps[1][:, 0:24], hTm[:, kc, :], wg[:, kc, :], kc == 0, kc == 7, [hTm, wg], [ps[1]])
        act(gates[0:R, :], ps[1][0:R, 0:24], AF.Sigmoid, [ps[1]], [gates])
        zchunk(1, ps[0])
        act(sa[0:R, :], ps[0][0:R, :], AF.Silu, [ps[0]], [sa])
        zchunk(2, ps[1])
        cp("act", cbf[0:R, :], ps[1][0:R, :], [ps[1]], [cbf])
        zchunk(3, ps[0])
        cp("act", ccf[0:R, :], ps[0][0:R, :], [ps[0]], [ccf])
        zchunk(4, ps[1])
        tt("dve", uu[0:R, :], ccf[0:R, :], ps[1][0:R, :], ALU.mult, [ccf, ps[1]], [uu])
        P.dma("sp", convs[:, 1, :], uu[0:R, :], reads=[uu], chan="uuo")
        P.dma("sp", ccp[0:R, :], sconv_d[:, 0, :], writes=[ccp], chan="scv")
        P.dma("sp", up[0:R, :], sconv_d[:, 1, :], writes=[up], chan="scv")
        tt("dve", co[0:R, :], uu[0:R, :], convwb[0:R, 2, :], ALU.mult, [uu, convwb], [co])
        tt("dve", qf[0:R, :], up[0:R, :], convwb[0:R, 1, :], ALU.mult, [up, convwb], [qf])
        tt("dve", co[0:R, :], co[0:R, :], qf[0:R, :], ALU.add, [co, qf], [co])
        tt("dve", qf[0:R, :], ccp[0:R, :], convwb[0:R, 0, :], ALU.mult, [ccp, convwb], [qf])
        tt("dve", co[0:R, :], co[0:R, :], qf[0:R, :], ALU.add, [co, qf], [co])
        zchunk(5, ps[0])
        act(sgc[0:R, :], ps[0][0:R, :], AF.Silu, [ps[0]], [sgc])
        tt("dve", co[0:R, :], co[0:R, :], cbf[0:R, :], ALU.mult, [co, cbf], [co])
        tt("dve", co[0:R, :], co[0:R, :], sgc[0:R, :], ALU.mult, [co, sgc], [co])
        for c in range(4):
            tr(ps[1][:, c * R:(c + 1) * R], co[0:R, c * 128:(c + 1) * 128], ident[0:R, 0:R], [co, ident], [ps[1]])
        cp("act", obT[:, :, 0:R], ps[1][:, 0:4 * R].rearrange("p (c s) -> p c s", s=R), [ps[1]], [obT])
        ms("pool", hsTp[:, 0, 0:2], 0.0, [hsTp, KselT])
        ms("pool", oat[:], 0.0, [oat])
        ms("pool", QT[:], 0.0, [QT])
        ms("pool", aQT[:], 0.0, [aQT])
        k = 0
        for sm in range(R):
            CTb = [KselT[:, 4096 + g_ * 2048:4096 + (g_ + 1) * 2048] for g_ in range(2)]
            for q4 in range(4):
                for pi in range(4):
                    pgi = q4 * 4 + pi
                    pb = pgb[k % 2]
                    col = sm * 16 + pgi
                    P.op("pool", lambda e, pb=pb, col=col: e.indirect_dma_start(
                        out=pb[:], out_offset=None, in_=cache_d,
                        in_offset=bass.IndirectOffsetOnAxis(ap=idxi[:, col:col + 1], axis=0)),
                        reads=[idxi], writes=[pb], chan="pgb%d" % (k % 2))
                    k += 1
                    tr(ps[4][:, pi * 128:(pi + 1) * 128], pb[:, 256:384], ident[:], [pb, ident], [ps[4]])
                    cp("dve", ctsrc[:].rearrange("p (g k d) -> p g k d", g=2, k=2),
                       pb[:, 0:256].rearrange("p (k g d) -> p g k d", k=2, g=2), [pb], [ctsrc])
                    for g_ in range(2):
                        tr(ps[5 + g_][:, pi * 128:(pi + 1) * 128], ctsrc[:, g_ * 128:(g_ + 1) * 128], ident[:],
                           [ctsrc, ident], [ps[5 + g_]])
                    cp("dve", Vsel[:, pgi, :, 0:64], pb[:, 384:512].rearrange("p (g d) -> p g d", g=2), [pb], [(Vsel, pgi)])
                cp("act", KselT[:, q4 * 512:(q4 + 1) * 512], ps[4][:], [ps[4]], [(KselT, 4 * q4 + i_) for i_ in range(4)])
                P.op("dve", lambda e: e.max(out=cmt8[:, 0:8], in_=ps[4][:]), [ps[4]], [cmt8])
                tt("dve", cm[:, 1:2], cm[:, 1:2], cmt8[:, 0:1], ALU.max, [cm, cmt8], [cm])
                ts("dve", sgc[:], ps[4][:], -1.0, None, ALU.mult, None, [ps[4]], [sgc])
                P.op("dve", lambda e: e.max(out=cmt8[:, 8:16], in_=sgc[:]), [sgc], [cmt8])
                tt("dve", cm[:, 1:2], cm[:, 1:2], cmt8[:, 8:9], ALU.max, [cm, cmt8], [cm])
                for g_ in range(2):
                    cp("act", CTb[g_][:, q4 * 512:(q4 + 1) * 512], ps[5 + g_][:], [ps[5 + g_]], [(KselT, "ctb%d" % g_)])
            hsv = Lexp[1][:, 0:512].rearrange("p (k h n) -> p k h n", k=2, h=2)
            for g_ in range(2):
                for half in range(2):
                    for r in range(32):
                        for kv in range(2):
                            rhs = CTb[g_][kv * 64:(kv + 1) * 64, r:r + 16 * 126 + 1:16]
                            mm(ps[5 + 2 * kv][:, half * 128:half * 128 + 127],
                               w1[kv * 64:(kv + 1) * 64, r, half * 128:(half + 1) * 128], rhs, r == 0, r == 31,
                               [w1, (KselT, "ctb%d" % g_)], [ps[5 + 2 * kv]])
                for kv in range(2):
                    for half in range(2):
                        act(hsv[:, kv, half, 0:127], ps[5 + 2 * kv][:, half * 128:half * 128 + 127], AF.Silu,
                            [ps[5 + 2 * kv], bias1], [Lexp[1]], bias=bias1[:, kv, half:half + 1])
                for half in range(2):
                    mm(ps[6][g_ * 64:(g_ + 1) * 64, 0:127], w2[:, 0, half, :], hsv[:, 0, half, 0:127], half == 0, half == 1,
                       [w2, Lexp[1]], [ps[6]])
                cp("act", kcT[g_ * 64:(g_ + 1) * 64, 0:127], ps[6][g_ * 64:(g_ + 1) * 64, 0:127], [ps[6]], [kcT])
                colmax_update(ps[6], ps[6][g_ * 64:(g_ + 1) * 64, 0:127], slice(g_ * 64, (g_ + 1) * 64), 0, 127)
                for half in range(2):
                    mm(ps[4][0:127, 0:64], hsv[:, 1, half, 0:127], w2[:, 1, half, :], half == 0, half == 1,
                       [Lexp[1], w2], [ps[4]])
                cp("dve", vca[0:127, 0, g_, 0:64], ps[4][0:127, 0:64], [ps[4]], [vca])
            for i in range(4):
                P.dma("sp", swt[:, 0:256], swin_d[sm, i * 128:(i + 1) * 128, :], writes=[swt], chan="swt")
                tr(ps[4][:, 0:128], swt[:, 0:128], ident[:], [swt, ident], [ps[4]])
                cp("act", KwinT[:, i * 128:(i + 1) * 128], ps[4][:, 0:128], [ps[4]], [(KwinT, i)])
                colmax_update(ps[4], ps[4][:, 0:128], slice(0, 128), 2, 128)
                cp("dve", Vwin[:, i, :, 0:64], swt[:, 128:256].rearrange("p (g d) -> p g d", g=2), [swt], [(Vwin, i)])
            P.dma("sp", wins[sm, 0:511, :], swin_d[sm, 1:512, :], chan="d2d")
            cp("dve", cmb[:], cm[:], [cm], [cmb])
            QTv = QT[:].rearrange("p (j q) -> p j q", q=128)
            aQTv = aQT[:].rearrange("p (j q) -> p j q", q=128)
            if sm > 0:
                ms("dve", QTv[:, :, sm - 1], 0.0, [QT])
                ms("dve", aQTv[:, :, sm - 1], 0.0, [aQT])
            cp("dve", QTv[:, :, sm], qsT[:, :, sm], [qsT], [QT])
            cp("dve", aQTv[:, :, sm], aqsT[:, :, sm], [aqsT], [aQT])
            for g in range(2):
                attention(0, g, samp=sm)
        tt("dve", oat[0:R, :], oat[0:R, :], sa[0:R, :], ALU.mult, [oat, sa], [oat])
        for c in range(4):
            tr(ps[2][:, c * R:(c + 1) * R], oat[0:R, c * 128:(c + 1) * 128], ident[0:R, 0:R], [oat, ident], [ps[2]])
        cp("act", oaT[:, :, 0:R], ps[2][:, 0:4 * R].rearrange("p (c s) -> p c s", s=R), [ps[2]], [oaT])
        for (jw, srcT, jg) in ((10, oaT, (6, 7)), (11, obT, (8, 9))):
            wb, kk = next_chunk(jw)
            wv = wb[:].rearrange("p (c n) -> p c n", c=4)
            for half in range(2):
                for kc in range(4):
                    mm(ps[0 + half][:], srcT[:, kc, :], wv[:, kc, half * 512:(half + 1) * 512], kc == 0, kc == 3,
                       [srcT, wb], [ps[half]])
            for half in range(2):
                zchunk(jg[half], ps[2 + half])
                act(sgc[0:R, :], ps[2 + half][0:R, :], AF.Sigmoid, [ps[2 + half]], [sgc])
                dst = mbuf[0:R, half * 512:(half + 1) * 512]
                if jw == 10:
                    tt("dve", dst, sgc[0:R, :], ps[half][0:R, :], ALU.mult, [sgc, ps[half]], [mbuf])
                else:
                    tt("dve", qf[0:R, :], sgc[0:R, :], ps[half][0:R, :], ALU.mult, [sgc, ps[half]], [qf])
                    tt("dve", dst, dst, qf[0:R, :], ALU.add, [mbuf, qf], [mbuf])
        for c in range(8):
            tr(ps[4 + c // 4][:, (c % 4) * R:(c % 4 + 1) * R], mbuf[0:R, c * 128:(c + 1) * 128], ident[0:R, 0:R],
               [mbuf, ident], [ps[4 + c // 4]])
        cp("act", mT[:, 0:4, 0:R], ps[4][:, 0:4 * R].rearrange("p (c s) -> p c s", s=R), [ps[4]], [mT])
        cp("dve", mT[:, 4:8, 0:R], ps[5][:, 0:4 * R].rearrange("p (c s) -> p c s", s=R), [ps[5]], [mT])
        for half in range(2):
            wb, kk = next_chunk(12 + half)
            wv = wb[:].rearrange("p (c n) -> p c n", c=8)
            for kc in range(8):
                mm(ps[6 + half][:], mT[:, kc, :], wv[:, kc, :], kc == 0, kc == 7, [mT, wb], [ps[6 + half]])
        for c in range(8):
            tr(ps[2 + c // 4][0:R, (c % 4) * 128:(c % 4 + 1) * 128], GST[:, c, :], ident[:], [GST, ident], [ps[2 + c // 4]])
        cp("dve", GP[0:R, 0:512], ps[2][0:R, :], [ps[2]], [GP])
        cp("dve", GP[0:R, 512:1024], ps[3][0:R, :], [ps[3]], [GP])
        for half in range(2):
            act(qf[0:R, :], ps[6 + half][0:R, :], AF.Square, [ps[6 + half]], [qf, ss], accum_out=ss[0:R, 4 + half:5 + half])
        tt("dve", ss[0:R, 6:7], ss[0:R, 4:5], ss[0:R, 5:6], ALU.add, [ss], [ss])
        ts("dve", ss[0:R, 6:7], ss[0:R, 6:7], 1.0 / D, 1e-6, ALU.mult, ALU.add, [ss], [ss])
        act(ss[0:R, 7:8], ss[0:R, 6:7], AF.Sqrt, [ss], [ss])
        P.op("dve", lambda e: e.reciprocal(out=ss[0:R, 6:7], in_=ss[0:R, 7:8]), [ss], [ss])
        for half in range(2):
            sl = slice(half * 512, (half + 1) * 512)
            stt(mbuf[0:R, sl], ps[6 + half][0:R, :], ss[0:R, 6:7], GP[0:R, sl], ALU.mult, ALU.mult, [ps[6 + half], ss, GP], [mbuf])
        tt("dve", mbuf[0:R, :], mbuf[0:R, :], xm[0:R, :], ALU.add, [mbuf, xm], [mbuf])
        P.dma("pool", ys, mbuf[0:R, :], reads=[mbuf], chan="yout")

    stop = ""
    if stop != "p0":
        load_x(xt[0], xp[0:128, :], "xt0")
        for t in range(nt):
            phase_a(t)
            if stop.startswith("a"):
                continue
            if t % 4 == 3 and t // 4 < nslot:
                phase_b(t // 4)
    if do_sample:
        phase_s()

    P.emit()
    return nc, P


def make_consts(r):
    f = np.float32
    c = {}
    c["ident"] = np.eye(128, dtype=f)
    inv = (np.float32(500000.0) ** (-(np.arange(8, dtype=f)) / np.float32(8))).astype(f)
    p = np.arange(128)

    def rope_tab(pos):
        ang = (pos.astype(f)[..., None] * inv).astype(f)
        return np.concatenate([np.cos(ang.astype(np.float64)), np.sin(ang.astype(np.float64))], -1).astype(f)

    c["ropeA"] = rope_tab(np.arange(NT)[None, :] * 128 + p[:, None])
    tn = 4 * np.arange(NSLOT) + r
    c["ropeB"] = rope_tab(tn[None, :] * 128 + p[:, None])
    j = np.arange(128)
    slotc = np.zeros((NSLOT, 128, 3, 128), f)
    cmpb = np.zeros((NSLOT, 128, 4, 128), f)
    cidx = (np.arange(4)[:, None] * 128 + np.arange(128)[None, :])
    for n in range(NSLOT):
        b = 4 * n + r
        cur = 2 * b + (p >= 64)
        A = (j[None, :] <= cur[:, None]).astype(f)
        slotc[n, :, 0] = A
        slotc[n, :, 1] = A - 1
        slotc[n, :, 2] = np.where((j[None, :] == cur[:, None]) | (j[None, :] == cur[:, None] - 1), 1e6, -2.0)
        valid = (16 * cidx[None] + 31 <= (128 * b + p)[:, None, None]) & (cidx[None] < 511)
        cmpb[n] = np.where(valid, 0.0, -BIG8)
    c["slotc"] = slotc
    c["cmpb"] = cmpb
    q = p[:, None]
    k = p[None, :]
    winb = np.zeros((128, 8, 128), f)
    for kr in range(8):
        dt = 128 * (r + 4 - kr) + q - k
        winb[:, kr] = np.where((dt >= 0) & (dt < 512), 0.0, -BIG8)
    c["winb"] = winb
    selc = np.zeros((128, 3, 4, 128), f)
    for kr in range(4):
        if kr < r:
            selc[:, 0, kr] = 1.0
        elif kr == r:
            selc[:, 1, kr] = 1.0
            selc[:, 2, kr] = np.where(k <= q, 0.0, -BIG8)
        else:
            selc[:, 1, kr] = 1.0
            selc[:, 2, kr] = -BIG8
    c["selc"] = selc
    cs = 16 * cidx
    ov = (cs[:, :, None] < 64 * j[None, None, :] + 64) & (cs[:, :, None] + 32 > 64 * j[None, None, :]) & (cidx[:, :, None] < 511)
    c["ovl"] = np.ascontiguousarray(ov.transpose(1, 0, 2)).astype(f)
    shm = np.zeros((128, 2, 128), f)
    shb = np.zeros((32, 2, 128), f)
    for s in (1, 2):
        for m in range(128):
            kk = m - s
            if kk >= 0:
                shm[kk, s - 1, m] = 1.0
            else:
                shb[32 + kk, s - 1, m] = 1.0
    c["shm"] = shm
    c["shb"] = shb
    ups = np.ones((32, NSLOT), f)
    if r == 0:
        ups[:, 0] = 0.0
    c["upsc"] = ups
    c["ropeS"] = np.repeat(rope_tab(np.array([2048])), 16, axis=0)
    slotS = np.zeros((128, 3, 128), f)
    A = (j <= 32).astype(f)
    slotS[:, 0] = A[None, :]
    slotS[:, 1] = A[None, :] - 1
    slotS[:, 2] = np.where((j == 31) | (j == 32), 1e6, -2.0)[None, :]
    c["slotS"] = slotS
    smask = np.zeros((128, 4, 128), f)
    smask[:, 0] = np.where(k <= 126, 0.0, -BIG8)
    smask[:, 1] = np.where(k == q, 0.0, -BIG8)
    smask[:, 2] = np.where(k >= 1, 0.0, -BIG8)
    c["smask"] = smask
    oh = np.zeros((128, 16), f)
    oh[np.arange(16), np.arange(16)] = 1.0
    c["onehot"] = oh
    c["pcol"] = np.arange(128, dtype=f).reshape(128, 1)
    return c


_CACHE = {}


def kernel(x_prompt, x_sample, cache_kv_pages, state_win_kv, state_conv, page_table, c_prompt, c_sample,
           w_ada, b_ada, g_pre, g_post, w_in, pe_cmp, w_cmp1, w_cmp2, conv_w, w_br_a, w_br_b, w_out,
           _nt=NT, _nslot=NSLOT):
    f = np.float32
    key = (_nt, _nslot)
    if key not in _CACHE:
        _CACHE[key] = build(_nt, _nslot)
    nc, P = _CACHE[key]
    in_maps = []
    cache_flat = np.ascontiguousarray(cache_kv_pages[0], dtype=f).reshape(-1, 512)
    for c in range(8):
        bi, r = c // 4, c % 4
        xb = np.ascontiguousarray(x_prompt[bi], dtype=f)
        tiles = xb.reshape(NT, 128, D)
        tn = 4 * np.arange(NSLOT) + r
        xmine = np.ascontiguousarray(tiles[tn]).reshape(NSLOT * 128, D)
        xprev = np.zeros((NSLOT, 32, D), f)
        for n in range(NSLOT):
            if tn[n] > 0:
                xprev[n] = xb[tn[n] * 128 - 32:tn[n] * 128]
        m = {
            "xp": xb, "xmine": xmine, "xprev": xprev.reshape(NSLOT * 32, D),
            "cpr": np.ascontiguousarray(c_prompt[bi:bi + 1], dtype=f),
            "w_ada": np.ascontiguousarray(w_ada[0]), "b_ada": np.ascontiguousarray(b_ada), "g_pre": np.ascontiguousarray(g_pre),
            "g_post": np.ascontiguousarray(g_post), "w_in": np.ascontiguousarray(w_in[0]), "pe_cmp": np.ascontiguousarray(pe_cmp[0]),
            "w_cmp1": np.ascontiguousarray(w_cmp1[0]), "w_cmp2": np.ascontiguousarray(w_cmp2[0]),
            "conv_w": np.ascontiguousarray(conv_w[0]), "w_br_a": np.ascontiguousarray(w_br_a[0]),
            "w_br_b": np.ascontiguousarray(w_br_b[0]), "w_out": np.ascontiguousarray(w_out[0]),
        }
        m.update(make_consts(r))
        sl = slice(16 * c, 16 * c + 16)
        m["xs"] = np.ascontiguousarray(x_sample[sl, 0, :], dtype=f)
        m["cs"] = np.ascontiguousarray(c_sample[sl], dtype=f)
        m["ptab"] = np.ascontiguousarray(page_table[sl], dtype=np.int32).reshape(1, 256)
        m["cache"] = cache_flat
        m["swin"] = np.ascontiguousarray(state_win_kv[0, sl], dtype=f).reshape(16, 512, 256)
        m["sconv"] = np.ascontiguousarray(state_conv[0, sl], dtype=f)
        in_maps.append(m)
    res = run_bass_kernel_spmd(nc, in_maps, core_ids=list(range(8)))
    B, T = x_prompt.shape[0], x_prompt.shape[1]
    y_prompt = np.zeros((B, T, D), f)
    kv_rows_prompt = np.zeros((1, B, T, 4, 2, 64), f)
    win_kv_prompt = np.zeros((1, B, 512, 2, 2, 64), f)
    conv_state_prompt = np.zeros((1, B, 2, 512), f)
    for c in range(8):
        bi, r = c // 4, c % 4
        o = res.results[c]
        for n in range(NSLOT):
            t = 4 * n + r
            y_prompt[bi, t * 128:(t + 1) * 128] = o["yp"][n * 128:(n + 1) * 128]
            kv_rows_prompt[0, bi, t * 128:(t + 1) * 128] = o["kvp"][n * 128:(n + 1) * 128].reshape(128, 4, 2, 64)
        win_kv_prompt[0, bi, r * 128:(r + 1) * 128] = o["winp"].reshape(128, 2, 2, 64)
        if r == 3:
            conv_state_prompt[0, bi] = o["convp"]
    nS = x_sample.shape[0]
    y_sample = np.zeros((nS, 1, D), f)
    kv_rows_sample = np.zeros((1, nS, 1, 4, 2, 64), f)
    win_kv_sample = np.zeros((1, nS, 512, 2, 2, 64), f)
    conv_state_sample = np.zeros((1, nS, 2, 512), f)
    for c in range(8):
        o = res.results[c]
        sl = slice(16 * c, 16 * c + 16)
        y_sample[sl, 0] = o["ys"]
        kv_rows_sample[0, sl, 0] = o["kvs"].reshape(16, 4, 2, 64)
        win_kv_sample[0, sl] = o["wins"].reshape(16, 512, 2, 2, 64)
        conv_state_sample[0, sl] = o["convs"]
    return (y_prompt, y_sample, kv_rows_prompt, win_kv_prompt, conv_state_prompt, kv_rows_sample, win_kv_sample,
            conv_state_sample)
```
